# Optimizing a Trainium2 kernel written in Bass

```python
import math
import jax
import jax.numpy as jnp
from jax import lax
import numpy as np


D_MODEL = 1024
BATCH = 4
SEQ = 4096
DEPTH = 2
DEC_BATCH = 8
DEC_SEQ = 16
PAST_LEN = 4096

CHUNK = 64
EPS = 1e-6
CONV_W = 4
NH_A = 4
DK_A = 128
DV_A = 256
D_A = NH_A * DV_A
NH_B = 16
HEADDIM_B = 64
D_B = NH_B * HEADDIM_B
D_STATE = 128
NG_B = 2
HPG_B = NH_B // NG_B
CONV_DIM_B = D_B + 2 * NG_B * D_STATE
NH_C = 8
DK_C = 128
DV_C = 128
D_CK = NH_C * DK_C
D_CV = NH_C * DV_C
CONV_DIM_C = 2 * D_CK + D_CV
D_FF = 2816

SIZES_AB = (NH_A * DK_A, NH_A * DK_A, D_A, D_A, NH_A, NH_A, D_B, CONV_DIM_B, NH_B)
D_IN_AB = 2 * NH_A * DK_A + 2 * D_A + 2 * NH_A + D_B + CONV_DIM_B + NH_B
SIZES_C = (CONV_DIM_C, D_CV, NH_C, NH_C)
D_IN_C = CONV_DIM_C + D_CV + 2 * NH_C

kernel_name = 'hybrid_mlstm_ssd_gdn_macaron_stream_step'


def f32(t):
    return t.astype(jnp.float32)


def rmsnorm(x, g):
    xf = f32(x)
    y = xf * lax.rsqrt(jnp.mean(xf * xf, axis=-1, keepdims=True) + EPS)
    return (y * f32(g)).astype(x.dtype)


def l2norm(x):
    return x * lax.rsqrt(jnp.sum(x * x, axis=-1, keepdims=True) + EPS)


def swiglu(x, w_gate, w_up, w_down):
    return (jax.nn.silu(x @ w_gate) * (x @ w_up)) @ w_down


def split_cols(a, sizes):
    out, off = [], 0
    for s in sizes:
        out.append(a[..., off:off + s])
        off += s
    return out


def tril(L, k=0):
    return jnp.tril(jnp.ones((L, L), dtype=bool), k)


def causal_conv(x, buf, w, b=None):
    xp = jnp.concatenate([buf.astype(x.dtype), x], axis=1)
    y = lax.conv_general_dilated(f32(xp), f32(w)[:, None, :], (1,), 'VALID',
                                 dimension_numbers=('NWC', 'WIO', 'NWC'),
                                 feature_group_count=x.shape[-1])
    if b is not None:
        y = y + f32(b)
    return y, xp[:, xp.shape[1] - (CONV_W - 1):]


def to_chunks(a):
    return jnp.moveaxis(a.reshape(a.shape[0], a.shape[1] // CHUNK, CHUNK, *a.shape[2:]), 1, 0)


def from_chunks(a):
    a = jnp.moveaxis(a, 0, 1)
    return a.reshape(a.shape[0], a.shape[1] * a.shape[2], *a.shape[3:])


def chunk_scan(step, state, xs):
    if xs[0].shape[1] <= CHUNK:
        return step(state, xs)
    state, ys = lax.scan(step, state, tuple(to_chunks(t) for t in xs))
    return state, from_chunks(ys)


def mlstm_step(state, xs):
    C, n, m = state
    q, k, v, ig, lf = xs
    L = q.shape[1]
    causal = tril(L)[None, :, :, None]
    b = jnp.cumsum(lf, axis=1)
    dmat = jnp.where(causal, b[:, :, None, :] - b[:, None, :, :] + ig[:, None, :, :], -jnp.inf)
    inter = b + m[:, None, :]
    m_s = jnp.maximum(inter, dmat.max(axis=2))
    scores = jnp.einsum('bshd,brhd->bsrh', q, k) * jnp.exp(dmat - m_s[:, :, None, :])
    w_inter = jnp.exp(inter - m_s)
    num = jnp.einsum('bsrh,brhe->bshe', scores, v) + w_inter[..., None] * jnp.einsum('bshd,bhde->bshe', q, C)
    den = scores.sum(axis=2) + w_inter * jnp.einsum('bshd,bhd->bsh', q, n)
    h = num / jnp.maximum(jnp.abs(den), jnp.exp(-m_s))[..., None]
    bL = b[:, -1]
    dl = bL[:, None, :] - b + ig
    m_new = jnp.maximum(bL + m, dl.max(axis=1))
    wr = jnp.exp(dl - m_new[:, None, :])
    decay = jnp.exp(bL + m - m_new)
    C = decay[..., None, None] * C + jnp.einsum('brh,brhd,brhe->bhde', wr, k, v)
    n = decay[..., None] * n + jnp.einsum('brh,brhd->bhd', wr, k)
    return (C, n, m_new), h


def ssd_step(S, xs):
    x, bm, cm, dt, a = xs
    L = x.shape[1]
    causal = tril(L)[None, :, :, None, None]
    b = jnp.cumsum(a, axis=1)
    seg = jnp.exp(jnp.where(causal, b[:, :, None] - b[:, None], -jnp.inf))
    cb = jnp.einsum('bsgn,brgn->bsrg', cm, bm)
    y = (jnp.einsum('bsrgj,brgjp->bsgjp', cb[..., None] * seg, dt[..., None] * x)
         + jnp.exp(b)[..., None] * jnp.einsum('bsgn,bgjpn->bsgjp', cm, S))
    bL = b[:, -1]
    S = (jnp.exp(bL)[..., None, None] * S
         + jnp.einsum('brgj,brgjp,brgn->bgjpn', jnp.exp(bL[:, None] - b) * dt, x, bm))
    return S, y


def gdn_step(S, xs):
    q, k, v, beta, g = xs
    L = q.shape[1]
    gam = jnp.swapaxes(jnp.cumsum(g, axis=1), 1, 2)
    dec = jnp.exp(jnp.where(tril(L), gam[..., :, None] - gam[..., None, :], -jnp.inf))
    qh, kh, vh = jnp.swapaxes(q, 1, 2), jnp.swapaxes(k, 1, 2), jnp.swapaxes(v, 1, 2)
    bh = jnp.swapaxes(beta, 1, 2)[..., None]
    a_mat = jnp.where(tril(L, -1), bh * dec * (kh @ jnp.swapaxes(kh, -1, -2)), 0.0)
    rhs = jnp.concatenate([bh * vh, bh * jnp.exp(gam)[..., None] * kh], axis=-1)
    sol = lax.linalg.triangular_solve(a_mat, rhs, left_side=True, lower=True, unit_diagonal=True)
    u = sol[..., :DV_C] - sol[..., DV_C:] @ S
    o = jnp.exp(gam)[..., None] * (qh @ S) + ((qh @ jnp.swapaxes(kh, -1, -2)) * dec) @ u
    gL = gam[..., -1]
    S = (jnp.exp(gL)[..., None, None] * S
         + jnp.einsum('bhr,bhrd,bhre->bhde', jnp.exp(gL[..., None] - gam), kh, u))
    return S, jnp.swapaxes(o, 1, 2)


def mixer_ab(h, state, w_in, w_out, b_i, b_f, mlstm_norm, conv_w, conv_b, dt_bias, a_log, d_skip, ssd_norm):
    bsz, T, _ = h.shape
    C0, n0, m0, S0, buf0 = state
    q, k, v, o, ig, fg, z, xbc, dt = split_cols(h @ w_in, SIZES_AB)
    q = f32(q).reshape(bsz, T, NH_A, DK_A)
    k = f32(k).reshape(bsz, T, NH_A, DK_A) * (DK_A ** -0.5)
    v = f32(v).reshape(bsz, T, NH_A, DV_A)
    ig = f32(ig) + f32(b_i)
    lf = jax.nn.log_sigmoid(f32(fg) + f32(b_f))
    (C1, n1, m1), h_a = chunk_scan(mlstm_step, (f32(C0), f32(n0), f32(m0)), (q, k, v, ig, lf))
    h_a = jax.nn.sigmoid(f32(o)) * rmsnorm(h_a, mlstm_norm).reshape(bsz, T, D_A)
    xbc, buf1 = causal_conv(xbc, buf0, conv_w, conv_b)
    xs, bm, cm = split_cols(jax.nn.silu(xbc), (D_B, NG_B * D_STATE, NG_B * D_STATE))
    dt = jax.nn.softplus(f32(dt) + f32(dt_bias))
    a = -jnp.exp(f32(a_log)) * dt
    x5 = xs.reshape(bsz, T, NG_B, HPG_B, HEADDIM_B)
    S1, y = chunk_scan(ssd_step, f32(S0).reshape(bsz, NG_B, HPG_B, HEADDIM_B, D_STATE),
                       (x5, bm.reshape(bsz, T, NG_B, D_STATE), cm.reshape(bsz, T, NG_B, D_STATE),
                        dt.reshape(bsz, T, NG_B, HPG_B), a.reshape(bsz, T, NG_B, HPG_B)))
    y = (y + f32(d_skip).reshape(NG_B, HPG_B, 1) * x5).reshape(bsz, T, D_B) * jax.nn.silu(f32(z))
    y = rmsnorm(y.reshape(bsz, T, NG_B, D_B // NG_B), ssd_norm.reshape(NG_B, D_B // NG_B)).reshape(bsz, T, D_B)
    out = jnp.concatenate([h_a, y], axis=-1).astype(h.dtype) @ w_out
    return out, (C1, n1, m1, S1.reshape(bsz, NH_B, HEADDIM_B, D_STATE), buf1)


def mixer_c(h, state, w_in, w_out, conv_w, dt_bias, a_log, gdn_norm):
    bsz, T, _ = h.shape
    S0, buf0 = state
    qkv, z, b, a = split_cols(h @ w_in, SIZES_C)
    qkv, buf1 = causal_conv(qkv, buf0, conv_w)
    q, k, v = split_cols(jax.nn.silu(qkv), (D_CK, D_CK, D_CV))
    q = l2norm(q.reshape(bsz, T, NH_C, DK_C)) * (DK_C ** -0.5)
    k = l2norm(k.reshape(bsz, T, NH_C, DK_C))
    v = v.reshape(bsz, T, NH_C, DV_C)
    beta = jax.nn.sigmoid(f32(b))
    g = -jnp.exp(f32(a_log)) * jax.nn.softplus(f32(a) + f32(dt_bias))
    S1, o = chunk_scan(gdn_step, f32(S0), (q, k, v, beta, g))
    o = rmsnorm(o, gdn_norm) * jax.nn.silu(f32(z).reshape(bsz, T, NH_C, DV_C))
    return o.reshape(bsz, T, D_CV).astype(h.dtype) @ w_out, (S1, buf1)


def trunk(x, ab_state, c_state, norm_g, norm_f, ffn_w_gate, ffn_w_up, ffn_w_down,
          ab_w_in, ab_w_out, mlstm_b_i, mlstm_b_f, mlstm_norm, ssd_conv_w, ssd_conv_b,
          ssd_dt_bias, ssd_a_log, ssd_d, ssd_norm, gdn_w_in, gdn_w_out, gdn_conv_w,
          gdn_dt_bias, gdn_a_log, gdn_norm):
    for l in range(DEPTH):
        x = x + 0.5 * swiglu(rmsnorm(x, norm_g[l, 0]), ffn_w_gate[l, 0], ffn_w_up[l, 0], ffn_w_down[l, 0])
        hn = rmsnorm(x, norm_g[l, 1])
        if l % 2 == 0:
            out, ab_state = mixer_ab(hn, ab_state, ab_w_in, ab_w_out, mlstm_b_i, mlstm_b_f, mlstm_norm,
                                     ssd_conv_w, ssd_conv_b, ssd_dt_bias, ssd_a_log, ssd_d, ssd_norm)
        else:
            out, c_state = mixer_c(hn, c_state, gdn_w_in, gdn_w_out, gdn_conv_w, gdn_dt_bias, gdn_a_log, gdn_norm)
        x = x + out
        x = x + 0.5 * swiglu(rmsnorm(x, norm_g[l, 2]), ffn_w_gate[l, 1], ffn_w_up[l, 1], ffn_w_down[l, 1])
    ab_state = tuple(s.astype(x.dtype) for s in ab_state)
    c_state = tuple(s.astype(x.dtype) for s in c_state)
    return rmsnorm(x, norm_f), ab_state, c_state


def _dt_bias(key, n):
    dt = jnp.exp(jax.random.uniform(key, (n,), jnp.float32, math.log(1e-3), math.log(1e-1)))
    return dt + jnp.log(-jnp.expm1(-dt))


def setup_inputs(seed: int = 0) -> dict:
    key = jax.random.key(seed)
    ks = jax.random.split(key, 32)

    def nrm(k, shape, scale):
        return scale * jax.random.normal(k, shape, jnp.float32)

    return {
        'x_prompt': nrm(ks[0], (BATCH, SEQ, D_MODEL), 1.0),
        'x_sample': nrm(ks[1], (DEC_BATCH, DEC_SEQ, D_MODEL), 1.0),
        'state_mlstm_C': nrm(ks[2], (DEC_BATCH, NH_A, DK_A, DV_A), 0.1),
        'state_mlstm_n': nrm(ks[3], (DEC_BATCH, NH_A, DK_A), 0.1),
        'state_mlstm_m': nrm(ks[4], (DEC_BATCH, NH_A), 0.5),
        'state_ssd': nrm(ks[5], (DEC_BATCH, NH_B, HEADDIM_B, D_STATE), 0.1),
        'cache_ssd_conv': nrm(ks[6], (DEC_BATCH, CONV_W - 1, CONV_DIM_B), 1.0),
        'state_gdn': nrm(ks[7], (DEC_BATCH, NH_C, DK_C, DV_C), 0.1),
        'cache_gdn_conv': nrm(ks[8], (DEC_BATCH, CONV_W - 1, CONV_DIM_C), 1.0),
        'norm_g': 1.0 + nrm(ks[9], (DEPTH, 3, D_MODEL), 0.02),
        'norm_f': 1.0 + nrm(ks[10], (D_MODEL,), 0.02),
        'ffn_w_gate': nrm(ks[11], (DEPTH, 2, D_MODEL, D_FF), D_MODEL ** -0.5),
        'ffn_w_up': nrm(ks[12], (DEPTH, 2, D_MODEL, D_FF), D_MODEL ** -0.5),
        'ffn_w_down': nrm(ks[13], (DEPTH, 2, D_FF, D_MODEL), D_FF ** -0.5),
        'ab_w_in': nrm(ks[14], (D_MODEL, D_IN_AB), D_MODEL ** -0.5),
        'ab_w_out': nrm(ks[15], (D_A + D_B, D_MODEL), (D_A + D_B) ** -0.5),
        'mlstm_b_i': nrm(ks[16], (NH_A,), 0.1),
        'mlstm_b_f': jnp.linspace(3.0, 6.0, NH_A, dtype=jnp.float32) + nrm(ks[17], (NH_A,), 0.1),
        'mlstm_norm': 1.0 + nrm(ks[18], (NH_A, DV_A), 0.02),
        'ssd_conv_w': nrm(ks[19], (CONV_W, CONV_DIM_B), CONV_W ** -0.5),
        'ssd_conv_b': nrm(ks[20], (CONV_DIM_B,), 0.02),
        'ssd_dt_bias': _dt_bias(ks[21], NH_B),
        'ssd_a_log': jnp.log(jax.random.uniform(ks[22], (NH_B,), jnp.float32, 1.0, 16.0)),
        'ssd_d': 1.0 + nrm(ks[23], (NH_B,), 0.1),
        'ssd_norm': 1.0 + nrm(ks[24], (D_B,), 0.02),
        'gdn_w_in': nrm(ks[25], (D_MODEL, D_IN_C), D_MODEL ** -0.5),
        'gdn_w_out': nrm(ks[26], (D_CV, D_MODEL), D_CV ** -0.5),
        'gdn_conv_w': nrm(ks[27], (CONV_W, CONV_DIM_C), CONV_W ** -0.5),
        'gdn_dt_bias': _dt_bias(ks[28], NH_C),
        'gdn_a_log': jnp.log(jax.random.uniform(ks[29], (NH_C,), jnp.float32, 1.0, 16.0)),
        'gdn_norm': 1.0 + nrm(ks[30], (DV_C,), 0.02),
    }


def reference(x_prompt, x_sample, state_mlstm_C, state_mlstm_n, state_mlstm_m, state_ssd,
              cache_ssd_conv, state_gdn, cache_gdn_conv, norm_g, norm_f, ffn_w_gate, ffn_w_up,
              ffn_w_down, ab_w_in, ab_w_out, mlstm_b_i, mlstm_b_f, mlstm_norm, ssd_conv_w,
              ssd_conv_b, ssd_dt_bias, ssd_a_log, ssd_d, ssd_norm, gdn_w_in, gdn_w_out,
              gdn_conv_w, gdn_dt_bias, gdn_a_log, gdn_norm):
    weights = (norm_g, norm_f, ffn_w_gate, ffn_w_up, ffn_w_down, ab_w_in, ab_w_out, mlstm_b_i,
               mlstm_b_f, mlstm_norm, ssd_conv_w, ssd_conv_b, ssd_dt_bias, ssd_a_log, ssd_d,
               ssd_norm, gdn_w_in, gdn_w_out, gdn_conv_w, gdn_dt_bias, gdn_a_log, gdn_norm)
    bp = x_prompt.shape[0]
    fdt = jnp.float32
    ab0 = (jnp.zeros((bp, NH_A, DK_A, DV_A), fdt), jnp.zeros((bp, NH_A, DK_A), fdt),
           jnp.zeros((bp, NH_A), fdt), jnp.zeros((bp, NH_B, HEADDIM_B, D_STATE), fdt),
           jnp.zeros((bp, CONV_W - 1, CONV_DIM_B), x_prompt.dtype))
    c0 = (jnp.zeros((bp, NH_C, DK_C, DV_C), fdt), jnp.zeros((bp, CONV_W - 1, CONV_DIM_C), x_prompt.dtype))
    y_prompt, (p_C, p_n, p_m, p_ssd, p_ssd_conv), (p_gdn, p_gdn_conv) = trunk(x_prompt, ab0, c0, *weights)
    y_sample, (s_C, s_n, s_m, s_ssd, s_ssd_conv), (s_gdn, s_gdn_conv) = trunk(
        x_sample, (state_mlstm_C, state_mlstm_n, state_mlstm_m, state_ssd, cache_ssd_conv),
        (state_gdn, cache_gdn_conv), *weights)
    return (y_prompt, y_sample, p_C, p_n, p_m, p_ssd, p_ssd_conv, p_gdn, p_gdn_conv,
            s_C, s_n, s_m, s_ssd, s_ssd_conv, s_gdn, s_gdn_conv)
```

```python
import numpy as np
from contextlib import ExitStack
import concourse.bass as bass
import concourse.mybir as mybir
from concourse.bass_utils import run_bass_kernel_spmd

F32 = mybir.dt.float32
BF16 = mybir.dt.bfloat16
AF = mybir.ActivationFunctionType
ALU = mybir.AluOpType

ENGS = ['pe', 'act', 'dve', 'pool', 'sp']
NDSEM = 12
import os
PLIM = int(os.environ.get('K_PLIM', '3'))

D = 1024
DFF = 2816
NFF = 22
EPS = 1e-6
TB = 512
DEC_SEQ = 16


class Buf:
    __slots__ = ('name', 'w', 'r', 'excl')

    def __init__(self, name='', excl=False):
        self.name = name
        self.w = None
        self.r = []
        self.excl = excl


class Sched:
    def __init__(self, nc, same_engine_sync=True):
        self.nc = nc
        self.ops = {e: [] for e in ENGS}
        self.seen = {e: {f: -1 for f in ENGS} for e in ENGS}
        self.seen_dma = {e: {} for e in ENGS}
        self.targets = {e: set() for e in ENGS}
        self.ndma = {e: 0 for e in ENGS}
        self.same_engine_sync = same_engine_sync
        self.out_tokens = []

    def _need(self, eng, tok, waits):
        if tok is None:
            return
        if tok[0] == 'e':
            _, f, k = tok
            if f == eng and (eng == 'pe' or not self.same_engine_sync):
                return
            if self.seen[eng][f] >= k:
                return
            self.seen[eng][f] = k
            self.targets[f].add(k)
            waits.append(tok)
        elif tok[0] == 'c':
            if getattr(self, '_seen_cc', {}).get(eng, -1) >= tok[1]:
                return
            if not hasattr(self, '_seen_cc'):
                self._seen_cc = {}
            self._seen_cc[eng] = tok[1]
            waits.append(tok)
        else:
            _, q, seq = tok
            key = (q, seq % NDSEM)
            if self.seen_dma[eng].get(key, -1) >= seq:
                return
            self.seen_dma[eng][key] = seq
            waits.append(tok)

    def op(self, eng, fn, reads=(), writes=(), dma=False, is_output=False, cc=False):
        waits = []
        for b in reads:
            self._need(eng, b.w, waits)
            if b.excl:
                for t in b.r:
                    if t[0] == 'e' and t[1] != eng:
                        self._need(eng, t, waits)
        for b in writes:
            self._need(eng, b.w, waits)
            for t in b.r:
                self._need(eng, t, waits)
        idx = len(self.ops[eng])
        if eng == 'pool' and not dma:
            lp = getattr(self, '_last_pool', None)
            if lp is not None:
                self._need('pool', lp, waits)
            self._last_pool = ('e', 'pool', idx)
        if dma:
            seq = self.ndma[eng]
            self.ndma[eng] += 1
            lim = PLIM if eng == 'pool' else NDSEM
            if seq >= lim:
                self._need(eng, ('d', eng, seq - lim), waits)
            tok = ('d', eng, seq)
        elif cc:
            seq = None
            self.ncc = getattr(self, 'ncc', 0) + 1
            tok = ('c', self.ncc - 1)
        else:
            seq = None
            tok = ('e', eng, idx)
        self.ops[eng].append(dict(fn=fn, waits=waits, dma=seq, cc=cc))
        for b in reads:
            if tok[0] == 'e':
                b.r = [t for t in b.r if not (t[0] == 'e' and t[1] == tok[1])]
            b.r.append(tok)
        for b in writes:
            b.w = tok
            b.r = []
        if is_output:
            self.out_tokens.append(tok)
        return tok

    def barrier(self):
        last = {}
        for e in ENGS:
            if self.ops[e]:
                last[e] = ('e', e, len(self.ops[e]) - 1)
        for e in ENGS:
            waits = []
            for f, t in last.items():
                if f != e:
                    self._need(e, t, waits)
            for q in ENGS:
                lim = PLIM if q == 'pool' else NDSEM
                for sq_ in range(max(0, self.ndma[q] - lim), self.ndma[q]):
                    self._need(e, ('d', q, sq_), waits)
            if waits:
                self.ops[e].append(dict(fn=None, waits=waits, dma=None, cc=False))

    def finish(self):
        waits = []
        for t in self.out_tokens:
            self._need('sp', t, waits)
        self.ops['sp'].append(dict(fn=None, waits=waits, dma=None, cc=False))

    def emit(self):
        nc = self.nc
        with ExitStack() as st:
            esem = {e: st.enter_context(nc.semaphore('es_' + e)) for e in ENGS}
            ccsem = st.enter_context(nc.semaphore('cc_sem'))
            dsem = {e: [st.enter_context(nc.semaphore('ds_%s%d' % (e, i))) for i in range(NDSEM)]
                    for e in ENGS if self.ndma[e] > 0}
            cnt = {}
            for e in ENGS:
                c = 0
                m = {}
                for k in sorted(self.targets[e]):
                    c += 1
                    m[k] = c
                cnt[e] = m
            block = st.enter_context(nc.Block())

            def body(ename, eh):
                for idx, o in enumerate(self.ops[ename]):
                    for t in o['waits']:
                        if t[0] == 'e':
                            eh.wait_ge(esem[t[1]], cnt[t[1]][t[2]])
                        elif t[0] == 'c':
                            eh.wait_ge(ccsem, t[1] + 1)
                        else:
                            eh.wait_ge(dsem[t[1]][t[2] % NDSEM], 16 * (t[2] // NDSEM + 1))
                    if o['fn'] is None:
                        if idx in cnt[ename]:
                            eh.nop().then_inc(esem[ename], 1)
                        continue
                    inst = o['fn'](eh)
                    if o.get('cc'):
                        inst.then_inc(ccsem, 1)
                        if idx in cnt[ename]:
                            eh.nop().then_inc(esem[ename], 1)
                    elif o['dma'] is not None:
                        inst.then_inc(dsem[ename][o['dma'] % NDSEM], 16)
                        if idx in cnt[ename]:
                            eh.nop().then_inc(esem[ename], 1)
                    elif idx in cnt[ename]:
                        inst.then_inc(esem[ename], 1)

            @block.tensor
            def _(eh):
                body('pe', eh)

            @block.scalar
            def _(eh):
                body('act', eh)

            @block.vector
            def _(eh):
                body('dve', eh)

            @block.gpsimd
            def _(eh):
                body('pool', eh)

            @block.sync
            def _(eh):
                body('sp', eh)


class V:
    __slots__ = ('ap', 'buf')

    def __init__(self, ap, buf):
        self.ap = ap
        self.buf = buf


class Tl:
    def __init__(self, h, name=''):
        self.h = h
        self.buf = Buf(name)

    def __getitem__(self, idx):
        return V(self.h[idx], self.buf)

    def v(self, ap):
        return V(ap, self.buf)


class DT:
    def __init__(self, ap, name=''):
        self.ap = ap
        self.buf = Buf(name)

    def v(self, ap=None):
        return V(self.ap if ap is None else ap, self.buf)


class Builder:
    SB_LO = 17536
    SB_HI = 229344

    def __init__(self, nblk):
        self.nblk = nblk
        self.ntok = nblk * TB + DEC_SEQ
        self.nc = bass.Bass("TRN2", target_bir_lowering=False)
        self.S = Sched(self.nc)
        self.persist_off = self.SB_LO
        self.phase_base = None
        self.phase_off = None
        self.nps = 0
        self.din = {}
        self.dout = {}

    def _alloc(self, name, shape, dt, off):
        n = 1
        for s in shape[1:]:
            n *= s
        nbytes = n * (4 if dt == F32 else 2)
        nbytes = (nbytes + 63) // 64 * 64
        h = self.nc.alloc_sbuf_tensor_at(name, list(shape), dt, offset=off)
        return Tl(h, name), off + nbytes

    def P(self, name, shape, dt=F32):
        t, self.persist_off = self._alloc(name, shape, dt, self.persist_off)
        assert self.persist_off <= self.SB_HI, name
        return t

    def phase_begin(self, kind=None):
        if self.phase_base is None:
            self.phase_base = self.persist_off
            self.cur_kind = None
            self.kind_tiles = {}
        if kind is None or kind != self.cur_kind:
            if self.cur_kind is not None or kind is None:
                self.S.barrier()
            self.cur_kind = kind
            self.kind_tiles = {}
        self.phase_off = self.phase_base

    def A(self, name, shape, dt=F32):
        if self.cur_kind is not None and name in self.kind_tiles:
            return self.kind_tiles[name]
        t, self.phase_off = self._alloc(name, shape, dt, self.phase_off)
        assert self.phase_off <= self.SB_HI, (name, self.phase_off)
        if self.cur_kind is not None:
            self.kind_tiles[name] = t
        return t

    def din_t(self, name, shape, dt=F32):
        ap = self.nc.dram_tensor(name, list(shape), dt, kind="ExternalInput").ap()
        d = DT(ap, name)
        self.din[name] = d
        return d

    def dout_t(self, name, shape, dt=F32):
        ap = self.nc.dram_tensor(name, list(shape), dt, kind="ExternalOutput").ap()
        d = DT(ap, name)
        self.dout[name] = d
        return d

    def mm(self, out, lhsT, rhs, start=True, stop=True):
        self.S.op('pe', lambda e: e.matmul(out.ap, lhsT=lhsT.ap, rhs=rhs.ap, start=start, stop=stop),
                  reads=[lhsT.buf, rhs.buf], writes=[out.buf])

    def tr(self, out, in_, ident):
        self.S.op('pe', lambda e: e.transpose(out.ap, in_.ap, ident.ap),
                  reads=[in_.buf, ident.buf], writes=[out.buf])

    def act(self, out, in_, func, bias=None, scale=1.0, accum=None, eng='act'):
        reads = [in_.buf]
        kw = {}
        if bias is not None:
            if isinstance(bias, V):
                reads.append(bias.buf)
                kw['bias'] = bias.ap
            else:
                kw['bias'] = bias
        if isinstance(scale, V):
            reads.append(scale.buf)
            kw['scale'] = scale.ap
        else:
            kw['scale'] = scale
        writes = [out.buf]
        if accum is not None:
            writes.append(accum.buf)
            kw['accum_out'] = accum.ap
        self.S.op('act', lambda e: e.activation(out.ap, in_.ap, func, **kw), reads=reads, writes=writes)

    def tt(self, out, in0, in1, op, eng='dve'):
        self.S.op(eng, lambda e: e.tensor_tensor(out.ap, in0.ap, in1.ap, op),
                  reads=[in0.buf, in1.buf], writes=[out.buf])

    def ts(self, out, in0, s1, op0, s2=None, op1=None, eng='dve', accum=None):
        reads = [in0.buf]
        a1 = s1
        a2 = s2
        if isinstance(s1, V):
            reads.append(s1.buf)
            a1 = s1.ap
        if isinstance(s2, V):
            reads.append(s2.buf)
            a2 = s2.ap
        kw = {}
        writes = [out.buf]
        if accum is not None:
            kw['accum_out'] = accum.ap
            writes.append(accum.buf)
        if op1 is None:
            self.S.op(eng, lambda e: e.tensor_scalar(out.ap, in0.ap, a1, None, op0, **kw), reads=reads, writes=writes)
        else:
            self.S.op(eng, lambda e: e.tensor_scalar(out.ap, in0.ap, a1, a2, op0, op1, **kw), reads=reads, writes=writes)

    def stt(self, out, in0, sc, in1, op0, op1):
        reads = [in0.buf, in1.buf]
        a = sc
        if isinstance(sc, V):
            reads.append(sc.buf)
            a = sc.ap
        self.S.op('dve', lambda e: e.scalar_tensor_tensor(out.ap, in0.ap, a, in1.ap, op0, op1),
                  reads=reads, writes=[out.buf])

    def cp(self, out, in_, eng='dve'):
        if eng == 'act':
            self.S.op('act', lambda e: e.copy(out.ap, in_.ap), reads=[in_.buf], writes=[out.buf])
        else:
            self.S.op(eng, lambda e: e.tensor_copy(out.ap, in_.ap), reads=[in_.buf], writes=[out.buf])

    def rsum(self, out, in_):
        self.S.op('dve', lambda e: e.tensor_reduce(out.ap, in_.ap, mybir.AxisListType.X, ALU.add),
                  reads=[in_.buf], writes=[out.buf])

    def recip(self, out, in_):
        self.S.op('dve', lambda e: e.reciprocal(out.ap, in_.ap), reads=[in_.buf], writes=[out.buf])

    def scan(self, out, d0, d1, init, op0, op1):
        reads = [d0.buf, d1.buf]
        a = init
        if isinstance(init, V):
            reads.append(init.buf)
            a = init.ap
        self.S.op('dve', lambda e: e.tensor_tensor_scan(out.ap, d0.ap, d1.ap, a, op0, op1),
                  reads=reads, writes=[out.buf])

    def memset(self, out, val, eng='pool'):
        self.S.op(eng, lambda e: e.memset(out.ap, val), writes=[out.buf])

    def dma(self, out, in_, q='sp', is_output=False, nc_ok=False):
        if nc_ok:
            fn = lambda e: e.dma_start(out=out.ap, in_=in_.ap, allow_slow_non_contiguous=True)
        else:
            fn = lambda e: e.dma_start(out=out.ap, in_=in_.ap)
        self.S.op(q, fn, reads=[in_.buf], writes=[out.buf], dma=True, is_output=is_output)

    def wload(self, dst, src_dt, src_ap, key):
        if not hasattr(self, 'wcache'):
            self.wcache = {}
        if key not in self.wcache:
            self.dma(dst, V(src_ap, src_dt.buf), q='pool')
            shp = list(dst.ap.shape)
            nm = "wc_" + "_".join(str(k) for k in key)
            ap = self.nc.dram_tensor(nm, shp, BF16, kind="Internal").ap()
            d = DT(ap, nm)
            self.wcache[key] = d
            self.dma(d.v(), dst, q='sp')
        else:
            self.dma(dst, self.wcache[key].v(), q='sp')

    def wprefetch(self, shape, src_dt, src_ap, key):
        if not hasattr(self, 'wcache'):
            self.wcache = {}
        if key in self.wcache:
            return
        nm = "wc_" + "_".join(str(k) for k in key)
        ap = self.nc.dram_tensor(nm, list(shape), BF16, kind="Internal").ap()
        d = DT(ap, nm)
        self.wcache[key] = d
        self.dma(d.v(), V(src_ap, src_dt.buf), q='pool')

    def ps(self):
        t = self.psb[self.nps % 8]
        self.nps += 1
        return t

    def build(self):
        nc = self.nc
        S = self.S
        A = self.A
        P = self.P
        dma = self.dma
        DKs = 128 ** -0.5
        NOWN = 4
        NTOK = NOWN * TB + DEC_SEQ
        RG = [[0, 1]] if os.environ.get("K_RG") == "pair" else [[0, 1], [2, 3], [4, 5], [6, 7]]
        xT = self.din_t("xT", [D, NTOK])
        yT = self.dout_t("yT", [D, NTOK])
        w_gate = self.din_t("ffn_w_gate", [2, 2, D, DFF])
        w_up = self.din_t("ffn_w_up", [2, 2, D, DFF])
        w_down = self.din_t("ffn_w_down", [2, 2, DFF, D])
        g_col = self.din_t("g_col", [128, 6, 8])
        gf_col = self.din_t("gf_col", [128, 8])
        rmask_d = self.din_t("rmask", [128, 2])
        abw = self.din_t("abw", [D, 2828])
        ab_w_out = self.din_t("ab_w_out", [2048, D])
        d_bi = self.din_t("b_i", [2, 1]); d_bf = self.din_t("b_f", [2, 1])
        d_gnB = self.din_t("mlstm_norm_b", [128, 512])
        d_cw = self.din_t("ssd_cw", [128, 6, 4]); d_cb = self.din_t("ssd_cb", [128, 6])
        d_dtb = self.din_t("ssd_dtb", [8, 1]); d_alog = self.din_t("ssd_alog", [8, 1])
        d_dskB = self.din_t("ssd_d_b", [128, 8]); d_snB = self.din_t("ssd_norm_b", [128, 512])
        gw = self.din_t("gw", [D, 2056])
        gdn_w_out = self.din_t("gdn_w_out", [1024, D])
        d_gcw = self.din_t("gdn_cw", [128, 12, 4])
        d_gdtb = self.din_t("gdn_dtb", [4, 1]); d_galog = self.din_t("gdn_alog", [4, 1])
        d_gdnB = self.din_t("gdn_norm_b", [128, 128])
        s_Cx = [self.din_t("s_Cx%d" % j, [128, 2, 257]) for j in range(2)]
        s_m = [self.din_t("s_m%d" % j, [2, 1]) for j in range(2)]
        s_ST = [self.din_t("s_ST%d" % j, [128, 8, 64]) for j in range(2)]
        s_hb = [self.din_t("s_hb%d" % j, [128, 6, 3]) for j in range(2)]
        s_SG = [self.din_t("s_SG%d" % j, [128, 4, 128]) for j in range(2)]
        s_hc = [self.din_t("s_hc%d" % j, [128, 12, 3]) for j in range(2)]
        o_Cx = [self.dout_t("o_Cx%d" % j, [128, 2, 257]) for j in range(3)]
        o_m = [self.dout_t("o_m%d" % j, [2, 1]) for j in range(3)]
        o_ST = [self.dout_t("o_ST%d" % j, [128, 8, 64]) for j in range(3)]
        o_hb = [self.dout_t("o_hb%d" % j, [128, 6, 3]) for j in range(3)]
        o_SG = [self.dout_t("o_SG%d" % j, [128, 4, 128]) for j in range(3)]
        o_hc = [self.dout_t("o_hc%d" % j, [128, 12, 3]) for j in range(3)]

        def internal(name, shape):
            return DT(nc.dram_tensor(name, list(shape), BF16, kind="Internal").ap(), name)
        own = [(b * TB, TB) for b in range(NOWN)] + [(NOWN * TB, DEC_SEQ)]
        G1in = [internal("g1in%d" % b, [1024, n]) for b, (_, n) in enumerate(own)]
        G1out = [internal("g1out%d" % b, [2048, n]) for b, (_, n) in enumerate(own)]
        G3in = [internal("g3in%d" % b, [1024, n]) for b, (_, n) in enumerate(own)]
        G3out = [internal("g3out%d" % b, [2048, n]) for b, (_, n) in enumerate(own)]
        seqn = [TB] * 8 + [DEC_SEQ] * 2
        G2in = [internal("g2in%d" % k, [1024, n]) for k, n in enumerate(seqn)]
        G2out = [internal("g2out%d" % k, [2048, n]) for k, n in enumerate(seqn)]
        G4in = [internal("g4in%d" % k, [512, n]) for k, n in enumerate(seqn)]
        G4out = [internal("g4out%d" % k, [1024, n]) for k, n in enumerate(seqn)]

        def allgather(gi, go):
            S.op('pool', lambda e: e.collective_compute("AllGather", ALU.bypass, replica_groups=RG,
                                                        ins=[gi.ap], outs=[go.ap]),
                 reads=[gi.buf], writes=[go.buf], cc=True)

        self.psb = []
        self._st = ExitStack()
        for i in range(8):
            h = self._st.enter_context(nc.psum_tensor("psb%d" % i, [128, 512], F32))
            t = Tl(h, "psb%d" % i)
            t.buf.excl = True
            t.hb = h[:, :].bitcast(BF16)
            self.psb.append(t)

        def pbv(t, *idx):
            return V(t.hb[idx], t.buf)

        xb = [P("x%d" % b, [128, 8, n]) for b, (_, n) in enumerate(own)]
        hn = P("hn", [128, 8, TB], BF16)
        gcol = P("gcol", [128, 6, 8]); gfcol = P("gfcol", [128, 8]); rmask = P("rmask", [128, 2])
        ones_f = P("ones_f", [128, 128]); ones_b = P("ones_b", [128, 128], BF16)
        rstd = P("rstd", [128, TB])
        ident_f = P("ident_f", [128, 128]); ident_b = P("ident_b", [128, 128], BF16)
        maskT = P("maskT", [128, 128]); maskS = P("maskS", [128, 128])
        ones16 = P("ones16", [16, 128]); onesr = P("onesr", [16, TB]); sel16 = P("sel16", [16, 8, 128])
        bi = P("bi", [2, 1]); nbf = P("nbf", [2, 1])
        gnB = P("gnB", [128, 512]); snB = P("snB", [128, 512]); dskB = P("dskB", [128, 8])
        cw = P("cw", [128, 6, 4]); cb = P("cb", [128, 6]); dtb = P("dtb", [8, 1]); negA = P("negA", [8, 1])
        Cx = P("Cx", [128, 2, 257]); ST = P("ST", [128, 8, 64]); Sb = P("Sb", [128, 8, 64], BF16)
        hist_b = P("hist_b", [128, 6, 3])
        Fc = P("Fc", [2, 1]); Gc = P("Gc", [2, 1]); mo = P("mo", [2, 1])
        gcw = P("gcw", [128, 12, 4]); gdtb = P("gdtb", [4, 1]); gnegA = P("gnegA", [4, 1]); gdnB = P("gdnB", [128, 128])
        SG = P("SG", [128, 4, 128]); SGb = P("SGb", [128, 4, 128], BF16); hist_c = P("hist_c", [128, 12, 3])

        for (dst, src) in ((gcol, g_col), (gfcol, gf_col), (rmask, rmask_d), (bi, d_bi), (nbf, d_bf), (gnB, d_gnB),
                           (snB, d_snB), (dskB, d_dskB), (cw, d_cw), (cb, d_cb), (dtb, d_dtb), (negA, d_alog),
                           (gcw, d_gcw), (gdtb, d_gdtb), (gnegA, d_galog), (gdnB, d_gdnB)):
            dma(dst[:], src.v())
        self.memset(ones_f[:], 1.0 / D); self.memset(ones_b[:], 1.0 / D)
        self.memset(ones16[:], 1.0); self.memset(onesr[:], 1.0)
        self.memset(ident_f[:], 0.0)
        S.op('pool', lambda e: e.affine_select(out=ident_f.h[:], in_=ident_f.h[:], pattern=[[-1, 128]],
                                               compare_op=ALU.not_equal, fill=1.0, base=0, channel_multiplier=1),
             reads=[ident_f.buf], writes=[ident_f.buf])
        self.cp(ident_b[:], ident_f[:])
        self.memset(maskT[:], 1.0)
        S.op('pool', lambda e: e.affine_select(out=maskT.h[:], in_=maskT.h[:], pattern=[[1, 128]],
                                               compare_op=ALU.is_ge, fill=0.0, base=0, channel_multiplier=-1),
             reads=[maskT.buf], writes=[maskT.buf])
        self.memset(maskS[:], 1.0)
        S.op('pool', lambda e: e.affine_select(out=maskS.h[:], in_=maskS.h[:], pattern=[[1, 128]],
                                               compare_op=ALU.is_gt, fill=0.0, base=0, channel_multiplier=-1),
             reads=[maskS.buf], writes=[maskS.buf])
        self.memset(sel16[:], 0.0)
        S.op('pool', lambda e: e.affine_select(out=sel16.h[:], in_=sel16.h[:], pattern=[[-1, 8], [0, 128]],
                                               compare_op=ALU.not_equal, fill=1.0, base=0, channel_multiplier=1),
             reads=[sel16.buf], writes=[sel16.buf])
        self.ts(nbf[:], nbf[:], -1.0, ALU.mult)
        self.act(negA[:], negA[:], AF.Exp); self.ts(negA[:], negA[:], -1.0, ALU.mult)
        self.act(gnegA[:], gnegA[:], AF.Exp); self.ts(gnegA[:], gnegA[:], -1.0, ALU.mult)

        def rmsnorm(n, gv, out_t, sq, x):
            self.act(sq[:, :, 0:n], x[:, :, 0:n], AF.Square)
            p = self.ps()
            for kc in range(8):
                self.mm(p[:, 0:n], ones_b[:], sq[:, kc, 0:n], start=(kc == 0), stop=(kc == 7))
            self.act(rstd[:, 0:n], p[:, 0:n], AF.Sqrt, bias=EPS)
            self.recip(rstd[:, 0:n], rstd[:, 0:n])
            for kc in range(8):
                self.stt(out_t[:, kc, 0:n], x[:, kc, 0:n], gv(kc), rstd[:, 0:n], ALU.mult, ALU.mult)

        def ffn(l, i, n, x):
            self.phase_begin(('ffn',))
            sq = A("sq", [128, 8, TB], BF16)
            act_t = A("ffn_act", [128, NFF, TB], BF16)
            wd = A("ffn_wd", [128, NFF, D], BF16)
            wgu = [[A("ffn_wg%d" % b, [128, 8, 512], BF16), A("ffn_wu%d" % b, [128, 8, 512], BF16)]
                   for b in range(2)]
            sil = [A("ffn_sil0", [128, TB])] * 2
            rmsnorm(n, lambda kc: gcol[:, l * 3 + 2 * i, kc:kc + 1], hn, sq, x)
            wg_src = w_gate.ap[l, i].rearrange("(k p) f -> p k f", p=128)
            wu_src = w_up.ap[l, i].rearrange("(k p) f -> p k f", p=128)
            wd_src = w_down.ap[l, i].rearrange("(f p) d -> p f d", p=128)
            groups = [(g * 512, 512) for g in range(5)] + [(2560, 256)]
            for gi, (c0, ncol) in enumerate(groups):
                b = gi % 2
                self.wload(wgu[b][0][:, :, 0:ncol], w_gate, wg_src[:, :, c0:c0 + ncol], ('wg', l, i, gi))
                self.wload(wgu[b][1][:, :, 0:ncol], w_up, wu_src[:, :, c0:c0 + ncol], ('wu', l, i, gi))
                if gi == 1:
                    for q in range(2):
                        self.wload(wd[:, q * 11:(q + 1) * 11, :], w_down, wd_src[:, q * 11:(q + 1) * 11, :], ('wd', l, i, q))
                for j in range(ncol // 128):
                    f = c0 // 128 + j
                    pg = self.ps()
                    pu = self.ps()
                    for kc in range(8):
                        self.mm(pg[:, 0:n], wgu[b][0][:, kc, j * 128:(j + 1) * 128], hn[:, kc, 0:n],
                                start=(kc == 0), stop=(kc == 7))
                    for kc in range(8):
                        self.mm(pu[:, 0:n], wgu[b][1][:, kc, j * 128:(j + 1) * 128], hn[:, kc, 0:n],
                                start=(kc == 0), stop=(kc == 7))
                    sl = sil[f % 2]
                    self.act(sl[:, 0:n], pg[:, 0:n], AF.Silu)
                    self.tt(act_t[:, f, 0:n], sl[:, 0:n], pu[:, 0:n], ALU.mult)
            for dc in range(8):
                p = self.ps()
                for f in range(NFF):
                    self.mm(p[:, 0:n], wd[:, f, dc * 128:(dc + 1) * 128], act_t[:, f, 0:n],
                            start=(f == 0), stop=(f == NFF - 1))
                self.stt(x[:, dc, 0:n], p[:, 0:n], 0.5, x[:, dc, 0:n], ALU.mult, ALU.add)

        def hn_exchange(gidx, n, x, gin, gout):
            self.phase_begin(('hnx',))
            sq = A("sq", [128, 8, TB], BF16)
            rmsnorm(n, lambda kc: gcol[:, gidx, kc:kc + 1], hn, sq, x)
            dma(gin.v(gin.ap.rearrange("(k p) t -> p k t", p=128)), hn[:, :, 0:n])
            allgather(gin, gout)

        def out_proj_sel(w_dt, nfc, gouts, n, x, tag):
            self.phase_begin(('ops', nfc))
            wb = [A("wb0", [128, 8, 512], BF16), A("wb1", [128, 8, 512], BF16)]
            cand = [A("cand0", [128, nfc, TB], BF16), A("cand1", [128, nfc, TB], BF16)]
            hT = A("hTf", [128, nfc, TB], BF16)
            half = nfc // 2 * 128
            for ci, go in enumerate(gouts):
                for r in range(2):
                    src = go.ap[r * half:(r + 1) * half, :].rearrange("(f p) t -> p f t", p=128)
                    if nfc == 16:
                        dma(cand[ci][:, r * 4:r * 4 + 4, 0:n], go.v(src[:, 0:4, :]))
                        dma(cand[ci][:, 8 + r * 4:8 + r * 4 + 4, 0:n], go.v(src[:, 4:8, :]))
                    else:
                        dma(cand[ci][:, r * 4:r * 4 + 4, 0:n], go.v(src[:, 0:4, :]))
            self.ts(hT[:, :, 0:n], cand[0][:, :, 0:n], rmask[:, 0:1], ALU.mult)
            self.stt(hT[:, :, 0:n], cand[1][:, :, 0:n], rmask[:, 1:2], hT[:, :, 0:n], ALU.mult, ALU.add)
            wo_src = w_dt.ap.rearrange("(f p) d -> p f d", p=128)
            nfh = nfc // 8
            li = 0
            for dh in range(2):
                pacc = [self.ps() for _ in range(4)]
                for fh in range(nfh):
                    w = wb[li % 2]
                    li += 1
                    self.wload(w[:], w_dt, wo_src[:, fh * 8:(fh + 1) * 8, dh * 512:(dh + 1) * 512], ('wo', nfc, dh, fh))
                    for j in range(4):
                        for f8 in range(8):
                            self.mm(pacc[j][:, 0:n], w[:, f8, j * 128:(j + 1) * 128], hT[:, fh * 8 + f8, 0:n],
                                    start=(fh == 0 and f8 == 0), stop=(fh == nfh - 1 and f8 == 7))
                for j in range(4):
                    dc = dh * 4 + j
                    self.tt(x[:, dc, 0:n], pacc[j][:, 0:n], x[:, dc, 0:n], ALU.add)

        def conv_chunk(p, n, fc, hist, cwt, cbt, stage, cacc, outT):
            stg = stage
            acc = cacc
            self.cp(stg[:, 0:3], hist[:, fc, :])
            self.cp(stg[:, 3:3 + n], p[:, 0:n], eng='act')
            self.ts(acc[:, 0:n], stg[:, 0:n], cwt[:, fc, 0:1], ALU.mult)
            for j in range(1, 4):
                self.stt(acc[:, 0:n], stg[:, j:j + n], cwt[:, fc, j:j + 1], acc[:, 0:n], ALU.mult, ALU.add)
            if cbt is not None:
                self.act(outT[:, fc, 0:n], acc[:, 0:n], AF.Silu, bias=cbt[:, fc:fc + 1])
            else:
                self.act(outT[:, fc, 0:n], acc[:, 0:n], AF.Silu)
            self.cp(hist[:, fc, :], stg[:, n:n + 3])

        def mixer_ab(n, L, g1, r, g2in, g2out):
            nch = n // L
            NM = 2
            self.phase_begin(('ab', n))
            hT = A("hT", [128, 8, n], BF16)
            wb = [A("wb0", [128, 8, 512], BF16), A("wb1", [128, 8, 512], BF16)]
            wgt = A("wgt", [128, 8, 4], BF16); wdt = A("wdt", [128, 8, 8], BF16)
            qT = A("qT", [128, NM, n], BF16); kT = A("kT", [128, NM, n], BF16)
            k_tok = A("k_tok", [128, nch, 256], BF16)
            v_ext = A("v_ext", [128, nch, NM, 257], BF16)
            so = A("so", [128, nch, 512], BF16); zs = A("zs", [128, nch, 512], BF16)
            xbcT = A("xbcT", [128, 6, n], BF16)
            x_tok = A("x_tok", [128, nch, 512], BF16); bm_tok = A("bm_tok", [128, nch, 128], BF16)
            h_tok = A("h_tok", [128, 1024], BF16)
            stage = A("stage0", [128, 3 + n]); cacc = A("cacc0", [128, n])
            R = [A("row%d" % i, [16, n]) for i in range(10)]
            gT = A("gT", [128, nch, 6]); gS = A("gS", [128, nch, 32])
            decB = A("decB", [128, nch, NM]); decS = A("decS", [128, nch, 8])
            D4 = A("D4", [NM, nch, NM]); D16 = A("D16", [8, nch, 8]); Gpv = A("Gpv", [NM, nch]); dec4 = A("dec4", [NM, nch])
            PTm = A("PTm", [128, 128], BF16)
            vu = [A("vu0", [128, 257], BF16), A("vu1", [128, 257], BF16)]
            Cb = A("Cb", [128, NM, 257], BF16)
            cbm = A("cbm", [128, 128])
            seg = [A("seg0", [128, 128]), A("seg1", [128, 128])]
            MT = A("MT", [128, 8, 128], BF16)
            xd = A("xd", [128, 512], BF16); xw = A("xw", [128, 512], BF16)
            ya = A("ya", [128, 512]); yb = A("yb", [128, 512]); hraw = A("hraw", [128, 256]); junk = yb
            c1 = A("c1", [128, 1]); c2 = A("c2", [128, 1]); c3 = A("c3", [128, 1]); c4 = A("c4", [128, 1])

            dma(hn[:, :, 0:n], g1.v(g1.ap[r * 1024:(r + 1) * 1024, :].rearrange("(k p) t -> p k t", p=128)))
            win = abw.ap.rearrange("(k p) c -> p k c", p=128)
            lw = [0]

            def loadw(c0, ncol):
                w = wb[lw[0] % 2]
                lw[0] += 1
                self.wload(w[:, :, 0:ncol], abw, win[:, :, c0:c0 + ncol], ('abin', c0))
                return w

            def fm_proj(w, j, M=128):
                p = self.ps()
                for kc in range(8):
                    self.mm(p[0:M, 0:n], w[:, kc, j * 128:j * 128 + M], hn[:, kc, 0:n], start=(kc == 0), stop=(kc == 7))
                return p

            def tm_proj(w, c, c0=0, ncol=512):
                p = self.ps()
                for kc in range(8):
                    self.mm(p[0:L, 0:ncol], hn[:, kc, c * L:(c + 1) * L], w[:, kc, c0:c0 + ncol], start=(kc == 0), stop=(kc == 7))
                return p

            self.wload(wgt[:], abw, win[:, :, 1536:1540], ('abg',))
            self.wload(wdt[:], abw, win[:, :, 2820:2828], ('abdt',))
            w = loadw(0, 512)
            for h in range(NM):
                p = fm_proj(w, h)
                self.cp(qT[:, h, 0:n], p[:, 0:n], eng='act')
            for h in range(NM):
                p = fm_proj(w, NM + h)
                self.ts(kT[:, h, 0:n], p[:, 0:n], DKs, ALU.mult)
            for c in range(nch):
                p = tm_proj(w, c, 256, 256)
                self.ts(k_tok[0:L, c, :], p[0:L, 0:256], DKs, ALU.mult)
            self.memset(v_ext[:, :, :, 256:257], 1.0)
            w = loadw(512, 512)
            for c in range(nch):
                p = tm_proj(w, c)
                self.cp(v_ext[0:L, c, 0:NM, 0:256], V(p.h[0:L, 0:512].rearrange("p (a b) -> p a b", b=256), p.buf), eng='act')
            w = loadw(1024, 512)
            for c in range(nch):
                p = tm_proj(w, c)
                self.act(so[0:L, c, :], p[0:L, 0:512], AF.Sigmoid)
            w = loadw(1540, 512)
            for c in range(nch):
                p = tm_proj(w, c)
                self.act(zs[0:L, c, :], p[0:L, 0:512], AF.Silu)
            w = loadw(2052, 512)
            for j in range(4):
                p = fm_proj(w, j)
                conv_chunk(p, n, j, hist_b, cw, cb, stage, cacc, xbcT)
            w = loadw(2564, 256)
            for j in range(2):
                p = fm_proj(w, j)
                conv_chunk(p, n, 4 + j, hist_b, cw, cb, stage, cacc, xbcT)
            t1, Fn, a_, G_, em, u_, w_, tmp = R[0], R[1], R[2], R[3], R[4], R[5], R[6], R[7]
            pig = self.ps()
            for kc in range(8):
                self.mm(pig[0:NM, 0:n], wgt[:, kc, 0:NM], hn[:, kc, 0:n], start=(kc == 0), stop=(kc == 7))
            pfg = self.ps()
            for kc in range(8):
                self.mm(pfg[0:NM, 0:n], wgt[:, kc, NM:2 * NM], hn[:, kc, 0:n], start=(kc == 0), stop=(kc == 7))
            self.act(t1[0:NM, 0:n], pfg[0:NM, 0:n], AF.Exp, bias=nbf[:], scale=-1.0)
            self.act(t1[0:NM, 0:n], t1[0:NM, 0:n], AF.Ln, bias=1.0)
            self.scan(Fn[0:NM, 0:n], onesr[0:NM, 0:n], t1[0:NM, 0:n], Fc[:], ALU.mult, ALU.add)
            self.stt(a_[0:NM, 0:n], pig[0:NM, 0:n], bi[:], Fn[0:NM, 0:n], ALU.add, ALU.add)
            self.scan(G_[0:NM, 0:n], onesr[0:NM, 0:n], a_[0:NM, 0:n], Gc[:], ALU.mult, ALU.max)
            self.tt(tmp[0:NM, 0:n], Fn[0:NM, 0:n], G_[0:NM, 0:n], ALU.subtract)
            self.act(em[0:NM, 0:n], tmp[0:NM, 0:n], AF.Exp)

            def r3(t, np_):
                return t.h[0:np_, 0:n].rearrange("p (c l) -> p c l", l=L)
            gend = V(r3(G_, NM)[:, :, L - 1:L].to_broadcast([NM, nch, L]), G_.buf)
            self.tt(V(r3(tmp, NM), tmp.buf), V(r3(a_, NM), a_.buf), gend, ALU.subtract)
            self.act(u_[0:NM, 0:n], tmp[0:NM, 0:n], AF.Exp)
            self.tt(V(r3(tmp, NM), tmp.buf), V(r3(G_, NM), G_.buf), gend, ALU.subtract)
            self.act(w_[0:NM, 0:n], tmp[0:NM, 0:n], AF.Exp, scale=-1.0)
            self.cp(Gpv[0:NM, 0:1], Gc[:])
            if nch > 1:
                self.cp(V(Gpv.h[0:NM, 1:nch].unsqueeze(2), Gpv.buf), V(r3(G_, NM)[:, 0:nch - 1, L - 1:L], G_.buf))
            self.tt(V(dec4.h[0:NM, 0:nch].unsqueeze(2), dec4.buf), V(Gpv.h[0:NM, 0:nch].unsqueeze(2), Gpv.buf),
                    V(r3(G_, NM)[:, :, L - 1:L], G_.buf), ALU.subtract)
            self.act(dec4[0:NM, 0:nch], dec4[0:NM, 0:nch], AF.Exp)
            self.tt(D4[0:NM, 0:nch, :], V(dec4.h[0:NM, 0:nch].unsqueeze(2).to_broadcast([NM, nch, NM]), dec4.buf),
                    V(ident_f.h[0:NM, 0:NM].unsqueeze(1).to_broadcast([NM, nch, NM]), ident_f.buf), ALU.mult)
            p = self.ps()
            self.mm(p[:, 0:nch * NM], ones16[0:NM, :], V(D4.h[0:NM, 0:nch, :].rearrange("p c h -> p (c h)"), D4.buf))
            self.cp(V(decB.h[:, 0:nch, :].rearrange("p c h -> p (c h)"), decB.buf), p[:, 0:nch * NM])
            self.cp(Fc[:], Fn[0:NM, n - 1:n])
            self.cp(Gc[:], G_[0:NM, n - 1:n])
            p = self.ps()
            for c in range(nch):
                for qi, rt in enumerate((u_, w_, em)):
                    self.tr(p[0:L, c * 6 + qi * NM:c * 6 + qi * NM + NM], rt[0:NM, c * L:(c + 1) * L], ident_f[0:NM, 0:NM])
            self.cp(V(gT.h[0:L, 0:nch, :].rearrange("p c h -> p (c h)"), gT.buf), p[0:L, 0:nch * 6])
            dt_, ar, b_, eb, nb, e2 = R[0], R[1], R[2], R[8], R[9], R[7]
            pdt = self.ps()
            for kc in range(8):
                self.mm(pdt[0:8, 0:n], wdt[:, kc, :], hn[:, kc, 0:n], start=(kc == 0), stop=(kc == 7))
            self.act(dt_[0:8, 0:n], pdt[0:8, 0:n], AF.Exp, bias=dtb[:])
            self.act(dt_[0:8, 0:n], dt_[0:8, 0:n], AF.Ln, bias=1.0)
            self.ts(ar[0:8, 0:n], dt_[0:8, 0:n], negA[:], ALU.mult)
            for c in range(nch):
                self.scan(b_[0:8, c * L:(c + 1) * L], onesr[0:8, 0:L], ar[0:8, c * L:(c + 1) * L], 0.0, ALU.mult, ALU.add)
            self.act(eb[0:8, 0:n], b_[0:8, 0:n], AF.Exp)
            self.ts(nb[0:8, 0:n], b_[0:8, 0:n], -1.0, ALU.mult)
            bLb = V(r3(b_, 8)[:, :, L - 1:L].to_broadcast([8, nch, L]), b_.buf)
            self.tt(V(r3(e2, 8), e2.buf), bLb, V(r3(b_, 8), b_.buf), ALU.subtract)
            self.act(e2[0:8, 0:n], e2[0:8, 0:n], AF.Exp)
            self.tt(e2[0:8, 0:n], e2[0:8, 0:n], dt_[0:8, 0:n], ALU.mult)
            self.tt(D16[:, 0:nch, :], V(r3(eb, 8)[:, :, L - 1:L].to_broadcast([8, nch, 8]), eb.buf),
                    V(ident_f.h[0:8, 0:8].unsqueeze(1).to_broadcast([8, nch, 8]), ident_f.buf), ALU.mult)
            p = self.ps()
            self.mm(p[:, 0:nch * 8], ones16[0:8, :], V(D16.h[:, 0:nch, :].rearrange("p c h -> p (c h)"), D16.buf))
            self.cp(V(decS.h[:, 0:nch, :].rearrange("p c h -> p (c h)"), decS.buf), p[:, 0:nch * 8])
            p = self.ps()
            for c in range(nch):
                for qi, rt in enumerate((dt_, nb, eb, e2)):
                    self.tr(p[0:L, c * 32 + qi * 8:c * 32 + qi * 8 + 8], rt[0:8, c * L:(c + 1) * L], ident_f[0:8, 0:8])
            self.cp(V(gS.h[0:L, 0:nch, :].rearrange("p c h -> p (c h)"), gS.buf), p[0:L, 0:nch * 32])
            for c in range(nch):
                p = self.ps()
                for fc in range(4):
                    self.tr(pbv(p, slice(0, L), slice(fc * 128, (fc + 1) * 128)), xbcT[:, fc, c * L:(c + 1) * L], ident_b[:])
                self.tr(pbv(p, slice(0, L), slice(512, 640)), xbcT[:, 4, c * L:(c + 1) * L], ident_b[:])
                self.cp(x_tok[0:L, c, :], pbv(p, slice(0, L), slice(0, 512)), eng='act')
                self.cp(bm_tok[0:L, c, :], pbv(p, slice(0, L), slice(512, 640)))
            junkm = [A("junkm%d" % i, [128, 256]) for i in range(NM)]
            PTms = [A("PTm%d" % i, [128, 128], BF16) for i in range(NM)]
            hraws = [A("hraw%d" % i, [128, 256]) for i in range(NM)]
            ccols = [[A("cm%d_%d" % (i, j), [128, 1]) for j in range(3)] for i in range(NM)]

            def m_chain(c, h):
                cs, ce = c * L, (c + 1) * L
                ht = h_tok
                PTm = PTms[h]; junk = junkm[h]; hraw = hraws[h]; c1, c2, c3 = ccols[h]
                v_u = vu[h % 2]
                p1 = self.psb[2 * h]
                self.mm(p1[0:L, 0:L], kT[:, h, cs:ce], qT[:, h, cs:ce])
                yield
                self.tt(PTm[0:L, 0:L], p1[0:L, 0:L], maskT[0:L, 0:L], ALU.mult)
                yield
                self.ts(v_u[0:L, :], v_ext[0:L, c, h, :], gT[0:L, c, h:h + 1], ALU.mult)
                yield
                self.ts(Cx[:, h, :], Cx[:, h, :], decB[:, c, h:h + 1], ALU.mult)
                yield
                self.cp(Cb[:, h, :], Cx[:, h, :], eng='act')
                yield
                p2 = self.psb[2 * h + 1]
                self.mm(p2[0:L, 0:257], PTm[0:L, 0:L], v_u[0:L, :], start=True, stop=False)
                yield
                self.mm(p2[0:L, 0:257], qT[:, h, cs:ce], Cb[:, h, :], start=False, stop=True)
                yield
                p3 = self.psb[2 * h]
                self.mm(p3[:, 0:257], k_tok[0:L, c, h * 128:(h + 1) * 128], v_u[0:L, :])
                yield
                self.tt(Cx[:, h, :], p3[:, 0:257], Cx[:, h, :], ALU.add)
                yield
                wcol = gT[0:L, c, NM + h:NM + h + 1]
                emcol = gT[0:L, c, 2 * NM + h:2 * NM + h + 1]
                self.act(c1[0:L, :], p2[0:L, 256:257], AF.Abs, scale=wcol)
                yield
                self.tt(c1[0:L, :], c1[0:L, :], emcol, ALU.max)
                yield
                self.recip(c1[0:L, :], c1[0:L, :])
                yield
                self.tt(c2[0:L, :], c1[0:L, :], wcol, ALU.mult)
                yield
                self.act(junk[0:L, 0:256], p2[0:L, 0:256], AF.Square, scale=c2[0:L, :])
                yield
                self.rsum(c3[0:L, :], junk[0:L, 0:256])
                yield
                self.ts(hraw[0:L, :], p2[0:L, 0:256], c2[0:L, :], ALU.mult)
                yield
                self.act(c3[0:L, :], c3[0:L, :], AF.Sqrt, bias=EPS, scale=1.0 / 256)
                yield
                self.recip(c3[0:L, :], c3[0:L, :])
                yield
                self.stt(hraw[0:L, :], hraw[0:L, :], c3[0:L, :], gnB[0:L, h * 256:(h + 1) * 256], ALU.mult, ALU.mult)
                yield
                self.tt(ht[0:L, h * 256:(h + 1) * 256], hraw[0:L, :], so[0:L, c, h * 256:(h + 1) * 256], ALU.mult)
                yield

            def s_chain(c):
                cs, ce = c * L, (c + 1) * L
                ht = h_tok
                junk = yb
                p1 = self.psb[4]
                self.mm(p1[0:L, 0:L], xbcT[:, 4, cs:ce], xbcT[:, 5, cs:ce])
                yield
                self.tt(cbm[0:L, 0:L], p1[0:L, 0:L], maskT[0:L, 0:L], ALU.mult)
                yield
                pbb = None
                for hh in range(8):
                    j = hh % 4
                    if j == 0:
                        pbb = self.psb[5 + hh // 4]
                    self.mm(pbb[0:L, j * 128:j * 128 + L], sel16[0:8, hh, 0:L], b_[0:8, cs:ce])
                    sg = seg[hh % 2]
                    self.ts(sg[0:L, 0:L], pbb[0:L, j * 128:j * 128 + L], gS[0:L, c, 8 + hh:9 + hh], ALU.add, 0.0, ALU.min)
                    self.act(sg[0:L, 0:L], sg[0:L, 0:L], AF.Exp)
                    self.tt(MT[0:L, hh, 0:L], sg[0:L, 0:L], cbm[0:L, 0:L], ALU.mult)

                def v3(t, ap):
                    return V(ap.rearrange("p (h e) -> p h e", e=64), t.buf)
                xg = x_tok.h[0:L, c, :]
                self.tt(v3(xd, xd.h[0:L, :]), v3(x_tok, xg),
                        V(gS.h[0:L, c, 0:8].unsqueeze(2).to_broadcast([L, 8, 64]), gS.buf), ALU.mult)
                self.tt(v3(xw, xw.h[0:L, :]), v3(x_tok, xg),
                        V(gS.h[0:L, c, 24:32].unsqueeze(2).to_broadcast([L, 8, 64]), gS.buf), ALU.mult)
                pY1 = self.psb[7]
                for hh in range(8):
                    self.mm(pY1[0:L, hh * 64:(hh + 1) * 64], MT[0:L, hh, 0:L], xd[0:L, hh * 64:(hh + 1) * 64])
                pY2 = self.psb[4]
                self.mm(pY2[0:L, 0:512], xbcT[:, 5, cs:ce], V(Sb.h[:, :, :].rearrange("p h e -> p (h e)"), Sb.buf))
                yield
                self.tt(v3(ya, ya.h[0:L, :]), v3(pY2, pY2.h[0:L, 0:512]),
                        V(gS.h[0:L, c, 16:24].unsqueeze(2).to_broadcast([L, 8, 64]), gS.buf), ALU.mult)
                self.tt(ya[0:L, :], pY1[0:L, 0:512], ya[0:L, :], ALU.add)
                yield
                self.tt(v3(yb, yb.h[0:L, :]), v3(x_tok, xg),
                        V(dskB.h[0:L, 0:8].unsqueeze(2).to_broadcast([L, 8, 64]), dskB.buf), ALU.mult)
                self.tt(ya[0:L, :], ya[0:L, :], yb[0:L, :], ALU.add)
                yield
                self.tt(ya[0:L, :], ya[0:L, :], zs[0:L, c, :], ALU.mult)
                yield
                self.act(junk[0:L, :], ya[0:L, :], AF.Square)
                yield
                self.rsum(c4[0:L, :], junk[0:L, :])
                yield
                self.act(c4[0:L, :], c4[0:L, :], AF.Sqrt, bias=EPS, scale=1.0 / 512)
                yield
                self.recip(c4[0:L, :], c4[0:L, :])
                yield
                self.stt(ht[0:L, 512:1024], ya[0:L, :], c4[0:L, :], snB[0:L, :], ALU.mult, ALU.mult)
                yield
                pS = self.psb[5]
                self.mm(pS[:, 0:512], bm_tok[0:L, c, :], xw[0:L, :])
                yield
                self.tt(ST[:, :, :], ST[:, :, :],
                        V(decS.h[:, c, 0:8].unsqueeze(2).to_broadcast([128, 8, 64]), decS.buf), ALU.mult)
                stf = V(ST.h[:, :, :].rearrange("p h e -> p (h e)"), ST.buf)
                self.tt(stf, pS[:, 0:512], stf, ALU.add)
                yield
                self.cp(Sb[:, :, :], ST[:, :, :], eng='act')
                yield

            def interleave(gens):
                gens = list(gens)
                while gens:
                    for g in list(gens):
                        try:
                            next(g)
                        except StopIteration:
                            gens.remove(g)

            for c in range(nch):
                cs, ce = c * L, (c + 1) * L
                ht = h_tok
                interleave([m_chain(c, h) for h in range(NM)] + [s_chain(c)])
                p = self.ps()
                for f8 in range(8):
                    self.tr(pbv(p, slice(0, 128), slice(f8 * 128, f8 * 128 + L)), ht[0:L, f8 * 128:(f8 + 1) * 128],
                            ident_b[0:L, 0:L])
                self.cp(hT[:, 0:8, cs:ce], V(p.hb[:, 0:1024].rearrange("p (f t) -> p f t", t=128)[:, :, 0:L], p.buf), eng='act')
            dma(g2in.v(g2in.ap.rearrange("(f p) t -> p f t", p=128)), hT[:, :, 0:n])
            allgather(g2in, g2out)

        def mixer_c(n, L, g3, r, g4in, g4out):
            nch = n // L
            nsq = {128: 6, 16: 3}[L]
            NH = 4
            self.phase_begin(('c', n))
            hT = A("hT", [128, NH, n], BF16)
            wb = [A("wb0", [128, 8, 512], BF16), A("wb1", [128, 8, 512], BF16)]
            wba = A("wba", [128, 8, 8], BF16)
            qkvT = A("qkvT", [128, 12, n], BF16)
            zs = A("zs", [128, nch, 512], BF16)
            k_tok = A("k_tok", [128, nch, 512], BF16); v_tok = A("v_tok", [128, nch, 512], BF16)
            h_tok = A("h_tok", [128, 512], BF16)
            stage = A("stage0", [128, 3 + n]); cacc = A("cacc0", [128, n])
            R = [A("row%d" % i, [16, n]) for i in range(7)]
            gC = A("gC", [128, nch, 24])
            decC = A("decC", [128, nch, NH]); D8 = A("D8", [NH, nch, NH])
            sqh = A("sqh0", [128, n], BF16); rst = A("rst0", [128, n])
            Xs = [A("X%d" % i, [128, 128]) for i in range(NH)]
            XTs = [A("XT%d" % i, [128, 128]) for i in range(NH)]
            TTs = [A("TT%d" % i, [128, 128]) for i in range(NH)]
            KKs = [A("KK%d" % i, [128, 128]) for i in range(NH)]
            dTs_ = [A("dT%d" % i, [128, 128]) for i in range(NH)]
            tmpm = [A("tmpm%d" % i, [128, 128]) for i in range(NH)]
            R1 = tmpm
            R2 = KKs
            U0s = [[A("U0s_%d_%d" % (c, h), [128, 128]) for h in range(NH)] for c in range(min(2, nch))]
            WTb = [[A("WTb_%d_%d" % (c, h), [128, 128], BF16) for h in range(NH)] for c in range(min(2, nch))]
            QKd = [[A("QKd_%d_%d" % (c, h), [128, 128], BF16) for h in range(NH)] for c in range(min(2, nch))]
            ub = [A("ub%d" % i, [128, 128], BF16) for i in range(NH)]
            kw = [A("kw%d" % i, [128, 128], BF16) for i in range(NH)]
            o1s = [A("o1s%d" % i, [128, 128]) for i in range(NH)]
            oo = [A("oo%d" % i, [128, 128]) for i in range(NH)]
            jk = o1s
            cc = [A("cc%d" % i, [128, 1]) for i in range(NH)]

            dma(hn[:, :, 0:n], g3.v(g3.ap[r * 1024:(r + 1) * 1024, :].rearrange("(k p) t -> p k t", p=128)))
            win = gw.ap.rearrange("(k p) c -> p k c", p=128)
            lw = [0]

            def loadw(c0, ncol):
                w = wb[lw[0] % 2]
                lw[0] += 1
                self.wload(w[:, :, 0:ncol], gw, win[:, :, c0:c0 + ncol], ('gin', c0))
                return w

            self.wload(wba[:], gw, win[:, :, 2048:2056], ('gba',))
            for g3_ in range(3):
                w = loadw(g3_ * 512, 512)
                for j in range(4):
                    fc = g3_ * 4 + j
                    p = self.ps()
                    for kc in range(8):
                        self.mm(p[:, 0:n], w[:, kc, j * 128:(j + 1) * 128], hn[:, kc, 0:n], start=(kc == 0), stop=(kc == 7))
                    conv_chunk(p, n, fc, hist_c, gcw, None, stage, cacc, qkvT)
            w = loadw(1536, 512)
            for c in range(nch):
                p = self.ps()
                for kc in range(8):
                    self.mm(p[0:L, 0:512], hn[:, kc, c * L:(c + 1) * L], w[:, kc, 0:512], start=(kc == 0), stop=(kc == 7))
                self.act(zs[0:L, c, :], p[0:L, 0:512], AF.Silu)
            for fc in range(2 * NH):
                self.act(sqh[:, 0:n], qkvT[:, fc, 0:n], AF.Square)
                p = self.ps()
                self.mm(p[:, 0:n], ones_b[:], sqh[:, 0:n])
                self.act(rst[:, 0:n], p[:, 0:n], AF.Sqrt, bias=EPS, scale=float(D))
                self.recip(rst[:, 0:n], rst[:, 0:n])
                if fc < NH:
                    self.stt(qkvT[:, fc, 0:n], qkvT[:, fc, 0:n], DKs, rst[:, 0:n], ALU.mult, ALU.mult)
                else:
                    self.tt(qkvT[:, fc, 0:n], qkvT[:, fc, 0:n], rst[:, 0:n], ALU.mult)
            beta, nbeta, sp_, gam, egam, ngam, e2 = R
            g_ = sp_
            pb_ = self.ps()
            for kc in range(8):
                self.mm(pb_[0:NH, 0:n], wba[:, kc, 0:NH], hn[:, kc, 0:n], start=(kc == 0), stop=(kc == 7))
            pa_ = self.ps()
            for kc in range(8):
                self.mm(pa_[0:NH, 0:n], wba[:, kc, NH:2 * NH], hn[:, kc, 0:n], start=(kc == 0), stop=(kc == 7))
            self.act(beta[0:NH, 0:n], pb_[0:NH, 0:n], AF.Sigmoid)
            self.ts(nbeta[0:NH, 0:n], beta[0:NH, 0:n], -1.0, ALU.mult)
            self.act(sp_[0:NH, 0:n], pa_[0:NH, 0:n], AF.Exp, bias=gdtb[:])
            self.act(sp_[0:NH, 0:n], sp_[0:NH, 0:n], AF.Ln, bias=1.0)
            self.ts(g_[0:NH, 0:n], sp_[0:NH, 0:n], gnegA[:], ALU.mult)
            for c in range(nch):
                self.scan(gam[0:NH, c * L:(c + 1) * L], onesr[0:NH, 0:L], g_[0:NH, c * L:(c + 1) * L], 0.0, ALU.mult, ALU.add)
            self.act(egam[0:NH, 0:n], gam[0:NH, 0:n], AF.Exp)
            self.ts(ngam[0:NH, 0:n], gam[0:NH, 0:n], -1.0, ALU.mult)

            def r3(t, np_):
                return t.h[0:np_, 0:n].rearrange("p (c l) -> p c l", l=L)
            gLb = V(r3(gam, NH)[:, :, L - 1:L].to_broadcast([NH, nch, L]), gam.buf)
            self.tt(V(r3(e2, NH), e2.buf), gLb, V(r3(gam, NH), gam.buf), ALU.subtract)
            self.act(e2[0:NH, 0:n], e2[0:NH, 0:n], AF.Exp)
            self.tt(sp_[0:NH, 0:n], beta[0:NH, 0:n], egam[0:NH, 0:n], ALU.mult)
            self.tt(D8[:, 0:nch, :], V(r3(egam, NH)[:, :, L - 1:L].to_broadcast([NH, nch, NH]), egam.buf),
                    V(ident_f.h[0:NH, 0:NH].unsqueeze(1).to_broadcast([NH, nch, NH]), ident_f.buf), ALU.mult)
            p = self.ps()
            self.mm(p[:, 0:nch * NH], ones16[0:NH, :], V(D8.h[:, 0:nch, :].rearrange("p c h -> p (c h)"), D8.buf))
            self.cp(V(decC.h[:, 0:nch, :].rearrange("p c h -> p (c h)"), decC.buf), p[:, 0:nch * NH])
            p = self.ps()
            for c in range(nch):
                for qi, rt in enumerate((beta, ngam, egam, e2, sp_, nbeta)):
                    self.tr(p[0:L, c * 24 + qi * NH:c * 24 + qi * NH + NH], rt[0:NH, c * L:(c + 1) * L], ident_f[0:NH, 0:NH])
            self.cp(V(gC.h[0:L, 0:nch, :].rearrange("p c h -> p (c h)"), gC.buf), p[0:L, 0:nch * 24])
            for c in range(nch):
                p = self.ps()
                for fc in range(NH):
                    self.tr(pbv(p, slice(0, L), slice(fc * 128, (fc + 1) * 128)), qkvT[:, NH + fc, c * L:(c + 1) * L], ident_b[:])
                    self.tr(pbv(p, slice(0, L), slice(512 + fc * 128, 512 + (fc + 1) * 128)), qkvT[:, 2 * NH + fc, c * L:(c + 1) * L], ident_b[:])
                self.cp(k_tok[0:L, c, :], pbv(p, slice(0, L), slice(0, 512)), eng='act')
                self.cp(v_tok[0:L, c, :], pbv(p, slice(0, L), slice(512, 1024)))

            def phaseA(c):
                cs, ce = c * L, (c + 1) * L
                hs = list(range(NH))
                for i, h in enumerate(hs):
                    pk = self.ps()
                    self.mm(pk[0:L, 0:L], qkvT[:, NH + h, cs:ce], qkvT[:, NH + h, cs:ce])
                    self.cp(KKs[i][0:L, 0:L], pk[0:L, 0:L], eng='act')
                    pbb = self.ps()
                    self.mm(pbb[0:L, 0:L], sel16[0:NH, h, 0:L], gam[0:NH, cs:ce])
                    self.ts(tmpm[i][0:L, 0:L], pbb[0:L, 0:L], gC[0:L, c, NH + h:NH + h + 1], ALU.add, 0.0, ALU.min)
                    self.act(tmpm[i][0:L, 0:L], tmpm[i][0:L, 0:L], AF.Exp)
                    self.tt(dTs_[i][0:L, 0:L], tmpm[i][0:L, 0:L], maskS[0:L, 0:L], ALU.mult)
                    self.tt(tmpm[i][0:L, 0:L], tmpm[i][0:L, 0:L], maskT[0:L, 0:L], ALU.mult)
                    pq = self.ps()
                    self.mm(pq[0:L, 0:L], qkvT[:, NH + h, cs:ce], qkvT[:, h, cs:ce])
                    self.tt(QKd[c % 2][h][0:L, 0:L], pq[0:L, 0:L], tmpm[i][0:L, 0:L], ALU.mult)
                for i, h in enumerate(hs):
                    pd = self.ps()
                    self.tr(pd[0:L, 0:L], dTs_[i][0:L, 0:L], ident_f[0:L, 0:L])
                    self.stt(Xs[i][0:L, 0:L], pd[0:L, 0:L], gC[0:L, c, 5 * NH + h:5 * NH + h + 1], KKs[i][0:L, 0:L], ALU.mult, ALU.mult)
                for i, h in enumerate(hs):
                    px = self.ps()
                    self.tr(px[0:L, 0:L], Xs[i][0:L, 0:L], ident_f[0:L, 0:L])
                    self.cp(XTs[i][0:L, 0:L], px[0:L, 0:L], eng='act')
                    self.tt(TTs[i][0:L, 0:L], px[0:L, 0:L], ident_f[0:L, 0:L], ALU.add)
                for j in range(nsq):
                    last = (j == nsq - 1)
                    for i, h in enumerate(hs):
                        pa2 = self.ps()
                        self.mm(pa2[0:L, 0:L], XTs[i][0:L, 0:L], Xs[i][0:L, 0:L])
                        if not last:
                            pb2 = self.ps()
                            self.mm(pb2[0:L, 0:L], Xs[i][0:L, 0:L], XTs[i][0:L, 0:L])
                        self.cp(Xs[i][0:L, 0:L], pa2[0:L, 0:L], eng='act')
                        if not last:
                            self.cp(XTs[i][0:L, 0:L], pb2[0:L, 0:L])
                    for i, h in enumerate(hs):
                        pc = self.ps()
                        self.mm(pc[0:L, 0:L], Xs[i][0:L, 0:L], TTs[i][0:L, 0:L])
                        self.tt(TTs[i][0:L, 0:L], pc[0:L, 0:L], TTs[i][0:L, 0:L], ALU.add)
                for i, h in enumerate(hs):
                    self.ts(R1[i][0:L, :], v_tok[0:L, c, h * 128:(h + 1) * 128], gC[0:L, c, h:h + 1], ALU.mult)
                    self.ts(R2[i][0:L, :], k_tok[0:L, c, h * 128:(h + 1) * 128], gC[0:L, c, 4 * NH + h:4 * NH + h + 1], ALU.mult)
                    pu = self.ps()
                    self.mm(pu[0:L, 0:128], TTs[i][0:L, 0:L], R1[i][0:L, :])
                    self.cp(U0s[c % 2][h][0:L, :], pu[0:L, 0:128], eng='act')
                    pw = self.ps()
                    self.mm(pw[:, 0:L], R2[i][0:L, :], TTs[i][0:L, 0:L])
                    self.cp(WTb[c % 2][h][:, 0:L], pw[:, 0:L])

            def phaseB(c):
                cs, ce = c * L, (c + 1) * L
                hs = list(range(NH))
                for i, h in enumerate(hs):
                    pws = self.ps()
                    self.mm(pws[0:L, 0:128], WTb[c % 2][h][:, 0:L], SGb[:, h, :])
                    self.tt(ub[i][0:L, :], U0s[c % 2][h][0:L, :], pws[0:L, 0:128], ALU.subtract)
                    self.ts(kw[i][0:L, :], k_tok[0:L, c, h * 128:(h + 1) * 128], gC[0:L, c, 3 * NH + h:3 * NH + h + 1], ALU.mult)
                for i, h in enumerate(hs):
                    po1 = self.ps()
                    self.mm(po1[0:L, 0:128], QKd[c % 2][h][0:L, 0:L], ub[i][0:L, :])
                    po2 = self.ps()
                    self.mm(po2[0:L, 0:128], qkvT[:, h, cs:ce], SGb[:, h, :])
                    self.cp(o1s[i][0:L, :], po1[0:L, 0:128], eng='act')
                    self.stt(oo[i][0:L, :], po2[0:L, 0:128], gC[0:L, c, 2 * NH + h:2 * NH + h + 1], o1s[i][0:L, :], ALU.mult, ALU.add)
                for i, h in enumerate(hs):
                    pS = self.ps()
                    self.mm(pS[:, 0:128], kw[i][0:L, :], ub[i][0:L, :])
                    self.stt(SG[:, h, :], SG[:, h, :], decC[:, c, h:h + 1], pS[:, 0:128], ALU.mult, ALU.add)
                    self.cp(SGb[:, h, :], SG[:, h, :], eng='act')
                for i, h in enumerate(hs):
                    self.act(jk[i][0:L, :], oo[i][0:L, :], AF.Square)
                    self.rsum(cc[i][0:L, :], jk[i][0:L, :])
                    self.act(cc[i][0:L, :], cc[i][0:L, :], AF.Sqrt, bias=EPS, scale=1.0 / 128)
                    self.recip(cc[i][0:L, :], cc[i][0:L, :])
                    self.stt(oo[i][0:L, :], oo[i][0:L, :], cc[i][0:L, :], gdnB[0:L, :], ALU.mult, ALU.mult)
                    self.tt(h_tok[0:L, h * 128:(h + 1) * 128], oo[i][0:L, :], zs[0:L, c, h * 128:(h + 1) * 128], ALU.mult)
                p = self.ps()
                for f8 in range(NH):
                    self.tr(pbv(p, slice(0, 128), slice(f8 * 128, f8 * 128 + L)), h_tok[0:L, f8 * 128:(f8 + 1) * 128],
                            ident_b[0:L, 0:L])
                self.cp(hT[:, 0:NH, cs:ce], V(p.hb[:, 0:512].rearrange("p (f t) -> p f t", t=128)[:, :, 0:L], p.buf), eng='act')

            phaseA(0)
            for c in range(nch):
                if c + 1 < nch:
                    phaseA(c + 1)
                phaseB(c)
            dma(g4in.v(g4in.ap.rearrange("(f p) t -> p f t", p=128)), hT[:, :, 0:n])
            allgather(g4in, g4out)

        xsrc = xT.ap.rearrange("(k p) t -> p k t", p=128)
        ydst = yT.ap.rearrange("(k p) t -> p k t", p=128)
        for b, (t0, n) in enumerate(own):
            dma(xb[b][:, :, 0:n], xT.v(xsrc[:, :, t0:t0 + n]))
        for b, (t0, n) in enumerate(own):
            ffn(0, 0, n, xb[b])
        for b, (t0, n) in enumerate(own):
            hn_exchange(1, n, xb[b], G1in[b], G1out[b])
        seqs = [(r * 4 + i, TB, 128, G1out[i], r) for r in range(2) for i in range(4)] + \
               [(8 + r, DEC_SEQ, DEC_SEQ, G1out[4], r) for r in range(2)]

        def ab_zero():
            self.memset(Cx[:], 0.0); self.memset(ST[:], 0.0); self.memset(hist_b[:], 0.0)
            self.memset(Fc[:], 0.0); self.memset(Gc[:], 0.0)
            self.cp(Sb[:], ST[:])

        def ab_store(k):
            self.tt(mo[:], Gc[:], Fc[:], ALU.subtract)
            dma(o_Cx[k].v(), Cx[:], is_output=True); dma(o_m[k].v(), mo[:], is_output=True)
            dma(o_ST[k].v(), ST[:], is_output=True); dma(o_hb[k].v(), hist_b[:], is_output=True)

        def prefetch_all():
            groups = [(g * 512, 512) for g in range(5)] + [(2560, 256)]
            wo_ab = ab_w_out.ap.rearrange("(f p) d -> p f d", p=128)
            for dh in range(2):
                for fh in range(2):
                    self.wprefetch([128, 8, 512], ab_w_out, wo_ab[:, fh * 8:(fh + 1) * 8, dh * 512:(dh + 1) * 512], ('wo', 16, dh, fh))
            for (l, i) in ((0, 1), (1, 0)):
                wg_src = w_gate.ap[l, i].rearrange("(k p) f -> p k f", p=128)
                wu_src = w_up.ap[l, i].rearrange("(k p) f -> p k f", p=128)
                wd_src = w_down.ap[l, i].rearrange("(f p) d -> p f d", p=128)
                for gi, (c0, ncol) in enumerate(groups):
                    self.wprefetch([128, 8, ncol], w_gate, wg_src[:, :, c0:c0 + ncol], ('wg', l, i, gi))
                    self.wprefetch([128, 8, ncol], w_up, wu_src[:, :, c0:c0 + ncol], ('wu', l, i, gi))
                    if gi == 1:
                        for q in range(2):
                            self.wprefetch([128, 11, D], w_down, wd_src[:, q * 11:(q + 1) * 11, :], ('wd', l, i, q))
            gwin = gw.ap.rearrange("(k p) c -> p k c", p=128)
            self.wprefetch([128, 8, 8], gw, gwin[:, :, 2048:2056], ('gba',))
            for c0 in (0, 512, 1024, 1536):
                self.wprefetch([128, 8, 512], gw, gwin[:, :, c0:c0 + 512], ('gin', c0))
            wo_g = gdn_w_out.ap.rearrange("(f p) d -> p f d", p=128)
            for dh in range(2):
                self.wprefetch([128, 8, 512], gdn_w_out, wo_g[:, 0:8, dh * 512:(dh + 1) * 512], ('wo', 8, dh, 0))
            l, i = 1, 1
            wg_src = w_gate.ap[l, i].rearrange("(k p) f -> p k f", p=128)
            wu_src = w_up.ap[l, i].rearrange("(k p) f -> p k f", p=128)
            wd_src = w_down.ap[l, i].rearrange("(f p) d -> p f d", p=128)
            for gi, (c0, ncol) in enumerate(groups):
                self.wprefetch([128, 8, ncol], w_gate, wg_src[:, :, c0:c0 + ncol], ('wg', l, i, gi))
                self.wprefetch([128, 8, ncol], w_up, wu_src[:, :, c0:c0 + ncol], ('wu', l, i, gi))
                if gi == 1:
                    for q in range(2):
                        self.wprefetch([128, 11, D], w_down, wd_src[:, q * 11:(q + 1) * 11, :], ('wd', l, i, q))

        ab_zero()
        for (k, n, L, g1, r) in seqs:
            if k >= 8:
                j = k - 8
                if j == 0:
                    ab_store(2)
                dma(Cx[:], s_Cx[j].v()); dma(ST[:], s_ST[j].v()); dma(hist_b[:], s_hb[j].v()); dma(Gc[:], s_m[j].v())
                self.memset(Fc[:], 0.0)
                self.cp(Sb[:], ST[:])
            mixer_ab(n, L, g1, r, G2in[k], G2out[k])
            if k == 0:
                prefetch_all()
            if k >= 8:
                ab_store(k - 8)
        for b, (t0, n) in enumerate(own):
            gouts = (G2out[b], G2out[4 + b]) if b < 4 else (G2out[8], G2out[9])
            out_proj_sel(ab_w_out, 16, gouts, n, xb[b], 'ab')
        for b, (t0, n) in enumerate(own):
            ffn(0, 1, n, xb[b])
        for b, (t0, n) in enumerate(own):
            ffn(1, 0, n, xb[b])
        for b, (t0, n) in enumerate(own):
            hn_exchange(4, n, xb[b], G3in[b], G3out[b])
        seqs_c = [(r * 4 + i, TB, 128, G3out[i], r) for r in range(2) for i in range(4)] + \
                 [(8 + r, DEC_SEQ, DEC_SEQ, G3out[4], r) for r in range(2)]

        def c_store(k):
            dma(o_SG[k].v(), SG[:], is_output=True); dma(o_hc[k].v(), hist_c[:], is_output=True)

        self.memset(SG[:], 0.0); self.memset(hist_c[:], 0.0)
        self.cp(SGb[:], SG[:])
        for (k, n, L, g3, r) in seqs_c:
            if k >= 8:
                j = k - 8
                if j == 0:
                    c_store(2)
                dma(SG[:], s_SG[j].v()); dma(hist_c[:], s_hc[j].v())
                self.cp(SGb[:], SG[:])
            mixer_c(n, L, g3, r, G4in[k], G4out[k])
            if k >= 8:
                c_store(k - 8)
        for b, (t0, n) in enumerate(own):
            gouts = (G4out[b], G4out[4 + b]) if b < 4 else (G4out[8], G4out[9])
            out_proj_sel(gdn_w_out, 8, gouts, n, xb[b], 'gdn')
        for b, (t0, n) in enumerate(own):
            ffn(1, 1, n, xb[b])
        for b, (t0, n) in enumerate(own):
            self.phase_begin(('fin',))
            yo = A("yo", [128, 8, TB]); sq = A("sq", [128, 8, TB], BF16)
            rmsnorm(n, lambda kc: gfcol[:, kc:kc + 1], yo, sq, xb[b])
            dma(yT.v(ydst[:, :, t0:t0 + n]), yo[:, :, 0:n], is_output=True)
        S.finish()
        S.emit()
        return nc


_NC_CACHE = {}


def _own_ch_ssd(r):
    return np.concatenate([np.arange(r * 512, (r + 1) * 512), 1024 + np.arange(r * 128, (r + 1) * 128),
                           1280 + np.arange(r * 128, (r + 1) * 128)])


def _own_ch_gdn(r):
    return np.concatenate([np.arange(r * 512, (r + 1) * 512), 1024 + np.arange(r * 512, (r + 1) * 512),
                           2048 + np.arange(r * 512, (r + 1) * 512)])


def _host_inputs(d, p, r):
    f = np.ascontiguousarray
    x = np.concatenate([d['x_prompt'][p, r * 2048:(r + 1) * 2048], d['x_sample'][2 * p + r]], 0)
    wi = d['ab_w_in']
    abw = np.concatenate([wi[:, r * 256:(r + 1) * 256], wi[:, 512 + r * 256:512 + (r + 1) * 256],
                          wi[:, 1024 + r * 512:1024 + (r + 1) * 512], wi[:, 2048 + r * 512:2048 + (r + 1) * 512],
                          wi[:, 3072 + 2 * r:3074 + 2 * r], wi[:, 3076 + 2 * r:3078 + 2 * r],
                          wi[:, 3080 + r * 512:3080 + (r + 1) * 512], wi[:, 4104 + r * 512:4104 + (r + 1) * 512],
                          wi[:, 5128 + r * 128:5128 + (r + 1) * 128], wi[:, 5384 + r * 128:5384 + (r + 1) * 128],
                          wi[:, 5640 + r * 8:5640 + (r + 1) * 8]], axis=1)
    gi = d['gdn_w_in']
    gw = np.concatenate([gi[:, r * 512:(r + 1) * 512], gi[:, 1024 + r * 512:1024 + (r + 1) * 512],
                         gi[:, 2048 + r * 512:2048 + (r + 1) * 512], gi[:, 3072 + r * 512:3072 + (r + 1) * 512],
                         gi[:, 4096 + 4 * r:4100 + 4 * r], gi[:, 4104 + 4 * r:4108 + 4 * r]], axis=1)
    cs, cg = _own_ch_ssd(r), _own_ch_gdn(r)
    rm = np.zeros((128, 2), np.float32)
    rm[:, r] = 1.0
    ins = {
        "xT": f(x.T), "ffn_w_gate": d['ffn_w_gate'], "ffn_w_up": d['ffn_w_up'], "ffn_w_down": d['ffn_w_down'],
        "g_col": f(d['norm_g'].reshape(6, 8, 128).transpose(2, 0, 1)),
        "gf_col": f(d['norm_f'].reshape(8, 128).T), "rmask": rm,
        "abw": f(abw), "ab_w_out": d['ab_w_out'],
        "b_i": f(d['mlstm_b_i'][2 * r:2 * r + 2].reshape(2, 1)), "b_f": f(d['mlstm_b_f'][2 * r:2 * r + 2].reshape(2, 1)),
        "mlstm_norm_b": f(np.broadcast_to(d['mlstm_norm'][2 * r:2 * r + 2].reshape(1, 512), (128, 512))),
        "ssd_cw": f(d['ssd_conv_w'][:, cs].reshape(4, 6, 128).transpose(2, 1, 0)),
        "ssd_cb": f(d['ssd_conv_b'][cs].reshape(6, 128).T),
        "ssd_dtb": f(d['ssd_dt_bias'][r * 8:(r + 1) * 8].reshape(8, 1)),
        "ssd_alog": f(d['ssd_a_log'][r * 8:(r + 1) * 8].reshape(8, 1)),
        "ssd_d_b": f(np.broadcast_to(d['ssd_d'][r * 8:(r + 1) * 8].reshape(1, 8), (128, 8))),
        "ssd_norm_b": f(np.broadcast_to(d['ssd_norm'][r * 512:(r + 1) * 512].reshape(1, 512), (128, 512))),
        "gw": f(gw), "gdn_w_out": d['gdn_w_out'],
        "gdn_cw": f(d['gdn_conv_w'][:, cg].reshape(4, 12, 128).transpose(2, 1, 0)),
        "gdn_dtb": f(d['gdn_dt_bias'][4 * r:4 * r + 4].reshape(4, 1)),
        "gdn_alog": f(d['gdn_a_log'][4 * r:4 * r + 4].reshape(4, 1)),
        "gdn_norm_b": f(np.broadcast_to(d['gdn_norm'].reshape(1, 128), (128, 128))),
    }
    for j in range(2):
        s = 2 * p + j
        ins["s_Cx%d" % j] = f(np.concatenate([d['state_mlstm_C'][s, 2 * r:2 * r + 2].transpose(1, 0, 2),
                                              d['state_mlstm_n'][s, 2 * r:2 * r + 2].T[:, :, None]], 2))
        ins["s_m%d" % j] = f(d['state_mlstm_m'][s, 2 * r:2 * r + 2].reshape(2, 1))
        ins["s_ST%d" % j] = f(d['state_ssd'][s, r * 8:(r + 1) * 8].transpose(2, 0, 1))
        ins["s_hb%d" % j] = f(d['cache_ssd_conv'][s][:, cs].reshape(3, 6, 128).transpose(2, 1, 0))
        ins["s_SG%d" % j] = f(d['state_gdn'][s, 4 * r:4 * r + 4].transpose(1, 0, 2))
        ins["s_hc%d" % j] = f(d['cache_gdn_conv'][s][:, cg].reshape(3, 12, 128).transpose(2, 1, 0))
    return ins


def _assemble(R, npair):
    f32 = lambda a: np.ascontiguousarray(np.asarray(a, dtype=np.float32))
    y_prompt = np.zeros((npair, 4096, 1024), np.float32)
    y_sample = np.zeros((2 * npair, 16, 1024), np.float32)

    def alloc(n):
        return [np.zeros((n, 4, 128, 256), np.float32), np.zeros((n, 4, 128), np.float32), np.zeros((n, 4), np.float32),
                np.zeros((n, 16, 64, 128), np.float32), np.zeros((n, 3, 1536), np.float32),
                np.zeros((n, 8, 128, 128), np.float32), np.zeros((n, 3, 3072), np.float32)]
    PS, SS = alloc(npair), alloc(2 * npair)

    def put(dst, idx, r, res, k):
        cs, cg = _own_ch_ssd(r), _own_ch_gdn(r)
        Cx = np.asarray(res["o_Cx%d" % k], np.float32)
        dst[0][idx, 2 * r:2 * r + 2] = Cx[:, :, :256].transpose(1, 0, 2)
        dst[1][idx, 2 * r:2 * r + 2] = Cx[:, :, 256].T
        dst[2][idx, 2 * r:2 * r + 2] = np.asarray(res["o_m%d" % k], np.float32)[:, 0]
        dst[3][idx, r * 8:(r + 1) * 8] = np.asarray(res["o_ST%d" % k], np.float32).transpose(1, 2, 0)
        dst[4][idx][:, cs] = np.asarray(res["o_hb%d" % k], np.float32).transpose(2, 1, 0).reshape(3, 768)
        dst[5][idx, 4 * r:4 * r + 4] = np.asarray(res["o_SG%d" % k], np.float32).transpose(1, 0, 2)
        dst[6][idx][:, cg] = np.asarray(res["o_hc%d" % k], np.float32).transpose(2, 1, 0).reshape(3, 1536)

    for p in range(npair):
        for r in range(2):
            res = R[2 * p + r]
            yT = np.asarray(res['yT'], np.float32)
            y_prompt[p, r * 2048:(r + 1) * 2048] = yT[:, :2048].T
            y_sample[2 * p + r] = yT[:, 2048:].T
            put(PS, p, r, res, 2)
            for j in range(2):
                put(SS, 2 * p + j, r, res, j)
    return tuple([y_prompt, y_sample] + PS + SS)


def kernel(**inputs):
    d = {k: np.asarray(v, dtype=np.float32) for k, v in inputs.items()}
    ncores = 8
    if 'nc' not in _NC_CACHE:
        B = Builder(8)
        _NC_CACHE['nc'] = (B.build(), set(B.din))
    nc, din = _NC_CACHE['nc']
    in_maps = []
    for c in range(ncores):
        hi = _host_inputs(d, c // 2, c % 2)
        in_maps.append({k: v for k, v in hi.items() if k in din})
    res = run_bass_kernel_spmd(nc, in_maps, core_ids=list(range(ncores)))
    return _assemble(list(res.results), 4)
```

```python
import numpy as np
from contextlib import ExitStack
import concourse.bass as bass
import concourse.mybir as mybir
from concourse.bass_utils import run_bass_kernel_spmd

F32 = mybir.dt.float32
BF16 = mybir.dt.bfloat16
AF = mybir.ActivationFunctionType
ALU = mybir.AluOpType

ENGS = ['pe', 'act', 'dve', 'pool', 'sp']
NDSEM = 12
import os
PLIM = int(os.environ.get('K_PLIM', '3'))

D = 1024
DFF = 2816
NFF = 22
EPS = 1e-6
TB = 512
DEC_SEQ = 16


class Buf:
    __slots__ = ('name', 'w', 'r', 'excl')

    def __init__(self, name='', excl=False):
        self.name = name
        self.w = None
        self.r = []
        self.excl = excl


class Sched:
    def __init__(self, nc, same_engine_sync=True):
        self.nc = nc
        self.ops = {e: [] for e in ENGS}
        self.seen = {e: {f: -1 for f in ENGS} for e in ENGS}
        self.seen_dma = {e: {} for e in ENGS}
        self.targets = {e: set() for e in ENGS}
        self.ndma = {e: 0 for e in ENGS}
        self.same_engine_sync = same_engine_sync
        self.out_tokens = []

    def _need(self, eng, tok, waits):
        if tok is None:
            return
        if tok[0] == 'e':
            _, f, k = tok
            if f == eng and (eng == 'pe' or not self.same_engine_sync):
                return
            if self.seen[eng][f] >= k:
                return
            self.seen[eng][f] = k
            self.targets[f].add(k)
            waits.append(tok)
        elif tok[0] == 'c':
            if getattr(self, '_seen_cc', {}).get(eng, -1) >= tok[1]:
                return
            if not hasattr(self, '_seen_cc'):
                self._seen_cc = {}
            self._seen_cc[eng] = tok[1]
            waits.append(tok)
        else:
            _, q, seq = tok
            key = (q, seq % NDSEM)
            if self.seen_dma[eng].get(key, -1) >= seq:
                return
            self.seen_dma[eng][key] = seq
            waits.append(tok)

    def op(self, eng, fn, reads=(), writes=(), dma=False, is_output=False, cc=False):
        waits = []
        for b in reads:
            self._need(eng, b.w, waits)
            if b.excl:
                for t in b.r:
                    if t[0] == 'e' and t[1] != eng:
                        self._need(eng, t, waits)
        for b in writes:
            self._need(eng, b.w, waits)
            for t in b.r:
                self._need(eng, t, waits)
        idx = len(self.ops[eng])
        if eng == 'pool' and not dma:
            lp = getattr(self, '_last_pool', None)
            if lp is not None:
                self._need('pool', lp, waits)
            self._last_pool = ('e', 'pool', idx)
        if dma:
            seq = self.ndma[eng]
            self.ndma[eng] += 1
            lim = PLIM if eng == 'pool' else NDSEM
            if seq >= lim:
                self._need(eng, ('d', eng, seq - lim), waits)
            tok = ('d', eng, seq)
        elif cc:
            seq = None
            self.ncc = getattr(self, 'ncc', 0) + 1
            tok = ('c', self.ncc - 1)
        else:
            seq = None
            tok = ('e', eng, idx)
        self.ops[eng].append(dict(fn=fn, waits=waits, dma=seq, cc=cc))
        for b in reads:
            if tok[0] == 'e':
                b.r = [t for t in b.r if not (t[0] == 'e' and t[1] == tok[1])]
            b.r.append(tok)
        for b in writes:
            b.w = tok
            b.r = []
        if is_output:
            self.out_tokens.append(tok)
        return tok

    def barrier(self):
        last = {}
        for e in ENGS:
            if self.ops[e]:
                last[e] = ('e', e, len(self.ops[e]) - 1)
        for e in ENGS:
            waits = []
            for f, t in last.items():
                if f != e:
                    self._need(e, t, waits)
            for q in ENGS:
                lim = PLIM if q == 'pool' else NDSEM
                for sq_ in range(max(0, self.ndma[q] - lim), self.ndma[q]):
                    self._need(e, ('d', q, sq_), waits)
            if waits:
                self.ops[e].append(dict(fn=None, waits=waits, dma=None, cc=False))

    def finish(self):
        waits = []
        for t in self.out_tokens:
            self._need('sp', t, waits)
        self.ops['sp'].append(dict(fn=None, waits=waits, dma=None, cc=False))

    def emit(self):
        nc = self.nc
        with ExitStack() as st:
            esem = {e: st.enter_context(nc.semaphore('es_' + e)) for e in ENGS}
            ccsem = st.enter_context(nc.semaphore('cc_sem'))
            dsem = {e: [st.enter_context(nc.semaphore('ds_%s%d' % (e, i))) for i in range(NDSEM)]
                    for e in ENGS if self.ndma[e] > 0}
            cnt = {}
            for e in ENGS:
                c = 0
                m = {}
                for k in sorted(self.targets[e]):
                    c += 1
                    m[k] = c
                cnt[e] = m
            block = st.enter_context(nc.Block())

            def body(ename, eh):
                for idx, o in enumerate(self.ops[ename]):
                    for t in o['waits']:
                        if t[0] == 'e':
                            eh.wait_ge(esem[t[1]], cnt[t[1]][t[2]])
                        elif t[0] == 'c':
                            eh.wait_ge(ccsem, t[1] + 1)
                        else:
                            eh.wait_ge(dsem[t[1]][t[2] % NDSEM], 16 * (t[2] // NDSEM + 1))
                    if o['fn'] is None:
                        if idx in cnt[ename]:
                            eh.nop().then_inc(esem[ename], 1)
                        continue
                    inst = o['fn'](eh)
                    if o.get('cc'):
                        inst.then_inc(ccsem, 1)
                        if idx in cnt[ename]:
                            eh.nop().then_inc(esem[ename], 1)
                    elif o['dma'] is not None:
                        inst.then_inc(dsem[ename][o['dma'] % NDSEM], 16)
                        if idx in cnt[ename]:
                            eh.nop().then_inc(esem[ename], 1)
                    elif idx in cnt[ename]:
                        inst.then_inc(esem[ename], 1)

            @block.tensor
            def _(eh):
                body('pe', eh)

            @block.scalar
            def _(eh):
                body('act', eh)

            @block.vector
            def _(eh):
                body('dve', eh)

            @block.gpsimd
            def _(eh):
                body('pool', eh)

            @block.sync
            def _(eh):
                body('sp', eh)


class V:
    __slots__ = ('ap', 'buf')

    def __init__(self, ap, buf):
        self.ap = ap
        self.buf = buf


class Tl:
    def __init__(self, h, name=''):
        self.h = h
        self.buf = Buf(name)

    def __getitem__(self, idx):
        return V(self.h[idx], self.buf)

    def v(self, ap):
        return V(ap, self.buf)


class DT:
    def __init__(self, ap, name=''):
        self.ap = ap
        self.buf = Buf(name)

    def v(self, ap=None):
        return V(self.ap if ap is None else ap, self.buf)


class Builder:
    SB_LO = 17536
    SB_HI = 229344

    def __init__(self, nblk):
        self.nblk = nblk
        self.ntok = nblk * TB + DEC_SEQ
        self.nc = bass.Bass("TRN2", target_bir_lowering=False)
        self.S = Sched(self.nc)
        self.persist_off = self.SB_LO
        self.phase_base = None
        self.phase_off = None
        self.nps = 0
        self.din = {}
        self.dout = {}

    def _alloc(self, name, shape, dt, off):
        n = 1
        for s in shape[1:]:
            n *= s
        nbytes = n * (4 if dt == F32 else 2)
        nbytes = (nbytes + 63) // 64 * 64
        h = self.nc.alloc_sbuf_tensor_at(name, list(shape), dt, offset=off)
        return Tl(h, name), off + nbytes

    def P(self, name, shape, dt=F32):
        t, self.persist_off = self._alloc(name, shape, dt, self.persist_off)
        assert self.persist_off <= self.SB_HI, name
        return t

    def phase_begin(self, kind=None):
        if self.phase_base is None:
            self.phase_base = self.persist_off
            self.cur_kind = None
            self.kind_tiles = {}
        if kind is None or kind != self.cur_kind:
            if self.cur_kind is not None or kind is None:
                self.S.barrier()
            self.cur_kind = kind
            self.kind_tiles = {}
            self.phase_gen = getattr(self, 'phase_gen', 0) + 1
        self.phase_off = self.phase_base

    def A(self, name, shape, dt=F32):
        if self.cur_kind is not None and name in self.kind_tiles:
            return self.kind_tiles[name]
        t, self.phase_off = self._alloc(name, shape, dt, self.phase_off)
        assert self.phase_off <= self.SB_HI, (name, self.phase_off)
        if self.cur_kind is not None:
            self.kind_tiles[name] = t
        return t

    def din_t(self, name, shape, dt=F32):
        ap = self.nc.dram_tensor(name, list(shape), dt, kind="ExternalInput").ap()
        d = DT(ap, name)
        self.din[name] = d
        return d

    def dout_t(self, name, shape, dt=F32):
        ap = self.nc.dram_tensor(name, list(shape), dt, kind="ExternalOutput").ap()
        d = DT(ap, name)
        self.dout[name] = d
        return d

    def mm(self, out, lhsT, rhs, start=True, stop=True):
        self.S.op('pe', lambda e: e.matmul(out.ap, lhsT=lhsT.ap, rhs=rhs.ap, start=start, stop=stop),
                  reads=[lhsT.buf, rhs.buf], writes=[out.buf])

    def tr(self, out, in_, ident):
        self.S.op('pe', lambda e: e.transpose(out.ap, in_.ap, ident.ap),
                  reads=[in_.buf, ident.buf], writes=[out.buf])

    def act(self, out, in_, func, bias=None, scale=1.0, accum=None, eng='act'):
        reads = [in_.buf]
        kw = {}
        if bias is not None:
            if isinstance(bias, V):
                reads.append(bias.buf)
                kw['bias'] = bias.ap
            else:
                kw['bias'] = bias
        if isinstance(scale, V):
            reads.append(scale.buf)
            kw['scale'] = scale.ap
        else:
            kw['scale'] = scale
        writes = [out.buf]
        if accum is not None:
            writes.append(accum.buf)
            kw['accum_out'] = accum.ap
        self.S.op('act', lambda e: e.activation(out.ap, in_.ap, func, **kw), reads=reads, writes=writes)

    def tt(self, out, in0, in1, op, eng='dve'):
        self.S.op(eng, lambda e: e.tensor_tensor(out.ap, in0.ap, in1.ap, op),
                  reads=[in0.buf, in1.buf], writes=[out.buf])

    def ts(self, out, in0, s1, op0, s2=None, op1=None, eng='dve', accum=None):
        reads = [in0.buf]
        a1 = s1
        a2 = s2
        if isinstance(s1, V):
            reads.append(s1.buf)
            a1 = s1.ap
        if isinstance(s2, V):
            reads.append(s2.buf)
            a2 = s2.ap
        kw = {}
        writes = [out.buf]
        if accum is not None:
            kw['accum_out'] = accum.ap
            writes.append(accum.buf)
        if op1 is None:
            self.S.op(eng, lambda e: e.tensor_scalar(out.ap, in0.ap, a1, None, op0, **kw), reads=reads, writes=writes)
        else:
            self.S.op(eng, lambda e: e.tensor_scalar(out.ap, in0.ap, a1, a2, op0, op1, **kw), reads=reads, writes=writes)

    def stt(self, out, in0, sc, in1, op0, op1):
        reads = [in0.buf, in1.buf]
        a = sc
        if isinstance(sc, V):
            reads.append(sc.buf)
            a = sc.ap
        self.S.op('dve', lambda e: e.scalar_tensor_tensor(out.ap, in0.ap, a, in1.ap, op0, op1),
                  reads=reads, writes=[out.buf])

    def cp(self, out, in_, eng='dve'):
        if eng == 'act':
            self.S.op('act', lambda e: e.copy(out.ap, in_.ap), reads=[in_.buf], writes=[out.buf])
        else:
            self.S.op(eng, lambda e: e.tensor_copy(out.ap, in_.ap), reads=[in_.buf], writes=[out.buf])

    def rsum(self, out, in_):
        self.S.op('dve', lambda e: e.tensor_reduce(out.ap, in_.ap, mybir.AxisListType.X, ALU.add),
                  reads=[in_.buf], writes=[out.buf])

    def recip(self, out, in_):
        self.S.op('dve', lambda e: e.reciprocal(out.ap, in_.ap), reads=[in_.buf], writes=[out.buf])

    def scan(self, out, d0, d1, init, op0, op1):
        reads = [d0.buf, d1.buf]
        a = init
        if isinstance(init, V):
            reads.append(init.buf)
            a = init.ap
        self.S.op('dve', lambda e: e.tensor_tensor_scan(out.ap, d0.ap, d1.ap, a, op0, op1),
                  reads=reads, writes=[out.buf])

    def memset(self, out, val, eng='pool'):
        self.S.op(eng, lambda e: e.memset(out.ap, val), writes=[out.buf])

    def dma(self, out, in_, q='sp', is_output=False, nc_ok=False):
        if nc_ok:
            fn = lambda e: e.dma_start(out=out.ap, in_=in_.ap, allow_slow_non_contiguous=True)
        else:
            fn = lambda e: e.dma_start(out=out.ap, in_=in_.ap)
        self.S.op(q, fn, reads=[in_.buf], writes=[out.buf], dma=True, is_output=is_output)

    def wload(self, dst, src_dt, src_ap, key):
        if not hasattr(self, 'wcache'):
            self.wcache = {}
        if key not in self.wcache:
            self.dma(dst, V(src_ap, src_dt.buf), q='pool')
            shp = list(dst.ap.shape)
            nm = "wc_" + "_".join(str(k) for k in key)
            ap = self.nc.dram_tensor(nm, shp, BF16, kind="Internal").ap()
            d = DT(ap, nm)
            self.wcache[key] = d
            self.dma(d.v(), dst, q='sp')
        else:
            self.dma(dst, self.wcache[key].v(), q='sp')

    def wprefetch(self, shape, src_dt, src_ap, key):
        if not hasattr(self, 'wcache'):
            self.wcache = {}
        if key in self.wcache:
            return
        nm = "wc_" + "_".join(str(k) for k in key)
        ap = self.nc.dram_tensor(nm, list(shape), BF16, kind="Internal").ap()
        d = DT(ap, nm)
        self.wcache[key] = d
        self.dma(d.v(), V(src_ap, src_dt.buf), q='pool')

    def ps(self):
        t = self.psb[self.nps % 8]
        self.nps += 1
        return t

    def build(self):
        nc = self.nc
        S = self.S
        A = self.A
        P = self.P
        dma = self.dma
        DKs = 128 ** -0.5
        NOWN = 4
        NTOK = NOWN * TB + DEC_SEQ
        RG = [[0, 1]] if os.environ.get("K_RG") == "pair" else [[0, 1], [2, 3], [4, 5], [6, 7]]
        xT = self.din_t("xT", [D, NTOK])
        yT = self.dout_t("yT", [D, NTOK])
        w_gate = self.din_t("ffn_w_gate", [2, 2, D, DFF])
        w_up = self.din_t("ffn_w_up", [2, 2, D, DFF])
        w_down = self.din_t("ffn_w_down", [2, 2, DFF, D])
        g_col = self.din_t("g_col", [128, 6, 8])
        gf_col = self.din_t("gf_col", [128, 8])
        rmask_d = self.din_t("rmask", [128, 2])
        abw = self.din_t("abw", [D, 2828])
        ab_w_out = self.din_t("ab_w_out", [2048, D])
        d_bi = self.din_t("b_i", [2, 1]); d_bf = self.din_t("b_f", [2, 1])
        d_gnB = self.din_t("mlstm_norm_b", [128, 512])
        d_cw = self.din_t("ssd_cw", [128, 6, 4]); d_cb = self.din_t("ssd_cb", [128, 6])
        d_dtb = self.din_t("ssd_dtb", [8, 1]); d_alog = self.din_t("ssd_alog", [8, 1])
        d_dskB = self.din_t("ssd_d_b", [128, 8]); d_snB = self.din_t("ssd_norm_b", [128, 512])
        gw = self.din_t("gw", [D, 2056])
        gdn_w_out = self.din_t("gdn_w_out", [1024, D])
        d_gcw = self.din_t("gdn_cw", [128, 12, 4])
        d_gdtb = self.din_t("gdn_dtb", [4, 1]); d_galog = self.din_t("gdn_alog", [4, 1])
        d_gdnB = self.din_t("gdn_norm_b", [128, 128])
        s_Cx = [self.din_t("s_Cx%d" % j, [128, 2, 257]) for j in range(2)]
        s_m = [self.din_t("s_m%d" % j, [2, 1]) for j in range(2)]
        s_ST = [self.din_t("s_ST%d" % j, [128, 8, 64]) for j in range(2)]
        s_hb = [self.din_t("s_hb%d" % j, [128, 6, 3]) for j in range(2)]
        s_SG = [self.din_t("s_SG%d" % j, [128, 4, 128]) for j in range(2)]
        s_hc = [self.din_t("s_hc%d" % j, [128, 12, 3]) for j in range(2)]
        o_Cx = [self.dout_t("o_Cx%d" % j, [128, 2, 257]) for j in range(3)]
        o_m = [self.dout_t("o_m%d" % j, [2, 1]) for j in range(3)]
        o_ST = [self.dout_t("o_ST%d" % j, [128, 8, 64]) for j in range(3)]
        o_hb = [self.dout_t("o_hb%d" % j, [128, 6, 3]) for j in range(3)]
        o_SG = [self.dout_t("o_SG%d" % j, [128, 4, 128]) for j in range(3)]
        o_hc = [self.dout_t("o_hc%d" % j, [128, 12, 3]) for j in range(3)]

        def internal(name, shape):
            return DT(nc.dram_tensor(name, list(shape), BF16, kind="Internal").ap(), name)
        own = [(b * TB, TB) for b in range(NOWN)] + [(NOWN * TB, DEC_SEQ)]
        G1in = [internal("g1in%d" % b, [1024, n]) for b, (_, n) in enumerate(own)]
        G1out = [internal("g1out%d" % b, [2048, n]) for b, (_, n) in enumerate(own)]
        G3in = [internal("g3in%d" % b, [1024, n]) for b, (_, n) in enumerate(own)]
        G3out = [internal("g3out%d" % b, [2048, n]) for b, (_, n) in enumerate(own)]
        seqn = [TB] * 8 + [DEC_SEQ] * 2
        G2in = [internal("g2in%d" % k, [1024, n]) for k, n in enumerate(seqn)]
        G2out = [internal("g2out%d" % k, [2048, n]) for k, n in enumerate(seqn)]
        G4in = [internal("g4in%d" % k, [512, n]) for k, n in enumerate(seqn)]
        G4out = [internal("g4out%d" % k, [1024, n]) for k, n in enumerate(seqn)]

        def allgather(gi, go):
            S.op('pool', lambda e: e.collective_compute("AllGather", ALU.bypass, replica_groups=RG,
                                                        ins=[gi.ap], outs=[go.ap]),
                 reads=[gi.buf], writes=[go.buf], cc=True)

        self.psb = []
        self._st = ExitStack()
        for i in range(8):
            h = self._st.enter_context(nc.psum_tensor("psb%d" % i, [128, 512], F32))
            t = Tl(h, "psb%d" % i)
            t.buf.excl = True
            t.hb = h[:, :].bitcast(BF16)
            self.psb.append(t)

        def pbv(t, *idx):
            return V(t.hb[idx], t.buf)

        xb = [P("x%d" % b, [128, 8, n]) for b, (_, n) in enumerate(own)]
        hn = P("hn", [128, 8, TB], BF16)
        gcol = P("gcol", [128, 6, 8]); gfcol = P("gfcol", [128, 8]); rmask = P("rmask", [128, 2])
        ones_f = P("ones_f", [128, 128]); ones_b = P("ones_b", [128, 128], BF16)
        rstd = P("rstd", [128, TB])
        ident_f = P("ident_f", [128, 128]); ident_b = P("ident_b", [128, 128], BF16)
        maskT = P("maskT", [128, 128]); maskS = P("maskS", [128, 128])
        ones16 = P("ones16", [16, 128]); onesr = P("onesr", [16, TB]); sel16 = P("sel16", [16, 8, 128])
        bi = P("bi", [2, 1]); nbf = P("nbf", [2, 1])
        gnB = P("gnB", [128, 512]); snB = P("snB", [128, 512]); dskB = P("dskB", [128, 8])
        cw = P("cw", [128, 6, 4]); cb = P("cb", [128, 6]); dtb = P("dtb", [8, 1]); negA = P("negA", [8, 1])
        Cx = P("Cx", [128, 2, 257]); ST = P("ST", [128, 8, 64]); Sb = P("Sb", [128, 8, 64], BF16)
        hist_b = P("hist_b", [128, 6, 3])
        Fc = P("Fc", [2, 1]); Gc = P("Gc", [2, 1]); mo = P("mo", [2, 1])
        gcw = P("gcw", [128, 12, 4]); gdtb = P("gdtb", [4, 1]); gnegA = P("gnegA", [4, 1]); gdnB = P("gdnB", [128, 128])
        SG = P("SG", [128, 4, 128]); SGb = P("SGb", [128, 4, 128], BF16); hist_c = P("hist_c", [128, 12, 3])

        for (dst, src) in ((gcol, g_col), (gfcol, gf_col), (rmask, rmask_d), (bi, d_bi), (nbf, d_bf), (gnB, d_gnB),
                           (snB, d_snB), (dskB, d_dskB), (cw, d_cw), (cb, d_cb), (dtb, d_dtb), (negA, d_alog),
                           (gcw, d_gcw), (gdtb, d_gdtb), (gnegA, d_galog), (gdnB, d_gdnB)):
            dma(dst[:], src.v())
        self.memset(ones_f[:], 1.0 / D); self.memset(ones_b[:], 1.0 / D)
        self.memset(ones16[:], 1.0); self.memset(onesr[:], 1.0)
        self.memset(ident_f[:], 0.0)
        S.op('pool', lambda e: e.affine_select(out=ident_f.h[:], in_=ident_f.h[:], pattern=[[-1, 128]],
                                               compare_op=ALU.not_equal, fill=1.0, base=0, channel_multiplier=1),
             reads=[ident_f.buf], writes=[ident_f.buf])
        self.cp(ident_b[:], ident_f[:])
        self.memset(maskT[:], 1.0)
        S.op('pool', lambda e: e.affine_select(out=maskT.h[:], in_=maskT.h[:], pattern=[[1, 128]],
                                               compare_op=ALU.is_ge, fill=0.0, base=0, channel_multiplier=-1),
             reads=[maskT.buf], writes=[maskT.buf])
        self.memset(maskS[:], 1.0)
        S.op('pool', lambda e: e.affine_select(out=maskS.h[:], in_=maskS.h[:], pattern=[[1, 128]],
                                               compare_op=ALU.is_gt, fill=0.0, base=0, channel_multiplier=-1),
             reads=[maskS.buf], writes=[maskS.buf])
        self.memset(sel16[:], 0.0)
        S.op('pool', lambda e: e.affine_select(out=sel16.h[:], in_=sel16.h[:], pattern=[[-1, 8], [0, 128]],
                                               compare_op=ALU.not_equal, fill=1.0, base=0, channel_multiplier=1),
             reads=[sel16.buf], writes=[sel16.buf])
        self.ts(nbf[:], nbf[:], -1.0, ALU.mult)
        self.act(negA[:], negA[:], AF.Exp); self.ts(negA[:], negA[:], -1.0, ALU.mult)
        self.act(gnegA[:], gnegA[:], AF.Exp); self.ts(gnegA[:], gnegA[:], -1.0, ALU.mult)

        def rmsnorm(n, gv, out_t, sq, x):
            self.act(sq[:, :, 0:n], x[:, :, 0:n], AF.Square)
            p = self.ps()
            for kc in range(8):
                self.mm(p[:, 0:n], ones_b[:], sq[:, kc, 0:n], start=(kc == 0), stop=(kc == 7))
            self.act(rstd[:, 0:n], p[:, 0:n], AF.Sqrt, bias=EPS)
            self.recip(rstd[:, 0:n], rstd[:, 0:n])
            for kc in range(8):
                self.stt(out_t[:, kc, 0:n], x[:, kc, 0:n], gv(kc), rstd[:, 0:n], ALU.mult, ALU.mult)

        def ffn(l, i, n, x):
            self.phase_begin(('ffn',))
            sq = A("sq", [128, 8, TB], BF16)
            act_t = A("ffn_act", [128, NFF, TB], BF16)
            wd = A("ffn_wd", [128, NFF, D], BF16)
            wgu = [[A("ffn_wg%d" % b, [128, 8, 512], BF16), A("ffn_wu%d" % b, [128, 8, 512], BF16)]
                   for b in range(2)]
            sil = [A("ffn_sil0", [128, TB])] * 2
            rmsnorm(n, lambda kc: gcol[:, l * 3 + 2 * i, kc:kc + 1], hn, sq, x)
            wg_src = w_gate.ap[l, i].rearrange("(k p) f -> p k f", p=128)
            wu_src = w_up.ap[l, i].rearrange("(k p) f -> p k f", p=128)
            wd_src = w_down.ap[l, i].rearrange("(f p) d -> p f d", p=128)
            groups = [(g * 512, 512) for g in range(5)] + [(2560, 256)]
            for gi, (c0, ncol) in enumerate(groups):
                b = gi % 2
                self.wload(wgu[b][0][:, :, 0:ncol], w_gate, wg_src[:, :, c0:c0 + ncol], ('wg', l, i, gi))
                self.wload(wgu[b][1][:, :, 0:ncol], w_up, wu_src[:, :, c0:c0 + ncol], ('wu', l, i, gi))
                if gi == 1 and getattr(self, '_wd_res', None) != (l, i, self.phase_gen):
                    for q in range(2):
                        self.wload(wd[:, q * 11:(q + 1) * 11, :], w_down, wd_src[:, q * 11:(q + 1) * 11, :], ('wd', l, i, q))
                    self._wd_res = (l, i, self.phase_gen)
                for j in range(ncol // 128):
                    f = c0 // 128 + j
                    pg = self.ps()
                    pu = self.ps()
                    for kc in range(8):
                        self.mm(pg[:, 0:n], wgu[b][0][:, kc, j * 128:(j + 1) * 128], hn[:, kc, 0:n],
                                start=(kc == 0), stop=(kc == 7))
                    for kc in range(8):
                        self.mm(pu[:, 0:n], wgu[b][1][:, kc, j * 128:(j + 1) * 128], hn[:, kc, 0:n],
                                start=(kc == 0), stop=(kc == 7))
                    sl = sil[f % 2]
                    self.act(sl[:, 0:n], pg[:, 0:n], AF.Silu)
                    self.tt(act_t[:, f, 0:n], sl[:, 0:n], pu[:, 0:n], ALU.mult)
            for dc in range(8):
                p = self.ps()
                for f in range(NFF):
                    self.mm(p[:, 0:n], wd[:, f, dc * 128:(dc + 1) * 128], act_t[:, f, 0:n],
                            start=(f == 0), stop=(f == NFF - 1))
                self.stt(x[:, dc, 0:n], p[:, 0:n], 0.5, x[:, dc, 0:n], ALU.mult, ALU.add)

        def hn_exchange(gidx, n, x, gin, gout):
            self.phase_begin(('ffn',))
            sq = A("sq", [128, 8, TB], BF16)
            rmsnorm(n, lambda kc: gcol[:, gidx, kc:kc + 1], hn, sq, x)
            dma(gin.v(gin.ap.rearrange("(k p) t -> p k t", p=128)), hn[:, :, 0:n])
            allgather(gin, gout)

        def out_proj_sel(w_dt, nfc, gouts, n, x, tag):
            self.phase_begin(('ops', nfc))
            wb = [A("wb0", [128, 8, 512], BF16), A("wb1", [128, 8, 512], BF16)]
            cand = [A("cand0", [128, nfc, TB], BF16), A("cand1", [128, nfc, TB], BF16)]
            hT = A("hTf", [128, nfc, TB], BF16)
            half = nfc // 2 * 128
            for ci, go in enumerate(gouts):
                for r in range(2):
                    src = go.ap[r * half:(r + 1) * half, :].rearrange("(f p) t -> p f t", p=128)
                    if nfc == 16:
                        dma(cand[ci][:, r * 4:r * 4 + 4, 0:n], go.v(src[:, 0:4, :]))
                        dma(cand[ci][:, 8 + r * 4:8 + r * 4 + 4, 0:n], go.v(src[:, 4:8, :]))
                    else:
                        dma(cand[ci][:, r * 4:r * 4 + 4, 0:n], go.v(src[:, 0:4, :]))
            self.ts(hT[:, :, 0:n], cand[0][:, :, 0:n], rmask[:, 0:1], ALU.mult)
            self.stt(hT[:, :, 0:n], cand[1][:, :, 0:n], rmask[:, 1:2], hT[:, :, 0:n], ALU.mult, ALU.add)
            wo_src = w_dt.ap.rearrange("(f p) d -> p f d", p=128)
            nfh = nfc // 8
            li = 0
            for dh in range(2):
                pacc = [self.ps() for _ in range(4)]
                for fh in range(nfh):
                    w = wb[li % 2]
                    li += 1
                    self.wload(w[:], w_dt, wo_src[:, fh * 8:(fh + 1) * 8, dh * 512:(dh + 1) * 512], ('wo', nfc, dh, fh))
                    for j in range(4):
                        for f8 in range(8):
                            self.mm(pacc[j][:, 0:n], w[:, f8, j * 128:(j + 1) * 128], hT[:, fh * 8 + f8, 0:n],
                                    start=(fh == 0 and f8 == 0), stop=(fh == nfh - 1 and f8 == 7))
                for j in range(4):
                    dc = dh * 4 + j
                    self.tt(x[:, dc, 0:n], pacc[j][:, 0:n], x[:, dc, 0:n], ALU.add)

        def conv_chunk(p, n, fc, hist, cwt, cbt, stage, cacc, outT):
            stg = stage
            acc = cacc
            self.cp(stg[:, 0:3], hist[:, fc, :])
            self.cp(stg[:, 3:3 + n], p[:, 0:n], eng='act')
            self.ts(acc[:, 0:n], stg[:, 0:n], cwt[:, fc, 0:1], ALU.mult)
            for j in range(1, 4):
                self.stt(acc[:, 0:n], stg[:, j:j + n], cwt[:, fc, j:j + 1], acc[:, 0:n], ALU.mult, ALU.add)
            if cbt is not None:
                self.act(outT[:, fc, 0:n], acc[:, 0:n], AF.Silu, bias=cbt[:, fc:fc + 1])
            else:
                self.act(outT[:, fc, 0:n], acc[:, 0:n], AF.Silu)
            self.cp(hist[:, fc, :], stg[:, n:n + 3])

        def mixer_ab(n, L, g1, r, g2in, g2out):
            nch = n // L
            NM = 2
            self.phase_begin(('ab', n))
            hT = A("hT", [128, 8, n], BF16)
            wb = [A("wb0", [128, 8, 512], BF16), A("wb1", [128, 8, 512], BF16)]
            wgt = A("wgt", [128, 8, 4], BF16); wdt = A("wdt", [128, 8, 8], BF16)
            qT = A("qT", [128, NM, n], BF16); kT = A("kT", [128, NM, n], BF16)
            k_tok = A("k_tok", [128, nch, 256], BF16)
            v_ext = A("v_ext", [128, nch, NM, 257], BF16)
            so = A("so", [128, nch, 512], BF16); zs = A("zs", [128, nch, 512], BF16)
            xbcT = A("xbcT", [128, 6, n], BF16)
            x_tok = A("x_tok", [128, nch, 512], BF16); bm_tok = A("bm_tok", [128, nch, 128], BF16)
            h_tok = A("h_tok", [128, 1024], BF16)
            stage = A("stage0", [128, 3 + n]); cacc = A("cacc0", [128, n])
            R = [A("row%d" % i, [16, n]) for i in range(10)]
            gT = A("gT", [128, nch, 6]); gS = A("gS", [128, nch, 32])
            decB = A("decB", [128, nch, NM]); decS = A("decS", [128, nch, 8])
            D4 = A("D4", [NM, nch, NM]); D16 = A("D16", [8, nch, 8]); Gpv = A("Gpv", [NM, nch]); dec4 = A("dec4", [NM, nch])
            PTm = A("PTm", [128, 128], BF16)
            vu = [A("vu0", [128, 257], BF16), A("vu1", [128, 257], BF16)]
            Cb = A("Cb", [128, NM, 257], BF16)
            cbm = A("cbm", [128, 128])
            seg = [A("seg0", [128, 128]), A("seg1", [128, 128])]
            MT = A("MT", [128, 8, 128], BF16)
            xd = A("xd", [128, 512], BF16); xw = A("xw", [128, 512], BF16)
            ya = A("ya", [128, 512]); yb = A("yb", [128, 512]); hraw = A("hraw", [128, 256]); junk = yb
            c1 = A("c1", [128, 1]); c2 = A("c2", [128, 1]); c3 = A("c3", [128, 1]); c4 = A("c4", [128, 1])

            dma(hn[:, :, 0:n], g1.v(g1.ap[r * 1024:(r + 1) * 1024, :].rearrange("(k p) t -> p k t", p=128)))
            win = abw.ap.rearrange("(k p) c -> p k c", p=128)
            lw = [0]

            def loadw(c0, ncol):
                w = wb[lw[0] % 2]
                lw[0] += 1
                self.wload(w[:, :, 0:ncol], abw, win[:, :, c0:c0 + ncol], ('abin', c0))
                return w

            def fm_proj(w, j, M=128):
                p = self.ps()
                for kc in range(8):
                    self.mm(p[0:M, 0:n], w[:, kc, j * 128:j * 128 + M], hn[:, kc, 0:n], start=(kc == 0), stop=(kc == 7))
                return p

            def tm_proj(w, c, c0=0, ncol=512):
                p = self.ps()
                for kc in range(8):
                    self.mm(p[0:L, 0:ncol], hn[:, kc, c * L:(c + 1) * L], w[:, kc, c0:c0 + ncol], start=(kc == 0), stop=(kc == 7))
                return p

            self.wload(wgt[:], abw, win[:, :, 1536:1540], ('abg',))
            self.wload(wdt[:], abw, win[:, :, 2820:2828], ('abdt',))
            w = loadw(0, 512)
            for h in range(NM):
                p = fm_proj(w, h)
                self.cp(qT[:, h, 0:n], p[:, 0:n], eng='act')
            for h in range(NM):
                p = fm_proj(w, NM + h)
                self.ts(kT[:, h, 0:n], p[:, 0:n], DKs, ALU.mult)
            for c in range(nch):
                p = tm_proj(w, c, 256, 256)
                self.ts(k_tok[0:L, c, :], p[0:L, 0:256], DKs, ALU.mult)
            self.memset(v_ext[:, :, :, 256:257], 1.0)
            w = loadw(512, 512)
            for c in range(nch):
                p = tm_proj(w, c)
                self.cp(v_ext[0:L, c, 0:NM, 0:256], V(p.h[0:L, 0:512].rearrange("p (a b) -> p a b", b=256), p.buf), eng='act')
            w = loadw(1024, 512)
            for c in range(nch):
                p = tm_proj(w, c)
                self.act(so[0:L, c, :], p[0:L, 0:512], AF.Sigmoid)
            w = loadw(1540, 512)
            for c in range(nch):
                p = tm_proj(w, c)
                self.act(zs[0:L, c, :], p[0:L, 0:512], AF.Silu)
            w = loadw(2052, 512)
            for j in range(4):
                p = fm_proj(w, j)
                conv_chunk(p, n, j, hist_b, cw, cb, stage, cacc, xbcT)
            w = loadw(2564, 256)
            for j in range(2):
                p = fm_proj(w, j)
                conv_chunk(p, n, 4 + j, hist_b, cw, cb, stage, cacc, xbcT)
            t1, Fn, a_, G_, em, u_, w_, tmp = R[0], R[1], R[2], R[3], R[4], R[5], R[6], R[7]
            pig = self.ps()
            for kc in range(8):
                self.mm(pig[0:NM, 0:n], wgt[:, kc, 0:NM], hn[:, kc, 0:n], start=(kc == 0), stop=(kc == 7))
            pfg = self.ps()
            for kc in range(8):
                self.mm(pfg[0:NM, 0:n], wgt[:, kc, NM:2 * NM], hn[:, kc, 0:n], start=(kc == 0), stop=(kc == 7))
            self.act(t1[0:NM, 0:n], pfg[0:NM, 0:n], AF.Exp, bias=nbf[:], scale=-1.0)
            self.act(t1[0:NM, 0:n], t1[0:NM, 0:n], AF.Ln, bias=1.0)
            self.scan(Fn[0:NM, 0:n], onesr[0:NM, 0:n], t1[0:NM, 0:n], Fc[:], ALU.mult, ALU.add)
            self.stt(a_[0:NM, 0:n], pig[0:NM, 0:n], bi[:], Fn[0:NM, 0:n], ALU.add, ALU.add)
            self.scan(G_[0:NM, 0:n], onesr[0:NM, 0:n], a_[0:NM, 0:n], Gc[:], ALU.mult, ALU.max)
            self.tt(tmp[0:NM, 0:n], Fn[0:NM, 0:n], G_[0:NM, 0:n], ALU.subtract)
            self.act(em[0:NM, 0:n], tmp[0:NM, 0:n], AF.Exp)

            def r3(t, np_):
                return t.h[0:np_, 0:n].rearrange("p (c l) -> p c l", l=L)
            gend = V(r3(G_, NM)[:, :, L - 1:L].to_broadcast([NM, nch, L]), G_.buf)
            self.tt(V(r3(tmp, NM), tmp.buf), V(r3(a_, NM), a_.buf), gend, ALU.subtract)
            self.act(u_[0:NM, 0:n], tmp[0:NM, 0:n], AF.Exp)
            self.tt(V(r3(tmp, NM), tmp.buf), V(r3(G_, NM), G_.buf), gend, ALU.subtract)
            self.act(w_[0:NM, 0:n], tmp[0:NM, 0:n], AF.Exp, scale=-1.0)
            self.cp(Gpv[0:NM, 0:1], Gc[:])
            if nch > 1:
                self.cp(V(Gpv.h[0:NM, 1:nch].unsqueeze(2), Gpv.buf), V(r3(G_, NM)[:, 0:nch - 1, L - 1:L], G_.buf))
            self.tt(V(dec4.h[0:NM, 0:nch].unsqueeze(2), dec4.buf), V(Gpv.h[0:NM, 0:nch].unsqueeze(2), Gpv.buf),
                    V(r3(G_, NM)[:, :, L - 1:L], G_.buf), ALU.subtract)
            self.act(dec4[0:NM, 0:nch], dec4[0:NM, 0:nch], AF.Exp)
            self.tt(D4[0:NM, 0:nch, :], V(dec4.h[0:NM, 0:nch].unsqueeze(2).to_broadcast([NM, nch, NM]), dec4.buf),
                    V(ident_f.h[0:NM, 0:NM].unsqueeze(1).to_broadcast([NM, nch, NM]), ident_f.buf), ALU.mult)
            p = self.ps()
            self.mm(p[:, 0:nch * NM], ones16[0:NM, :], V(D4.h[0:NM, 0:nch, :].rearrange("p c h -> p (c h)"), D4.buf))
            self.cp(V(decB.h[:, 0:nch, :].rearrange("p c h -> p (c h)"), decB.buf), p[:, 0:nch * NM])
            self.cp(Fc[:], Fn[0:NM, n - 1:n])
            self.cp(Gc[:], G_[0:NM, n - 1:n])
            p = self.ps()
            for c in range(nch):
                for qi, rt in enumerate((u_, w_, em)):
                    self.tr(p[0:L, c * 6 + qi * NM:c * 6 + qi * NM + NM], rt[0:NM, c * L:(c + 1) * L], ident_f[0:NM, 0:NM])
            self.cp(V(gT.h[0:L, 0:nch, :].rearrange("p c h -> p (c h)"), gT.buf), p[0:L, 0:nch * 6])
            dt_, ar, b_, eb, nb, e2 = R[0], R[1], R[2], R[8], R[9], R[7]
            pdt = self.ps()
            for kc in range(8):
                self.mm(pdt[0:8, 0:n], wdt[:, kc, :], hn[:, kc, 0:n], start=(kc == 0), stop=(kc == 7))
            self.act(dt_[0:8, 0:n], pdt[0:8, 0:n], AF.Exp, bias=dtb[:])
            self.act(dt_[0:8, 0:n], dt_[0:8, 0:n], AF.Ln, bias=1.0)
            self.ts(ar[0:8, 0:n], dt_[0:8, 0:n], negA[:], ALU.mult)
            for c in range(nch):
                self.scan(b_[0:8, c * L:(c + 1) * L], onesr[0:8, 0:L], ar[0:8, c * L:(c + 1) * L], 0.0, ALU.mult, ALU.add)
            self.act(eb[0:8, 0:n], b_[0:8, 0:n], AF.Exp)
            self.ts(nb[0:8, 0:n], b_[0:8, 0:n], -1.0, ALU.mult)
            bLb = V(r3(b_, 8)[:, :, L - 1:L].to_broadcast([8, nch, L]), b_.buf)
            self.tt(V(r3(e2, 8), e2.buf), bLb, V(r3(b_, 8), b_.buf), ALU.subtract)
            self.act(e2[0:8, 0:n], e2[0:8, 0:n], AF.Exp)
            self.tt(e2[0:8, 0:n], e2[0:8, 0:n], dt_[0:8, 0:n], ALU.mult)
            self.tt(D16[:, 0:nch, :], V(r3(eb, 8)[:, :, L - 1:L].to_broadcast([8, nch, 8]), eb.buf),
                    V(ident_f.h[0:8, 0:8].unsqueeze(1).to_broadcast([8, nch, 8]), ident_f.buf), ALU.mult)
            p = self.ps()
            self.mm(p[:, 0:nch * 8], ones16[0:8, :], V(D16.h[:, 0:nch, :].rearrange("p c h -> p (c h)"), D16.buf))
            self.cp(V(decS.h[:, 0:nch, :].rearrange("p c h -> p (c h)"), decS.buf), p[:, 0:nch * 8])
            p = self.ps()
            for c in range(nch):
                for qi, rt in enumerate((dt_, nb, eb, e2)):
                    self.tr(p[0:L, c * 32 + qi * 8:c * 32 + qi * 8 + 8], rt[0:8, c * L:(c + 1) * L], ident_f[0:8, 0:8])
            self.cp(V(gS.h[0:L, 0:nch, :].rearrange("p c h -> p (c h)"), gS.buf), p[0:L, 0:nch * 32])
            for c in range(nch):
                p = self.ps()
                for fc in range(4):
                    self.tr(pbv(p, slice(0, L), slice(fc * 128, (fc + 1) * 128)), xbcT[:, fc, c * L:(c + 1) * L], ident_b[:])
                self.tr(pbv(p, slice(0, L), slice(512, 640)), xbcT[:, 4, c * L:(c + 1) * L], ident_b[:])
                self.cp(x_tok[0:L, c, :], pbv(p, slice(0, L), slice(0, 512)), eng='act')
                self.cp(bm_tok[0:L, c, :], pbv(p, slice(0, L), slice(512, 640)))
            junkm = [A("junkm%d" % i, [128, 256]) for i in range(NM)]
            PTms = [A("PTm%d" % i, [128, 128], BF16) for i in range(NM)]
            hraws = [A("hraw%d" % i, [128, 256]) for i in range(NM)]
            ccols = [[A("cm%d_%d" % (i, j), [128, 1]) for j in range(3)] for i in range(NM)]

            def m_chain(c, h):
                cs, ce = c * L, (c + 1) * L
                ht = h_tok
                PTm = PTms[h]; junk = junkm[h]; hraw = hraws[h]; c1, c2, c3 = ccols[h]
                v_u = vu[h % 2]
                p1 = self.psb[2 * h]
                self.mm(p1[0:L, 0:L], kT[:, h, cs:ce], qT[:, h, cs:ce])
                yield
                self.tt(PTm[0:L, 0:L], p1[0:L, 0:L], maskT[0:L, 0:L], ALU.mult)
                yield
                self.ts(v_u[0:L, :], v_ext[0:L, c, h, :], gT[0:L, c, h:h + 1], ALU.mult)
                yield
                self.ts(Cx[:, h, :], Cx[:, h, :], decB[:, c, h:h + 1], ALU.mult)
                yield
                self.cp(Cb[:, h, :], Cx[:, h, :], eng='act')
                yield
                p2 = self.psb[2 * h + 1]
                self.mm(p2[0:L, 0:257], PTm[0:L, 0:L], v_u[0:L, :], start=True, stop=False)
                yield
                self.mm(p2[0:L, 0:257], qT[:, h, cs:ce], Cb[:, h, :], start=False, stop=True)
                yield
                p3 = self.psb[2 * h]
                self.mm(p3[:, 0:257], k_tok[0:L, c, h * 128:(h + 1) * 128], v_u[0:L, :])
                yield
                self.tt(Cx[:, h, :], p3[:, 0:257], Cx[:, h, :], ALU.add)
                yield
                wcol = gT[0:L, c, NM + h:NM + h + 1]
                emcol = gT[0:L, c, 2 * NM + h:2 * NM + h + 1]
                self.act(c1[0:L, :], p2[0:L, 256:257], AF.Abs, scale=wcol)
                yield
                self.tt(c1[0:L, :], c1[0:L, :], emcol, ALU.max)
                yield
                self.recip(c1[0:L, :], c1[0:L, :])
                yield
                self.tt(c2[0:L, :], c1[0:L, :], wcol, ALU.mult)
                yield
                self.act(junk[0:L, 0:256], p2[0:L, 0:256], AF.Square, scale=c2[0:L, :])
                yield
                self.rsum(c3[0:L, :], junk[0:L, 0:256])
                yield
                self.ts(hraw[0:L, :], p2[0:L, 0:256], c2[0:L, :], ALU.mult)
                yield
                self.act(c3[0:L, :], c3[0:L, :], AF.Sqrt, bias=EPS, scale=1.0 / 256)
                yield
                self.recip(c3[0:L, :], c3[0:L, :])
                yield
                self.stt(hraw[0:L, :], hraw[0:L, :], c3[0:L, :], gnB[0:L, h * 256:(h + 1) * 256], ALU.mult, ALU.mult)
                yield
                self.tt(ht[0:L, h * 256:(h + 1) * 256], hraw[0:L, :], so[0:L, c, h * 256:(h + 1) * 256], ALU.mult)
                yield

            def s_chain(c):
                cs, ce = c * L, (c + 1) * L
                ht = h_tok
                junk = yb
                p1 = self.psb[4]
                self.mm(p1[0:L, 0:L], xbcT[:, 4, cs:ce], xbcT[:, 5, cs:ce])
                yield
                self.tt(cbm[0:L, 0:L], p1[0:L, 0:L], maskT[0:L, 0:L], ALU.mult)
                yield
                pbb = None
                for hh in range(8):
                    j = hh % 4
                    if j == 0:
                        pbb = self.psb[5 + hh // 4]
                    self.mm(pbb[0:L, j * 128:j * 128 + L], sel16[0:8, hh, 0:L], b_[0:8, cs:ce])
                    sg = seg[hh % 2]
                    self.ts(sg[0:L, 0:L], pbb[0:L, j * 128:j * 128 + L], gS[0:L, c, 8 + hh:9 + hh], ALU.add, 0.0, ALU.min)
                    self.act(sg[0:L, 0:L], sg[0:L, 0:L], AF.Exp)
                    self.tt(MT[0:L, hh, 0:L], sg[0:L, 0:L], cbm[0:L, 0:L], ALU.mult)

                def v3(t, ap):
                    return V(ap.rearrange("p (h e) -> p h e", e=64), t.buf)
                xg = x_tok.h[0:L, c, :]
                self.tt(v3(xd, xd.h[0:L, :]), v3(x_tok, xg),
                        V(gS.h[0:L, c, 0:8].unsqueeze(2).to_broadcast([L, 8, 64]), gS.buf), ALU.mult)
                self.tt(v3(xw, xw.h[0:L, :]), v3(x_tok, xg),
                        V(gS.h[0:L, c, 24:32].unsqueeze(2).to_broadcast([L, 8, 64]), gS.buf), ALU.mult)
                pY1 = self.psb[7]
                for hh in range(8):
                    self.mm(pY1[0:L, hh * 64:(hh + 1) * 64], MT[0:L, hh, 0:L], xd[0:L, hh * 64:(hh + 1) * 64])
                pY2 = self.psb[4]
                self.mm(pY2[0:L, 0:512], xbcT[:, 5, cs:ce], V(Sb.h[:, :, :].rearrange("p h e -> p (h e)"), Sb.buf))
                yield
                self.tt(v3(ya, ya.h[0:L, :]), v3(pY2, pY2.h[0:L, 0:512]),
                        V(gS.h[0:L, c, 16:24].unsqueeze(2).to_broadcast([L, 8, 64]), gS.buf), ALU.mult)
                self.tt(ya[0:L, :], pY1[0:L, 0:512], ya[0:L, :], ALU.add)
                yield
                self.tt(v3(yb, yb.h[0:L, :]), v3(x_tok, xg),
                        V(dskB.h[0:L, 0:8].unsqueeze(2).to_broadcast([L, 8, 64]), dskB.buf), ALU.mult)
                self.tt(ya[0:L, :], ya[0:L, :], yb[0:L, :], ALU.add)
                yield
                self.tt(ya[0:L, :], ya[0:L, :], zs[0:L, c, :], ALU.mult)
                yield
                self.act(junk[0:L, :], ya[0:L, :], AF.Square)
                yield
                self.rsum(c4[0:L, :], junk[0:L, :])
                yield
                self.act(c4[0:L, :], c4[0:L, :], AF.Sqrt, bias=EPS, scale=1.0 / 512)
                yield
                self.recip(c4[0:L, :], c4[0:L, :])
                yield
                self.stt(ht[0:L, 512:1024], ya[0:L, :], c4[0:L, :], snB[0:L, :], ALU.mult, ALU.mult)
                yield
                pS = self.psb[5]
                self.mm(pS[:, 0:512], bm_tok[0:L, c, :], xw[0:L, :])
                yield
                self.tt(ST[:, :, :], ST[:, :, :],
                        V(decS.h[:, c, 0:8].unsqueeze(2).to_broadcast([128, 8, 64]), decS.buf), ALU.mult)
                stf = V(ST.h[:, :, :].rearrange("p h e -> p (h e)"), ST.buf)
                self.tt(stf, pS[:, 0:512], stf, ALU.add)
                yield
                self.cp(Sb[:, :, :], ST[:, :, :], eng='act')
                yield

            def interleave(gens):
                gens = list(gens)
                while gens:
                    for g in list(gens):
                        try:
                            next(g)
                        except StopIteration:
                            gens.remove(g)

            for c in range(nch):
                cs, ce = c * L, (c + 1) * L
                ht = h_tok
                interleave([m_chain(c, h) for h in range(NM)] + [s_chain(c)])
                p = self.ps()
                for f8 in range(8):
                    self.tr(pbv(p, slice(0, 128), slice(f8 * 128, f8 * 128 + L)), ht[0:L, f8 * 128:(f8 + 1) * 128],
                            ident_b[0:L, 0:L])
                self.cp(hT[:, 0:8, cs:ce], V(p.hb[:, 0:1024].rearrange("p (f t) -> p f t", t=128)[:, :, 0:L], p.buf), eng='act')
            dma(g2in.v(g2in.ap.rearrange("(f p) t -> p f t", p=128)), hT[:, :, 0:n])
            allgather(g2in, g2out)

        def mixer_c(n, L, g3, r, g4in, g4out):
            nch = n // L
            nsq = {128: 6, 16: 3}[L]
            NH = 4
            self.phase_begin(('c', n))
            hT = A("hT", [128, NH, n], BF16)
            wb = [A("wb0", [128, 8, 512], BF16), A("wb1", [128, 8, 512], BF16)]
            wba = A("wba", [128, 8, 8], BF16)
            qkvT = A("qkvT", [128, 12, n], BF16)
            zs = A("zs", [128, nch, 512], BF16)
            k_tok = A("k_tok", [128, nch, 512], BF16); v_tok = A("v_tok", [128, nch, 512], BF16)
            h_tok = A("h_tok", [128, 512], BF16)
            stage = A("stage0", [128, 3 + n]); cacc = A("cacc0", [128, n])
            R = [A("row%d" % i, [16, n]) for i in range(7)]
            gC = A("gC", [128, nch, 24])
            decC = A("decC", [128, nch, NH]); D8 = A("D8", [NH, nch, NH])
            sqh = A("sqh0", [128, n], BF16); rst = A("rst0", [128, n])
            Xs = [A("X%d" % i, [128, 128]) for i in range(NH)]
            XTs = [A("XT%d" % i, [128, 128]) for i in range(NH)]
            TTs = [A("TT%d" % i, [128, 128]) for i in range(NH)]
            KKs = [A("KK%d" % i, [128, 128]) for i in range(NH)]
            dTs_ = [A("dT%d" % i, [128, 128]) for i in range(NH)]
            tmpm = [A("tmpm%d" % i, [128, 128]) for i in range(NH)]
            R1 = tmpm
            R2 = KKs
            U0s = [[A("U0s_%d_%d" % (c, h), [128, 128]) for h in range(NH)] for c in range(min(2, nch))]
            WTb = [[A("WTb_%d_%d" % (c, h), [128, 128], BF16) for h in range(NH)] for c in range(min(2, nch))]
            QKd = [[A("QKd_%d_%d" % (c, h), [128, 128], BF16) for h in range(NH)] for c in range(min(2, nch))]
            ub = [A("ub%d" % i, [128, 128], BF16) for i in range(NH)]
            kw = [A("kw%d" % i, [128, 128], BF16) for i in range(NH)]
            o1s = [A("o1s%d" % i, [128, 128]) for i in range(NH)]
            oo = [A("oo%d" % i, [128, 128]) for i in range(NH)]
            jk = o1s
            cc = [A("cc%d" % i, [128, 1]) for i in range(NH)]

            dma(hn[:, :, 0:n], g3.v(g3.ap[r * 1024:(r + 1) * 1024, :].rearrange("(k p) t -> p k t", p=128)))
            win = gw.ap.rearrange("(k p) c -> p k c", p=128)
            lw = [0]

            def loadw(c0, ncol):
                w = wb[lw[0] % 2]
                lw[0] += 1
                self.wload(w[:, :, 0:ncol], gw, win[:, :, c0:c0 + ncol], ('gin', c0))
                return w

            self.wload(wba[:], gw, win[:, :, 2048:2056], ('gba',))
            for g3_ in range(3):
                w = loadw(g3_ * 512, 512)
                for j in range(4):
                    fc = g3_ * 4 + j
                    p = self.ps()
                    for kc in range(8):
                        self.mm(p[:, 0:n], w[:, kc, j * 128:(j + 1) * 128], hn[:, kc, 0:n], start=(kc == 0), stop=(kc == 7))
                    conv_chunk(p, n, fc, hist_c, gcw, None, stage, cacc, qkvT)
            w = loadw(1536, 512)
            for c in range(nch):
                p = self.ps()
                for kc in range(8):
                    self.mm(p[0:L, 0:512], hn[:, kc, c * L:(c + 1) * L], w[:, kc, 0:512], start=(kc == 0), stop=(kc == 7))
                self.act(zs[0:L, c, :], p[0:L, 0:512], AF.Silu)
            for fc in range(2 * NH):
                self.act(sqh[:, 0:n], qkvT[:, fc, 0:n], AF.Square)
                p = self.ps()
                self.mm(p[:, 0:n], ones_b[:], sqh[:, 0:n])
                self.act(rst[:, 0:n], p[:, 0:n], AF.Sqrt, bias=EPS, scale=float(D))
                self.recip(rst[:, 0:n], rst[:, 0:n])
                if fc < NH:
                    self.stt(qkvT[:, fc, 0:n], qkvT[:, fc, 0:n], DKs, rst[:, 0:n], ALU.mult, ALU.mult)
                else:
                    self.tt(qkvT[:, fc, 0:n], qkvT[:, fc, 0:n], rst[:, 0:n], ALU.mult)
            beta, nbeta, sp_, gam, egam, ngam, e2 = R
            g_ = sp_
            pb_ = self.ps()
            for kc in range(8):
                self.mm(pb_[0:NH, 0:n], wba[:, kc, 0:NH], hn[:, kc, 0:n], start=(kc == 0), stop=(kc == 7))
            pa_ = self.ps()
            for kc in range(8):
                self.mm(pa_[0:NH, 0:n], wba[:, kc, NH:2 * NH], hn[:, kc, 0:n], start=(kc == 0), stop=(kc == 7))
            self.act(beta[0:NH, 0:n], pb_[0:NH, 0:n], AF.Sigmoid)
            self.ts(nbeta[0:NH, 0:n], beta[0:NH, 0:n], -1.0, ALU.mult)
            self.act(sp_[0:NH, 0:n], pa_[0:NH, 0:n], AF.Exp, bias=gdtb[:])
            self.act(sp_[0:NH, 0:n], sp_[0:NH, 0:n], AF.Ln, bias=1.0)
            self.ts(g_[0:NH, 0:n], sp_[0:NH, 0:n], gnegA[:], ALU.mult)
            for c in range(nch):
                self.scan(gam[0:NH, c * L:(c + 1) * L], onesr[0:NH, 0:L], g_[0:NH, c * L:(c + 1) * L], 0.0, ALU.mult, ALU.add)
            self.act(egam[0:NH, 0:n], gam[0:NH, 0:n], AF.Exp)
            self.ts(ngam[0:NH, 0:n], gam[0:NH, 0:n], -1.0, ALU.mult)

            def r3(t, np_):
                return t.h[0:np_, 0:n].rearrange("p (c l) -> p c l", l=L)
            gLb = V(r3(gam, NH)[:, :, L - 1:L].to_broadcast([NH, nch, L]), gam.buf)
            self.tt(V(r3(e2, NH), e2.buf), gLb, V(r3(gam, NH), gam.buf), ALU.subtract)
            self.act(e2[0:NH, 0:n], e2[0:NH, 0:n], AF.Exp)
            self.tt(sp_[0:NH, 0:n], beta[0:NH, 0:n], egam[0:NH, 0:n], ALU.mult)
            self.tt(D8[:, 0:nch, :], V(r3(egam, NH)[:, :, L - 1:L].to_broadcast([NH, nch, NH]), egam.buf),
                    V(ident_f.h[0:NH, 0:NH].unsqueeze(1).to_broadcast([NH, nch, NH]), ident_f.buf), ALU.mult)
            p = self.ps()
            self.mm(p[:, 0:nch * NH], ones16[0:NH, :], V(D8.h[:, 0:nch, :].rearrange("p c h -> p (c h)"), D8.buf))
            self.cp(V(decC.h[:, 0:nch, :].rearrange("p c h -> p (c h)"), decC.buf), p[:, 0:nch * NH])
            p = self.ps()
            for c in range(nch):
                for qi, rt in enumerate((beta, ngam, egam, e2, sp_, nbeta)):
                    self.tr(p[0:L, c * 24 + qi * NH:c * 24 + qi * NH + NH], rt[0:NH, c * L:(c + 1) * L], ident_f[0:NH, 0:NH])
            self.cp(V(gC.h[0:L, 0:nch, :].rearrange("p c h -> p (c h)"), gC.buf), p[0:L, 0:nch * 24])
            for c in range(nch):
                p = self.ps()
                for fc in range(NH):
                    self.tr(pbv(p, slice(0, L), slice(fc * 128, (fc + 1) * 128)), qkvT[:, NH + fc, c * L:(c + 1) * L], ident_b[:])
                    self.tr(pbv(p, slice(0, L), slice(512 + fc * 128, 512 + (fc + 1) * 128)), qkvT[:, 2 * NH + fc, c * L:(c + 1) * L], ident_b[:])
                self.cp(k_tok[0:L, c, :], pbv(p, slice(0, L), slice(0, 512)), eng='act')
                self.cp(v_tok[0:L, c, :], pbv(p, slice(0, L), slice(512, 1024)))

            def phaseA(c):
                cs, ce = c * L, (c + 1) * L
                hs = list(range(NH))
                for i, h in enumerate(hs):
                    pk = self.ps()
                    self.mm(pk[0:L, 0:L], qkvT[:, NH + h, cs:ce], qkvT[:, NH + h, cs:ce])
                    self.cp(KKs[i][0:L, 0:L], pk[0:L, 0:L], eng='act')
                    pbb = self.ps()
                    self.mm(pbb[0:L, 0:L], sel16[0:NH, h, 0:L], gam[0:NH, cs:ce])
                    self.ts(tmpm[i][0:L, 0:L], pbb[0:L, 0:L], gC[0:L, c, NH + h:NH + h + 1], ALU.add, 0.0, ALU.min)
                    self.act(tmpm[i][0:L, 0:L], tmpm[i][0:L, 0:L], AF.Exp)
                    self.tt(dTs_[i][0:L, 0:L], tmpm[i][0:L, 0:L], maskS[0:L, 0:L], ALU.mult)
                    self.tt(tmpm[i][0:L, 0:L], tmpm[i][0:L, 0:L], maskT[0:L, 0:L], ALU.mult)
                    pq = self.ps()
                    self.mm(pq[0:L, 0:L], qkvT[:, NH + h, cs:ce], qkvT[:, h, cs:ce])
                    self.tt(QKd[c % 2][h][0:L, 0:L], pq[0:L, 0:L], tmpm[i][0:L, 0:L], ALU.mult)
                for i, h in enumerate(hs):
                    pd = self.ps()
                    self.tr(pd[0:L, 0:L], dTs_[i][0:L, 0:L], ident_f[0:L, 0:L])
                    self.stt(Xs[i][0:L, 0:L], pd[0:L, 0:L], gC[0:L, c, 5 * NH + h:5 * NH + h + 1], KKs[i][0:L, 0:L], ALU.mult, ALU.mult)
                for i, h in enumerate(hs):
                    px = self.ps()
                    self.tr(px[0:L, 0:L], Xs[i][0:L, 0:L], ident_f[0:L, 0:L])
                    self.cp(XTs[i][0:L, 0:L], px[0:L, 0:L], eng='act')
                    self.tt(TTs[i][0:L, 0:L], px[0:L, 0:L], ident_f[0:L, 0:L], ALU.add)
                for j in range(nsq):
                    last = (j == nsq - 1)
                    for i, h in enumerate(hs):
                        pa2 = self.ps()
                        self.mm(pa2[0:L, 0:L], XTs[i][0:L, 0:L], Xs[i][0:L, 0:L])
                        if not last:
                            pb2 = self.ps()
                            self.mm(pb2[0:L, 0:L], Xs[i][0:L, 0:L], XTs[i][0:L, 0:L])
                        self.cp(Xs[i][0:L, 0:L], pa2[0:L, 0:L], eng='act')
                        if not last:
                            self.cp(XTs[i][0:L, 0:L], pb2[0:L, 0:L])
                    for i, h in enumerate(hs):
                        pc = self.ps()
                        self.mm(pc[0:L, 0:L], Xs[i][0:L, 0:L], TTs[i][0:L, 0:L])
                        self.tt(TTs[i][0:L, 0:L], pc[0:L, 0:L], TTs[i][0:L, 0:L], ALU.add)
                for i, h in enumerate(hs):
                    self.ts(R1[i][0:L, :], v_tok[0:L, c, h * 128:(h + 1) * 128], gC[0:L, c, h:h + 1], ALU.mult)
                    self.ts(R2[i][0:L, :], k_tok[0:L, c, h * 128:(h + 1) * 128], gC[0:L, c, 4 * NH + h:4 * NH + h + 1], ALU.mult)
                    pu = self.ps()
                    self.mm(pu[0:L, 0:128], TTs[i][0:L, 0:L], R1[i][0:L, :])
                    self.cp(U0s[c % 2][h][0:L, :], pu[0:L, 0:128], eng='act')
                    pw = self.ps()
                    self.mm(pw[:, 0:L], R2[i][0:L, :], TTs[i][0:L, 0:L])
                    self.cp(WTb[c % 2][h][:, 0:L], pw[:, 0:L])

            def phaseB(c):
                cs, ce = c * L, (c + 1) * L
                hs = list(range(NH))
                for i, h in enumerate(hs):
                    pws = self.ps()
                    self.mm(pws[0:L, 0:128], WTb[c % 2][h][:, 0:L], SGb[:, h, :])
                    self.tt(ub[i][0:L, :], U0s[c % 2][h][0:L, :], pws[0:L, 0:128], ALU.subtract)
                    self.ts(kw[i][0:L, :], k_tok[0:L, c, h * 128:(h + 1) * 128], gC[0:L, c, 3 * NH + h:3 * NH + h + 1], ALU.mult)
                for i, h in enumerate(hs):
                    po1 = self.ps()
                    self.mm(po1[0:L, 0:128], QKd[c % 2][h][0:L, 0:L], ub[i][0:L, :])
                    po2 = self.ps()
                    self.mm(po2[0:L, 0:128], qkvT[:, h, cs:ce], SGb[:, h, :])
                    self.cp(o1s[i][0:L, :], po1[0:L, 0:128], eng='act')
                    self.stt(oo[i][0:L, :], po2[0:L, 0:128], gC[0:L, c, 2 * NH + h:2 * NH + h + 1], o1s[i][0:L, :], ALU.mult, ALU.add)
                for i, h in enumerate(hs):
                    pS = self.ps()
                    self.mm(pS[:, 0:128], kw[i][0:L, :], ub[i][0:L, :])
                    self.stt(SG[:, h, :], SG[:, h, :], decC[:, c, h:h + 1], pS[:, 0:128], ALU.mult, ALU.add)
                    self.cp(SGb[:, h, :], SG[:, h, :], eng='act')
                for i, h in enumerate(hs):
                    self.act(jk[i][0:L, :], oo[i][0:L, :], AF.Square)
                    self.rsum(cc[i][0:L, :], jk[i][0:L, :])
                    self.act(cc[i][0:L, :], cc[i][0:L, :], AF.Sqrt, bias=EPS, scale=1.0 / 128)
                    self.recip(cc[i][0:L, :], cc[i][0:L, :])
                    self.stt(oo[i][0:L, :], oo[i][0:L, :], cc[i][0:L, :], gdnB[0:L, :], ALU.mult, ALU.mult)
                    self.tt(h_tok[0:L, h * 128:(h + 1) * 128], oo[i][0:L, :], zs[0:L, c, h * 128:(h + 1) * 128], ALU.mult)
                p = self.ps()
                for f8 in range(NH):
                    self.tr(pbv(p, slice(0, 128), slice(f8 * 128, f8 * 128 + L)), h_tok[0:L, f8 * 128:(f8 + 1) * 128],
                            ident_b[0:L, 0:L])
                self.cp(hT[:, 0:NH, cs:ce], V(p.hb[:, 0:512].rearrange("p (f t) -> p f t", t=128)[:, :, 0:L], p.buf), eng='act')

            phaseA(0)
            for c in range(nch):
                if c + 1 < nch:
                    phaseA(c + 1)
                phaseB(c)
            dma(g4in.v(g4in.ap.rearrange("(f p) t -> p f t", p=128)), hT[:, :, 0:n])
            allgather(g4in, g4out)

        xsrc = xT.ap.rearrange("(k p) t -> p k t", p=128)
        ydst = yT.ap.rearrange("(k p) t -> p k t", p=128)
        for b, (t0, n) in enumerate(own):
            dma(xb[b][:, :, 0:n], xT.v(xsrc[:, :, t0:t0 + n]))
        for b, (t0, n) in enumerate(own):
            ffn(0, 0, n, xb[b])
            hn_exchange(1, n, xb[b], G1in[b], G1out[b])
        seqs = [(r * 4 + i, TB, 128, G1out[i], r) for r in range(2) for i in range(4)] + \
               [(8 + r, DEC_SEQ, DEC_SEQ, G1out[4], r) for r in range(2)]

        def ab_zero():
            self.memset(Cx[:], 0.0); self.memset(ST[:], 0.0); self.memset(hist_b[:], 0.0)
            self.memset(Fc[:], 0.0); self.memset(Gc[:], 0.0)
            self.cp(Sb[:], ST[:])

        def ab_store(k):
            self.tt(mo[:], Gc[:], Fc[:], ALU.subtract)
            dma(o_Cx[k].v(), Cx[:], is_output=True); dma(o_m[k].v(), mo[:], is_output=True)
            dma(o_ST[k].v(), ST[:], is_output=True); dma(o_hb[k].v(), hist_b[:], is_output=True)

        def prefetch_all():
            groups = [(g * 512, 512) for g in range(5)] + [(2560, 256)]
            wo_ab = ab_w_out.ap.rearrange("(f p) d -> p f d", p=128)
            for dh in range(2):
                for fh in range(2):
                    self.wprefetch([128, 8, 512], ab_w_out, wo_ab[:, fh * 8:(fh + 1) * 8, dh * 512:(dh + 1) * 512], ('wo', 16, dh, fh))
            for (l, i) in ((0, 1), (1, 0)):
                wg_src = w_gate.ap[l, i].rearrange("(k p) f -> p k f", p=128)
                wu_src = w_up.ap[l, i].rearrange("(k p) f -> p k f", p=128)
                wd_src = w_down.ap[l, i].rearrange("(f p) d -> p f d", p=128)
                for gi, (c0, ncol) in enumerate(groups):
                    self.wprefetch([128, 8, ncol], w_gate, wg_src[:, :, c0:c0 + ncol], ('wg', l, i, gi))
                    self.wprefetch([128, 8, ncol], w_up, wu_src[:, :, c0:c0 + ncol], ('wu', l, i, gi))
                    if gi == 1:
                        for q in range(2):
                            self.wprefetch([128, 11, D], w_down, wd_src[:, q * 11:(q + 1) * 11, :], ('wd', l, i, q))
            gwin = gw.ap.rearrange("(k p) c -> p k c", p=128)
            self.wprefetch([128, 8, 8], gw, gwin[:, :, 2048:2056], ('gba',))
            for c0 in (0, 512, 1024, 1536):
                self.wprefetch([128, 8, 512], gw, gwin[:, :, c0:c0 + 512], ('gin', c0))
            wo_g = gdn_w_out.ap.rearrange("(f p) d -> p f d", p=128)
            for dh in range(2):
                self.wprefetch([128, 8, 512], gdn_w_out, wo_g[:, 0:8, dh * 512:(dh + 1) * 512], ('wo', 8, dh, 0))
            l, i = 1, 1
            wg_src = w_gate.ap[l, i].rearrange("(k p) f -> p k f", p=128)
            wu_src = w_up.ap[l, i].rearrange("(k p) f -> p k f", p=128)
            wd_src = w_down.ap[l, i].rearrange("(f p) d -> p f d", p=128)
            for gi, (c0, ncol) in enumerate(groups):
                self.wprefetch([128, 8, ncol], w_gate, wg_src[:, :, c0:c0 + ncol], ('wg', l, i, gi))
                self.wprefetch([128, 8, ncol], w_up, wu_src[:, :, c0:c0 + ncol], ('wu', l, i, gi))
                if gi == 1:
                    for q in range(2):
                        self.wprefetch([128, 11, D], w_down, wd_src[:, q * 11:(q + 1) * 11, :], ('wd', l, i, q))

        ab_zero()
        for (k, n, L, g1, r) in seqs:
            if k >= 8:
                j = k - 8
                if j == 0:
                    ab_store(2)
                dma(Cx[:], s_Cx[j].v()); dma(ST[:], s_ST[j].v()); dma(hist_b[:], s_hb[j].v()); dma(Gc[:], s_m[j].v())
                self.memset(Fc[:], 0.0)
                self.cp(Sb[:], ST[:])
            mixer_ab(n, L, g1, r, G2in[k], G2out[k])
            if k >= 8:
                ab_store(k - 8)
        for b, (t0, n) in enumerate(own):
            gouts = (G2out[b], G2out[4 + b]) if b < 4 else (G2out[8], G2out[9])
            out_proj_sel(ab_w_out, 16, gouts, n, xb[b], 'ab')
        for b, (t0, n) in enumerate(own):
            ffn(0, 1, n, xb[b])
        for b, (t0, n) in enumerate(own):
            ffn(1, 0, n, xb[b])
            hn_exchange(4, n, xb[b], G3in[b], G3out[b])
        seqs_c = [(r * 4 + i, TB, 128, G3out[i], r) for r in range(2) for i in range(4)] + \
                 [(8 + r, DEC_SEQ, DEC_SEQ, G3out[4], r) for r in range(2)]

        def c_store(k):
            dma(o_SG[k].v(), SG[:], is_output=True); dma(o_hc[k].v(), hist_c[:], is_output=True)

        self.memset(SG[:], 0.0); self.memset(hist_c[:], 0.0)
        self.cp(SGb[:], SG[:])
        for (k, n, L, g3, r) in seqs_c:
            if k >= 8:
                j = k - 8
                if j == 0:
                    c_store(2)
                dma(SG[:], s_SG[j].v()); dma(hist_c[:], s_hc[j].v())
                self.cp(SGb[:], SG[:])
            mixer_c(n, L, g3, r, G4in[k], G4out[k])
            if k >= 8:
                c_store(k - 8)
        for b, (t0, n) in enumerate(own):
            gouts = (G4out[b], G4out[4 + b]) if b < 4 else (G4out[8], G4out[9])
            out_proj_sel(gdn_w_out, 8, gouts, n, xb[b], 'gdn')
        for b, (t0, n) in enumerate(own):
            ffn(1, 1, n, xb[b])
        for b, (t0, n) in enumerate(own):
            self.phase_begin(('fin',))
            yo = A("yo", [128, 8, TB]); sq = A("sq", [128, 8, TB], BF16)
            rmsnorm(n, lambda kc: gfcol[:, kc:kc + 1], yo, sq, xb[b])
            dma(yT.v(ydst[:, :, t0:t0 + n]), yo[:, :, 0:n], is_output=True)
        S.finish()
        S.emit()
        return nc


_NC_CACHE = {}


def _own_ch_ssd(r):
    return np.concatenate([np.arange(r * 512, (r + 1) * 512), 1024 + np.arange(r * 128, (r + 1) * 128),
                           1280 + np.arange(r * 128, (r + 1) * 128)])


def _own_ch_gdn(r):
    return np.concatenate([np.arange(r * 512, (r + 1) * 512), 1024 + np.arange(r * 512, (r + 1) * 512),
                           2048 + np.arange(r * 512, (r + 1) * 512)])


def _host_inputs(d, p, r):
    f = np.ascontiguousarray
    x = np.concatenate([d['x_prompt'][p, r * 2048:(r + 1) * 2048], d['x_sample'][2 * p + r]], 0)
    wi = d['ab_w_in']
    abw = np.concatenate([wi[:, r * 256:(r + 1) * 256], wi[:, 512 + r * 256:512 + (r + 1) * 256],
                          wi[:, 1024 + r * 512:1024 + (r + 1) * 512], wi[:, 2048 + r * 512:2048 + (r + 1) * 512],
                          wi[:, 3072 + 2 * r:3074 + 2 * r], wi[:, 3076 + 2 * r:3078 + 2 * r],
                          wi[:, 3080 + r * 512:3080 + (r + 1) * 512], wi[:, 4104 + r * 512:4104 + (r + 1) * 512],
                          wi[:, 5128 + r * 128:5128 + (r + 1) * 128], wi[:, 5384 + r * 128:5384 + (r + 1) * 128],
                          wi[:, 5640 + r * 8:5640 + (r + 1) * 8]], axis=1)
    gi = d['gdn_w_in']
    gw = np.concatenate([gi[:, r * 512:(r + 1) * 512], gi[:, 1024 + r * 512:1024 + (r + 1) * 512],
                         gi[:, 2048 + r * 512:2048 + (r + 1) * 512], gi[:, 3072 + r * 512:3072 + (r + 1) * 512],
                         gi[:, 4096 + 4 * r:4100 + 4 * r], gi[:, 4104 + 4 * r:4108 + 4 * r]], axis=1)
    cs, cg = _own_ch_ssd(r), _own_ch_gdn(r)
    rm = np.zeros((128, 2), np.float32)
    rm[:, r] = 1.0
    ins = {
        "xT": f(x.T), "ffn_w_gate": d['ffn_w_gate'], "ffn_w_up": d['ffn_w_up'], "ffn_w_down": d['ffn_w_down'],
        "g_col": f(d['norm_g'].reshape(6, 8, 128).transpose(2, 0, 1)),
        "gf_col": f(d['norm_f'].reshape(8, 128).T), "rmask": rm,
        "abw": f(abw), "ab_w_out": d['ab_w_out'],
        "b_i": f(d['mlstm_b_i'][2 * r:2 * r + 2].reshape(2, 1)), "b_f": f(d['mlstm_b_f'][2 * r:2 * r + 2].reshape(2, 1)),
        "mlstm_norm_b": f(np.broadcast_to(d['mlstm_norm'][2 * r:2 * r + 2].reshape(1, 512), (128, 512))),
        "ssd_cw": f(d['ssd_conv_w'][:, cs].reshape(4, 6, 128).transpose(2, 1, 0)),
        "ssd_cb": f(d['ssd_conv_b'][cs].reshape(6, 128).T),
        "ssd_dtb": f(d['ssd_dt_bias'][r * 8:(r + 1) * 8].reshape(8, 1)),
        "ssd_alog": f(d['ssd_a_log'][r * 8:(r + 1) * 8].reshape(8, 1)),
        "ssd_d_b": f(np.broadcast_to(d['ssd_d'][r * 8:(r + 1) * 8].reshape(1, 8), (128, 8))),
        "ssd_norm_b": f(np.broadcast_to(d['ssd_norm'][r * 512:(r + 1) * 512].reshape(1, 512), (128, 512))),
        "gw": f(gw), "gdn_w_out": d['gdn_w_out'],
        "gdn_cw": f(d['gdn_conv_w'][:, cg].reshape(4, 12, 128).transpose(2, 1, 0)),
        "gdn_dtb": f(d['gdn_dt_bias'][4 * r:4 * r + 4].reshape(4, 1)),
        "gdn_alog": f(d['gdn_a_log'][4 * r:4 * r + 4].reshape(4, 1)),
        "gdn_norm_b": f(np.broadcast_to(d['gdn_norm'].reshape(1, 128), (128, 128))),
    }
    for j in range(2):
        s = 2 * p + j
        ins["s_Cx%d" % j] = f(np.concatenate([d['state_mlstm_C'][s, 2 * r:2 * r + 2].transpose(1, 0, 2),
                                              d['state_mlstm_n'][s, 2 * r:2 * r + 2].T[:, :, None]], 2))
        ins["s_m%d" % j] = f(d['state_mlstm_m'][s, 2 * r:2 * r + 2].reshape(2, 1))
        ins["s_ST%d" % j] = f(d['state_ssd'][s, r * 8:(r + 1) * 8].transpose(2, 0, 1))
        ins["s_hb%d" % j] = f(d['cache_ssd_conv'][s][:, cs].reshape(3, 6, 128).transpose(2, 1, 0))
        ins["s_SG%d" % j] = f(d['state_gdn'][s, 4 * r:4 * r + 4].transpose(1, 0, 2))
        ins["s_hc%d" % j] = f(d['cache_gdn_conv'][s][:, cg].reshape(3, 12, 128).transpose(2, 1, 0))
    return ins


def _assemble(R, npair):
    f32 = lambda a: np.ascontiguousarray(np.asarray(a, dtype=np.float32))
    y_prompt = np.zeros((npair, 4096, 1024), np.float32)
    y_sample = np.zeros((2 * npair, 16, 1024), np.float32)

    def alloc(n):
        return [np.zeros((n, 4, 128, 256), np.float32), np.zeros((n, 4, 128), np.float32), np.zeros((n, 4), np.float32),
                np.zeros((n, 16, 64, 128), np.float32), np.zeros((n, 3, 1536), np.float32),
                np.zeros((n, 8, 128, 128), np.float32), np.zeros((n, 3, 3072), np.float32)]
    PS, SS = alloc(npair), alloc(2 * npair)

    def put(dst, idx, r, res, k):
        cs, cg = _own_ch_ssd(r), _own_ch_gdn(r)
        Cx = np.asarray(res["o_Cx%d" % k], np.float32)
        dst[0][idx, 2 * r:2 * r + 2] = Cx[:, :, :256].transpose(1, 0, 2)
        dst[1][idx, 2 * r:2 * r + 2] = Cx[:, :, 256].T
        dst[2][idx, 2 * r:2 * r + 2] = np.asarray(res["o_m%d" % k], np.float32)[:, 0]
        dst[3][idx, r * 8:(r + 1) * 8] = np.asarray(res["o_ST%d" % k], np.float32).transpose(1, 2, 0)
        dst[4][idx][:, cs] = np.asarray(res["o_hb%d" % k], np.float32).transpose(2, 1, 0).reshape(3, 768)
        dst[5][idx, 4 * r:4 * r + 4] = np.asarray(res["o_SG%d" % k], np.float32).transpose(1, 0, 2)
        dst[6][idx][:, cg] = np.asarray(res["o_hc%d" % k], np.float32).transpose(2, 1, 0).reshape(3, 1536)

    for p in range(npair):
        for r in range(2):
            res = R[2 * p + r]
            yT = np.asarray(res['yT'], np.float32)
            y_prompt[p, r * 2048:(r + 1) * 2048] = yT[:, :2048].T
            y_sample[2 * p + r] = yT[:, 2048:].T
            put(PS, p, r, res, 2)
            for j in range(2):
                put(SS, 2 * p + j, r, res, j)
    return tuple([y_prompt, y_sample] + PS + SS)


def kernel(**inputs):
    d = {k: np.asarray(v, dtype=np.float32) for k, v in inputs.items()}
    ncores = 8
    if 'nc' not in _NC_CACHE:
        B = Builder(8)
        _NC_CACHE['nc'] = (B.build(), set(B.din))
    nc, din = _NC_CACHE['nc']
    in_maps = []
    for c in range(ncores):
        hi = _host_inputs(d, c // 2, c % 2)
        in_maps.append({k: v for k, v in hi.items() if k in din})
    res = run_bass_kernel_spmd(nc, in_maps, core_ids=list(range(ncores)))
    return _assemble(list(res.results), 4)
```

```python
import numpy as np
from contextlib import ExitStack
import concourse.bass as bass
import concourse.mybir as mybir
from concourse.bass_utils import run_bass_kernel_spmd

F32 = mybir.dt.float32
BF16 = mybir.dt.bfloat16
AF = mybir.ActivationFunctionType
ALU = mybir.AluOpType

ENGS = ['pe', 'act', 'dve', 'pool', 'sp']
NDSEM = 12
import os
PLIM = int(os.environ.get('K_PLIM', '3'))

D = 1024
DFF = 2816
NFF = 22
EPS = 1e-6
TB = 512
DEC_SEQ = 16


class Buf:
    __slots__ = ('name', 'w', 'r', 'excl')

    def __init__(self, name='', excl=False):
        self.name = name
        self.w = None
        self.r = []
        self.excl = excl


class Sched:
    def __init__(self, nc, same_engine_sync=True):
        self.nc = nc
        self.ops = {e: [] for e in ENGS}
        self.seen = {e: {f: -1 for f in ENGS} for e in ENGS}
        self.seen_dma = {e: {} for e in ENGS}
        self.targets = {e: set() for e in ENGS}
        self.ndma = {e: 0 for e in ENGS}
        self.same_engine_sync = same_engine_sync
        self.out_tokens = []

    def _need(self, eng, tok, waits):
        if tok is None:
            return
        if tok[0] == 'e':
            _, f, k = tok
            if f == eng and (eng == 'pe' or not self.same_engine_sync):
                return
            if self.seen[eng][f] >= k:
                return
            self.seen[eng][f] = k
            self.targets[f].add(k)
            waits.append(tok)
        elif tok[0] == 'c':
            if getattr(self, '_seen_cc', {}).get(eng, -1) >= tok[1]:
                return
            if not hasattr(self, '_seen_cc'):
                self._seen_cc = {}
            self._seen_cc[eng] = tok[1]
            waits.append(tok)
        else:
            _, q, seq = tok
            key = (q, seq % NDSEM)
            if self.seen_dma[eng].get(key, -1) >= seq:
                return
            self.seen_dma[eng][key] = seq
            waits.append(tok)

    def op(self, eng, fn, reads=(), writes=(), dma=False, is_output=False, cc=False):
        waits = []
        for b in reads:
            self._need(eng, b.w, waits)
            if b.excl:
                for t in b.r:
                    if t[0] == 'e' and t[1] != eng:
                        self._need(eng, t, waits)
        for b in writes:
            self._need(eng, b.w, waits)
            for t in b.r:
                self._need(eng, t, waits)
        idx = len(self.ops[eng])
        if eng == 'pool' and not dma:
            lp = getattr(self, '_last_pool', None)
            if lp is not None:
                self._need('pool', lp, waits)
            self._last_pool = ('e', 'pool', idx)
        if dma:
            seq = self.ndma[eng]
            self.ndma[eng] += 1
            lim = PLIM if eng == 'pool' else NDSEM
            if seq >= lim:
                self._need(eng, ('d', eng, seq - lim), waits)
            tok = ('d', eng, seq)
        elif cc:
            seq = None
            self.ncc = getattr(self, 'ncc', 0) + 1
            tok = ('c', self.ncc - 1)
        else:
            seq = None
            tok = ('e', eng, idx)
        self.ops[eng].append(dict(fn=fn, waits=waits, dma=seq, cc=cc))
        for b in reads:
            if tok[0] == 'e':
                b.r = [t for t in b.r if not (t[0] == 'e' and t[1] == tok[1])]
            b.r.append(tok)
        for b in writes:
            b.w = tok
            b.r = []
        if is_output:
            self.out_tokens.append(tok)
        return tok

    def barrier(self):
        last = {}
        for e in ENGS:
            if self.ops[e]:
                last[e] = ('e', e, len(self.ops[e]) - 1)
        for e in ENGS:
            waits = []
            for f, t in last.items():
                if f != e:
                    self._need(e, t, waits)
            for q in ENGS:
                lim = PLIM if q == 'pool' else NDSEM
                for sq_ in range(max(0, self.ndma[q] - lim), self.ndma[q]):
                    self._need(e, ('d', q, sq_), waits)
            if waits:
                self.ops[e].append(dict(fn=None, waits=waits, dma=None, cc=False))

    def finish(self):
        waits = []
        for t in self.out_tokens:
            self._need('sp', t, waits)
        self.ops['sp'].append(dict(fn=None, waits=waits, dma=None, cc=False))

    def emit(self):
        nc = self.nc
        with ExitStack() as st:
            esem = {e: st.enter_context(nc.semaphore('es_' + e)) for e in ENGS}
            ccsem = st.enter_context(nc.semaphore('cc_sem'))
            dsem = {e: [st.enter_context(nc.semaphore('ds_%s%d' % (e, i))) for i in range(NDSEM)]
                    for e in ENGS if self.ndma[e] > 0}
            cnt = {}
            for e in ENGS:
                c = 0
                m = {}
                for k in sorted(self.targets[e]):
                    c += 1
                    m[k] = c
                cnt[e] = m
            block = st.enter_context(nc.Block())

            def body(ename, eh):
                for idx, o in enumerate(self.ops[ename]):
                    for t in o['waits']:
                        if t[0] == 'e':
                            eh.wait_ge(esem[t[1]], cnt[t[1]][t[2]])
                        elif t[0] == 'c':
                            eh.wait_ge(ccsem, t[1] + 1)
                        else:
                            eh.wait_ge(dsem[t[1]][t[2] % NDSEM], 16 * (t[2] // NDSEM + 1))
                    if o['fn'] is None:
                        if idx in cnt[ename]:
                            eh.nop().then_inc(esem[ename], 1)
                        continue
                    inst = o['fn'](eh)
                    if o.get('cc'):
                        inst.then_inc(ccsem, 1)
                        if idx in cnt[ename]:
                            eh.nop().then_inc(esem[ename], 1)
                    elif o['dma'] is not None:
                        inst.then_inc(dsem[ename][o['dma'] % NDSEM], 16)
                        if idx in cnt[ename]:
                            eh.nop().then_inc(esem[ename], 1)
                    elif idx in cnt[ename]:
                        inst.then_inc(esem[ename], 1)

            @block.tensor
            def _(eh):
                body('pe', eh)

            @block.scalar
            def _(eh):
                body('act', eh)

            @block.vector
            def _(eh):
                body('dve', eh)

            @block.gpsimd
            def _(eh):
                body('pool', eh)

            @block.sync
            def _(eh):
                body('sp', eh)


class V:
    __slots__ = ('ap', 'buf')

    def __init__(self, ap, buf):
        self.ap = ap
        self.buf = buf


class Tl:
    def __init__(self, h, name=''):
        self.h = h
        self.buf = Buf(name)

    def __getitem__(self, idx):
        return V(self.h[idx], self.buf)

    def v(self, ap):
        return V(ap, self.buf)


class DT:
    def __init__(self, ap, name=''):
        self.ap = ap
        self.buf = Buf(name)

    def v(self, ap=None):
        return V(self.ap if ap is None else ap, self.buf)


class Builder:
    SB_LO = 17536
    SB_HI = 229344

    def __init__(self, nblk):
        self.nblk = nblk
        self.ntok = nblk * TB + DEC_SEQ
        self.nc = bass.Bass("TRN2", target_bir_lowering=False)
        self.S = Sched(self.nc)
        self.persist_off = self.SB_LO
        self.phase_base = None
        self.phase_off = None
        self.nps = 0
        self.din = {}
        self.dout = {}

    def _alloc(self, name, shape, dt, off):
        n = 1
        for s in shape[1:]:
            n *= s
        nbytes = n * (4 if dt == F32 else 2)
        nbytes = (nbytes + 63) // 64 * 64
        h = self.nc.alloc_sbuf_tensor_at(name, list(shape), dt, offset=off)
        return Tl(h, name), off + nbytes

    def P(self, name, shape, dt=F32):
        t, self.persist_off = self._alloc(name, shape, dt, self.persist_off)
        assert self.persist_off <= self.SB_HI, name
        return t

    def phase_begin(self, kind=None):
        if self.phase_base is None:
            self.phase_base = self.persist_off
            self.cur_kind = None
            self.kind_tiles = {}
        if kind is None or kind != self.cur_kind:
            if self.cur_kind is not None or kind is None:
                self.S.barrier()
            self.cur_kind = kind
            self.kind_tiles = {}
            self.phase_gen = getattr(self, 'phase_gen', 0) + 1
        self.phase_off = self.phase_base

    def A(self, name, shape, dt=F32):
        if self.cur_kind is not None and name in self.kind_tiles:
            return self.kind_tiles[name]
        t, self.phase_off = self._alloc(name, shape, dt, self.phase_off)
        assert self.phase_off <= self.SB_HI, (name, self.phase_off)
        if self.cur_kind is not None:
            self.kind_tiles[name] = t
        return t

    def din_t(self, name, shape, dt=F32):
        ap = self.nc.dram_tensor(name, list(shape), dt, kind="ExternalInput").ap()
        d = DT(ap, name)
        self.din[name] = d
        return d

    def dout_t(self, name, shape, dt=F32):
        ap = self.nc.dram_tensor(name, list(shape), dt, kind="ExternalOutput").ap()
        d = DT(ap, name)
        self.dout[name] = d
        return d

    def mm(self, out, lhsT, rhs, start=True, stop=True):
        self.S.op('pe', lambda e: e.matmul(out.ap, lhsT=lhsT.ap, rhs=rhs.ap, start=start, stop=stop),
                  reads=[lhsT.buf, rhs.buf], writes=[out.buf])

    def tr(self, out, in_, ident):
        self.S.op('pe', lambda e: e.transpose(out.ap, in_.ap, ident.ap),
                  reads=[in_.buf, ident.buf], writes=[out.buf])

    def act(self, out, in_, func, bias=None, scale=1.0, accum=None, eng='act'):
        reads = [in_.buf]
        kw = {}
        if bias is not None:
            if isinstance(bias, V):
                reads.append(bias.buf)
                kw['bias'] = bias.ap
            else:
                kw['bias'] = bias
        if isinstance(scale, V):
            reads.append(scale.buf)
            kw['scale'] = scale.ap
        else:
            kw['scale'] = scale
        writes = [out.buf]
        if accum is not None:
            writes.append(accum.buf)
            kw['accum_out'] = accum.ap
        self.S.op('act', lambda e: e.activation(out.ap, in_.ap, func, **kw), reads=reads, writes=writes)

    def tt(self, out, in0, in1, op, eng='dve'):
        self.S.op(eng, lambda e: e.tensor_tensor(out.ap, in0.ap, in1.ap, op),
                  reads=[in0.buf, in1.buf], writes=[out.buf])

    def ts(self, out, in0, s1, op0, s2=None, op1=None, eng='dve', accum=None):
        reads = [in0.buf]
        a1 = s1
        a2 = s2
        if isinstance(s1, V):
            reads.append(s1.buf)
            a1 = s1.ap
        if isinstance(s2, V):
            reads.append(s2.buf)
            a2 = s2.ap
        kw = {}
        writes = [out.buf]
        if accum is not None:
            kw['accum_out'] = accum.ap
            writes.append(accum.buf)
        if op1 is None:
            self.S.op(eng, lambda e: e.tensor_scalar(out.ap, in0.ap, a1, None, op0, **kw), reads=reads, writes=writes)
        else:
            self.S.op(eng, lambda e: e.tensor_scalar(out.ap, in0.ap, a1, a2, op0, op1, **kw), reads=reads, writes=writes)

    def stt(self, out, in0, sc, in1, op0, op1):
        reads = [in0.buf, in1.buf]
        a = sc
        if isinstance(sc, V):
            reads.append(sc.buf)
            a = sc.ap
        self.S.op('dve', lambda e: e.scalar_tensor_tensor(out.ap, in0.ap, a, in1.ap, op0, op1),
                  reads=reads, writes=[out.buf])

    def cp(self, out, in_, eng='dve'):
        if eng == 'act':
            self.S.op('act', lambda e: e.copy(out.ap, in_.ap), reads=[in_.buf], writes=[out.buf])
        else:
            self.S.op(eng, lambda e: e.tensor_copy(out.ap, in_.ap), reads=[in_.buf], writes=[out.buf])

    def rsum(self, out, in_):
        self.S.op('dve', lambda e: e.tensor_reduce(out.ap, in_.ap, mybir.AxisListType.X, ALU.add),
                  reads=[in_.buf], writes=[out.buf])

    def recip(self, out, in_):
        self.S.op('dve', lambda e: e.reciprocal(out.ap, in_.ap), reads=[in_.buf], writes=[out.buf])

    def scan(self, out, d0, d1, init, op0, op1):
        reads = [d0.buf, d1.buf]
        a = init
        if isinstance(init, V):
            reads.append(init.buf)
            a = init.ap
        self.S.op('dve', lambda e: e.tensor_tensor_scan(out.ap, d0.ap, d1.ap, a, op0, op1),
                  reads=reads, writes=[out.buf])

    def memset(self, out, val, eng='pool'):
        self.S.op(eng, lambda e: e.memset(out.ap, val), writes=[out.buf])

    def dma(self, out, in_, q='sp', is_output=False, nc_ok=False):
        if nc_ok:
            fn = lambda e: e.dma_start(out=out.ap, in_=in_.ap, allow_slow_non_contiguous=True)
        else:
            fn = lambda e: e.dma_start(out=out.ap, in_=in_.ap)
        self.S.op(q, fn, reads=[in_.buf], writes=[out.buf], dma=True, is_output=is_output)

    def wload(self, dst, src_dt, src_ap, key):
        if not hasattr(self, 'wcache'):
            self.wcache = {}
        if key not in self.wcache:
            self.dma(dst, V(src_ap, src_dt.buf), q='pool')
            shp = list(dst.ap.shape)
            nm = "wc_" + "_".join(str(k) for k in key)
            ap = self.nc.dram_tensor(nm, shp, BF16, kind="Internal").ap()
            d = DT(ap, nm)
            self.wcache[key] = d
            self.dma(d.v(), dst, q='sp')
        else:
            self.dma(dst, self.wcache[key].v(), q='sp')

    def wprefetch(self, shape, src_dt, src_ap, key):
        if not hasattr(self, 'wcache'):
            self.wcache = {}
        if key in self.wcache:
            return
        nm = "wc_" + "_".join(str(k) for k in key)
        ap = self.nc.dram_tensor(nm, list(shape), BF16, kind="Internal").ap()
        d = DT(ap, nm)
        self.wcache[key] = d
        self.dma(d.v(), V(src_ap, src_dt.buf), q='pool')

    def ps(self):
        t = self.psb[self.nps % 8]
        self.nps += 1
        return t

    def build(self):
        nc = self.nc
        S = self.S
        A = self.A
        P = self.P
        dma = self.dma
        DKs = 128 ** -0.5
        NOWN = 4
        NTOK = NOWN * TB + DEC_SEQ
        RG = [[0, 1]] if os.environ.get("K_RG") == "pair" else [[0, 1], [2, 3], [4, 5], [6, 7]]
        xT = self.din_t("xT", [D, NTOK])
        yT = self.dout_t("yT", [D, NTOK])
        w_gate = self.din_t("ffn_w_gate", [2, 2, D, DFF])
        w_up = self.din_t("ffn_w_up", [2, 2, D, DFF])
        w_down = self.din_t("ffn_w_down", [2, 2, DFF, D])
        g_col = self.din_t("g_col", [128, 6, 8])
        gf_col = self.din_t("gf_col", [128, 8])
        rmask_d = self.din_t("rmask", [128, 2])
        abw = self.din_t("abw", [D, 2828])
        ab_w_out = self.din_t("ab_w_out", [2048, D])
        d_bi = self.din_t("b_i", [2, 1]); d_bf = self.din_t("b_f", [2, 1])
        d_gnB = self.din_t("mlstm_norm_b", [128, 512])
        d_cw = self.din_t("ssd_cw", [128, 6, 4]); d_cb = self.din_t("ssd_cb", [128, 6])
        d_dtb = self.din_t("ssd_dtb", [8, 1]); d_alog = self.din_t("ssd_alog", [8, 1])
        d_dskB = self.din_t("ssd_d_b", [128, 8]); d_snB = self.din_t("ssd_norm_b", [128, 512])
        gw = self.din_t("gw", [D, 2056])
        gdn_w_out = self.din_t("gdn_w_out", [1024, D])
        d_gcw = self.din_t("gdn_cw", [128, 12, 4])
        d_gdtb = self.din_t("gdn_dtb", [4, 1]); d_galog = self.din_t("gdn_alog", [4, 1])
        d_gdnB = self.din_t("gdn_norm_b", [128, 128])
        s_Cx = [self.din_t("s_Cx%d" % j, [128, 2, 257]) for j in range(2)]
        s_m = [self.din_t("s_m%d" % j, [2, 1]) for j in range(2)]
        s_ST = [self.din_t("s_ST%d" % j, [128, 8, 64]) for j in range(2)]
        s_hb = [self.din_t("s_hb%d" % j, [128, 6, 3]) for j in range(2)]
        s_SG = [self.din_t("s_SG%d" % j, [128, 4, 128]) for j in range(2)]
        s_hc = [self.din_t("s_hc%d" % j, [128, 12, 3]) for j in range(2)]
        o_Cx = [self.dout_t("o_Cx%d" % j, [128, 2, 257]) for j in range(3)]
        o_m = [self.dout_t("o_m%d" % j, [2, 1]) for j in range(3)]
        o_ST = [self.dout_t("o_ST%d" % j, [128, 8, 64]) for j in range(3)]
        o_hb = [self.dout_t("o_hb%d" % j, [128, 6, 3]) for j in range(3)]
        o_SG = [self.dout_t("o_SG%d" % j, [128, 4, 128]) for j in range(3)]
        o_hc = [self.dout_t("o_hc%d" % j, [128, 12, 3]) for j in range(3)]

        def internal(name, shape):
            return DT(nc.dram_tensor(name, list(shape), BF16, kind="Internal").ap(), name)
        own = [(b * TB, TB) for b in range(NOWN)] + [(NOWN * TB, DEC_SEQ)]
        G1in = [internal("g1in%d" % b, [1024, n]) for b, (_, n) in enumerate(own)]
        G1out = [internal("g1out%d" % b, [2048, n]) for b, (_, n) in enumerate(own)]
        G3in = [internal("g3in%d" % b, [1024, n]) for b, (_, n) in enumerate(own)]
        G3out = [internal("g3out%d" % b, [2048, n]) for b, (_, n) in enumerate(own)]
        seqn = [TB] * 8 + [DEC_SEQ] * 2
        G2in = [internal("g2in%d" % k, [1024, n]) for k, n in enumerate(seqn)]
        G2out = [internal("g2out%d" % k, [2048, n]) for k, n in enumerate(seqn)]
        G4in = [internal("g4in%d" % k, [512, n]) for k, n in enumerate(seqn)]
        G4out = [internal("g4out%d" % k, [1024, n]) for k, n in enumerate(seqn)]

        def allgather(gi, go):
            S.op('pool', lambda e: e.collective_compute("AllGather", ALU.bypass, replica_groups=RG,
                                                        ins=[gi.ap], outs=[go.ap]),
                 reads=[gi.buf], writes=[go.buf], cc=True)

        self.psb = []
        self._st = ExitStack()
        for i in range(8):
            h = self._st.enter_context(nc.psum_tensor("psb%d" % i, [128, 512], F32))
            t = Tl(h, "psb%d" % i)
            t.buf.excl = True
            t.hb = h[:, :].bitcast(BF16)
            self.psb.append(t)

        def pbv(t, *idx):
            return V(t.hb[idx], t.buf)

        xb = [P("x%d" % b, [128, 8, n]) for b, (_, n) in enumerate(own)]
        hn = P("hn", [128, 8, TB], BF16)
        gcol = P("gcol", [128, 6, 8]); gfcol = P("gfcol", [128, 8]); rmask = P("rmask", [128, 2])
        ones_f = P("ones_f", [128, 128]); ones_b = P("ones_b", [128, 128], BF16)
        rstd = P("rstd", [128, TB])
        ident_f = P("ident_f", [128, 128]); ident_b = P("ident_b", [128, 128], BF16)
        maskT = P("maskT", [128, 128]); maskS = P("maskS", [128, 128])
        ones16 = P("ones16", [16, 128]); onesr = P("onesr", [16, TB]); sel16 = P("sel16", [16, 8, 128])
        bi = P("bi", [2, 1]); nbf = P("nbf", [2, 1])
        gnB = P("gnB", [128, 512]); snB = P("snB", [128, 512]); dskB = P("dskB", [128, 8])
        cw = P("cw", [128, 6, 4]); cb = P("cb", [128, 6]); dtb = P("dtb", [8, 1]); negA = P("negA", [8, 1])
        Cx = P("Cx", [128, 2, 257]); ST = P("ST", [128, 8, 64]); Sb = P("Sb", [128, 8, 64], BF16)
        hist_b = P("hist_b", [128, 6, 3])
        Fc = P("Fc", [2, 1]); Gc = P("Gc", [2, 1]); mo = P("mo", [2, 1])
        gcw = P("gcw", [128, 12, 4]); gdtb = P("gdtb", [4, 1]); gnegA = P("gnegA", [4, 1]); gdnB = P("gdnB", [128, 128])
        SG = P("SG", [128, 4, 128]); SGb = P("SGb", [128, 4, 128], BF16); hist_c = P("hist_c", [128, 12, 3])

        for (dst, src) in ((gcol, g_col), (gfcol, gf_col), (rmask, rmask_d), (bi, d_bi), (nbf, d_bf), (gnB, d_gnB),
                           (snB, d_snB), (dskB, d_dskB), (cw, d_cw), (cb, d_cb), (dtb, d_dtb), (negA, d_alog),
                           (gcw, d_gcw), (gdtb, d_gdtb), (gnegA, d_galog), (gdnB, d_gdnB)):
            dma(dst[:], src.v())
        self.memset(ones_f[:], 1.0 / D); self.memset(ones_b[:], 1.0 / D)
        self.memset(ones16[:], 1.0); self.memset(onesr[:], 1.0)
        self.memset(ident_f[:], 0.0)
        S.op('pool', lambda e: e.affine_select(out=ident_f.h[:], in_=ident_f.h[:], pattern=[[-1, 128]],
                                               compare_op=ALU.not_equal, fill=1.0, base=0, channel_multiplier=1),
             reads=[ident_f.buf], writes=[ident_f.buf])
        self.cp(ident_b[:], ident_f[:])
        self.memset(maskT[:], 1.0)
        S.op('pool', lambda e: e.affine_select(out=maskT.h[:], in_=maskT.h[:], pattern=[[1, 128]],
                                               compare_op=ALU.is_ge, fill=0.0, base=0, channel_multiplier=-1),
             reads=[maskT.buf], writes=[maskT.buf])
        self.memset(maskS[:], 1.0)
        S.op('pool', lambda e: e.affine_select(out=maskS.h[:], in_=maskS.h[:], pattern=[[1, 128]],
                                               compare_op=ALU.is_gt, fill=0.0, base=0, channel_multiplier=-1),
             reads=[maskS.buf], writes=[maskS.buf])
        self.memset(sel16[:], 0.0)
        S.op('pool', lambda e: e.affine_select(out=sel16.h[:], in_=sel16.h[:], pattern=[[-1, 8], [0, 128]],
                                               compare_op=ALU.not_equal, fill=1.0, base=0, channel_multiplier=1),
             reads=[sel16.buf], writes=[sel16.buf])
        self.ts(nbf[:], nbf[:], -1.0, ALU.mult)
        self.act(negA[:], negA[:], AF.Exp); self.ts(negA[:], negA[:], -1.0, ALU.mult)
        self.act(gnegA[:], gnegA[:], AF.Exp); self.ts(gnegA[:], gnegA[:], -1.0, ALU.mult)

        def rmsnorm(n, gv, out_t, sq, x):
            self.act(sq[:, :, 0:n], x[:, :, 0:n], AF.Square)
            p = self.ps()
            for kc in range(8):
                self.mm(p[:, 0:n], ones_b[:], sq[:, kc, 0:n], start=(kc == 0), stop=(kc == 7))
            self.act(rstd[:, 0:n], p[:, 0:n], AF.Sqrt, bias=EPS)
            self.recip(rstd[:, 0:n], rstd[:, 0:n])
            for kc in range(8):
                self.stt(out_t[:, kc, 0:n], x[:, kc, 0:n], gv(kc), rstd[:, 0:n], ALU.mult, ALU.mult)

        def ffn(l, i, parts):
            self.phase_begin(('ffn',))
            sq = A("sq", [128, 8, TB], BF16)
            act_t = A("ffn_act", [128, NFF, TB], BF16)
            act_s = A("ffn_act_s", [128, NFF, DEC_SEQ], BF16)
            hn_s = A("hn_s", [128, 8, DEC_SEQ], BF16)
            wd = A("ffn_wd", [128, NFF, D], BF16)
            wgu = [[A("ffn_wg%d" % b, [128, 8, 256], BF16), A("ffn_wu%d" % b, [128, 8, 256], BF16)]
                   for b in range(3)]
            sil = [A("ffn_sil0", [128, TB])] * 2
            hns = [hn, hn_s]
            acts = [act_t, act_s]
            for pi, (x, n) in enumerate(parts):
                rmsnorm(n, lambda kc: gcol[:, l * 3 + 2 * i, kc:kc + 1], hns[pi], sq, x)
            wg_src = w_gate.ap[l, i].rearrange("(k p) f -> p k f", p=128)
            wu_src = w_up.ap[l, i].rearrange("(k p) f -> p k f", p=128)
            wd_src = w_down.ap[l, i].rearrange("(f p) d -> p f d", p=128)
            groups = [(g * 256, 256) for g in range(11)]
            for gi, (c0, ncol) in enumerate(groups):
                b = gi % 3
                self.wload(wgu[b][0][:, :, 0:ncol], w_gate, wg_src[:, :, c0:c0 + ncol], ('wg', l, i, gi))
                self.wload(wgu[b][1][:, :, 0:ncol], w_up, wu_src[:, :, c0:c0 + ncol], ('wu', l, i, gi))
                if gi == 2 and getattr(self, '_wd_res', None) != (l, i, self.phase_gen):
                    for q in range(2):
                        self.wload(wd[:, q * 11:(q + 1) * 11, :], w_down, wd_src[:, q * 11:(q + 1) * 11, :], ('wd', l, i, q))
                    self._wd_res = (l, i, self.phase_gen)
                for j in range(ncol // 128):
                    f = c0 // 128 + j
                    for pi, (x, n) in enumerate(parts):
                        pg = self.ps()
                        pu = self.ps()
                        for kc in range(8):
                            self.mm(pg[:, 0:n], wgu[b][0][:, kc, j * 128:(j + 1) * 128], hns[pi][:, kc, 0:n],
                                    start=(kc == 0), stop=(kc == 7))
                        for kc in range(8):
                            self.mm(pu[:, 0:n], wgu[b][1][:, kc, j * 128:(j + 1) * 128], hns[pi][:, kc, 0:n],
                                    start=(kc == 0), stop=(kc == 7))
                        sl = sil[f % 2]
                        self.act(sl[:, 0:n], pg[:, 0:n], AF.Silu)
                        self.tt(acts[pi][:, f, 0:n], sl[:, 0:n], pu[:, 0:n], ALU.mult)
            for dc in range(8):
                for pi, (x, n) in enumerate(parts):
                    p = self.ps()
                    for f in range(NFF):
                        self.mm(p[:, 0:n], wd[:, f, dc * 128:(dc + 1) * 128], acts[pi][:, f, 0:n],
                                start=(f == 0), stop=(f == NFF - 1))
                    self.stt(x[:, dc, 0:n], p[:, 0:n], 0.5, x[:, dc, 0:n], ALU.mult, ALU.add)

        def hn_exchange(gidx, n, x, gin, gout):
            self.phase_begin(('ffn',))
            sq = A("sq", [128, 8, TB], BF16)
            rmsnorm(n, lambda kc: gcol[:, gidx, kc:kc + 1], hn, sq, x)
            dma(gin.v(gin.ap.rearrange("(k p) t -> p k t", p=128)), hn[:, :, 0:n])
            allgather(gin, gout)

        def out_proj_sel(w_dt, nfc, gouts, n, x, tag):
            self.phase_begin(('ops', nfc))
            wb = [A("wb0", [128, 8, 512], BF16), A("wb1", [128, 8, 512], BF16)]
            cand = [A("cand0", [128, nfc, TB], BF16), A("cand1", [128, nfc, TB], BF16)]
            hT = A("hTf", [128, nfc, TB], BF16)
            half = nfc // 2 * 128
            for ci, go in enumerate(gouts):
                for r in range(2):
                    src = go.ap[r * half:(r + 1) * half, :].rearrange("(f p) t -> p f t", p=128)
                    if nfc == 16:
                        dma(cand[ci][:, r * 4:r * 4 + 4, 0:n], go.v(src[:, 0:4, :]))
                        dma(cand[ci][:, 8 + r * 4:8 + r * 4 + 4, 0:n], go.v(src[:, 4:8, :]))
                    else:
                        dma(cand[ci][:, r * 4:r * 4 + 4, 0:n], go.v(src[:, 0:4, :]))
            self.ts(hT[:, :, 0:n], cand[0][:, :, 0:n], rmask[:, 0:1], ALU.mult)
            self.stt(hT[:, :, 0:n], cand[1][:, :, 0:n], rmask[:, 1:2], hT[:, :, 0:n], ALU.mult, ALU.add)
            wo_src = w_dt.ap.rearrange("(f p) d -> p f d", p=128)
            nfh = nfc // 8
            li = 0
            for dh in range(2):
                pacc = [self.ps() for _ in range(4)]
                for fh in range(nfh):
                    w = wb[li % 2]
                    li += 1
                    self.wload(w[:], w_dt, wo_src[:, fh * 8:(fh + 1) * 8, dh * 512:(dh + 1) * 512], ('wo', nfc, dh, fh))
                    for j in range(4):
                        for f8 in range(8):
                            self.mm(pacc[j][:, 0:n], w[:, f8, j * 128:(j + 1) * 128], hT[:, fh * 8 + f8, 0:n],
                                    start=(fh == 0 and f8 == 0), stop=(fh == nfh - 1 and f8 == 7))
                for j in range(4):
                    dc = dh * 4 + j
                    self.tt(x[:, dc, 0:n], pacc[j][:, 0:n], x[:, dc, 0:n], ALU.add)

        def conv_chunk(p, n, fc, hist, cwt, cbt, stage, cacc, outT):
            stg = stage
            acc = cacc
            self.cp(stg[:, 0:3], hist[:, fc, :])
            self.cp(stg[:, 3:3 + n], p[:, 0:n], eng='act')
            self.ts(acc[:, 0:n], stg[:, 0:n], cwt[:, fc, 0:1], ALU.mult)
            for j in range(1, 4):
                self.stt(acc[:, 0:n], stg[:, j:j + n], cwt[:, fc, j:j + 1], acc[:, 0:n], ALU.mult, ALU.add)
            if cbt is not None:
                self.act(outT[:, fc, 0:n], acc[:, 0:n], AF.Silu, bias=cbt[:, fc:fc + 1])
            else:
                self.act(outT[:, fc, 0:n], acc[:, 0:n], AF.Silu)
            self.cp(hist[:, fc, :], stg[:, n:n + 3])

        def mixer_ab(n, L, g1, r, g2in, g2out):
            nch = n // L
            NM = 2
            self.phase_begin(('ab', n))
            hT = A("hT", [128, 8, n], BF16)
            wb = [A("wb0", [128, 8, 512], BF16), A("wb1", [128, 8, 512], BF16)]
            wgt = A("wgt", [128, 8, 4], BF16); wdt = A("wdt", [128, 8, 8], BF16)
            qT = A("qT", [128, NM, n], BF16); kT = A("kT", [128, NM, n], BF16)
            k_tok = A("k_tok", [128, nch, 256], BF16)
            v_ext = A("v_ext", [128, nch, NM, 257], BF16)
            so = A("so", [128, nch, 512], BF16); zs = A("zs", [128, nch, 512], BF16)
            xbcT = A("xbcT", [128, 6, n], BF16)
            x_tok = A("x_tok", [128, nch, 512], BF16); bm_tok = A("bm_tok", [128, nch, 128], BF16)
            h_tok = A("h_tok", [128, 1024], BF16)
            stage = A("stage0", [128, 3 + n]); cacc = A("cacc0", [128, n])
            R = [A("row%d" % i, [16, n]) for i in range(10)]
            gT = A("gT", [128, nch, 6]); gS = A("gS", [128, nch, 32])
            decB = A("decB", [128, nch, NM]); decS = A("decS", [128, nch, 8])
            D4 = A("D4", [NM, nch, NM]); D16 = A("D16", [8, nch, 8]); Gpv = A("Gpv", [NM, nch]); dec4 = A("dec4", [NM, nch])
            PTm = A("PTm", [128, 128], BF16)
            vu = [A("vu0", [128, 257], BF16), A("vu1", [128, 257], BF16)]
            Cb = A("Cb", [128, NM, 257], BF16)
            cbm = A("cbm", [128, 128])
            seg = [A("seg0", [128, 128]), A("seg1", [128, 128])]
            MT = A("MT", [128, 8, 128], BF16)
            xd = A("xd", [128, 512], BF16); xw = A("xw", [128, 512], BF16)
            ya = A("ya", [128, 512]); yb = A("yb", [128, 512]); hraw = A("hraw", [128, 256]); junk = yb
            c1 = A("c1", [128, 1]); c2 = A("c2", [128, 1]); c3 = A("c3", [128, 1]); c4 = A("c4", [128, 1])

            dma(hn[:, :, 0:n], g1.v(g1.ap[r * 1024:(r + 1) * 1024, :].rearrange("(k p) t -> p k t", p=128)))
            win = abw.ap.rearrange("(k p) c -> p k c", p=128)
            lw = [0]

            def loadw(c0, ncol):
                w = wb[lw[0] % 2]
                lw[0] += 1
                self.wload(w[:, :, 0:ncol], abw, win[:, :, c0:c0 + ncol], ('abin', c0))
                return w

            def fm_proj(w, j, M=128):
                p = self.ps()
                for kc in range(8):
                    self.mm(p[0:M, 0:n], w[:, kc, j * 128:j * 128 + M], hn[:, kc, 0:n], start=(kc == 0), stop=(kc == 7))
                return p

            def tm_proj(w, c, c0=0, ncol=512):
                p = self.ps()
                for kc in range(8):
                    self.mm(p[0:L, 0:ncol], hn[:, kc, c * L:(c + 1) * L], w[:, kc, c0:c0 + ncol], start=(kc == 0), stop=(kc == 7))
                return p

            self.wload(wgt[:], abw, win[:, :, 1536:1540], ('abg',))
            self.wload(wdt[:], abw, win[:, :, 2820:2828], ('abdt',))
            w = loadw(0, 512)
            for h in range(NM):
                p = fm_proj(w, h)
                self.cp(qT[:, h, 0:n], p[:, 0:n], eng='act')
            for h in range(NM):
                p = fm_proj(w, NM + h)
                self.ts(kT[:, h, 0:n], p[:, 0:n], DKs, ALU.mult)
            for c in range(nch):
                p = tm_proj(w, c, 256, 256)
                self.ts(k_tok[0:L, c, :], p[0:L, 0:256], DKs, ALU.mult)
            self.memset(v_ext[:, :, :, 256:257], 1.0)
            w = loadw(512, 512)
            for c in range(nch):
                p = tm_proj(w, c)
                self.cp(v_ext[0:L, c, 0:NM, 0:256], V(p.h[0:L, 0:512].rearrange("p (a b) -> p a b", b=256), p.buf), eng='act')
            w = loadw(1024, 512)
            for c in range(nch):
                p = tm_proj(w, c)
                self.act(so[0:L, c, :], p[0:L, 0:512], AF.Sigmoid)
            w = loadw(1540, 512)
            for c in range(nch):
                p = tm_proj(w, c)
                self.act(zs[0:L, c, :], p[0:L, 0:512], AF.Silu)
            w = loadw(2052, 512)
            for j in range(4):
                p = fm_proj(w, j)
                conv_chunk(p, n, j, hist_b, cw, cb, stage, cacc, xbcT)
            w = loadw(2564, 256)
            for j in range(2):
                p = fm_proj(w, j)
                conv_chunk(p, n, 4 + j, hist_b, cw, cb, stage, cacc, xbcT)
            t1, Fn, a_, G_, em, u_, w_, tmp = R[0], R[1], R[2], R[3], R[4], R[5], R[6], R[7]
            pig = self.ps()
            for kc in range(8):
                self.mm(pig[0:NM, 0:n], wgt[:, kc, 0:NM], hn[:, kc, 0:n], start=(kc == 0), stop=(kc == 7))
            pfg = self.ps()
            for kc in range(8):
                self.mm(pfg[0:NM, 0:n], wgt[:, kc, NM:2 * NM], hn[:, kc, 0:n], start=(kc == 0), stop=(kc == 7))
            self.act(t1[0:NM, 0:n], pfg[0:NM, 0:n], AF.Exp, bias=nbf[:], scale=-1.0)
            self.act(t1[0:NM, 0:n], t1[0:NM, 0:n], AF.Ln, bias=1.0)
            self.scan(Fn[0:NM, 0:n], onesr[0:NM, 0:n], t1[0:NM, 0:n], Fc[:], ALU.mult, ALU.add)
            self.stt(a_[0:NM, 0:n], pig[0:NM, 0:n], bi[:], Fn[0:NM, 0:n], ALU.add, ALU.add)
            self.scan(G_[0:NM, 0:n], onesr[0:NM, 0:n], a_[0:NM, 0:n], Gc[:], ALU.mult, ALU.max)
            self.tt(tmp[0:NM, 0:n], Fn[0:NM, 0:n], G_[0:NM, 0:n], ALU.subtract)
            self.act(em[0:NM, 0:n], tmp[0:NM, 0:n], AF.Exp)

            def r3(t, np_):
                return t.h[0:np_, 0:n].rearrange("p (c l) -> p c l", l=L)
            gend = V(r3(G_, NM)[:, :, L - 1:L].to_broadcast([NM, nch, L]), G_.buf)
            self.tt(V(r3(tmp, NM), tmp.buf), V(r3(a_, NM), a_.buf), gend, ALU.subtract)
            self.act(u_[0:NM, 0:n], tmp[0:NM, 0:n], AF.Exp)
            self.tt(V(r3(tmp, NM), tmp.buf), V(r3(G_, NM), G_.buf), gend, ALU.subtract)
            self.act(w_[0:NM, 0:n], tmp[0:NM, 0:n], AF.Exp, scale=-1.0)
            self.cp(Gpv[0:NM, 0:1], Gc[:])
            if nch > 1:
                self.cp(V(Gpv.h[0:NM, 1:nch].unsqueeze(2), Gpv.buf), V(r3(G_, NM)[:, 0:nch - 1, L - 1:L], G_.buf))
            self.tt(V(dec4.h[0:NM, 0:nch].unsqueeze(2), dec4.buf), V(Gpv.h[0:NM, 0:nch].unsqueeze(2), Gpv.buf),
                    V(r3(G_, NM)[:, :, L - 1:L], G_.buf), ALU.subtract)
            self.act(dec4[0:NM, 0:nch], dec4[0:NM, 0:nch], AF.Exp)
            self.tt(D4[0:NM, 0:nch, :], V(dec4.h[0:NM, 0:nch].unsqueeze(2).to_broadcast([NM, nch, NM]), dec4.buf),
                    V(ident_f.h[0:NM, 0:NM].unsqueeze(1).to_broadcast([NM, nch, NM]), ident_f.buf), ALU.mult)
            p = self.ps()
            self.mm(p[:, 0:nch * NM], ones16[0:NM, :], V(D4.h[0:NM, 0:nch, :].rearrange("p c h -> p (c h)"), D4.buf))
            self.cp(V(decB.h[:, 0:nch, :].rearrange("p c h -> p (c h)"), decB.buf), p[:, 0:nch * NM])
            self.cp(Fc[:], Fn[0:NM, n - 1:n])
            self.cp(Gc[:], G_[0:NM, n - 1:n])
            p = self.ps()
            for c in range(nch):
                for qi, rt in enumerate((u_, w_, em)):
                    self.tr(p[0:L, c * 6 + qi * NM:c * 6 + qi * NM + NM], rt[0:NM, c * L:(c + 1) * L], ident_f[0:NM, 0:NM])
            self.cp(V(gT.h[0:L, 0:nch, :].rearrange("p c h -> p (c h)"), gT.buf), p[0:L, 0:nch * 6])
            dt_, ar, b_, eb, nb, e2 = R[0], R[1], R[2], R[8], R[9], R[7]
            pdt = self.ps()
            for kc in range(8):
                self.mm(pdt[0:8, 0:n], wdt[:, kc, :], hn[:, kc, 0:n], start=(kc == 0), stop=(kc == 7))
            self.act(dt_[0:8, 0:n], pdt[0:8, 0:n], AF.Exp, bias=dtb[:])
            self.act(dt_[0:8, 0:n], dt_[0:8, 0:n], AF.Ln, bias=1.0)
            self.ts(ar[0:8, 0:n], dt_[0:8, 0:n], negA[:], ALU.mult)
            for c in range(nch):
                self.scan(b_[0:8, c * L:(c + 1) * L], onesr[0:8, 0:L], ar[0:8, c * L:(c + 1) * L], 0.0, ALU.mult, ALU.add)
            self.act(eb[0:8, 0:n], b_[0:8, 0:n], AF.Exp)
            self.ts(nb[0:8, 0:n], b_[0:8, 0:n], -1.0, ALU.mult)
            bLb = V(r3(b_, 8)[:, :, L - 1:L].to_broadcast([8, nch, L]), b_.buf)
            self.tt(V(r3(e2, 8), e2.buf), bLb, V(r3(b_, 8), b_.buf), ALU.subtract)
            self.act(e2[0:8, 0:n], e2[0:8, 0:n], AF.Exp)
            self.tt(e2[0:8, 0:n], e2[0:8, 0:n], dt_[0:8, 0:n], ALU.mult)
            self.tt(D16[:, 0:nch, :], V(r3(eb, 8)[:, :, L - 1:L].to_broadcast([8, nch, 8]), eb.buf),
                    V(ident_f.h[0:8, 0:8].unsqueeze(1).to_broadcast([8, nch, 8]), ident_f.buf), ALU.mult)
            p = self.ps()
            self.mm(p[:, 0:nch * 8], ones16[0:8, :], V(D16.h[:, 0:nch, :].rearrange("p c h -> p (c h)"), D16.buf))
            self.cp(V(decS.h[:, 0:nch, :].rearrange("p c h -> p (c h)"), decS.buf), p[:, 0:nch * 8])
            p = self.ps()
            for c in range(nch):
                for qi, rt in enumerate((dt_, nb, eb, e2)):
                    self.tr(p[0:L, c * 32 + qi * 8:c * 32 + qi * 8 + 8], rt[0:8, c * L:(c + 1) * L], ident_f[0:8, 0:8])
            self.cp(V(gS.h[0:L, 0:nch, :].rearrange("p c h -> p (c h)"), gS.buf), p[0:L, 0:nch * 32])
            for c in range(nch):
                p = self.ps()
                for fc in range(4):
                    self.tr(pbv(p, slice(0, L), slice(fc * 128, (fc + 1) * 128)), xbcT[:, fc, c * L:(c + 1) * L], ident_b[:])
                self.tr(pbv(p, slice(0, L), slice(512, 640)), xbcT[:, 4, c * L:(c + 1) * L], ident_b[:])
                self.cp(x_tok[0:L, c, :], pbv(p, slice(0, L), slice(0, 512)), eng='act')
                self.cp(bm_tok[0:L, c, :], pbv(p, slice(0, L), slice(512, 640)))
            junkm = [A("junkm%d" % i, [128, 256]) for i in range(NM)]
            PTms = [A("PTm%d" % i, [128, 128], BF16) for i in range(NM)]
            hraws = [A("hraw%d" % i, [128, 256]) for i in range(NM)]
            ccols = [[A("cm%d_%d" % (i, j), [128, 1]) for j in range(3)] for i in range(NM)]

            def m_chain(c, h):
                cs, ce = c * L, (c + 1) * L
                ht = h_tok
                PTm = PTms[h]; junk = junkm[h]; hraw = hraws[h]; c1, c2, c3 = ccols[h]
                v_u = vu[h % 2]
                p1 = self.psb[2 * h]
                self.mm(p1[0:L, 0:L], kT[:, h, cs:ce], qT[:, h, cs:ce])
                yield
                self.tt(PTm[0:L, 0:L], p1[0:L, 0:L], maskT[0:L, 0:L], ALU.mult)
                yield
                self.ts(v_u[0:L, :], v_ext[0:L, c, h, :], gT[0:L, c, h:h + 1], ALU.mult)
                yield
                self.ts(Cx[:, h, :], Cx[:, h, :], decB[:, c, h:h + 1], ALU.mult)
                yield
                self.cp(Cb[:, h, :], Cx[:, h, :], eng='act')
                yield
                p2 = self.psb[2 * h + 1]
                self.mm(p2[0:L, 0:257], PTm[0:L, 0:L], v_u[0:L, :], start=True, stop=False)
                yield
                self.mm(p2[0:L, 0:257], qT[:, h, cs:ce], Cb[:, h, :], start=False, stop=True)
                yield
                p3 = self.psb[2 * h]
                self.mm(p3[:, 0:257], k_tok[0:L, c, h * 128:(h + 1) * 128], v_u[0:L, :])
                yield
                self.tt(Cx[:, h, :], p3[:, 0:257], Cx[:, h, :], ALU.add)
                yield
                wcol = gT[0:L, c, NM + h:NM + h + 1]
                emcol = gT[0:L, c, 2 * NM + h:2 * NM + h + 1]
                self.act(c1[0:L, :], p2[0:L, 256:257], AF.Abs, scale=wcol)
                yield
                self.tt(c1[0:L, :], c1[0:L, :], emcol, ALU.max)
                yield
                self.recip(c1[0:L, :], c1[0:L, :])
                yield
                self.tt(c2[0:L, :], c1[0:L, :], wcol, ALU.mult)
                yield
                self.act(junk[0:L, 0:256], p2[0:L, 0:256], AF.Square, scale=c2[0:L, :])
                yield
                self.rsum(c3[0:L, :], junk[0:L, 0:256])
                yield
                self.ts(hraw[0:L, :], p2[0:L, 0:256], c2[0:L, :], ALU.mult)
                yield
                self.act(c3[0:L, :], c3[0:L, :], AF.Sqrt, bias=EPS, scale=1.0 / 256)
                yield
                self.recip(c3[0:L, :], c3[0:L, :])
                yield
                self.stt(hraw[0:L, :], hraw[0:L, :], c3[0:L, :], gnB[0:L, h * 256:(h + 1) * 256], ALU.mult, ALU.mult)
                yield
                self.tt(ht[0:L, h * 256:(h + 1) * 256], hraw[0:L, :], so[0:L, c, h * 256:(h + 1) * 256], ALU.mult)
                yield

            def s_chain(c):
                cs, ce = c * L, (c + 1) * L
                ht = h_tok
                junk = yb
                p1 = self.psb[4]
                self.mm(p1[0:L, 0:L], xbcT[:, 4, cs:ce], xbcT[:, 5, cs:ce])
                yield
                self.tt(cbm[0:L, 0:L], p1[0:L, 0:L], maskT[0:L, 0:L], ALU.mult)
                yield
                pbb = None
                for hh in range(8):
                    j = hh % 4
                    if j == 0:
                        pbb = self.psb[5 + hh // 4]
                    self.mm(pbb[0:L, j * 128:j * 128 + L], sel16[0:8, hh, 0:L], b_[0:8, cs:ce])
                    sg = seg[hh % 2]
                    self.ts(sg[0:L, 0:L], pbb[0:L, j * 128:j * 128 + L], gS[0:L, c, 8 + hh:9 + hh], ALU.add, 0.0, ALU.min)
                    self.act(sg[0:L, 0:L], sg[0:L, 0:L], AF.Exp)
                    self.tt(MT[0:L, hh, 0:L], sg[0:L, 0:L], cbm[0:L, 0:L], ALU.mult)

                def v3(t, ap):
                    return V(ap.rearrange("p (h e) -> p h e", e=64), t.buf)
                xg = x_tok.h[0:L, c, :]
                self.tt(v3(xd, xd.h[0:L, :]), v3(x_tok, xg),
                        V(gS.h[0:L, c, 0:8].unsqueeze(2).to_broadcast([L, 8, 64]), gS.buf), ALU.mult)
                self.tt(v3(xw, xw.h[0:L, :]), v3(x_tok, xg),
                        V(gS.h[0:L, c, 24:32].unsqueeze(2).to_broadcast([L, 8, 64]), gS.buf), ALU.mult)
                pY1 = self.psb[7]
                for hh in range(8):
                    self.mm(pY1[0:L, hh * 64:(hh + 1) * 64], MT[0:L, hh, 0:L], xd[0:L, hh * 64:(hh + 1) * 64])
                pY2 = self.psb[4]
                self.mm(pY2[0:L, 0:512], xbcT[:, 5, cs:ce], V(Sb.h[:, :, :].rearrange("p h e -> p (h e)"), Sb.buf))
                yield
                self.tt(v3(ya, ya.h[0:L, :]), v3(pY2, pY2.h[0:L, 0:512]),
                        V(gS.h[0:L, c, 16:24].unsqueeze(2).to_broadcast([L, 8, 64]), gS.buf), ALU.mult)
                self.tt(ya[0:L, :], pY1[0:L, 0:512], ya[0:L, :], ALU.add)
                yield
                self.tt(v3(yb, yb.h[0:L, :]), v3(x_tok, xg),
                        V(dskB.h[0:L, 0:8].unsqueeze(2).to_broadcast([L, 8, 64]), dskB.buf), ALU.mult)
                self.tt(ya[0:L, :], ya[0:L, :], yb[0:L, :], ALU.add)
                yield
                self.tt(ya[0:L, :], ya[0:L, :], zs[0:L, c, :], ALU.mult)
                yield
                self.act(junk[0:L, :], ya[0:L, :], AF.Square)
                yield
                self.rsum(c4[0:L, :], junk[0:L, :])
                yield
                self.act(c4[0:L, :], c4[0:L, :], AF.Sqrt, bias=EPS, scale=1.0 / 512)
                yield
                self.recip(c4[0:L, :], c4[0:L, :])
                yield
                self.stt(ht[0:L, 512:1024], ya[0:L, :], c4[0:L, :], snB[0:L, :], ALU.mult, ALU.mult)
                yield
                pS = self.psb[5]
                self.mm(pS[:, 0:512], bm_tok[0:L, c, :], xw[0:L, :])
                yield
                self.tt(ST[:, :, :], ST[:, :, :],
                        V(decS.h[:, c, 0:8].unsqueeze(2).to_broadcast([128, 8, 64]), decS.buf), ALU.mult)
                stf = V(ST.h[:, :, :].rearrange("p h e -> p (h e)"), ST.buf)
                self.tt(stf, pS[:, 0:512], stf, ALU.add)
                yield
                self.cp(Sb[:, :, :], ST[:, :, :], eng='act')
                yield

            def interleave(gens):
                gens = list(gens)
                while gens:
                    for g in list(gens):
                        try:
                            next(g)
                        except StopIteration:
                            gens.remove(g)

            for c in range(nch):
                cs, ce = c * L, (c + 1) * L
                ht = h_tok
                interleave([m_chain(c, h) for h in range(NM)] + [s_chain(c)])
                p = self.ps()
                for f8 in range(8):
                    self.tr(pbv(p, slice(0, 128), slice(f8 * 128, f8 * 128 + L)), ht[0:L, f8 * 128:(f8 + 1) * 128],
                            ident_b[0:L, 0:L])
                self.cp(hT[:, 0:8, cs:ce], V(p.hb[:, 0:1024].rearrange("p (f t) -> p f t", t=128)[:, :, 0:L], p.buf), eng='act')
            dma(g2in.v(g2in.ap.rearrange("(f p) t -> p f t", p=128)), hT[:, :, 0:n])
            allgather(g2in, g2out)

        def mixer_c(n, L, g3, r, g4in, g4out):
            nch = n // L
            nsq = {128: 6, 16: 3}[L]
            NH = 4
            self.phase_begin(('c', n))
            hT = A("hT", [128, NH, n], BF16)
            wb = [A("wb0", [128, 8, 512], BF16), A("wb1", [128, 8, 512], BF16)]
            wba = A("wba", [128, 8, 8], BF16)
            qkvT = A("qkvT", [128, 12, n], BF16)
            zs = A("zs", [128, nch, 512], BF16)
            k_tok = A("k_tok", [128, nch, 512], BF16); v_tok = A("v_tok", [128, nch, 512], BF16)
            h_tok = A("h_tok", [128, 512], BF16)
            stage = A("stage0", [128, 3 + n]); cacc = A("cacc0", [128, n])
            R = [A("row%d" % i, [16, n]) for i in range(7)]
            gC = A("gC", [128, nch, 24])
            decC = A("decC", [128, nch, NH]); D8 = A("D8", [NH, nch, NH])
            sqh = A("sqh0", [128, n], BF16); rst = A("rst0", [128, n])
            Xs = [A("X%d" % i, [128, 128]) for i in range(NH)]
            XTs = [A("XT%d" % i, [128, 128]) for i in range(NH)]
            TTs = [A("TT%d" % i, [128, 128]) for i in range(NH)]
            KKs = [A("KK%d" % i, [128, 128]) for i in range(NH)]
            dTs_ = [A("dT%d" % i, [128, 128]) for i in range(NH)]
            tmpm = [A("tmpm%d" % i, [128, 128]) for i in range(NH)]
            R1 = tmpm
            R2 = KKs
            U0s = [[A("U0s_%d_%d" % (c, h), [128, 128]) for h in range(NH)] for c in range(min(2, nch))]
            WTb = [[A("WTb_%d_%d" % (c, h), [128, 128], BF16) for h in range(NH)] for c in range(min(2, nch))]
            QKd = [[A("QKd_%d_%d" % (c, h), [128, 128], BF16) for h in range(NH)] for c in range(min(2, nch))]
            ub = [A("ub%d" % i, [128, 128], BF16) for i in range(NH)]
            kw = [A("kw%d" % i, [128, 128], BF16) for i in range(NH)]
            o1s = [A("o1s%d" % i, [128, 128]) for i in range(NH)]
            oo = [A("oo%d" % i, [128, 128]) for i in range(NH)]
            jk = o1s
            cc = [A("cc%d" % i, [128, 1]) for i in range(NH)]

            dma(hn[:, :, 0:n], g3.v(g3.ap[r * 1024:(r + 1) * 1024, :].rearrange("(k p) t -> p k t", p=128)))
            win = gw.ap.rearrange("(k p) c -> p k c", p=128)
            lw = [0]

            def loadw(c0, ncol):
                w = wb[lw[0] % 2]
                lw[0] += 1
                self.wload(w[:, :, 0:ncol], gw, win[:, :, c0:c0 + ncol], ('gin', c0))
                return w

            self.wload(wba[:], gw, win[:, :, 2048:2056], ('gba',))
            for g3_ in range(3):
                w = loadw(g3_ * 512, 512)
                for j in range(4):
                    fc = g3_ * 4 + j
                    p = self.ps()
                    for kc in range(8):
                        self.mm(p[:, 0:n], w[:, kc, j * 128:(j + 1) * 128], hn[:, kc, 0:n], start=(kc == 0), stop=(kc == 7))
                    conv_chunk(p, n, fc, hist_c, gcw, None, stage, cacc, qkvT)
            w = loadw(1536, 512)
            for c in range(nch):
                p = self.ps()
                for kc in range(8):
                    self.mm(p[0:L, 0:512], hn[:, kc, c * L:(c + 1) * L], w[:, kc, 0:512], start=(kc == 0), stop=(kc == 7))
                self.act(zs[0:L, c, :], p[0:L, 0:512], AF.Silu)
            for fc in range(2 * NH):
                self.act(sqh[:, 0:n], qkvT[:, fc, 0:n], AF.Square)
                p = self.ps()
                self.mm(p[:, 0:n], ones_b[:], sqh[:, 0:n])
                self.act(rst[:, 0:n], p[:, 0:n], AF.Sqrt, bias=EPS, scale=float(D))
                self.recip(rst[:, 0:n], rst[:, 0:n])
                if fc < NH:
                    self.stt(qkvT[:, fc, 0:n], qkvT[:, fc, 0:n], DKs, rst[:, 0:n], ALU.mult, ALU.mult)
                else:
                    self.tt(qkvT[:, fc, 0:n], qkvT[:, fc, 0:n], rst[:, 0:n], ALU.mult)
            beta, nbeta, sp_, gam, egam, ngam, e2 = R
            g_ = sp_
            pb_ = self.ps()
            for kc in range(8):
                self.mm(pb_[0:NH, 0:n], wba[:, kc, 0:NH], hn[:, kc, 0:n], start=(kc == 0), stop=(kc == 7))
            pa_ = self.ps()
            for kc in range(8):
                self.mm(pa_[0:NH, 0:n], wba[:, kc, NH:2 * NH], hn[:, kc, 0:n], start=(kc == 0), stop=(kc == 7))
            self.act(beta[0:NH, 0:n], pb_[0:NH, 0:n], AF.Sigmoid)
            self.ts(nbeta[0:NH, 0:n], beta[0:NH, 0:n], -1.0, ALU.mult)
            self.act(sp_[0:NH, 0:n], pa_[0:NH, 0:n], AF.Exp, bias=gdtb[:])
            self.act(sp_[0:NH, 0:n], sp_[0:NH, 0:n], AF.Ln, bias=1.0)
            self.ts(g_[0:NH, 0:n], sp_[0:NH, 0:n], gnegA[:], ALU.mult)
            for c in range(nch):
                self.scan(gam[0:NH, c * L:(c + 1) * L], onesr[0:NH, 0:L], g_[0:NH, c * L:(c + 1) * L], 0.0, ALU.mult, ALU.add)
            self.act(egam[0:NH, 0:n], gam[0:NH, 0:n], AF.Exp)
            self.ts(ngam[0:NH, 0:n], gam[0:NH, 0:n], -1.0, ALU.mult)

            def r3(t, np_):
                return t.h[0:np_, 0:n].rearrange("p (c l) -> p c l", l=L)
            gLb = V(r3(gam, NH)[:, :, L - 1:L].to_broadcast([NH, nch, L]), gam.buf)
            self.tt(V(r3(e2, NH), e2.buf), gLb, V(r3(gam, NH), gam.buf), ALU.subtract)
            self.act(e2[0:NH, 0:n], e2[0:NH, 0:n], AF.Exp)
            self.tt(sp_[0:NH, 0:n], beta[0:NH, 0:n], egam[0:NH, 0:n], ALU.mult)
            self.tt(D8[:, 0:nch, :], V(r3(egam, NH)[:, :, L - 1:L].to_broadcast([NH, nch, NH]), egam.buf),
                    V(ident_f.h[0:NH, 0:NH].unsqueeze(1).to_broadcast([NH, nch, NH]), ident_f.buf), ALU.mult)
            p = self.ps()
            self.mm(p[:, 0:nch * NH], ones16[0:NH, :], V(D8.h[:, 0:nch, :].rearrange("p c h -> p (c h)"), D8.buf))
            self.cp(V(decC.h[:, 0:nch, :].rearrange("p c h -> p (c h)"), decC.buf), p[:, 0:nch * NH])
            p = self.ps()
            for c in range(nch):
                for qi, rt in enumerate((beta, ngam, egam, e2, sp_, nbeta)):
                    self.tr(p[0:L, c * 24 + qi * NH:c * 24 + qi * NH + NH], rt[0:NH, c * L:(c + 1) * L], ident_f[0:NH, 0:NH])
            self.cp(V(gC.h[0:L, 0:nch, :].rearrange("p c h -> p (c h)"), gC.buf), p[0:L, 0:nch * 24])
            for c in range(nch):
                p = self.ps()
                for fc in range(NH):
                    self.tr(pbv(p, slice(0, L), slice(fc * 128, (fc + 1) * 128)), qkvT[:, NH + fc, c * L:(c + 1) * L], ident_b[:])
                    self.tr(pbv(p, slice(0, L), slice(512 + fc * 128, 512 + (fc + 1) * 128)), qkvT[:, 2 * NH + fc, c * L:(c + 1) * L], ident_b[:])
                self.cp(k_tok[0:L, c, :], pbv(p, slice(0, L), slice(0, 512)), eng='act')
                self.cp(v_tok[0:L, c, :], pbv(p, slice(0, L), slice(512, 1024)))

            def phaseA(c):
                cs, ce = c * L, (c + 1) * L
                hs = list(range(NH))
                for i, h in enumerate(hs):
                    pk = self.ps()
                    self.mm(pk[0:L, 0:L], qkvT[:, NH + h, cs:ce], qkvT[:, NH + h, cs:ce])
                    self.cp(KKs[i][0:L, 0:L], pk[0:L, 0:L], eng='act')
                    pbb = self.ps()
                    self.mm(pbb[0:L, 0:L], sel16[0:NH, h, 0:L], gam[0:NH, cs:ce])
                    self.ts(tmpm[i][0:L, 0:L], pbb[0:L, 0:L], gC[0:L, c, NH + h:NH + h + 1], ALU.add, 0.0, ALU.min)
                    self.act(tmpm[i][0:L, 0:L], tmpm[i][0:L, 0:L], AF.Exp)
                    self.tt(dTs_[i][0:L, 0:L], tmpm[i][0:L, 0:L], maskS[0:L, 0:L], ALU.mult)
                    self.tt(tmpm[i][0:L, 0:L], tmpm[i][0:L, 0:L], maskT[0:L, 0:L], ALU.mult)
                    pq = self.ps()
                    self.mm(pq[0:L, 0:L], qkvT[:, NH + h, cs:ce], qkvT[:, h, cs:ce])
                    self.tt(QKd[c % 2][h][0:L, 0:L], pq[0:L, 0:L], tmpm[i][0:L, 0:L], ALU.mult)
                for i, h in enumerate(hs):
                    pd = self.ps()
                    self.tr(pd[0:L, 0:L], dTs_[i][0:L, 0:L], ident_f[0:L, 0:L])
                    self.stt(Xs[i][0:L, 0:L], pd[0:L, 0:L], gC[0:L, c, 5 * NH + h:5 * NH + h + 1], KKs[i][0:L, 0:L], ALU.mult, ALU.mult)
                for i, h in enumerate(hs):
                    px = self.ps()
                    self.tr(px[0:L, 0:L], Xs[i][0:L, 0:L], ident_f[0:L, 0:L])
                    self.cp(XTs[i][0:L, 0:L], px[0:L, 0:L], eng='act')
                    self.tt(TTs[i][0:L, 0:L], px[0:L, 0:L], ident_f[0:L, 0:L], ALU.add)
                for j in range(nsq):
                    last = (j == nsq - 1)
                    for i, h in enumerate(hs):
                        pa2 = self.ps()
                        self.mm(pa2[0:L, 0:L], XTs[i][0:L, 0:L], Xs[i][0:L, 0:L])
                        if not last:
                            pb2 = self.ps()
                            self.mm(pb2[0:L, 0:L], Xs[i][0:L, 0:L], XTs[i][0:L, 0:L])
                        self.cp(Xs[i][0:L, 0:L], pa2[0:L, 0:L], eng='act')
                        if not last:
                            self.cp(XTs[i][0:L, 0:L], pb2[0:L, 0:L])
                    for i, h in enumerate(hs):
                        pc = self.ps()
                        self.mm(pc[0:L, 0:L], Xs[i][0:L, 0:L], TTs[i][0:L, 0:L])
                        self.tt(TTs[i][0:L, 0:L], pc[0:L, 0:L], TTs[i][0:L, 0:L], ALU.add)
                for i, h in enumerate(hs):
                    self.ts(R1[i][0:L, :], v_tok[0:L, c, h * 128:(h + 1) * 128], gC[0:L, c, h:h + 1], ALU.mult)
                    self.ts(R2[i][0:L, :], k_tok[0:L, c, h * 128:(h + 1) * 128], gC[0:L, c, 4 * NH + h:4 * NH + h + 1], ALU.mult)
                    pu = self.ps()
                    self.mm(pu[0:L, 0:128], TTs[i][0:L, 0:L], R1[i][0:L, :])
                    self.cp(U0s[c % 2][h][0:L, :], pu[0:L, 0:128], eng='act')
                    pw = self.ps()
                    self.mm(pw[:, 0:L], R2[i][0:L, :], TTs[i][0:L, 0:L])
                    self.cp(WTb[c % 2][h][:, 0:L], pw[:, 0:L])

            def phaseB(c):
                cs, ce = c * L, (c + 1) * L
                hs = list(range(NH))
                for i, h in enumerate(hs):
                    pws = self.ps()
                    self.mm(pws[0:L, 0:128], WTb[c % 2][h][:, 0:L], SGb[:, h, :])
                    self.tt(ub[i][0:L, :], U0s[c % 2][h][0:L, :], pws[0:L, 0:128], ALU.subtract)
                    self.ts(kw[i][0:L, :], k_tok[0:L, c, h * 128:(h + 1) * 128], gC[0:L, c, 3 * NH + h:3 * NH + h + 1], ALU.mult)
                for i, h in enumerate(hs):
                    po1 = self.ps()
                    self.mm(po1[0:L, 0:128], QKd[c % 2][h][0:L, 0:L], ub[i][0:L, :])
                    po2 = self.ps()
                    self.mm(po2[0:L, 0:128], qkvT[:, h, cs:ce], SGb[:, h, :])
                    self.cp(o1s[i][0:L, :], po1[0:L, 0:128], eng='act')
                    self.stt(oo[i][0:L, :], po2[0:L, 0:128], gC[0:L, c, 2 * NH + h:2 * NH + h + 1], o1s[i][0:L, :], ALU.mult, ALU.add)
                for i, h in enumerate(hs):
                    pS = self.ps()
                    self.mm(pS[:, 0:128], kw[i][0:L, :], ub[i][0:L, :])
                    self.stt(SG[:, h, :], SG[:, h, :], decC[:, c, h:h + 1], pS[:, 0:128], ALU.mult, ALU.add)
                    self.cp(SGb[:, h, :], SG[:, h, :], eng='act')
                for i, h in enumerate(hs):
                    self.act(jk[i][0:L, :], oo[i][0:L, :], AF.Square)
                    self.rsum(cc[i][0:L, :], jk[i][0:L, :])
                    self.act(cc[i][0:L, :], cc[i][0:L, :], AF.Sqrt, bias=EPS, scale=1.0 / 128)
                    self.recip(cc[i][0:L, :], cc[i][0:L, :])
                    self.stt(oo[i][0:L, :], oo[i][0:L, :], cc[i][0:L, :], gdnB[0:L, :], ALU.mult, ALU.mult)
                    self.tt(h_tok[0:L, h * 128:(h + 1) * 128], oo[i][0:L, :], zs[0:L, c, h * 128:(h + 1) * 128], ALU.mult)
                p = self.ps()
                for f8 in range(NH):
                    self.tr(pbv(p, slice(0, 128), slice(f8 * 128, f8 * 128 + L)), h_tok[0:L, f8 * 128:(f8 + 1) * 128],
                            ident_b[0:L, 0:L])
                self.cp(hT[:, 0:NH, cs:ce], V(p.hb[:, 0:512].rearrange("p (f t) -> p f t", t=128)[:, :, 0:L], p.buf), eng='act')

            phaseA(0)
            for c in range(nch):
                if c + 1 < nch:
                    phaseA(c + 1)
                phaseB(c)
            dma(g4in.v(g4in.ap.rearrange("(f p) t -> p f t", p=128)), hT[:, :, 0:n])
            allgather(g4in, g4out)

        xsrc = xT.ap.rearrange("(k p) t -> p k t", p=128)
        ydst = yT.ap.rearrange("(k p) t -> p k t", p=128)
        for b, (t0, n) in enumerate(own):
            dma(xb[b][:, :, 0:n], xT.v(xsrc[:, :, t0:t0 + n]))
        ffn_parts = [[(xb[b], TB)] for b in range(3)] + [[(xb[3], TB), (xb[4], DEC_SEQ)]]
        for pi_, parts in enumerate(ffn_parts):
            ffn(0, 0, parts)
            for (xx, n) in parts:
                b = 4 if n == DEC_SEQ else pi_
                hn_exchange(1, n, xx, G1in[b], G1out[b])
        seqs = [(r * 4 + i, TB, 128, G1out[i], r) for r in range(2) for i in range(4)] + \
               [(8 + r, DEC_SEQ, DEC_SEQ, G1out[4], r) for r in range(2)]

        def ab_zero():
            self.memset(Cx[:], 0.0); self.memset(ST[:], 0.0); self.memset(hist_b[:], 0.0)
            self.memset(Fc[:], 0.0); self.memset(Gc[:], 0.0)
            self.cp(Sb[:], ST[:])

        def ab_store(k):
            self.tt(mo[:], Gc[:], Fc[:], ALU.subtract)
            dma(o_Cx[k].v(), Cx[:], is_output=True); dma(o_m[k].v(), mo[:], is_output=True)
            dma(o_ST[k].v(), ST[:], is_output=True); dma(o_hb[k].v(), hist_b[:], is_output=True)

        def prefetch_all():
            groups = [(g * 512, 512) for g in range(5)] + [(2560, 256)]
            wo_ab = ab_w_out.ap.rearrange("(f p) d -> p f d", p=128)
            for dh in range(2):
                for fh in range(2):
                    self.wprefetch([128, 8, 512], ab_w_out, wo_ab[:, fh * 8:(fh + 1) * 8, dh * 512:(dh + 1) * 512], ('wo', 16, dh, fh))
            for (l, i) in ((0, 1), (1, 0)):
                wg_src = w_gate.ap[l, i].rearrange("(k p) f -> p k f", p=128)
                wu_src = w_up.ap[l, i].rearrange("(k p) f -> p k f", p=128)
                wd_src = w_down.ap[l, i].rearrange("(f p) d -> p f d", p=128)
                for gi, (c0, ncol) in enumerate(groups):
                    self.wprefetch([128, 8, ncol], w_gate, wg_src[:, :, c0:c0 + ncol], ('wg', l, i, gi))
                    self.wprefetch([128, 8, ncol], w_up, wu_src[:, :, c0:c0 + ncol], ('wu', l, i, gi))
                    if gi == 1:
                        for q in range(2):
                            self.wprefetch([128, 11, D], w_down, wd_src[:, q * 11:(q + 1) * 11, :], ('wd', l, i, q))
            gwin = gw.ap.rearrange("(k p) c -> p k c", p=128)
            self.wprefetch([128, 8, 8], gw, gwin[:, :, 2048:2056], ('gba',))
            for c0 in (0, 512, 1024, 1536):
                self.wprefetch([128, 8, 512], gw, gwin[:, :, c0:c0 + 512], ('gin', c0))
            wo_g = gdn_w_out.ap.rearrange("(f p) d -> p f d", p=128)
            for dh in range(2):
                self.wprefetch([128, 8, 512], gdn_w_out, wo_g[:, 0:8, dh * 512:(dh + 1) * 512], ('wo', 8, dh, 0))
            l, i = 1, 1
            wg_src = w_gate.ap[l, i].rearrange("(k p) f -> p k f", p=128)
            wu_src = w_up.ap[l, i].rearrange("(k p) f -> p k f", p=128)
            wd_src = w_down.ap[l, i].rearrange("(f p) d -> p f d", p=128)
            for gi, (c0, ncol) in enumerate(groups):
                self.wprefetch([128, 8, ncol], w_gate, wg_src[:, :, c0:c0 + ncol], ('wg', l, i, gi))
                self.wprefetch([128, 8, ncol], w_up, wu_src[:, :, c0:c0 + ncol], ('wu', l, i, gi))
                if gi == 1:
                    for q in range(2):
                        self.wprefetch([128, 11, D], w_down, wd_src[:, q * 11:(q + 1) * 11, :], ('wd', l, i, q))

        ab_zero()
        for (k, n, L, g1, r) in seqs:
            if k >= 8:
                j = k - 8
                if j == 0:
                    ab_store(2)
                dma(Cx[:], s_Cx[j].v()); dma(ST[:], s_ST[j].v()); dma(hist_b[:], s_hb[j].v()); dma(Gc[:], s_m[j].v())
                self.memset(Fc[:], 0.0)
                self.cp(Sb[:], ST[:])
            mixer_ab(n, L, g1, r, G2in[k], G2out[k])
            if k >= 8:
                ab_store(k - 8)
        for b, (t0, n) in enumerate(own):
            gouts = (G2out[b], G2out[4 + b]) if b < 4 else (G2out[8], G2out[9])
            out_proj_sel(ab_w_out, 16, gouts, n, xb[b], 'ab')
        for parts in ffn_parts:
            ffn(0, 1, parts)
        for pi_, parts in enumerate(ffn_parts):
            ffn(1, 0, parts)
            for (xx, n) in parts:
                b = 4 if n == DEC_SEQ else pi_
                hn_exchange(4, n, xx, G3in[b], G3out[b])
        seqs_c = [(r * 4 + i, TB, 128, G3out[i], r) for r in range(2) for i in range(4)] + \
                 [(8 + r, DEC_SEQ, DEC_SEQ, G3out[4], r) for r in range(2)]

        def c_store(k):
            dma(o_SG[k].v(), SG[:], is_output=True); dma(o_hc[k].v(), hist_c[:], is_output=True)

        self.memset(SG[:], 0.0); self.memset(hist_c[:], 0.0)
        self.cp(SGb[:], SG[:])
        for (k, n, L, g3, r) in seqs_c:
            if k >= 8:
                j = k - 8
                if j == 0:
                    c_store(2)
                dma(SG[:], s_SG[j].v()); dma(hist_c[:], s_hc[j].v())
                self.cp(SGb[:], SG[:])
            mixer_c(n, L, g3, r, G4in[k], G4out[k])
            if k >= 8:
                c_store(k - 8)
        for b, (t0, n) in enumerate(own):
            gouts = (G4out[b], G4out[4 + b]) if b < 4 else (G4out[8], G4out[9])
            out_proj_sel(gdn_w_out, 8, gouts, n, xb[b], 'gdn')
        for parts in ffn_parts:
            ffn(1, 1, parts)
        for b, (t0, n) in enumerate(own):
            self.phase_begin(('fin',))
            yo = A("yo", [128, 8, TB]); sq = A("sq", [128, 8, TB], BF16)
            rmsnorm(n, lambda kc: gfcol[:, kc:kc + 1], yo, sq, xb[b])
            dma(yT.v(ydst[:, :, t0:t0 + n]), yo[:, :, 0:n], is_output=True)
        S.finish()
        S.emit()
        return nc


_NC_CACHE = {}


def _own_ch_ssd(r):
    return np.concatenate([np.arange(r * 512, (r + 1) * 512), 1024 + np.arange(r * 128, (r + 1) * 128),
                           1280 + np.arange(r * 128, (r + 1) * 128)])


def _own_ch_gdn(r):
    return np.concatenate([np.arange(r * 512, (r + 1) * 512), 1024 + np.arange(r * 512, (r + 1) * 512),
                           2048 + np.arange(r * 512, (r + 1) * 512)])


def _host_inputs(d, p, r):
    f = np.ascontiguousarray
    x = np.concatenate([d['x_prompt'][p, r * 2048:(r + 1) * 2048], d['x_sample'][2 * p + r]], 0)
    wi = d['ab_w_in']
    abw = np.concatenate([wi[:, r * 256:(r + 1) * 256], wi[:, 512 + r * 256:512 + (r + 1) * 256],
                          wi[:, 1024 + r * 512:1024 + (r + 1) * 512], wi[:, 2048 + r * 512:2048 + (r + 1) * 512],
                          wi[:, 3072 + 2 * r:3074 + 2 * r], wi[:, 3076 + 2 * r:3078 + 2 * r],
                          wi[:, 3080 + r * 512:3080 + (r + 1) * 512], wi[:, 4104 + r * 512:4104 + (r + 1) * 512],
                          wi[:, 5128 + r * 128:5128 + (r + 1) * 128], wi[:, 5384 + r * 128:5384 + (r + 1) * 128],
                          wi[:, 5640 + r * 8:5640 + (r + 1) * 8]], axis=1)
    gi = d['gdn_w_in']
    gw = np.concatenate([gi[:, r * 512:(r + 1) * 512], gi[:, 1024 + r * 512:1024 + (r + 1) * 512],
                         gi[:, 2048 + r * 512:2048 + (r + 1) * 512], gi[:, 3072 + r * 512:3072 + (r + 1) * 512],
                         gi[:, 4096 + 4 * r:4100 + 4 * r], gi[:, 4104 + 4 * r:4108 + 4 * r]], axis=1)
    cs, cg = _own_ch_ssd(r), _own_ch_gdn(r)
    rm = np.zeros((128, 2), np.float32)
    rm[:, r] = 1.0
    ins = {
        "xT": f(x.T), "ffn_w_gate": d['ffn_w_gate'], "ffn_w_up": d['ffn_w_up'], "ffn_w_down": d['ffn_w_down'],
        "g_col": f(d['norm_g'].reshape(6, 8, 128).transpose(2, 0, 1)),
        "gf_col": f(d['norm_f'].reshape(8, 128).T), "rmask": rm,
        "abw": f(abw), "ab_w_out": d['ab_w_out'],
        "b_i": f(d['mlstm_b_i'][2 * r:2 * r + 2].reshape(2, 1)), "b_f": f(d['mlstm_b_f'][2 * r:2 * r + 2].reshape(2, 1)),
        "mlstm_norm_b": f(np.broadcast_to(d['mlstm_norm'][2 * r:2 * r + 2].reshape(1, 512), (128, 512))),
        "ssd_cw": f(d['ssd_conv_w'][:, cs].reshape(4, 6, 128).transpose(2, 1, 0)),
        "ssd_cb": f(d['ssd_conv_b'][cs].reshape(6, 128).T),
        "ssd_dtb": f(d['ssd_dt_bias'][r * 8:(r + 1) * 8].reshape(8, 1)),
        "ssd_alog": f(d['ssd_a_log'][r * 8:(r + 1) * 8].reshape(8, 1)),
        "ssd_d_b": f(np.broadcast_to(d['ssd_d'][r * 8:(r + 1) * 8].reshape(1, 8), (128, 8))),
        "ssd_norm_b": f(np.broadcast_to(d['ssd_norm'][r * 512:(r + 1) * 512].reshape(1, 512), (128, 512))),
        "gw": f(gw), "gdn_w_out": d['gdn_w_out'],
        "gdn_cw": f(d['gdn_conv_w'][:, cg].reshape(4, 12, 128).transpose(2, 1, 0)),
        "gdn_dtb": f(d['gdn_dt_bias'][4 * r:4 * r + 4].reshape(4, 1)),
        "gdn_alog": f(d['gdn_a_log'][4 * r:4 * r + 4].reshape(4, 1)),
        "gdn_norm_b": f(np.broadcast_to(d['gdn_norm'].reshape(1, 128), (128, 128))),
    }
    for j in range(2):
        s = 2 * p + j
        ins["s_Cx%d" % j] = f(np.concatenate([d['state_mlstm_C'][s, 2 * r:2 * r + 2].transpose(1, 0, 2),
                                              d['state_mlstm_n'][s, 2 * r:2 * r + 2].T[:, :, None]], 2))
        ins["s_m%d" % j] = f(d['state_mlstm_m'][s, 2 * r:2 * r + 2].reshape(2, 1))
        ins["s_ST%d" % j] = f(d['state_ssd'][s, r * 8:(r + 1) * 8].transpose(2, 0, 1))
        ins["s_hb%d" % j] = f(d['cache_ssd_conv'][s][:, cs].reshape(3, 6, 128).transpose(2, 1, 0))
        ins["s_SG%d" % j] = f(d['state_gdn'][s, 4 * r:4 * r + 4].transpose(1, 0, 2))
        ins["s_hc%d" % j] = f(d['cache_gdn_conv'][s][:, cg].reshape(3, 12, 128).transpose(2, 1, 0))
    return ins


def _assemble(R, npair):
    f32 = lambda a: np.ascontiguousarray(np.asarray(a, dtype=np.float32))
    y_prompt = np.zeros((npair, 4096, 1024), np.float32)
    y_sample = np.zeros((2 * npair, 16, 1024), np.float32)

    def alloc(n):
        return [np.zeros((n, 4, 128, 256), np.float32), np.zeros((n, 4, 128), np.float32), np.zeros((n, 4), np.float32),
                np.zeros((n, 16, 64, 128), np.float32), np.zeros((n, 3, 1536), np.float32),
                np.zeros((n, 8, 128, 128), np.float32), np.zeros((n, 3, 3072), np.float32)]
    PS, SS = alloc(npair), alloc(2 * npair)

    def put(dst, idx, r, res, k):
        cs, cg = _own_ch_ssd(r), _own_ch_gdn(r)
        Cx = np.asarray(res["o_Cx%d" % k], np.float32)
        dst[0][idx, 2 * r:2 * r + 2] = Cx[:, :, :256].transpose(1, 0, 2)
        dst[1][idx, 2 * r:2 * r + 2] = Cx[:, :, 256].T
        dst[2][idx, 2 * r:2 * r + 2] = np.asarray(res["o_m%d" % k], np.float32)[:, 0]
        dst[3][idx, r * 8:(r + 1) * 8] = np.asarray(res["o_ST%d" % k], np.float32).transpose(1, 2, 0)
        dst[4][idx][:, cs] = np.asarray(res["o_hb%d" % k], np.float32).transpose(2, 1, 0).reshape(3, 768)
        dst[5][idx, 4 * r:4 * r + 4] = np.asarray(res["o_SG%d" % k], np.float32).transpose(1, 0, 2)
        dst[6][idx][:, cg] = np.asarray(res["o_hc%d" % k], np.float32).transpose(2, 1, 0).reshape(3, 1536)

    for p in range(npair):
        for r in range(2):
            res = R[2 * p + r]
            yT = np.asarray(res['yT'], np.float32)
            y_prompt[p, r * 2048:(r + 1) * 2048] = yT[:, :2048].T
            y_sample[2 * p + r] = yT[:, 2048:].T
            put(PS, p, r, res, 2)
            for j in range(2):
                put(SS, 2 * p + j, r, res, j)
    return tuple([y_prompt, y_sample] + PS + SS)


def kernel(**inputs):
    d = {k: np.asarray(v, dtype=np.float32) for k, v in inputs.items()}
    ncores = 8
    if 'nc' not in _NC_CACHE:
        B = Builder(8)
        _NC_CACHE['nc'] = (B.build(), set(B.din))
    nc, din = _NC_CACHE['nc']
    in_maps = []
    for c in range(ncores):
        hi = _host_inputs(d, c // 2, c % 2)
        in_maps.append({k: v for k, v in hi.items() if k in din})
    res = run_bass_kernel_spmd(nc, in_maps, core_ids=list(range(ncores)))
    return _assemble(list(res.results), 4)
```

```python
import numpy as np
from contextlib import ExitStack
import concourse.bass as bass
import concourse.mybir as mybir
from concourse.bass_utils import run_bass_kernel_spmd

F32 = mybir.dt.float32
BF16 = mybir.dt.bfloat16
AF = mybir.ActivationFunctionType
ALU = mybir.AluOpType

ENGS = ['pe', 'act', 'dve', 'pool', 'sp']
NDSEM = 12
import os
PLIM = int(os.environ.get('K_PLIM', '3'))

D = 1024
DFF = 2816
NFF = 22
EPS = 1e-6
TB = 512
DEC_SEQ = 16


class Buf:
    __slots__ = ('name', 'w', 'r', 'excl')

    def __init__(self, name='', excl=False):
        self.name = name
        self.w = None
        self.r = []
        self.excl = excl


class Sched:
    def __init__(self, nc, same_engine_sync=True):
        self.nc = nc
        self.ops = {e: [] for e in ENGS}
        self.seen = {e: {f: -1 for f in ENGS} for e in ENGS}
        self.seen_dma = {e: {} for e in ENGS}
        self.targets = {e: set() for e in ENGS}
        self.ndma = {e: 0 for e in ENGS}
        self.same_engine_sync = same_engine_sync
        self.out_tokens = []

    def _need(self, eng, tok, waits):
        if tok is None:
            return
        if tok[0] == 'e':
            _, f, k = tok
            if f == eng and (eng == 'pe' or not self.same_engine_sync):
                return
            if self.seen[eng][f] >= k:
                return
            self.seen[eng][f] = k
            self.targets[f].add(k)
            waits.append(tok)
        elif tok[0] == 'c':
            if getattr(self, '_seen_cc', {}).get(eng, -1) >= tok[1]:
                return
            if not hasattr(self, '_seen_cc'):
                self._seen_cc = {}
            self._seen_cc[eng] = tok[1]
            waits.append(tok)
        else:
            _, q, seq = tok
            key = (q, seq % NDSEM)
            if self.seen_dma[eng].get(key, -1) >= seq:
                return
            self.seen_dma[eng][key] = seq
            waits.append(tok)

    def op(self, eng, fn, reads=(), writes=(), dma=False, is_output=False, cc=False):
        waits = []
        for b in reads:
            self._need(eng, b.w, waits)
            if b.excl:
                for t in b.r:
                    if t[0] == 'e' and t[1] != eng:
                        self._need(eng, t, waits)
        for b in writes:
            self._need(eng, b.w, waits)
            for t in b.r:
                self._need(eng, t, waits)
        idx = len(self.ops[eng])
        if eng == 'pool' and not dma:
            lp = getattr(self, '_last_pool', None)
            if lp is not None:
                self._need('pool', lp, waits)
            self._last_pool = ('e', 'pool', idx)
        if dma:
            seq = self.ndma[eng]
            self.ndma[eng] += 1
            lim = PLIM if eng == 'pool' else NDSEM
            if seq >= lim:
                self._need(eng, ('d', eng, seq - lim), waits)
            tok = ('d', eng, seq)
        elif cc:
            seq = None
            self.ncc = getattr(self, 'ncc', 0) + 1
            tok = ('c', self.ncc - 1)
        else:
            seq = None
            tok = ('e', eng, idx)
        self.ops[eng].append(dict(fn=fn, waits=waits, dma=seq, cc=cc))
        for b in reads:
            if tok[0] == 'e':
                b.r = [t for t in b.r if not (t[0] == 'e' and t[1] == tok[1])]
            b.r.append(tok)
        for b in writes:
            b.w = tok
            b.r = []
        if is_output:
            self.out_tokens.append(tok)
        return tok

    def barrier(self):
        last = {}
        for e in ENGS:
            if self.ops[e]:
                last[e] = ('e', e, len(self.ops[e]) - 1)
        for e in ENGS:
            waits = []
            for f, t in last.items():
                if f != e:
                    self._need(e, t, waits)
            for q in ENGS:
                lim = PLIM if q == 'pool' else NDSEM
                for sq_ in range(max(0, self.ndma[q] - lim), self.ndma[q]):
                    self._need(e, ('d', q, sq_), waits)
            if waits:
                self.ops[e].append(dict(fn=None, waits=waits, dma=None, cc=False))

    def finish(self):
        waits = []
        for t in self.out_tokens:
            self._need('sp', t, waits)
        self.ops['sp'].append(dict(fn=None, waits=waits, dma=None, cc=False))

    def emit(self):
        nc = self.nc
        with ExitStack() as st:
            esem = {e: st.enter_context(nc.semaphore('es_' + e)) for e in ENGS}
            ccsem = st.enter_context(nc.semaphore('cc_sem'))
            dsem = {e: [st.enter_context(nc.semaphore('ds_%s%d' % (e, i))) for i in range(NDSEM)]
                    for e in ENGS if self.ndma[e] > 0}
            cnt = {}
            for e in ENGS:
                c = 0
                m = {}
                for k in sorted(self.targets[e]):
                    c += 1
                    m[k] = c
                cnt[e] = m
            block = st.enter_context(nc.Block())

            def body(ename, eh):
                for idx, o in enumerate(self.ops[ename]):
                    for t in o['waits']:
                        if t[0] == 'e':
                            eh.wait_ge(esem[t[1]], cnt[t[1]][t[2]])
                        elif t[0] == 'c':
                            eh.wait_ge(ccsem, t[1] + 1)
                        else:
                            eh.wait_ge(dsem[t[1]][t[2] % NDSEM], 16 * (t[2] // NDSEM + 1))
                    if o['fn'] is None:
                        if idx in cnt[ename]:
                            eh.nop().then_inc(esem[ename], 1)
                        continue
                    inst = o['fn'](eh)
                    if o.get('cc'):
                        inst.then_inc(ccsem, 1)
                        if idx in cnt[ename]:
                            eh.nop().then_inc(esem[ename], 1)
                    elif o['dma'] is not None:
                        inst.then_inc(dsem[ename][o['dma'] % NDSEM], 16)
                        if idx in cnt[ename]:
                            eh.nop().then_inc(esem[ename], 1)
                    elif idx in cnt[ename]:
                        inst.then_inc(esem[ename], 1)

            @block.tensor
            def _(eh):
                body('pe', eh)

            @block.scalar
            def _(eh):
                body('act', eh)

            @block.vector
            def _(eh):
                body('dve', eh)

            @block.gpsimd
            def _(eh):
                body('pool', eh)

            @block.sync
            def _(eh):
                body('sp', eh)


class V:
    __slots__ = ('ap', 'buf')

    def __init__(self, ap, buf):
        self.ap = ap
        self.buf = buf


class Tl:
    def __init__(self, h, name=''):
        self.h = h
        self.buf = Buf(name)

    def __getitem__(self, idx):
        return V(self.h[idx], self.buf)

    def v(self, ap):
        return V(ap, self.buf)


class DT:
    def __init__(self, ap, name=''):
        self.ap = ap
        self.buf = Buf(name)

    def v(self, ap=None):
        return V(self.ap if ap is None else ap, self.buf)


class Builder:
    SB_LO = 17536
    SB_HI = 229344

    def __init__(self, nblk):
        self.nblk = nblk
        self.ntok = nblk * TB + DEC_SEQ
        self.nc = bass.Bass("TRN2", target_bir_lowering=False)
        self.S = Sched(self.nc)
        self.persist_off = self.SB_LO
        self.phase_base = None
        self.phase_off = None
        self.nps = 0
        self.din = {}
        self.dout = {}

    def _alloc(self, name, shape, dt, off):
        n = 1
        for s in shape[1:]:
            n *= s
        nbytes = n * (4 if dt == F32 else 2)
        nbytes = (nbytes + 63) // 64 * 64
        h = self.nc.alloc_sbuf_tensor_at(name, list(shape), dt, offset=off)
        return Tl(h, name), off + nbytes

    def P(self, name, shape, dt=F32):
        t, self.persist_off = self._alloc(name, shape, dt, self.persist_off)
        assert self.persist_off <= self.SB_HI, name
        return t

    def phase_begin(self, kind=None):
        if self.phase_base is None:
            self.phase_base = self.persist_off
            self.cur_kind = None
            self.kind_tiles = {}
        if kind is None or kind != self.cur_kind:
            if self.cur_kind is not None or kind is None:
                self.S.barrier()
            self.cur_kind = kind
            self.kind_tiles = {}
            self.phase_gen = getattr(self, 'phase_gen', 0) + 1
        self.phase_off = self.phase_base

    def A(self, name, shape, dt=F32):
        if self.cur_kind is not None and name in self.kind_tiles:
            return self.kind_tiles[name]
        t, self.phase_off = self._alloc(name, shape, dt, self.phase_off)
        assert self.phase_off <= self.SB_HI, (name, self.phase_off)
        if self.cur_kind is not None:
            self.kind_tiles[name] = t
        return t

    def din_t(self, name, shape, dt=F32):
        ap = self.nc.dram_tensor(name, list(shape), dt, kind="ExternalInput").ap()
        d = DT(ap, name)
        self.din[name] = d
        return d

    def dout_t(self, name, shape, dt=F32):
        ap = self.nc.dram_tensor(name, list(shape), dt, kind="ExternalOutput").ap()
        d = DT(ap, name)
        self.dout[name] = d
        return d

    def mm(self, out, lhsT, rhs, start=True, stop=True):
        self.S.op('pe', lambda e: e.matmul(out.ap, lhsT=lhsT.ap, rhs=rhs.ap, start=start, stop=stop),
                  reads=[lhsT.buf, rhs.buf], writes=[out.buf])

    def tr(self, out, in_, ident):
        self.S.op('pe', lambda e: e.transpose(out.ap, in_.ap, ident.ap),
                  reads=[in_.buf, ident.buf], writes=[out.buf])

    def act(self, out, in_, func, bias=None, scale=1.0, accum=None, eng='act'):
        reads = [in_.buf]
        kw = {}
        if bias is not None:
            if isinstance(bias, V):
                reads.append(bias.buf)
                kw['bias'] = bias.ap
            else:
                kw['bias'] = bias
        if isinstance(scale, V):
            reads.append(scale.buf)
            kw['scale'] = scale.ap
        else:
            kw['scale'] = scale
        writes = [out.buf]
        if accum is not None:
            writes.append(accum.buf)
            kw['accum_out'] = accum.ap
        self.S.op('act', lambda e: e.activation(out.ap, in_.ap, func, **kw), reads=reads, writes=writes)

    def tt(self, out, in0, in1, op, eng='dve'):
        self.S.op(eng, lambda e: e.tensor_tensor(out.ap, in0.ap, in1.ap, op),
                  reads=[in0.buf, in1.buf], writes=[out.buf])

    def ts(self, out, in0, s1, op0, s2=None, op1=None, eng='dve', accum=None):
        reads = [in0.buf]
        a1 = s1
        a2 = s2
        if isinstance(s1, V):
            reads.append(s1.buf)
            a1 = s1.ap
        if isinstance(s2, V):
            reads.append(s2.buf)
            a2 = s2.ap
        kw = {}
        writes = [out.buf]
        if accum is not None:
            kw['accum_out'] = accum.ap
            writes.append(accum.buf)
        if op1 is None:
            self.S.op(eng, lambda e: e.tensor_scalar(out.ap, in0.ap, a1, None, op0, **kw), reads=reads, writes=writes)
        else:
            self.S.op(eng, lambda e: e.tensor_scalar(out.ap, in0.ap, a1, a2, op0, op1, **kw), reads=reads, writes=writes)

    def stt(self, out, in0, sc, in1, op0, op1):
        reads = [in0.buf, in1.buf]
        a = sc
        if isinstance(sc, V):
            reads.append(sc.buf)
            a = sc.ap
        self.S.op('dve', lambda e: e.scalar_tensor_tensor(out.ap, in0.ap, a, in1.ap, op0, op1),
                  reads=reads, writes=[out.buf])

    def cp(self, out, in_, eng='dve'):
        if eng == 'act':
            self.S.op('act', lambda e: e.copy(out.ap, in_.ap), reads=[in_.buf], writes=[out.buf])
        else:
            self.S.op(eng, lambda e: e.tensor_copy(out.ap, in_.ap), reads=[in_.buf], writes=[out.buf])

    def rsum(self, out, in_):
        self.S.op('dve', lambda e: e.tensor_reduce(out.ap, in_.ap, mybir.AxisListType.X, ALU.add),
                  reads=[in_.buf], writes=[out.buf])

    def recip(self, out, in_):
        self.S.op('dve', lambda e: e.reciprocal(out.ap, in_.ap), reads=[in_.buf], writes=[out.buf])

    def scan(self, out, d0, d1, init, op0, op1):
        reads = [d0.buf, d1.buf]
        a = init
        if isinstance(init, V):
            reads.append(init.buf)
            a = init.ap
        self.S.op('dve', lambda e: e.tensor_tensor_scan(out.ap, d0.ap, d1.ap, a, op0, op1),
                  reads=reads, writes=[out.buf])

    def memset(self, out, val, eng='pool'):
        self.S.op(eng, lambda e: e.memset(out.ap, val), writes=[out.buf])

    def dma(self, out, in_, q='sp', is_output=False, nc_ok=False):
        if nc_ok:
            fn = lambda e: e.dma_start(out=out.ap, in_=in_.ap, allow_slow_non_contiguous=True)
        else:
            fn = lambda e: e.dma_start(out=out.ap, in_=in_.ap)
        self.S.op(q, fn, reads=[in_.buf], writes=[out.buf], dma=True, is_output=is_output)

    def wload(self, dst, src_dt, src_ap, key):
        if not hasattr(self, 'wcache'):
            self.wcache = {}
        if key not in self.wcache:
            self.dma(dst, V(src_ap, src_dt.buf), q='pool')
            shp = list(dst.ap.shape)
            nm = "wc_" + "_".join(str(k) for k in key)
            ap = self.nc.dram_tensor(nm, shp, BF16, kind="Internal").ap()
            d = DT(ap, nm)
            self.wcache[key] = d
            self.dma(d.v(), dst, q='sp')
        else:
            self.dma(dst, self.wcache[key].v(), q='sp')

    def wprefetch(self, shape, src_dt, src_ap, key):
        if not hasattr(self, 'wcache'):
            self.wcache = {}
        if key in self.wcache:
            return
        nm = "wc_" + "_".join(str(k) for k in key)
        ap = self.nc.dram_tensor(nm, list(shape), BF16, kind="Internal").ap()
        d = DT(ap, nm)
        self.wcache[key] = d
        self.dma(d.v(), V(src_ap, src_dt.buf), q='pool')

    def ps(self):
        t = self.psb[self.nps % 8]
        self.nps += 1
        return t

    def build(self):
        nc = self.nc
        S = self.S
        A = self.A
        P = self.P
        dma = self.dma
        DKs = 128 ** -0.5
        NOWN = 4
        NTOK = NOWN * TB + DEC_SEQ
        RG = [[0, 1]] if os.environ.get("K_RG") == "pair" else [[0, 1], [2, 3], [4, 5], [6, 7]]
        xT = self.din_t("xT", [D, NTOK])
        yT = self.dout_t("yT", [D, NTOK])
        w_gate = self.din_t("ffn_w_gate", [2, 2, D, DFF])
        w_up = self.din_t("ffn_w_up", [2, 2, D, DFF])
        w_down = self.din_t("ffn_w_down", [2, 2, DFF, D])
        g_col = self.din_t("g_col", [128, 6, 8])
        gf_col = self.din_t("gf_col", [128, 8])
        rmask_d = self.din_t("rmask", [128, 2])
        abw = self.din_t("abw", [D, 2828])
        ab_w_out = self.din_t("ab_w_out", [2048, D])
        d_bi = self.din_t("b_i", [2, 1]); d_bf = self.din_t("b_f", [2, 1])
        d_gnB = self.din_t("mlstm_norm_b", [128, 512])
        d_cw = self.din_t("ssd_cw", [128, 6, 4]); d_cb = self.din_t("ssd_cb", [128, 6])
        d_dtb = self.din_t("ssd_dtb", [8, 1]); d_alog = self.din_t("ssd_alog", [8, 1])
        d_dskB = self.din_t("ssd_d_b", [128, 8]); d_snB = self.din_t("ssd_norm_b", [128, 512])
        gw = self.din_t("gw", [D, 2056])
        gdn_w_out = self.din_t("gdn_w_out", [1024, D])
        d_gcw = self.din_t("gdn_cw", [128, 12, 4])
        d_gdtb = self.din_t("gdn_dtb", [4, 1]); d_galog = self.din_t("gdn_alog", [4, 1])
        d_gdnB = self.din_t("gdn_norm_b", [128, 128])
        s_Cx = [self.din_t("s_Cx%d" % j, [128, 2, 257]) for j in range(2)]
        s_m = [self.din_t("s_m%d" % j, [2, 1]) for j in range(2)]
        s_ST = [self.din_t("s_ST%d" % j, [128, 8, 64]) for j in range(2)]
        s_hb = [self.din_t("s_hb%d" % j, [128, 6, 3]) for j in range(2)]
        s_SG = [self.din_t("s_SG%d" % j, [128, 4, 128]) for j in range(2)]
        s_hc = [self.din_t("s_hc%d" % j, [128, 12, 3]) for j in range(2)]
        o_Cx = [self.dout_t("o_Cx%d" % j, [128, 2, 257]) for j in range(3)]
        o_m = [self.dout_t("o_m%d" % j, [2, 1]) for j in range(3)]
        o_ST = [self.dout_t("o_ST%d" % j, [128, 8, 64]) for j in range(3)]
        o_hb = [self.dout_t("o_hb%d" % j, [128, 6, 3]) for j in range(3)]
        o_SG = [self.dout_t("o_SG%d" % j, [128, 4, 128]) for j in range(3)]
        o_hc = [self.dout_t("o_hc%d" % j, [128, 12, 3]) for j in range(3)]

        def internal(name, shape):
            return DT(nc.dram_tensor(name, list(shape), BF16, kind="Internal").ap(), name)
        own = [(b * TB, TB) for b in range(NOWN)] + [(NOWN * TB, DEC_SEQ)]
        G1in = [internal("g1in%d" % b, [1024, n]) for b, (_, n) in enumerate(own)]
        G1out = [internal("g1out%d" % b, [2048, n]) for b, (_, n) in enumerate(own)]
        G3in = [internal("g3in%d" % b, [1024, n]) for b, (_, n) in enumerate(own)]
        G3out = [internal("g3out%d" % b, [2048, n]) for b, (_, n) in enumerate(own)]
        seqn = [TB] * 8 + [DEC_SEQ] * 2
        G2in = [internal("g2in%d" % k, [1024, n]) for k, n in enumerate(seqn)]
        G2out = [internal("g2out%d" % k, [2048, n]) for k, n in enumerate(seqn)]
        G4in = [internal("g4in%d" % k, [512, n]) for k, n in enumerate(seqn)]
        G4out = [internal("g4out%d" % k, [1024, n]) for k, n in enumerate(seqn)]

        def allgather(gi, go):
            S.op('pool', lambda e: e.collective_compute("AllGather", ALU.bypass, replica_groups=RG,
                                                        ins=[gi.ap], outs=[go.ap]),
                 reads=[gi.buf], writes=[go.buf], cc=True)

        self.psb = []
        self._st = ExitStack()
        for i in range(8):
            h = self._st.enter_context(nc.psum_tensor("psb%d" % i, [128, 512], F32))
            t = Tl(h, "psb%d" % i)
            t.buf.excl = True
            t.hb = h[:, :].bitcast(BF16)
            self.psb.append(t)

        def pbv(t, *idx):
            return V(t.hb[idx], t.buf)

        xb = [P("x%d" % b, [128, 8, n]) for b, (_, n) in enumerate(own)]
        hn = P("hn", [128, 8, TB], BF16)
        gcol = P("gcol", [128, 6, 8]); gfcol = P("gfcol", [128, 8]); rmask = P("rmask", [128, 2])
        ones_f = P("ones_f", [128, 128]); ones_b = P("ones_b", [128, 128], BF16)
        rstd = P("rstd", [128, TB])
        ident_f = P("ident_f", [128, 128]); ident_b = P("ident_b", [128, 128], BF16)
        maskT = P("maskT", [128, 128]); maskS = P("maskS", [128, 128])
        ones16 = P("ones16", [16, 128]); onesr = P("onesr", [16, TB]); sel16 = P("sel16", [16, 8, 128])
        bi = P("bi", [2, 1]); nbf = P("nbf", [2, 1])
        gnB = P("gnB", [128, 512]); snB = P("snB", [128, 512]); dskB = P("dskB", [128, 8])
        cw = P("cw", [128, 6, 4]); cb = P("cb", [128, 6]); dtb = P("dtb", [8, 1]); negA = P("negA", [8, 1])
        Cx = P("Cx", [128, 2, 257]); ST = P("ST", [128, 8, 64]); Sb = P("Sb", [128, 8, 64], BF16)
        hist_b = P("hist_b", [128, 6, 3])
        Fc = P("Fc", [2, 1]); Gc = P("Gc", [2, 1]); mo = P("mo", [2, 1])
        gcw = P("gcw", [128, 12, 4]); gdtb = P("gdtb", [4, 1]); gnegA = P("gnegA", [4, 1]); gdnB = P("gdnB", [128, 128])
        SG = P("SG", [128, 4, 128]); SGb = P("SGb", [128, 4, 128], BF16); hist_c = P("hist_c", [128, 12, 3])

        for (dst, src) in ((gcol, g_col), (gfcol, gf_col), (rmask, rmask_d), (bi, d_bi), (nbf, d_bf), (gnB, d_gnB),
                           (snB, d_snB), (dskB, d_dskB), (cw, d_cw), (cb, d_cb), (dtb, d_dtb), (negA, d_alog),
                           (gcw, d_gcw), (gdtb, d_gdtb), (gnegA, d_galog), (gdnB, d_gdnB)):
            dma(dst[:], src.v())
        self.memset(ones_f[:], 1.0 / D); self.memset(ones_b[:], 1.0 / D)
        self.memset(ones16[:], 1.0); self.memset(onesr[:], 1.0)
        self.memset(ident_f[:], 0.0)
        S.op('pool', lambda e: e.affine_select(out=ident_f.h[:], in_=ident_f.h[:], pattern=[[-1, 128]],
                                               compare_op=ALU.not_equal, fill=1.0, base=0, channel_multiplier=1),
             reads=[ident_f.buf], writes=[ident_f.buf])
        self.cp(ident_b[:], ident_f[:])
        self.memset(maskT[:], 1.0)
        S.op('pool', lambda e: e.affine_select(out=maskT.h[:], in_=maskT.h[:], pattern=[[1, 128]],
                                               compare_op=ALU.is_ge, fill=0.0, base=0, channel_multiplier=-1),
             reads=[maskT.buf], writes=[maskT.buf])
        self.memset(maskS[:], 1.0)
        S.op('pool', lambda e: e.affine_select(out=maskS.h[:], in_=maskS.h[:], pattern=[[1, 128]],
                                               compare_op=ALU.is_gt, fill=0.0, base=0, channel_multiplier=-1),
             reads=[maskS.buf], writes=[maskS.buf])
        self.memset(sel16[:], 0.0)
        S.op('pool', lambda e: e.affine_select(out=sel16.h[:], in_=sel16.h[:], pattern=[[-1, 8], [0, 128]],
                                               compare_op=ALU.not_equal, fill=1.0, base=0, channel_multiplier=1),
             reads=[sel16.buf], writes=[sel16.buf])
        self.ts(nbf[:], nbf[:], -1.0, ALU.mult)
        self.act(negA[:], negA[:], AF.Exp); self.ts(negA[:], negA[:], -1.0, ALU.mult)
        self.act(gnegA[:], gnegA[:], AF.Exp); self.ts(gnegA[:], gnegA[:], -1.0, ALU.mult)

        def rmsnorm(n, gv, out_t, sq, x):
            self.act(sq[:, :, 0:n], x[:, :, 0:n], AF.Square)
            p = self.ps()
            for kc in range(8):
                self.mm(p[:, 0:n], ones_b[:], sq[:, kc, 0:n], start=(kc == 0), stop=(kc == 7))
            self.act(rstd[:, 0:n], p[:, 0:n], AF.Sqrt, bias=EPS)
            self.recip(rstd[:, 0:n], rstd[:, 0:n])
            for kc in range(8):
                self.stt(out_t[:, kc, 0:n], x[:, kc, 0:n], gv(kc), rstd[:, 0:n], ALU.mult, ALU.mult)

        def ffn(l, i, parts):
            self.phase_begin(('ffn',))
            sq = A("sq", [128, 8, TB], BF16)
            act_t = A("ffn_act", [128, NFF, TB], BF16)
            act_s = A("ffn_act_s", [128, NFF, DEC_SEQ], BF16)
            hn_s = A("hn_s", [128, 8, DEC_SEQ], BF16)
            wd = A("ffn_wd", [128, NFF, D], BF16)
            wgu = [[A("ffn_wg%d" % b, [128, 8, 256], BF16), A("ffn_wu%d" % b, [128, 8, 256], BF16)]
                   for b in range(3)]
            sil = [A("ffn_sil0", [128, TB])] * 2
            hns = [hn, hn_s]
            acts = [act_t, act_s]
            for pi, (x, n) in enumerate(parts):
                rmsnorm(n, lambda kc: gcol[:, l * 3 + 2 * i, kc:kc + 1], hns[pi], sq, x)
            wg_src = w_gate.ap[l, i].rearrange("(k p) f -> p k f", p=128)
            wu_src = w_up.ap[l, i].rearrange("(k p) f -> p k f", p=128)
            wd_src = w_down.ap[l, i].rearrange("(f p) d -> p f d", p=128)
            groups = [(g * 256, 256) for g in range(11)]
            for gi, (c0, ncol) in enumerate(groups):
                b = gi % 3
                self.wload(wgu[b][0][:, :, 0:ncol], w_gate, wg_src[:, :, c0:c0 + ncol], ('wg', l, i, gi))
                self.wload(wgu[b][1][:, :, 0:ncol], w_up, wu_src[:, :, c0:c0 + ncol], ('wu', l, i, gi))
                if gi == 2 and getattr(self, '_wd_res', None) != (l, i, self.phase_gen):
                    for q in range(2):
                        self.wload(wd[:, q * 11:(q + 1) * 11, :], w_down, wd_src[:, q * 11:(q + 1) * 11, :], ('wd', l, i, q))
                    self._wd_res = (l, i, self.phase_gen)
                for j in range(ncol // 128):
                    f = c0 // 128 + j
                    for pi, (x, n) in enumerate(parts):
                        pg = self.ps()
                        pu = self.ps()
                        for kc in range(8):
                            self.mm(pg[:, 0:n], wgu[b][0][:, kc, j * 128:(j + 1) * 128], hns[pi][:, kc, 0:n],
                                    start=(kc == 0), stop=(kc == 7))
                        for kc in range(8):
                            self.mm(pu[:, 0:n], wgu[b][1][:, kc, j * 128:(j + 1) * 128], hns[pi][:, kc, 0:n],
                                    start=(kc == 0), stop=(kc == 7))
                        sl = sil[f % 2]
                        self.act(sl[:, 0:n], pg[:, 0:n], AF.Silu)
                        self.tt(acts[pi][:, f, 0:n], sl[:, 0:n], pu[:, 0:n], ALU.mult)
            for dc in range(8):
                for pi, (x, n) in enumerate(parts):
                    p = self.ps()
                    for f in range(NFF):
                        self.mm(p[:, 0:n], wd[:, f, dc * 128:(dc + 1) * 128], acts[pi][:, f, 0:n],
                                start=(f == 0), stop=(f == NFF - 1))
                    self.stt(x[:, dc, 0:n], p[:, 0:n], 0.5, x[:, dc, 0:n], ALU.mult, ALU.add)

        def hn_exchange(gidx, n, x, gin, gout):
            self.phase_begin(('ffn',))
            sq = A("sq", [128, 8, TB], BF16)
            rmsnorm(n, lambda kc: gcol[:, gidx, kc:kc + 1], hn, sq, x)
            dma(gin.v(gin.ap.rearrange("(k p) t -> p k t", p=128)), hn[:, :, 0:n])
            allgather(gin, gout)

        def out_proj_sel(w_dt, nfc, gouts, n, x, tag):
            self.phase_begin(('ops', nfc))
            wb = [A("wb0", [128, 8, 512], BF16), A("wb1", [128, 8, 512], BF16)]
            cand = [A("cand0", [128, nfc, TB], BF16), A("cand1", [128, nfc, TB], BF16)]
            hT = A("hTf", [128, nfc, TB], BF16)
            half = nfc // 2 * 128
            for ci, go in enumerate(gouts):
                for r in range(2):
                    src = go.ap[r * half:(r + 1) * half, :].rearrange("(f p) t -> p f t", p=128)
                    if nfc == 16:
                        dma(cand[ci][:, r * 4:r * 4 + 4, 0:n], go.v(src[:, 0:4, :]))
                        dma(cand[ci][:, 8 + r * 4:8 + r * 4 + 4, 0:n], go.v(src[:, 4:8, :]))
                    else:
                        dma(cand[ci][:, r * 4:r * 4 + 4, 0:n], go.v(src[:, 0:4, :]))
            self.ts(hT[:, :, 0:n], cand[0][:, :, 0:n], rmask[:, 0:1], ALU.mult)
            self.stt(hT[:, :, 0:n], cand[1][:, :, 0:n], rmask[:, 1:2], hT[:, :, 0:n], ALU.mult, ALU.add)
            wo_src = w_dt.ap.rearrange("(f p) d -> p f d", p=128)
            nfh = nfc // 8
            li = 0
            for dh in range(2):
                pacc = [self.ps() for _ in range(4)]
                for fh in range(nfh):
                    w = wb[li % 2]
                    li += 1
                    self.wload(w[:], w_dt, wo_src[:, fh * 8:(fh + 1) * 8, dh * 512:(dh + 1) * 512], ('wo', nfc, dh, fh))
                    for j in range(4):
                        for f8 in range(8):
                            self.mm(pacc[j][:, 0:n], w[:, f8, j * 128:(j + 1) * 128], hT[:, fh * 8 + f8, 0:n],
                                    start=(fh == 0 and f8 == 0), stop=(fh == nfh - 1 and f8 == 7))
                for j in range(4):
                    dc = dh * 4 + j
                    self.tt(x[:, dc, 0:n], pacc[j][:, 0:n], x[:, dc, 0:n], ALU.add)

        def conv_chunk(p, n, fc, hist, cwt, cbt, stage, cacc, outT):
            stg = stage
            acc = cacc
            self.cp(stg[:, 0:3], hist[:, fc, :])
            self.cp(stg[:, 3:3 + n], p[:, 0:n], eng='act')
            self.ts(acc[:, 0:n], stg[:, 0:n], cwt[:, fc, 0:1], ALU.mult)
            for j in range(1, 4):
                self.stt(acc[:, 0:n], stg[:, j:j + n], cwt[:, fc, j:j + 1], acc[:, 0:n], ALU.mult, ALU.add)
            if cbt is not None:
                self.act(outT[:, fc, 0:n], acc[:, 0:n], AF.Silu, bias=cbt[:, fc:fc + 1])
            else:
                self.act(outT[:, fc, 0:n], acc[:, 0:n], AF.Silu)
            self.cp(hist[:, fc, :], stg[:, n:n + 3])

        def mixer_ab(n, L, g1, r, g2in, g2out):
            nch = n // L
            NM = 2
            self.phase_begin(('ab', n))
            hT = A("hT", [128, 8, n], BF16)
            wb = [A("wb0", [128, 8, 512], BF16), A("wb1", [128, 8, 512], BF16)]
            wgt = A("wgt", [128, 8, 4], BF16); wdt = A("wdt", [128, 8, 8], BF16)
            qT = A("qT", [128, NM, n], BF16); kT = A("kT", [128, NM, n], BF16)
            k_tok = A("k_tok", [128, nch, 256], BF16)
            v_ext = A("v_ext", [128, nch, NM, 257], BF16)
            so = A("so", [128, nch, 512], BF16); zs = A("zs", [128, nch, 512], BF16)
            xbcT = A("xbcT", [128, 6, n], BF16)
            x_tok = A("x_tok", [128, nch, 512], BF16); bm_tok = A("bm_tok", [128, nch, 128], BF16)
            h_tok = A("h_tok", [128, 1024], BF16)
            stage = A("stage0", [128, 3 + n]); cacc = A("cacc0", [128, n])
            R = [A("row%d" % i, [16, n]) for i in range(10)]
            gT = A("gT", [128, nch, 6]); gS = A("gS", [128, nch, 32])
            decB = A("decB", [128, nch, NM]); decS = A("decS", [128, nch, 8])
            D4 = A("D4", [NM, nch, NM]); D16 = A("D16", [8, nch, 8]); Gpv = A("Gpv", [NM, nch]); dec4 = A("dec4", [NM, nch])
            PTm = A("PTm", [128, 128], BF16)
            vu = [A("vu0", [128, 257], BF16), A("vu1", [128, 257], BF16)]
            Cb = A("Cb", [128, NM, 257], BF16)
            cbm = A("cbm", [128, 128])
            seg = [A("seg0", [128, 128]), A("seg1", [128, 128])]
            MT = A("MT", [128, 8, 128], BF16)
            xd = A("xd", [128, 512], BF16); xw = A("xw", [128, 512], BF16)
            ya = A("ya", [128, 512]); yb = A("yb", [128, 512]); hraw = A("hraw", [128, 256]); junk = yb
            c1 = A("c1", [128, 1]); c2 = A("c2", [128, 1]); c3 = A("c3", [128, 1]); c4 = A("c4", [128, 1])

            dma(hn[:, :, 0:n], g1.v(g1.ap[r * 1024:(r + 1) * 1024, :].rearrange("(k p) t -> p k t", p=128)))
            win = abw.ap.rearrange("(k p) c -> p k c", p=128)
            lw = [0]

            def loadw(c0, ncol):
                w = wb[lw[0] % 2]
                lw[0] += 1
                self.wload(w[:, :, 0:ncol], abw, win[:, :, c0:c0 + ncol], ('abin', c0))
                return w

            def fm_proj(w, j, M=128):
                p = self.ps()
                for kc in range(8):
                    self.mm(p[0:M, 0:n], w[:, kc, j * 128:j * 128 + M], hn[:, kc, 0:n], start=(kc == 0), stop=(kc == 7))
                return p

            def tm_proj(w, c, c0=0, ncol=512):
                p = self.ps()
                for kc in range(8):
                    self.mm(p[0:L, 0:ncol], hn[:, kc, c * L:(c + 1) * L], w[:, kc, c0:c0 + ncol], start=(kc == 0), stop=(kc == 7))
                return p

            self.wload(wgt[:], abw, win[:, :, 1536:1540], ('abg',))
            self.wload(wdt[:], abw, win[:, :, 2820:2828], ('abdt',))
            w = loadw(0, 512)
            for h in range(NM):
                p = fm_proj(w, h)
                self.cp(qT[:, h, 0:n], p[:, 0:n], eng='act')
            for h in range(NM):
                p = fm_proj(w, NM + h)
                self.ts(kT[:, h, 0:n], p[:, 0:n], DKs, ALU.mult)
            for c in range(nch):
                p = tm_proj(w, c, 256, 256)
                self.ts(k_tok[0:L, c, :], p[0:L, 0:256], DKs, ALU.mult)
            self.memset(v_ext[:, :, :, 256:257], 1.0)
            w = loadw(512, 512)
            for c in range(nch):
                p = tm_proj(w, c)
                self.cp(v_ext[0:L, c, 0:NM, 0:256], V(p.h[0:L, 0:512].rearrange("p (a b) -> p a b", b=256), p.buf), eng='act')
            w = loadw(1024, 512)
            for c in range(nch):
                p = tm_proj(w, c)
                self.act(so[0:L, c, :], p[0:L, 0:512], AF.Sigmoid)
            w = loadw(1540, 512)
            for c in range(nch):
                p = tm_proj(w, c)
                self.act(zs[0:L, c, :], p[0:L, 0:512], AF.Silu)
            w = loadw(2052, 512)
            for j in range(4):
                p = fm_proj(w, j)
                conv_chunk(p, n, j, hist_b, cw, cb, stage, cacc, xbcT)
            w = loadw(2564, 256)
            for j in range(2):
                p = fm_proj(w, j)
                conv_chunk(p, n, 4 + j, hist_b, cw, cb, stage, cacc, xbcT)
            t1, Fn, a_, G_, em, u_, w_, tmp = R[0], R[1], R[2], R[3], R[4], R[5], R[6], R[7]
            pig = self.ps()
            for kc in range(8):
                self.mm(pig[0:NM, 0:n], wgt[:, kc, 0:NM], hn[:, kc, 0:n], start=(kc == 0), stop=(kc == 7))
            pfg = self.ps()
            for kc in range(8):
                self.mm(pfg[0:NM, 0:n], wgt[:, kc, NM:2 * NM], hn[:, kc, 0:n], start=(kc == 0), stop=(kc == 7))
            self.act(t1[0:NM, 0:n], pfg[0:NM, 0:n], AF.Exp, bias=nbf[:], scale=-1.0)
            self.act(t1[0:NM, 0:n], t1[0:NM, 0:n], AF.Ln, bias=1.0)
            self.scan(Fn[0:NM, 0:n], onesr[0:NM, 0:n], t1[0:NM, 0:n], Fc[:], ALU.mult, ALU.add)
            self.stt(a_[0:NM, 0:n], pig[0:NM, 0:n], bi[:], Fn[0:NM, 0:n], ALU.add, ALU.add)
            self.scan(G_[0:NM, 0:n], onesr[0:NM, 0:n], a_[0:NM, 0:n], Gc[:], ALU.mult, ALU.max)
            self.tt(tmp[0:NM, 0:n], Fn[0:NM, 0:n], G_[0:NM, 0:n], ALU.subtract)
            self.act(em[0:NM, 0:n], tmp[0:NM, 0:n], AF.Exp)

            def r3(t, np_):
                return t.h[0:np_, 0:n].rearrange("p (c l) -> p c l", l=L)
            gend = V(r3(G_, NM)[:, :, L - 1:L].to_broadcast([NM, nch, L]), G_.buf)
            self.tt(V(r3(tmp, NM), tmp.buf), V(r3(a_, NM), a_.buf), gend, ALU.subtract)
            self.act(u_[0:NM, 0:n], tmp[0:NM, 0:n], AF.Exp)
            self.tt(V(r3(tmp, NM), tmp.buf), V(r3(G_, NM), G_.buf), gend, ALU.subtract)
            self.act(w_[0:NM, 0:n], tmp[0:NM, 0:n], AF.Exp, scale=-1.0)
            self.cp(Gpv[0:NM, 0:1], Gc[:])
            if nch > 1:
                self.cp(V(Gpv.h[0:NM, 1:nch].unsqueeze(2), Gpv.buf), V(r3(G_, NM)[:, 0:nch - 1, L - 1:L], G_.buf))
            self.tt(V(dec4.h[0:NM, 0:nch].unsqueeze(2), dec4.buf), V(Gpv.h[0:NM, 0:nch].unsqueeze(2), Gpv.buf),
                    V(r3(G_, NM)[:, :, L - 1:L], G_.buf), ALU.subtract)
            self.act(dec4[0:NM, 0:nch], dec4[0:NM, 0:nch], AF.Exp)
            self.tt(D4[0:NM, 0:nch, :], V(dec4.h[0:NM, 0:nch].unsqueeze(2).to_broadcast([NM, nch, NM]), dec4.buf),
                    V(ident_f.h[0:NM, 0:NM].unsqueeze(1).to_broadcast([NM, nch, NM]), ident_f.buf), ALU.mult)
            p = self.ps()
            self.mm(p[:, 0:nch * NM], ones16[0:NM, :], V(D4.h[0:NM, 0:nch, :].rearrange("p c h -> p (c h)"), D4.buf))
            self.cp(V(decB.h[:, 0:nch, :].rearrange("p c h -> p (c h)"), decB.buf), p[:, 0:nch * NM])
            self.cp(Fc[:], Fn[0:NM, n - 1:n])
            self.cp(Gc[:], G_[0:NM, n - 1:n])
            p = self.ps()
            for c in range(nch):
                for qi, rt in enumerate((u_, w_, em)):
                    self.tr(p[0:L, c * 6 + qi * NM:c * 6 + qi * NM + NM], rt[0:NM, c * L:(c + 1) * L], ident_f[0:NM, 0:NM])
            self.cp(V(gT.h[0:L, 0:nch, :].rearrange("p c h -> p (c h)"), gT.buf), p[0:L, 0:nch * 6])
            dt_, ar, b_, eb, nb, e2 = R[0], R[1], R[2], R[8], R[9], R[7]
            pdt = self.ps()
            for kc in range(8):
                self.mm(pdt[0:8, 0:n], wdt[:, kc, :], hn[:, kc, 0:n], start=(kc == 0), stop=(kc == 7))
            self.act(dt_[0:8, 0:n], pdt[0:8, 0:n], AF.Exp, bias=dtb[:])
            self.act(dt_[0:8, 0:n], dt_[0:8, 0:n], AF.Ln, bias=1.0)
            self.ts(ar[0:8, 0:n], dt_[0:8, 0:n], negA[:], ALU.mult)
            for c in range(nch):
                self.scan(b_[0:8, c * L:(c + 1) * L], onesr[0:8, 0:L], ar[0:8, c * L:(c + 1) * L], 0.0, ALU.mult, ALU.add)
            self.act(eb[0:8, 0:n], b_[0:8, 0:n], AF.Exp)
            self.ts(nb[0:8, 0:n], b_[0:8, 0:n], -1.0, ALU.mult)
            bLb = V(r3(b_, 8)[:, :, L - 1:L].to_broadcast([8, nch, L]), b_.buf)
            self.tt(V(r3(e2, 8), e2.buf), bLb, V(r3(b_, 8), b_.buf), ALU.subtract)
            self.act(e2[0:8, 0:n], e2[0:8, 0:n], AF.Exp)
            self.tt(e2[0:8, 0:n], e2[0:8, 0:n], dt_[0:8, 0:n], ALU.mult)
            self.tt(D16[:, 0:nch, :], V(r3(eb, 8)[:, :, L - 1:L].to_broadcast([8, nch, 8]), eb.buf),
                    V(ident_f.h[0:8, 0:8].unsqueeze(1).to_broadcast([8, nch, 8]), ident_f.buf), ALU.mult)
            p = self.ps()
            self.mm(p[:, 0:nch * 8], ones16[0:8, :], V(D16.h[:, 0:nch, :].rearrange("p c h -> p (c h)"), D16.buf))
            self.cp(V(decS.h[:, 0:nch, :].rearrange("p c h -> p (c h)"), decS.buf), p[:, 0:nch * 8])
            p = self.ps()
            for c in range(nch):
                for qi, rt in enumerate((dt_, nb, eb, e2)):
                    self.tr(p[0:L, c * 32 + qi * 8:c * 32 + qi * 8 + 8], rt[0:8, c * L:(c + 1) * L], ident_f[0:8, 0:8])
            self.cp(V(gS.h[0:L, 0:nch, :].rearrange("p c h -> p (c h)"), gS.buf), p[0:L, 0:nch * 32])
            for c in range(nch):
                p = self.ps()
                for fc in range(4):
                    self.tr(pbv(p, slice(0, L), slice(fc * 128, (fc + 1) * 128)), xbcT[:, fc, c * L:(c + 1) * L], ident_b[:])
                self.tr(pbv(p, slice(0, L), slice(512, 640)), xbcT[:, 4, c * L:(c + 1) * L], ident_b[:])
                self.cp(x_tok[0:L, c, :], pbv(p, slice(0, L), slice(0, 512)), eng='act')
                self.cp(bm_tok[0:L, c, :], pbv(p, slice(0, L), slice(512, 640)))
            junkm = [A("junkm%d" % i, [128, 256]) for i in range(NM)]
            PTms = [A("PTm%d" % i, [128, 128], BF16) for i in range(NM)]
            hraws = [A("hraw%d" % i, [128, 256]) for i in range(NM)]
            ccols = [[A("cm%d_%d" % (i, j), [128, 1]) for j in range(3)] for i in range(NM)]

            def m_chain(c, h):
                cs, ce = c * L, (c + 1) * L
                ht = h_tok
                PTm = PTms[h]; junk = junkm[h]; hraw = hraws[h]; c1, c2, c3 = ccols[h]
                v_u = vu[h % 2]
                p1 = self.psb[2 * h]
                self.mm(p1[0:L, 0:L], kT[:, h, cs:ce], qT[:, h, cs:ce])
                yield
                self.tt(PTm[0:L, 0:L], p1[0:L, 0:L], maskT[0:L, 0:L], ALU.mult)
                yield
                self.act(v_u[0:L, :], v_ext[0:L, c, h, :], AF.Copy, scale=gT[0:L, c, h:h + 1])
                yield
                self.ts(Cx[:, h, :], Cx[:, h, :], decB[:, c, h:h + 1], ALU.mult)
                yield
                self.cp(Cb[:, h, :], Cx[:, h, :], eng='act')
                yield
                p2 = self.psb[2 * h + 1]
                self.mm(p2[0:L, 0:257], PTm[0:L, 0:L], v_u[0:L, :], start=True, stop=False)
                yield
                self.mm(p2[0:L, 0:257], qT[:, h, cs:ce], Cb[:, h, :], start=False, stop=True)
                yield
                p3 = self.psb[2 * h]
                self.mm(p3[:, 0:257], k_tok[0:L, c, h * 128:(h + 1) * 128], v_u[0:L, :])
                yield
                self.tt(Cx[:, h, :], p3[:, 0:257], Cx[:, h, :], ALU.add)
                yield
                wcol = gT[0:L, c, NM + h:NM + h + 1]
                emcol = gT[0:L, c, 2 * NM + h:2 * NM + h + 1]
                self.act(c1[0:L, :], p2[0:L, 256:257], AF.Abs, scale=wcol)
                yield
                self.tt(c1[0:L, :], c1[0:L, :], emcol, ALU.max)
                yield
                self.recip(c1[0:L, :], c1[0:L, :])
                yield
                self.tt(c2[0:L, :], c1[0:L, :], wcol, ALU.mult)
                yield
                self.act(junk[0:L, 0:256], p2[0:L, 0:256], AF.Square, scale=c2[0:L, :])
                yield
                self.rsum(c3[0:L, :], junk[0:L, 0:256])
                yield
                self.act(hraw[0:L, :], p2[0:L, 0:256], AF.Copy, scale=c2[0:L, :])
                yield
                self.act(c3[0:L, :], c3[0:L, :], AF.Sqrt, bias=EPS, scale=1.0 / 256)
                yield
                self.recip(c3[0:L, :], c3[0:L, :])
                yield
                self.stt(hraw[0:L, :], hraw[0:L, :], c3[0:L, :], gnB[0:L, h * 256:(h + 1) * 256], ALU.mult, ALU.mult)
                yield
                self.tt(ht[0:L, h * 256:(h + 1) * 256], hraw[0:L, :], so[0:L, c, h * 256:(h + 1) * 256], ALU.mult)
                yield

            def s_chain(c):
                cs, ce = c * L, (c + 1) * L
                ht = h_tok
                junk = yb
                p1 = self.psb[4]
                self.mm(p1[0:L, 0:L], xbcT[:, 4, cs:ce], xbcT[:, 5, cs:ce])
                yield
                self.tt(cbm[0:L, 0:L], p1[0:L, 0:L], maskT[0:L, 0:L], ALU.mult)
                yield
                pbb = None
                for hh in range(8):
                    j = hh % 4
                    if j == 0:
                        pbb = self.psb[5 + hh // 4]
                    self.mm(pbb[0:L, j * 128:j * 128 + L], sel16[0:8, hh, 0:L], b_[0:8, cs:ce])
                    sg = seg[hh % 2]
                    self.ts(sg[0:L, 0:L], pbb[0:L, j * 128:j * 128 + L], gS[0:L, c, 8 + hh:9 + hh], ALU.add, 0.0, ALU.min)
                    self.act(sg[0:L, 0:L], sg[0:L, 0:L], AF.Exp)
                    self.tt(MT[0:L, hh, 0:L], sg[0:L, 0:L], cbm[0:L, 0:L], ALU.mult)

                def v3(t, ap):
                    return V(ap.rearrange("p (h e) -> p h e", e=64), t.buf)
                xg = x_tok.h[0:L, c, :]
                self.tt(v3(xd, xd.h[0:L, :]), v3(x_tok, xg),
                        V(gS.h[0:L, c, 0:8].unsqueeze(2).to_broadcast([L, 8, 64]), gS.buf), ALU.mult)
                self.tt(v3(xw, xw.h[0:L, :]), v3(x_tok, xg),
                        V(gS.h[0:L, c, 24:32].unsqueeze(2).to_broadcast([L, 8, 64]), gS.buf), ALU.mult)
                pY1 = self.psb[7]
                for hh in range(8):
                    self.mm(pY1[0:L, hh * 64:(hh + 1) * 64], MT[0:L, hh, 0:L], xd[0:L, hh * 64:(hh + 1) * 64])
                pY2 = self.psb[4]
                self.mm(pY2[0:L, 0:512], xbcT[:, 5, cs:ce], V(Sb.h[:, :, :].rearrange("p h e -> p (h e)"), Sb.buf))
                yield
                self.tt(v3(ya, ya.h[0:L, :]), v3(pY2, pY2.h[0:L, 0:512]),
                        V(gS.h[0:L, c, 16:24].unsqueeze(2).to_broadcast([L, 8, 64]), gS.buf), ALU.mult)
                self.tt(ya[0:L, :], pY1[0:L, 0:512], ya[0:L, :], ALU.add)
                yield
                self.tt(v3(yb, yb.h[0:L, :]), v3(x_tok, xg),
                        V(dskB.h[0:L, 0:8].unsqueeze(2).to_broadcast([L, 8, 64]), dskB.buf), ALU.mult)
                self.tt(ya[0:L, :], ya[0:L, :], yb[0:L, :], ALU.add)
                yield
                self.tt(ya[0:L, :], ya[0:L, :], zs[0:L, c, :], ALU.mult)
                yield
                self.act(junk[0:L, :], ya[0:L, :], AF.Square)
                yield
                self.rsum(c4[0:L, :], junk[0:L, :])
                yield
                self.act(c4[0:L, :], c4[0:L, :], AF.Sqrt, bias=EPS, scale=1.0 / 512)
                yield
                self.recip(c4[0:L, :], c4[0:L, :])
                yield
                self.stt(ht[0:L, 512:1024], ya[0:L, :], c4[0:L, :], snB[0:L, :], ALU.mult, ALU.mult)
                yield
                pS = self.psb[5]
                self.mm(pS[:, 0:512], bm_tok[0:L, c, :], xw[0:L, :])
                yield
                self.tt(ST[:, :, :], ST[:, :, :],
                        V(decS.h[:, c, 0:8].unsqueeze(2).to_broadcast([128, 8, 64]), decS.buf), ALU.mult)
                stf = V(ST.h[:, :, :].rearrange("p h e -> p (h e)"), ST.buf)
                self.tt(stf, pS[:, 0:512], stf, ALU.add)
                yield
                self.cp(Sb[:, :, :], ST[:, :, :], eng='act')
                yield

            def interleave(gens):
                gens = list(gens)
                while gens:
                    for g in list(gens):
                        try:
                            next(g)
                        except StopIteration:
                            gens.remove(g)

            for c in range(nch):
                cs, ce = c * L, (c + 1) * L
                ht = h_tok
                interleave([m_chain(c, h) for h in range(NM)] + [s_chain(c)])
                p = self.ps()
                for f8 in range(8):
                    self.tr(pbv(p, slice(0, 128), slice(f8 * 128, f8 * 128 + L)), ht[0:L, f8 * 128:(f8 + 1) * 128],
                            ident_b[0:L, 0:L])
                self.cp(hT[:, 0:8, cs:ce], V(p.hb[:, 0:1024].rearrange("p (f t) -> p f t", t=128)[:, :, 0:L], p.buf), eng='act')
            dma(g2in.v(g2in.ap.rearrange("(f p) t -> p f t", p=128)), hT[:, :, 0:n])
            allgather(g2in, g2out)

        def mixer_c(n, L, g3, r, g4in, g4out):
            nch = n // L
            nsq = {128: 6, 16: 3}[L]
            NH = 4
            self.phase_begin(('c', n))
            hT = A("hT", [128, NH, n], BF16)
            wb = [A("wb0", [128, 8, 512], BF16), A("wb1", [128, 8, 512], BF16)]
            wba = A("wba", [128, 8, 8], BF16)
            qkvT = A("qkvT", [128, 12, n], BF16)
            zs = A("zs", [128, nch, 512], BF16)
            k_tok = A("k_tok", [128, nch, 512], BF16); v_tok = A("v_tok", [128, nch, 512], BF16)
            h_tok = A("h_tok", [128, 512], BF16)
            stage = A("stage0", [128, 3 + n]); cacc = A("cacc0", [128, n])
            R = [A("row%d" % i, [16, n]) for i in range(7)]
            gC = A("gC", [128, nch, 24])
            decC = A("decC", [128, nch, NH]); D8 = A("D8", [NH, nch, NH])
            sqh = A("sqh0", [128, n], BF16); rst = A("rst0", [128, n])
            Xs = [A("X%d" % i, [128, 128]) for i in range(NH)]
            XTs = [A("XT%d" % i, [128, 128]) for i in range(NH)]
            TTs = [A("TT%d" % i, [128, 128]) for i in range(NH)]
            KKs = [A("KK%d" % i, [128, 128]) for i in range(NH)]
            dTs_ = [A("dT%d" % i, [128, 128]) for i in range(NH)]
            tmpm = [A("tmpm%d" % i, [128, 128]) for i in range(NH)]
            R1 = tmpm
            R2 = KKs
            U0s = [[A("U0s_%d_%d" % (c, h), [128, 128]) for h in range(NH)] for c in range(min(2, nch))]
            WTb = [[A("WTb_%d_%d" % (c, h), [128, 128], BF16) for h in range(NH)] for c in range(min(2, nch))]
            QKd = [[A("QKd_%d_%d" % (c, h), [128, 128], BF16) for h in range(NH)] for c in range(min(2, nch))]
            ub = [A("ub%d" % i, [128, 128], BF16) for i in range(NH)]
            kw = [A("kw%d" % i, [128, 128], BF16) for i in range(NH)]
            o1s = [A("o1s%d" % i, [128, 128]) for i in range(NH)]
            oo = [A("oo%d" % i, [128, 128]) for i in range(NH)]
            jk = o1s
            cc = [A("cc%d" % i, [128, 1]) for i in range(NH)]

            dma(hn[:, :, 0:n], g3.v(g3.ap[r * 1024:(r + 1) * 1024, :].rearrange("(k p) t -> p k t", p=128)))
            win = gw.ap.rearrange("(k p) c -> p k c", p=128)
            lw = [0]

            def loadw(c0, ncol):
                w = wb[lw[0] % 2]
                lw[0] += 1
                self.wload(w[:, :, 0:ncol], gw, win[:, :, c0:c0 + ncol], ('gin', c0))
                return w

            self.wload(wba[:], gw, win[:, :, 2048:2056], ('gba',))
            for g3_ in range(3):
                w = loadw(g3_ * 512, 512)
                for j in range(4):
                    fc = g3_ * 4 + j
                    p = self.ps()
                    for kc in range(8):
                        self.mm(p[:, 0:n], w[:, kc, j * 128:(j + 1) * 128], hn[:, kc, 0:n], start=(kc == 0), stop=(kc == 7))
                    conv_chunk(p, n, fc, hist_c, gcw, None, stage, cacc, qkvT)
            w = loadw(1536, 512)
            for c in range(nch):
                p = self.ps()
                for kc in range(8):
                    self.mm(p[0:L, 0:512], hn[:, kc, c * L:(c + 1) * L], w[:, kc, 0:512], start=(kc == 0), stop=(kc == 7))
                self.act(zs[0:L, c, :], p[0:L, 0:512], AF.Silu)
            for fc in range(2 * NH):
                self.act(sqh[:, 0:n], qkvT[:, fc, 0:n], AF.Square)
                p = self.ps()
                self.mm(p[:, 0:n], ones_b[:], sqh[:, 0:n])
                self.act(rst[:, 0:n], p[:, 0:n], AF.Sqrt, bias=EPS, scale=float(D))
                self.recip(rst[:, 0:n], rst[:, 0:n])
                if fc < NH:
                    self.stt(qkvT[:, fc, 0:n], qkvT[:, fc, 0:n], DKs, rst[:, 0:n], ALU.mult, ALU.mult)
                else:
                    self.tt(qkvT[:, fc, 0:n], qkvT[:, fc, 0:n], rst[:, 0:n], ALU.mult)
            beta, nbeta, sp_, gam, egam, ngam, e2 = R
            g_ = sp_
            pb_ = self.ps()
            for kc in range(8):
                self.mm(pb_[0:NH, 0:n], wba[:, kc, 0:NH], hn[:, kc, 0:n], start=(kc == 0), stop=(kc == 7))
            pa_ = self.ps()
            for kc in range(8):
                self.mm(pa_[0:NH, 0:n], wba[:, kc, NH:2 * NH], hn[:, kc, 0:n], start=(kc == 0), stop=(kc == 7))
            self.act(beta[0:NH, 0:n], pb_[0:NH, 0:n], AF.Sigmoid)
            self.ts(nbeta[0:NH, 0:n], beta[0:NH, 0:n], -1.0, ALU.mult)
            self.act(sp_[0:NH, 0:n], pa_[0:NH, 0:n], AF.Exp, bias=gdtb[:])
            self.act(sp_[0:NH, 0:n], sp_[0:NH, 0:n], AF.Ln, bias=1.0)
            self.ts(g_[0:NH, 0:n], sp_[0:NH, 0:n], gnegA[:], ALU.mult)
            for c in range(nch):
                self.scan(gam[0:NH, c * L:(c + 1) * L], onesr[0:NH, 0:L], g_[0:NH, c * L:(c + 1) * L], 0.0, ALU.mult, ALU.add)
            self.act(egam[0:NH, 0:n], gam[0:NH, 0:n], AF.Exp)
            self.ts(ngam[0:NH, 0:n], gam[0:NH, 0:n], -1.0, ALU.mult)

            def r3(t, np_):
                return t.h[0:np_, 0:n].rearrange("p (c l) -> p c l", l=L)
            gLb = V(r3(gam, NH)[:, :, L - 1:L].to_broadcast([NH, nch, L]), gam.buf)
            self.tt(V(r3(e2, NH), e2.buf), gLb, V(r3(gam, NH), gam.buf), ALU.subtract)
            self.act(e2[0:NH, 0:n], e2[0:NH, 0:n], AF.Exp)
            self.tt(sp_[0:NH, 0:n], beta[0:NH, 0:n], egam[0:NH, 0:n], ALU.mult)
            self.tt(D8[:, 0:nch, :], V(r3(egam, NH)[:, :, L - 1:L].to_broadcast([NH, nch, NH]), egam.buf),
                    V(ident_f.h[0:NH, 0:NH].unsqueeze(1).to_broadcast([NH, nch, NH]), ident_f.buf), ALU.mult)
            p = self.ps()
            self.mm(p[:, 0:nch * NH], ones16[0:NH, :], V(D8.h[:, 0:nch, :].rearrange("p c h -> p (c h)"), D8.buf))
            self.cp(V(decC.h[:, 0:nch, :].rearrange("p c h -> p (c h)"), decC.buf), p[:, 0:nch * NH])
            p = self.ps()
            for c in range(nch):
                for qi, rt in enumerate((beta, ngam, egam, e2, sp_, nbeta)):
                    self.tr(p[0:L, c * 24 + qi * NH:c * 24 + qi * NH + NH], rt[0:NH, c * L:(c + 1) * L], ident_f[0:NH, 0:NH])
            self.cp(V(gC.h[0:L, 0:nch, :].rearrange("p c h -> p (c h)"), gC.buf), p[0:L, 0:nch * 24])
            for c in range(nch):
                p = self.ps()
                for fc in range(NH):
                    self.tr(pbv(p, slice(0, L), slice(fc * 128, (fc + 1) * 128)), qkvT[:, NH + fc, c * L:(c + 1) * L], ident_b[:])
                    self.tr(pbv(p, slice(0, L), slice(512 + fc * 128, 512 + (fc + 1) * 128)), qkvT[:, 2 * NH + fc, c * L:(c + 1) * L], ident_b[:])
                self.cp(k_tok[0:L, c, :], pbv(p, slice(0, L), slice(0, 512)), eng='act')
                self.cp(v_tok[0:L, c, :], pbv(p, slice(0, L), slice(512, 1024)))

            def phaseA(c):
                cs, ce = c * L, (c + 1) * L
                hs = list(range(NH))
                for i, h in enumerate(hs):
                    pk = self.ps()
                    self.mm(pk[0:L, 0:L], qkvT[:, NH + h, cs:ce], qkvT[:, NH + h, cs:ce])
                    self.cp(KKs[i][0:L, 0:L], pk[0:L, 0:L], eng='act')
                    pbb = self.ps()
                    self.mm(pbb[0:L, 0:L], sel16[0:NH, h, 0:L], gam[0:NH, cs:ce])
                    self.ts(tmpm[i][0:L, 0:L], pbb[0:L, 0:L], gC[0:L, c, NH + h:NH + h + 1], ALU.add, 0.0, ALU.min)
                    self.act(tmpm[i][0:L, 0:L], tmpm[i][0:L, 0:L], AF.Exp)
                    self.tt(dTs_[i][0:L, 0:L], tmpm[i][0:L, 0:L], maskS[0:L, 0:L], ALU.mult)
                    self.tt(tmpm[i][0:L, 0:L], tmpm[i][0:L, 0:L], maskT[0:L, 0:L], ALU.mult)
                    pq = self.ps()
                    self.mm(pq[0:L, 0:L], qkvT[:, NH + h, cs:ce], qkvT[:, h, cs:ce])
                    self.tt(QKd[c % 2][h][0:L, 0:L], pq[0:L, 0:L], tmpm[i][0:L, 0:L], ALU.mult)
                for i, h in enumerate(hs):
                    pd = self.ps()
                    self.tr(pd[0:L, 0:L], dTs_[i][0:L, 0:L], ident_f[0:L, 0:L])
                    self.stt(Xs[i][0:L, 0:L], pd[0:L, 0:L], gC[0:L, c, 5 * NH + h:5 * NH + h + 1], KKs[i][0:L, 0:L], ALU.mult, ALU.mult)
                for i, h in enumerate(hs):
                    px = self.ps()
                    self.tr(px[0:L, 0:L], Xs[i][0:L, 0:L], ident_f[0:L, 0:L])
                    self.cp(XTs[i][0:L, 0:L], px[0:L, 0:L], eng='act')
                    self.tt(TTs[i][0:L, 0:L], px[0:L, 0:L], ident_f[0:L, 0:L], ALU.add)
                for j in range(nsq):
                    last = (j == nsq - 1)
                    for i, h in enumerate(hs):
                        pa2 = self.ps()
                        self.mm(pa2[0:L, 0:L], XTs[i][0:L, 0:L], Xs[i][0:L, 0:L])
                        if not last:
                            pb2 = self.ps()
                            self.mm(pb2[0:L, 0:L], Xs[i][0:L, 0:L], XTs[i][0:L, 0:L])
                        self.cp(Xs[i][0:L, 0:L], pa2[0:L, 0:L], eng='act')
                        if not last:
                            self.cp(XTs[i][0:L, 0:L], pb2[0:L, 0:L])
                    for i, h in enumerate(hs):
                        pc = self.ps()
                        self.mm(pc[0:L, 0:L], Xs[i][0:L, 0:L], TTs[i][0:L, 0:L])
                        self.tt(TTs[i][0:L, 0:L], pc[0:L, 0:L], TTs[i][0:L, 0:L], ALU.add)
                for i, h in enumerate(hs):
                    self.act(R1[i][0:L, :], v_tok[0:L, c, h * 128:(h + 1) * 128], AF.Copy, scale=gC[0:L, c, h:h + 1])
                    self.act(R2[i][0:L, :], k_tok[0:L, c, h * 128:(h + 1) * 128], AF.Copy, scale=gC[0:L, c, 4 * NH + h:4 * NH + h + 1])
                    pu = self.ps()
                    self.mm(pu[0:L, 0:128], TTs[i][0:L, 0:L], R1[i][0:L, :])
                    self.cp(U0s[c % 2][h][0:L, :], pu[0:L, 0:128], eng='act')
                    pw = self.ps()
                    self.mm(pw[:, 0:L], R2[i][0:L, :], TTs[i][0:L, 0:L])
                    self.cp(WTb[c % 2][h][:, 0:L], pw[:, 0:L])

            def phaseB(c):
                cs, ce = c * L, (c + 1) * L
                hs = list(range(NH))
                for i, h in enumerate(hs):
                    pws = self.ps()
                    self.mm(pws[0:L, 0:128], WTb[c % 2][h][:, 0:L], SGb[:, h, :])
                    self.tt(ub[i][0:L, :], U0s[c % 2][h][0:L, :], pws[0:L, 0:128], ALU.subtract)
                    self.act(kw[i][0:L, :], k_tok[0:L, c, h * 128:(h + 1) * 128], AF.Copy, scale=gC[0:L, c, 3 * NH + h:3 * NH + h + 1])
                for i, h in enumerate(hs):
                    po1 = self.ps()
                    self.mm(po1[0:L, 0:128], QKd[c % 2][h][0:L, 0:L], ub[i][0:L, :])
                    po2 = self.ps()
                    self.mm(po2[0:L, 0:128], qkvT[:, h, cs:ce], SGb[:, h, :])
                    self.cp(o1s[i][0:L, :], po1[0:L, 0:128], eng='act')
                    self.stt(oo[i][0:L, :], po2[0:L, 0:128], gC[0:L, c, 2 * NH + h:2 * NH + h + 1], o1s[i][0:L, :], ALU.mult, ALU.add)
                for i, h in enumerate(hs):
                    pS = self.ps()
                    self.mm(pS[:, 0:128], kw[i][0:L, :], ub[i][0:L, :])
                    self.stt(SG[:, h, :], SG[:, h, :], decC[:, c, h:h + 1], pS[:, 0:128], ALU.mult, ALU.add)
                    self.cp(SGb[:, h, :], SG[:, h, :], eng='act')
                for i, h in enumerate(hs):
                    self.act(jk[i][0:L, :], oo[i][0:L, :], AF.Square)
                    self.rsum(cc[i][0:L, :], jk[i][0:L, :])
                    self.act(cc[i][0:L, :], cc[i][0:L, :], AF.Sqrt, bias=EPS, scale=1.0 / 128)
                    self.recip(cc[i][0:L, :], cc[i][0:L, :])
                    self.stt(oo[i][0:L, :], oo[i][0:L, :], cc[i][0:L, :], gdnB[0:L, :], ALU.mult, ALU.mult)
                    self.tt(h_tok[0:L, h * 128:(h + 1) * 128], oo[i][0:L, :], zs[0:L, c, h * 128:(h + 1) * 128], ALU.mult)
                p = self.ps()
                for f8 in range(NH):
                    self.tr(pbv(p, slice(0, 128), slice(f8 * 128, f8 * 128 + L)), h_tok[0:L, f8 * 128:(f8 + 1) * 128],
                            ident_b[0:L, 0:L])
                self.cp(hT[:, 0:NH, cs:ce], V(p.hb[:, 0:512].rearrange("p (f t) -> p f t", t=128)[:, :, 0:L], p.buf), eng='act')

            phaseA(0)
            for c in range(nch):
                if c + 1 < nch:
                    phaseA(c + 1)
                phaseB(c)
            dma(g4in.v(g4in.ap.rearrange("(f p) t -> p f t", p=128)), hT[:, :, 0:n])
            allgather(g4in, g4out)

        xsrc = xT.ap.rearrange("(k p) t -> p k t", p=128)
        ydst = yT.ap.rearrange("(k p) t -> p k t", p=128)
        for b, (t0, n) in enumerate(own):
            dma(xb[b][:, :, 0:n], xT.v(xsrc[:, :, t0:t0 + n]))
        ffn_parts = [[(xb[b], TB)] for b in range(3)] + [[(xb[3], TB), (xb[4], DEC_SEQ)]]
        for pi_, parts in enumerate(ffn_parts):
            ffn(0, 0, parts)
            for (xx, n) in parts:
                b = 4 if n == DEC_SEQ else pi_
                hn_exchange(1, n, xx, G1in[b], G1out[b])
        seqs = [(r * 4 + i, TB, 128, G1out[i], r) for r in range(2) for i in range(4)] + \
               [(8 + r, DEC_SEQ, DEC_SEQ, G1out[4], r) for r in range(2)]

        def ab_zero():
            self.memset(Cx[:], 0.0); self.memset(ST[:], 0.0); self.memset(hist_b[:], 0.0)
            self.memset(Fc[:], 0.0); self.memset(Gc[:], 0.0)
            self.cp(Sb[:], ST[:])

        def ab_store(k):
            self.tt(mo[:], Gc[:], Fc[:], ALU.subtract)
            dma(o_Cx[k].v(), Cx[:], is_output=True); dma(o_m[k].v(), mo[:], is_output=True)
            dma(o_ST[k].v(), ST[:], is_output=True); dma(o_hb[k].v(), hist_b[:], is_output=True)

        def ffn_items(l, i):
            groups = [(g * 256, 256) for g in range(11)]
            wg_src = w_gate.ap[l, i].rearrange("(k p) f -> p k f", p=128)
            wu_src = w_up.ap[l, i].rearrange("(k p) f -> p k f", p=128)
            wd_src = w_down.ap[l, i].rearrange("(f p) d -> p f d", p=128)
            it = []
            for gi, (c0, ncol) in enumerate(groups):
                it.append(([128, 8, ncol], w_gate, wg_src[:, :, c0:c0 + ncol], ('wg', l, i, gi)))
                it.append(([128, 8, ncol], w_up, wu_src[:, :, c0:c0 + ncol], ('wu', l, i, gi)))
                if gi == 2:
                    for q in range(2):
                        it.append(([128, 11, D], w_down, wd_src[:, q * 11:(q + 1) * 11, :], ('wd', l, i, q)))
            return it
        wo_ab = ab_w_out.ap.rearrange("(f p) d -> p f d", p=128)
        pf_B = [([128, 8, 512], ab_w_out, wo_ab[:, fh * 8:(fh + 1) * 8, dh * 512:(dh + 1) * 512], ('wo', 16, dh, fh))
                for dh in range(2) for fh in range(2)] + ffn_items(0, 1) + ffn_items(1, 0)
        gwin = gw.ap.rearrange("(k p) c -> p k c", p=128)
        wo_g = gdn_w_out.ap.rearrange("(f p) d -> p f d", p=128)
        pf_C = [([128, 8, 8], gw, gwin[:, :, 2048:2056], ('gba',))] + \
               [([128, 8, 512], gw, gwin[:, :, c0:c0 + 512], ('gin', c0)) for c0 in (0, 512, 1024, 1536)] + \
               [([128, 8, 512], gdn_w_out, wo_g[:, 0:8, dh * 512:(dh + 1) * 512], ('wo', 8, dh, 0)) for dh in range(2)]
        pf_D = ffn_items(1, 1)

        def pf_emit(lst, cnt):
            for _ in range(cnt):
                if lst:
                    self.wprefetch(*lst.pop(0))

        ab_zero()
        for (k, n, L, g1, r) in seqs:
            if k >= 8:
                j = k - 8
                if j == 0:
                    ab_store(2)
                dma(Cx[:], s_Cx[j].v()); dma(ST[:], s_ST[j].v()); dma(hist_b[:], s_hb[j].v()); dma(Gc[:], s_m[j].v())
                self.memset(Fc[:], 0.0)
                self.cp(Sb[:], ST[:])
            mixer_ab(n, L, g1, r, G2in[k], G2out[k])
            pf_emit(pf_B, 7)
            if k >= 8:
                ab_store(k - 8)
        for b, (t0, n) in enumerate(own):
            gouts = (G2out[b], G2out[4 + b]) if b < 4 else (G2out[8], G2out[9])
            out_proj_sel(ab_w_out, 16, gouts, n, xb[b], 'ab')
            pf_emit(pf_C, 2)
        for parts in ffn_parts:
            ffn(0, 1, parts)
        for pi_, parts in enumerate(ffn_parts):
            ffn(1, 0, parts)
            for (xx, n) in parts:
                b = 4 if n == DEC_SEQ else pi_
                hn_exchange(4, n, xx, G3in[b], G3out[b])
        seqs_c = [(r * 4 + i, TB, 128, G3out[i], r) for r in range(2) for i in range(4)] + \
                 [(8 + r, DEC_SEQ, DEC_SEQ, G3out[4], r) for r in range(2)]

        def c_store(k):
            dma(o_SG[k].v(), SG[:], is_output=True); dma(o_hc[k].v(), hist_c[:], is_output=True)

        self.memset(SG[:], 0.0); self.memset(hist_c[:], 0.0)
        self.cp(SGb[:], SG[:])
        for (k, n, L, g3, r) in seqs_c:
            if k >= 8:
                j = k - 8
                if j == 0:
                    c_store(2)
                dma(SG[:], s_SG[j].v()); dma(hist_c[:], s_hc[j].v())
                self.cp(SGb[:], SG[:])
            mixer_c(n, L, g3, r, G4in[k], G4out[k])
            pf_emit(pf_D, 4)
            if k >= 8:
                c_store(k - 8)
        for b, (t0, n) in enumerate(own):
            gouts = (G4out[b], G4out[4 + b]) if b < 4 else (G4out[8], G4out[9])
            out_proj_sel(gdn_w_out, 8, gouts, n, xb[b], 'gdn')
        for parts in ffn_parts:
            ffn(1, 1, parts)
        for b, (t0, n) in enumerate(own):
            self.phase_begin(('fin',))
            yo = A("yo", [128, 8, TB]); sq = A("sq", [128, 8, TB], BF16)
            rmsnorm(n, lambda kc: gfcol[:, kc:kc + 1], yo, sq, xb[b])
            dma(yT.v(ydst[:, :, t0:t0 + n]), yo[:, :, 0:n], is_output=True)
        S.finish()
        S.emit()
        return nc


_NC_CACHE = {}


def _own_ch_ssd(r):
    return np.concatenate([np.arange(r * 512, (r + 1) * 512), 1024 + np.arange(r * 128, (r + 1) * 128),
                           1280 + np.arange(r * 128, (r + 1) * 128)])


def _own_ch_gdn(r):
    return np.concatenate([np.arange(r * 512, (r + 1) * 512), 1024 + np.arange(r * 512, (r + 1) * 512),
                           2048 + np.arange(r * 512, (r + 1) * 512)])


def _host_inputs(d, p, r):
    f = np.ascontiguousarray
    x = np.concatenate([d['x_prompt'][p, r * 2048:(r + 1) * 2048], d['x_sample'][2 * p + r]], 0)
    wi = d['ab_w_in']
    abw = np.concatenate([wi[:, r * 256:(r + 1) * 256], wi[:, 512 + r * 256:512 + (r + 1) * 256],
                          wi[:, 1024 + r * 512:1024 + (r + 1) * 512], wi[:, 2048 + r * 512:2048 + (r + 1) * 512],
                          wi[:, 3072 + 2 * r:3074 + 2 * r], wi[:, 3076 + 2 * r:3078 + 2 * r],
                          wi[:, 3080 + r * 512:3080 + (r + 1) * 512], wi[:, 4104 + r * 512:4104 + (r + 1) * 512],
                          wi[:, 5128 + r * 128:5128 + (r + 1) * 128], wi[:, 5384 + r * 128:5384 + (r + 1) * 128],
                          wi[:, 5640 + r * 8:5640 + (r + 1) * 8]], axis=1)
    gi = d['gdn_w_in']
    gw = np.concatenate([gi[:, r * 512:(r + 1) * 512], gi[:, 1024 + r * 512:1024 + (r + 1) * 512],
                         gi[:, 2048 + r * 512:2048 + (r + 1) * 512], gi[:, 3072 + r * 512:3072 + (r + 1) * 512],
                         gi[:, 4096 + 4 * r:4100 + 4 * r], gi[:, 4104 + 4 * r:4108 + 4 * r]], axis=1)
    cs, cg = _own_ch_ssd(r), _own_ch_gdn(r)
    rm = np.zeros((128, 2), np.float32)
    rm[:, r] = 1.0
    ins = {
        "xT": f(x.T), "ffn_w_gate": d['ffn_w_gate'], "ffn_w_up": d['ffn_w_up'], "ffn_w_down": d['ffn_w_down'],
        "g_col": f(d['norm_g'].reshape(6, 8, 128).transpose(2, 0, 1)),
        "gf_col": f(d['norm_f'].reshape(8, 128).T), "rmask": rm,
        "abw": f(abw), "ab_w_out": d['ab_w_out'],
        "b_i": f(d['mlstm_b_i'][2 * r:2 * r + 2].reshape(2, 1)), "b_f": f(d['mlstm_b_f'][2 * r:2 * r + 2].reshape(2, 1)),
        "mlstm_norm_b": f(np.broadcast_to(d['mlstm_norm'][2 * r:2 * r + 2].reshape(1, 512), (128, 512))),
        "ssd_cw": f(d['ssd_conv_w'][:, cs].reshape(4, 6, 128).transpose(2, 1, 0)),
        "ssd_cb": f(d['ssd_conv_b'][cs].reshape(6, 128).T),
        "ssd_dtb": f(d['ssd_dt_bias'][r * 8:(r + 1) * 8].reshape(8, 1)),
        "ssd_alog": f(d['ssd_a_log'][r * 8:(r + 1) * 8].reshape(8, 1)),
        "ssd_d_b": f(np.broadcast_to(d['ssd_d'][r * 8:(r + 1) * 8].reshape(1, 8), (128, 8))),
        "ssd_norm_b": f(np.broadcast_to(d['ssd_norm'][r * 512:(r + 1) * 512].reshape(1, 512), (128, 512))),
        "gw": f(gw), "gdn_w_out": d['gdn_w_out'],
        "gdn_cw": f(d['gdn_conv_w'][:, cg].reshape(4, 12, 128).transpose(2, 1, 0)),
        "gdn_dtb": f(d['gdn_dt_bias'][4 * r:4 * r + 4].reshape(4, 1)),
        "gdn_alog": f(d['gdn_a_log'][4 * r:4 * r + 4].reshape(4, 1)),
        "gdn_norm_b": f(np.broadcast_to(d['gdn_norm'].reshape(1, 128), (128, 128))),
    }
    for j in range(2):
        s = 2 * p + j
        ins["s_Cx%d" % j] = f(np.concatenate([d['state_mlstm_C'][s, 2 * r:2 * r + 2].transpose(1, 0, 2),
                                              d['state_mlstm_n'][s, 2 * r:2 * r + 2].T[:, :, None]], 2))
        ins["s_m%d" % j] = f(d['state_mlstm_m'][s, 2 * r:2 * r + 2].reshape(2, 1))
        ins["s_ST%d" % j] = f(d['state_ssd'][s, r * 8:(r + 1) * 8].transpose(2, 0, 1))
        ins["s_hb%d" % j] = f(d['cache_ssd_conv'][s][:, cs].reshape(3, 6, 128).transpose(2, 1, 0))
        ins["s_SG%d" % j] = f(d['state_gdn'][s, 4 * r:4 * r + 4].transpose(1, 0, 2))
        ins["s_hc%d" % j] = f(d['cache_gdn_conv'][s][:, cg].reshape(3, 12, 128).transpose(2, 1, 0))
    return ins


def _assemble(R, npair):
    f32 = lambda a: np.ascontiguousarray(np.asarray(a, dtype=np.float32))
    y_prompt = np.zeros((npair, 4096, 1024), np.float32)
    y_sample = np.zeros((2 * npair, 16, 1024), np.float32)

    def alloc(n):
        return [np.zeros((n, 4, 128, 256), np.float32), np.zeros((n, 4, 128), np.float32), np.zeros((n, 4), np.float32),
                np.zeros((n, 16, 64, 128), np.float32), np.zeros((n, 3, 1536), np.float32),
                np.zeros((n, 8, 128, 128), np.float32), np.zeros((n, 3, 3072), np.float32)]
    PS, SS = alloc(npair), alloc(2 * npair)

    def put(dst, idx, r, res, k):
        cs, cg = _own_ch_ssd(r), _own_ch_gdn(r)
        Cx = np.asarray(res["o_Cx%d" % k], np.float32)
        dst[0][idx, 2 * r:2 * r + 2] = Cx[:, :, :256].transpose(1, 0, 2)
        dst[1][idx, 2 * r:2 * r + 2] = Cx[:, :, 256].T
        dst[2][idx, 2 * r:2 * r + 2] = np.asarray(res["o_m%d" % k], np.float32)[:, 0]
        dst[3][idx, r * 8:(r + 1) * 8] = np.asarray(res["o_ST%d" % k], np.float32).transpose(1, 2, 0)
        dst[4][idx][:, cs] = np.asarray(res["o_hb%d" % k], np.float32).transpose(2, 1, 0).reshape(3, 768)
        dst[5][idx, 4 * r:4 * r + 4] = np.asarray(res["o_SG%d" % k], np.float32).transpose(1, 0, 2)
        dst[6][idx][:, cg] = np.asarray(res["o_hc%d" % k], np.float32).transpose(2, 1, 0).reshape(3, 1536)

    for p in range(npair):
        for r in range(2):
            res = R[2 * p + r]
            yT = np.asarray(res['yT'], np.float32)
            y_prompt[p, r * 2048:(r + 1) * 2048] = yT[:, :2048].T
            y_sample[2 * p + r] = yT[:, 2048:].T
            put(PS, p, r, res, 2)
            for j in range(2):
                put(SS, 2 * p + j, r, res, j)
    return tuple([y_prompt, y_sample] + PS + SS)


def kernel(**inputs):
    d = {k: np.asarray(v, dtype=np.float32) for k, v in inputs.items()}
    ncores = 8
    if 'nc' not in _NC_CACHE:
        B = Builder(8)
        _NC_CACHE['nc'] = (B.build(), set(B.din))
    nc, din = _NC_CACHE['nc']
    in_maps = []
    for c in range(ncores):
        hi = _host_inputs(d, c // 2, c % 2)
        in_maps.append({k: v for k, v in hi.items() if k in din})
    res = run_bass_kernel_spmd(nc, in_maps, core_ids=list(range(ncores)))
    return _assemble(list(res.results), 4)
```

```python
import numpy as np
from contextlib import ExitStack
import concourse.bass as bass
import concourse.mybir as mybir
from concourse.bass_utils import run_bass_kernel_spmd

F32 = mybir.dt.float32
BF16 = mybir.dt.bfloat16
AF = mybir.ActivationFunctionType
ALU = mybir.AluOpType

ENGS = ['pe', 'act', 'dve', 'pool', 'sp']
NDSEM = 12
import os
PLIM = int(os.environ.get('K_PLIM', '3'))

D = 1024
DFF = 2816
NFF = 22
EPS = 1e-6
TB = 512
DEC_SEQ = 16


class Buf:
    __slots__ = ('name', 'w', 'r', 'excl')

    def __init__(self, name='', excl=False):
        self.name = name
        self.w = None
        self.r = []
        self.excl = excl


class Sched:
    def __init__(self, nc, same_engine_sync=True):
        self.nc = nc
        self.ops = {e: [] for e in ENGS}
        self.seen = {e: {f: -1 for f in ENGS} for e in ENGS}
        self.seen_dma = {e: {} for e in ENGS}
        self.targets = {e: set() for e in ENGS}
        self.ndma = {e: 0 for e in ENGS}
        self.same_engine_sync = same_engine_sync
        self.out_tokens = []

    def _need(self, eng, tok, waits):
        if tok is None:
            return
        if tok[0] == 'e':
            _, f, k = tok
            if f == eng and (eng == 'pe' or not self.same_engine_sync):
                return
            if self.seen[eng][f] >= k:
                return
            self.seen[eng][f] = k
            self.targets[f].add(k)
            waits.append(tok)
        elif tok[0] == 'c':
            if getattr(self, '_seen_cc', {}).get(eng, -1) >= tok[1]:
                return
            if not hasattr(self, '_seen_cc'):
                self._seen_cc = {}
            self._seen_cc[eng] = tok[1]
            waits.append(tok)
        else:
            _, q, seq = tok
            key = (q, seq % NDSEM)
            if self.seen_dma[eng].get(key, -1) >= seq:
                return
            self.seen_dma[eng][key] = seq
            waits.append(tok)

    def op(self, eng, fn, reads=(), writes=(), dma=False, is_output=False, cc=False):
        waits = []
        for b in reads:
            self._need(eng, b.w, waits)
            if b.excl:
                for t in b.r:
                    if t[0] == 'e' and t[1] != eng:
                        self._need(eng, t, waits)
        for b in writes:
            self._need(eng, b.w, waits)
            for t in b.r:
                self._need(eng, t, waits)
        idx = len(self.ops[eng])
        if eng == 'pool' and not dma:
            lp = getattr(self, '_last_pool', None)
            if lp is not None:
                self._need('pool', lp, waits)
            self._last_pool = ('e', 'pool', idx)
        if dma:
            seq = self.ndma[eng]
            self.ndma[eng] += 1
            lim = PLIM if eng == 'pool' else NDSEM
            if seq >= lim:
                self._need(eng, ('d', eng, seq - lim), waits)
            tok = ('d', eng, seq)
        elif cc:
            seq = None
            self.ncc = getattr(self, 'ncc', 0) + 1
            tok = ('c', self.ncc - 1)
        else:
            seq = None
            tok = ('e', eng, idx)
        self.ops[eng].append(dict(fn=fn, waits=waits, dma=seq, cc=cc))
        for b in reads:
            if tok[0] == 'e':
                b.r = [t for t in b.r if not (t[0] == 'e' and t[1] == tok[1])]
            b.r.append(tok)
        for b in writes:
            b.w = tok
            b.r = []
        if is_output:
            self.out_tokens.append(tok)
        return tok

    def barrier(self):
        last = {}
        for e in ENGS:
            if self.ops[e]:
                last[e] = ('e', e, len(self.ops[e]) - 1)
        for e in ENGS:
            waits = []
            for f, t in last.items():
                if f != e:
                    self._need(e, t, waits)
            for q in ENGS:
                lim = PLIM if q == 'pool' else NDSEM
                for sq_ in range(max(0, self.ndma[q] - lim), self.ndma[q]):
                    self._need(e, ('d', q, sq_), waits)
            if waits:
                self.ops[e].append(dict(fn=None, waits=waits, dma=None, cc=False))

    def finish(self):
        waits = []
        for t in self.out_tokens:
            self._need('sp', t, waits)
        self.ops['sp'].append(dict(fn=None, waits=waits, dma=None, cc=False))

    def emit(self):
        nc = self.nc
        with ExitStack() as st:
            esem = {e: st.enter_context(nc.semaphore('es_' + e)) for e in ENGS}
            ccsem = st.enter_context(nc.semaphore('cc_sem'))
            dsem = {e: [st.enter_context(nc.semaphore('ds_%s%d' % (e, i))) for i in range(NDSEM)]
                    for e in ENGS if self.ndma[e] > 0}
            cnt = {}
            for e in ENGS:
                c = 0
                m = {}
                for k in sorted(self.targets[e]):
                    c += 1
                    m[k] = c
                cnt[e] = m
            block = st.enter_context(nc.Block())

            def body(ename, eh):
                for idx, o in enumerate(self.ops[ename]):
                    for t in o['waits']:
                        if t[0] == 'e':
                            eh.wait_ge(esem[t[1]], cnt[t[1]][t[2]])
                        elif t[0] == 'c':
                            eh.wait_ge(ccsem, t[1] + 1)
                        else:
                            eh.wait_ge(dsem[t[1]][t[2] % NDSEM], 16 * (t[2] // NDSEM + 1))
                    if o['fn'] is None:
                        if idx in cnt[ename]:
                            eh.nop().then_inc(esem[ename], 1)
                        continue
                    inst = o['fn'](eh)
                    if o.get('cc'):
                        inst.then_inc(ccsem, 1)
                        if idx in cnt[ename]:
                            eh.nop().then_inc(esem[ename], 1)
                    elif o['dma'] is not None:
                        inst.then_inc(dsem[ename][o['dma'] % NDSEM], 16)
                        if idx in cnt[ename]:
                            eh.nop().then_inc(esem[ename], 1)
                    elif idx in cnt[ename]:
                        inst.then_inc(esem[ename], 1)

            @block.tensor
            def _(eh):
                body('pe', eh)

            @block.scalar
            def _(eh):
                body('act', eh)

            @block.vector
            def _(eh):
                body('dve', eh)

            @block.gpsimd
            def _(eh):
                body('pool', eh)

            @block.sync
            def _(eh):
                body('sp', eh)


class V:
    __slots__ = ('ap', 'buf')

    def __init__(self, ap, buf):
        self.ap = ap
        self.buf = buf


class Tl:
    def __init__(self, h, name=''):
        self.h = h
        self.buf = Buf(name)

    def __getitem__(self, idx):
        return V(self.h[idx], self.buf)

    def v(self, ap):
        return V(ap, self.buf)


class DT:
    def __init__(self, ap, name=''):
        self.ap = ap
        self.buf = Buf(name)

    def v(self, ap=None):
        return V(self.ap if ap is None else ap, self.buf)


class Builder:
    SB_LO = 17536
    SB_HI = 229344

    def __init__(self, nblk):
        self.nblk = nblk
        self.ntok = nblk * TB + DEC_SEQ
        self.nc = bass.Bass("TRN2", target_bir_lowering=False)
        self.S = Sched(self.nc)
        self.persist_off = self.SB_LO
        self.phase_base = None
        self.phase_off = None
        self.nps = 0
        self.din = {}
        self.dout = {}

    def _alloc(self, name, shape, dt, off):
        n = 1
        for s in shape[1:]:
            n *= s
        nbytes = n * (4 if dt == F32 else 2)
        nbytes = (nbytes + 63) // 64 * 64
        h = self.nc.alloc_sbuf_tensor_at(name, list(shape), dt, offset=off)
        return Tl(h, name), off + nbytes

    def P(self, name, shape, dt=F32):
        t, self.persist_off = self._alloc(name, shape, dt, self.persist_off)
        assert self.persist_off <= self.SB_HI, name
        return t

    def phase_begin(self, kind=None):
        if self.phase_base is None:
            self.phase_base = self.persist_off
            self.cur_kind = None
            self.kind_tiles = {}
        if kind is None or kind != self.cur_kind:
            if self.cur_kind is not None or kind is None:
                self.S.barrier()
            self.cur_kind = kind
            self.kind_tiles = {}
            self.phase_gen = getattr(self, 'phase_gen', 0) + 1
        self.phase_off = self.phase_base

    def A(self, name, shape, dt=F32):
        if self.cur_kind is not None and name in self.kind_tiles:
            return self.kind_tiles[name]
        t, self.phase_off = self._alloc(name, shape, dt, self.phase_off)
        assert self.phase_off <= self.SB_HI, (name, self.phase_off)
        if self.cur_kind is not None:
            self.kind_tiles[name] = t
        return t

    def din_t(self, name, shape, dt=F32):
        ap = self.nc.dram_tensor(name, list(shape), dt, kind="ExternalInput").ap()
        d = DT(ap, name)
        self.din[name] = d
        return d

    def dout_t(self, name, shape, dt=F32):
        ap = self.nc.dram_tensor(name, list(shape), dt, kind="ExternalOutput").ap()
        d = DT(ap, name)
        self.dout[name] = d
        return d

    def mm(self, out, lhsT, rhs, start=True, stop=True):
        self.S.op('pe', lambda e: e.matmul(out.ap, lhsT=lhsT.ap, rhs=rhs.ap, start=start, stop=stop),
                  reads=[lhsT.buf, rhs.buf], writes=[out.buf])

    def tr(self, out, in_, ident):
        self.S.op('pe', lambda e: e.transpose(out.ap, in_.ap, ident.ap),
                  reads=[in_.buf, ident.buf], writes=[out.buf])

    def act(self, out, in_, func, bias=None, scale=1.0, accum=None, eng='act'):
        reads = [in_.buf]
        kw = {}
        if bias is not None:
            if isinstance(bias, V):
                reads.append(bias.buf)
                kw['bias'] = bias.ap
            else:
                kw['bias'] = bias
        if isinstance(scale, V):
            reads.append(scale.buf)
            kw['scale'] = scale.ap
        else:
            kw['scale'] = scale
        writes = [out.buf]
        if accum is not None:
            writes.append(accum.buf)
            kw['accum_out'] = accum.ap
        self.S.op('act', lambda e: e.activation(out.ap, in_.ap, func, **kw), reads=reads, writes=writes)

    def tt(self, out, in0, in1, op, eng='dve'):
        self.S.op(eng, lambda e: e.tensor_tensor(out.ap, in0.ap, in1.ap, op),
                  reads=[in0.buf, in1.buf], writes=[out.buf])

    def ts(self, out, in0, s1, op0, s2=None, op1=None, eng='dve', accum=None):
        reads = [in0.buf]
        a1 = s1
        a2 = s2
        if isinstance(s1, V):
            reads.append(s1.buf)
            a1 = s1.ap
        if isinstance(s2, V):
            reads.append(s2.buf)
            a2 = s2.ap
        kw = {}
        writes = [out.buf]
        if accum is not None:
            kw['accum_out'] = accum.ap
            writes.append(accum.buf)
        if op1 is None:
            self.S.op(eng, lambda e: e.tensor_scalar(out.ap, in0.ap, a1, None, op0, **kw), reads=reads, writes=writes)
        else:
            self.S.op(eng, lambda e: e.tensor_scalar(out.ap, in0.ap, a1, a2, op0, op1, **kw), reads=reads, writes=writes)

    def stt(self, out, in0, sc, in1, op0, op1):
        reads = [in0.buf, in1.buf]
        a = sc
        if isinstance(sc, V):
            reads.append(sc.buf)
            a = sc.ap
        self.S.op('dve', lambda e: e.scalar_tensor_tensor(out.ap, in0.ap, a, in1.ap, op0, op1),
                  reads=reads, writes=[out.buf])

    def cp(self, out, in_, eng='dve'):
        if eng == 'act':
            self.S.op('act', lambda e: e.copy(out.ap, in_.ap), reads=[in_.buf], writes=[out.buf])
        else:
            self.S.op(eng, lambda e: e.tensor_copy(out.ap, in_.ap), reads=[in_.buf], writes=[out.buf])

    def rsum(self, out, in_):
        self.S.op('dve', lambda e: e.tensor_reduce(out.ap, in_.ap, mybir.AxisListType.X, ALU.add),
                  reads=[in_.buf], writes=[out.buf])

    def recip(self, out, in_):
        self.S.op('dve', lambda e: e.reciprocal(out.ap, in_.ap), reads=[in_.buf], writes=[out.buf])

    def scan(self, out, d0, d1, init, op0, op1):
        reads = [d0.buf, d1.buf]
        a = init
        if isinstance(init, V):
            reads.append(init.buf)
            a = init.ap
        self.S.op('dve', lambda e: e.tensor_tensor_scan(out.ap, d0.ap, d1.ap, a, op0, op1),
                  reads=reads, writes=[out.buf])

    def memset(self, out, val, eng='pool'):
        self.S.op(eng, lambda e: e.memset(out.ap, val), writes=[out.buf])

    def dma(self, out, in_, q='sp', is_output=False, nc_ok=False):
        if nc_ok:
            fn = lambda e: e.dma_start(out=out.ap, in_=in_.ap, allow_slow_non_contiguous=True)
        else:
            fn = lambda e: e.dma_start(out=out.ap, in_=in_.ap)
        self.S.op(q, fn, reads=[in_.buf], writes=[out.buf], dma=True, is_output=is_output)

    def wload(self, dst, src_dt, src_ap, key):
        if not hasattr(self, 'wcache'):
            self.wcache = {}
        if key not in self.wcache:
            self.dma(dst, V(src_ap, src_dt.buf), q='pool')
            shp = list(dst.ap.shape)
            nm = "wc_" + "_".join(str(k) for k in key)
            ap = self.nc.dram_tensor(nm, shp, BF16, kind="Internal").ap()
            d = DT(ap, nm)
            self.wcache[key] = d
            self.dma(d.v(), dst, q='sp')
        else:
            self.dma(dst, self.wcache[key].v(), q='sp')

    def wprefetch(self, shape, src_dt, src_ap, key):
        if not hasattr(self, 'wcache'):
            self.wcache = {}
        if key in self.wcache:
            return
        nm = "wc_" + "_".join(str(k) for k in key)
        ap = self.nc.dram_tensor(nm, list(shape), BF16, kind="Internal").ap()
        d = DT(ap, nm)
        self.wcache[key] = d
        self.dma(d.v(), V(src_ap, src_dt.buf), q='pool')

    def ps(self):
        t = self.psb[self.nps % 8]
        self.nps += 1
        return t

    def build(self):
        nc = self.nc
        S = self.S
        A = self.A
        P = self.P
        dma = self.dma
        DKs = 128 ** -0.5
        NOWN = 4
        NTOK = NOWN * TB + DEC_SEQ
        RG = [[0, 1]] if os.environ.get("K_RG") == "pair" else [[0, 1], [2, 3], [4, 5], [6, 7]]
        xT = self.din_t("xT", [D, NTOK])
        yT = self.dout_t("yT", [D, NTOK])
        w_gate = self.din_t("ffn_w_gate", [2, 2, D, DFF])
        w_up = self.din_t("ffn_w_up", [2, 2, D, DFF])
        w_down = self.din_t("ffn_w_down", [2, 2, DFF, D])
        g_col = self.din_t("g_col", [128, 6, 8])
        gf_col = self.din_t("gf_col", [128, 8])
        rmask_d = self.din_t("rmask", [128, 2])
        abw = self.din_t("abw", [D, 2828])
        ab_w_out = self.din_t("ab_w_out", [2048, D])
        d_bi = self.din_t("b_i", [2, 1]); d_bf = self.din_t("b_f", [2, 1])
        d_gnB = self.din_t("mlstm_norm_b", [128, 512])
        d_cw = self.din_t("ssd_cw", [128, 6, 4]); d_cb = self.din_t("ssd_cb", [128, 6])
        d_dtb = self.din_t("ssd_dtb", [8, 1]); d_alog = self.din_t("ssd_alog", [8, 1])
        d_dskB = self.din_t("ssd_d_b", [128, 8]); d_snB = self.din_t("ssd_norm_b", [128, 512])
        gw = self.din_t("gw", [D, 2056])
        gdn_w_out = self.din_t("gdn_w_out", [1024, D])
        d_gcw = self.din_t("gdn_cw", [128, 12, 4])
        d_gdtb = self.din_t("gdn_dtb", [4, 1]); d_galog = self.din_t("gdn_alog", [4, 1])
        d_gdnB = self.din_t("gdn_norm_b", [128, 128])
        s_Cx = [self.din_t("s_Cx%d" % j, [128, 2, 257]) for j in range(2)]
        s_m = [self.din_t("s_m%d" % j, [2, 1]) for j in range(2)]
        s_ST = [self.din_t("s_ST%d" % j, [128, 8, 64]) for j in range(2)]
        s_hb = [self.din_t("s_hb%d" % j, [128, 6, 3]) for j in range(2)]
        s_SG = [self.din_t("s_SG%d" % j, [128, 4, 128]) for j in range(2)]
        s_hc = [self.din_t("s_hc%d" % j, [128, 12, 3]) for j in range(2)]
        o_Cx = [self.dout_t("o_Cx%d" % j, [128, 2, 257]) for j in range(3)]
        o_m = [self.dout_t("o_m%d" % j, [2, 1]) for j in range(3)]
        o_ST = [self.dout_t("o_ST%d" % j, [128, 8, 64]) for j in range(3)]
        o_hb = [self.dout_t("o_hb%d" % j, [128, 6, 3]) for j in range(3)]
        o_SG = [self.dout_t("o_SG%d" % j, [128, 4, 128]) for j in range(3)]
        o_hc = [self.dout_t("o_hc%d" % j, [128, 12, 3]) for j in range(3)]

        def internal(name, shape):
            return DT(nc.dram_tensor(name, list(shape), BF16, kind="Internal").ap(), name)
        own = [(b * TB, TB) for b in range(NOWN)] + [(NOWN * TB, DEC_SEQ)]
        G1in = [internal("g1in%d" % b, [1024, n]) for b, (_, n) in enumerate(own)]
        G1out = [internal("g1out%d" % b, [2048, n]) for b, (_, n) in enumerate(own)]
        G3in = [internal("g3in%d" % b, [1024, n]) for b, (_, n) in enumerate(own)]
        G3out = [internal("g3out%d" % b, [2048, n]) for b, (_, n) in enumerate(own)]
        seqn = [TB] * 8 + [DEC_SEQ] * 2
        G2in = [internal("g2in%d" % k, [1024, n]) for k, n in enumerate(seqn)]
        G2out = [internal("g2out%d" % k, [2048, n]) for k, n in enumerate(seqn)]
        G4in = [internal("g4in%d" % k, [512, n]) for k, n in enumerate(seqn)]
        G4out = [internal("g4out%d" % k, [1024, n]) for k, n in enumerate(seqn)]

        def allgather(gi, go):
            S.op('pool', lambda e: e.collective_compute("AllGather", ALU.bypass, replica_groups=RG,
                                                        ins=[gi.ap], outs=[go.ap]),
                 reads=[gi.buf], writes=[go.buf], cc=True)

        self.psb = []
        self._st = ExitStack()
        for i in range(8):
            h = self._st.enter_context(nc.psum_tensor("psb%d" % i, [128, 512], F32))
            t = Tl(h, "psb%d" % i)
            t.buf.excl = True
            t.hb = h[:, :].bitcast(BF16)
            self.psb.append(t)

        def pbv(t, *idx):
            return V(t.hb[idx], t.buf)

        xb = [P("x%d" % b, [128, 8, n]) for b, (_, n) in enumerate(own)]
        hn = P("hn", [128, 8, TB], BF16)
        gcol = P("gcol", [128, 6, 8]); gfcol = P("gfcol", [128, 8]); rmask = P("rmask", [128, 2])
        ones_f = P("ones_f", [128, 128]); ones_b = P("ones_b", [128, 128], BF16)
        rstd = P("rstd", [128, TB])
        ident_f = P("ident_f", [128, 128]); ident_b = P("ident_b", [128, 128], BF16)
        maskT = P("maskT", [128, 128]); maskS = P("maskS", [128, 128])
        ones16 = P("ones16", [16, 128]); onesr = P("onesr", [16, TB]); sel16 = P("sel16", [16, 8, 128])
        bi = P("bi", [2, 1]); nbf = P("nbf", [2, 1])
        gnB = P("gnB", [128, 512]); snB = P("snB", [128, 512]); dskB = P("dskB", [128, 8])
        cw = P("cw", [128, 6, 4]); cb = P("cb", [128, 6]); dtb = P("dtb", [8, 1]); negA = P("negA", [8, 1])
        Cx = P("Cx", [128, 2, 257]); ST = P("ST", [128, 8, 64]); Sb = P("Sb", [128, 8, 64], BF16)
        hist_b = P("hist_b", [128, 6, 3])
        Fc = P("Fc", [2, 1]); Gc = P("Gc", [2, 1]); mo = P("mo", [2, 1])
        gcw = P("gcw", [128, 12, 4]); gdtb = P("gdtb", [4, 1]); gnegA = P("gnegA", [4, 1]); gdnB = P("gdnB", [128, 128])
        SG = P("SG", [128, 4, 128]); SGb = P("SGb", [128, 4, 128], BF16); hist_c = P("hist_c", [128, 12, 3])

        for (dst, src) in ((gcol, g_col), (gfcol, gf_col), (rmask, rmask_d), (bi, d_bi), (nbf, d_bf), (gnB, d_gnB),
                           (snB, d_snB), (dskB, d_dskB), (cw, d_cw), (cb, d_cb), (dtb, d_dtb), (negA, d_alog),
                           (gcw, d_gcw), (gdtb, d_gdtb), (gnegA, d_galog), (gdnB, d_gdnB)):
            dma(dst[:], src.v())
        self.memset(ones_f[:], 1.0 / D); self.memset(ones_b[:], 1.0 / D)
        self.memset(ones16[:], 1.0); self.memset(onesr[:], 1.0)
        self.memset(ident_f[:], 0.0)
        S.op('pool', lambda e: e.affine_select(out=ident_f.h[:], in_=ident_f.h[:], pattern=[[-1, 128]],
                                               compare_op=ALU.not_equal, fill=1.0, base=0, channel_multiplier=1),
             reads=[ident_f.buf], writes=[ident_f.buf])
        self.cp(ident_b[:], ident_f[:])
        self.memset(maskT[:], 1.0)
        S.op('pool', lambda e: e.affine_select(out=maskT.h[:], in_=maskT.h[:], pattern=[[1, 128]],
                                               compare_op=ALU.is_ge, fill=0.0, base=0, channel_multiplier=-1),
             reads=[maskT.buf], writes=[maskT.buf])
        self.memset(maskS[:], 1.0)
        S.op('pool', lambda e: e.affine_select(out=maskS.h[:], in_=maskS.h[:], pattern=[[1, 128]],
                                               compare_op=ALU.is_gt, fill=0.0, base=0, channel_multiplier=-1),
             reads=[maskS.buf], writes=[maskS.buf])
        self.memset(sel16[:], 0.0)
        S.op('pool', lambda e: e.affine_select(out=sel16.h[:], in_=sel16.h[:], pattern=[[-1, 8], [0, 128]],
                                               compare_op=ALU.not_equal, fill=1.0, base=0, channel_multiplier=1),
             reads=[sel16.buf], writes=[sel16.buf])
        self.ts(nbf[:], nbf[:], -1.0, ALU.mult)
        self.act(negA[:], negA[:], AF.Exp); self.ts(negA[:], negA[:], -1.0, ALU.mult)
        self.act(gnegA[:], gnegA[:], AF.Exp); self.ts(gnegA[:], gnegA[:], -1.0, ALU.mult)

        def rmsnorm(n, gv, out_t, sq, x):
            self.act(sq[:, :, 0:n], x[:, :, 0:n], AF.Square)
            p = self.ps()
            for kc in range(8):
                self.mm(p[:, 0:n], ones_b[:], sq[:, kc, 0:n], start=(kc == 0), stop=(kc == 7))
            self.act(rstd[:, 0:n], p[:, 0:n], AF.Sqrt, bias=EPS)
            self.recip(rstd[:, 0:n], rstd[:, 0:n])
            for kc in range(8):
                self.stt(out_t[:, kc, 0:n], x[:, kc, 0:n], gv(kc), rstd[:, 0:n], ALU.mult, ALU.mult)

        def ffn(l, i, parts):
            self.phase_begin(('ffn',))
            sq = A("sq", [128, 8, TB], BF16)
            act_t = A("ffn_act", [128, NFF, TB], BF16)
            act_s = A("ffn_act_s", [128, NFF, DEC_SEQ], BF16)
            hn_s = A("hn_s", [128, 8, DEC_SEQ], BF16)
            wd = A("ffn_wd", [128, NFF, D], BF16)
            wgu = [[A("ffn_wg%d" % b, [128, 8, 256], BF16), A("ffn_wu%d" % b, [128, 8, 256], BF16)]
                   for b in range(3)]
            sil = [A("ffn_sil0", [128, TB])] * 2
            hns = [hn, hn_s]
            acts = [act_t, act_s]
            for pi, (x, n) in enumerate(parts):
                rmsnorm(n, lambda kc: gcol[:, l * 3 + 2 * i, kc:kc + 1], hns[pi], sq, x)
            wg_src = w_gate.ap[l, i].rearrange("(k p) f -> p k f", p=128)
            wu_src = w_up.ap[l, i].rearrange("(k p) f -> p k f", p=128)
            wd_src = w_down.ap[l, i].rearrange("(f p) d -> p f d", p=128)
            groups = [(g * 256, 256) for g in range(11)]
            for gi, (c0, ncol) in enumerate(groups):
                b = gi % 3
                self.wload(wgu[b][0][:, :, 0:ncol], w_gate, wg_src[:, :, c0:c0 + ncol], ('wg', l, i, gi))
                self.wload(wgu[b][1][:, :, 0:ncol], w_up, wu_src[:, :, c0:c0 + ncol], ('wu', l, i, gi))
                if gi == 2 and getattr(self, '_wd_res', None) != (l, i, self.phase_gen):
                    for q in range(2):
                        self.wload(wd[:, q * 11:(q + 1) * 11, :], w_down, wd_src[:, q * 11:(q + 1) * 11, :], ('wd', l, i, q))
                    self._wd_res = (l, i, self.phase_gen)
                for j in range(ncol // 128):
                    f = c0 // 128 + j
                    for pi, (x, n) in enumerate(parts):
                        pg = self.ps()
                        pu = self.ps()
                        for kc in range(8):
                            self.mm(pg[:, 0:n], wgu[b][0][:, kc, j * 128:(j + 1) * 128], hns[pi][:, kc, 0:n],
                                    start=(kc == 0), stop=(kc == 7))
                        for kc in range(8):
                            self.mm(pu[:, 0:n], wgu[b][1][:, kc, j * 128:(j + 1) * 128], hns[pi][:, kc, 0:n],
                                    start=(kc == 0), stop=(kc == 7))
                        sl = sil[f % 2]
                        self.act(sl[:, 0:n], pg[:, 0:n], AF.Silu)
                        self.tt(acts[pi][:, f, 0:n], sl[:, 0:n], pu[:, 0:n], ALU.mult)
            for dc in range(8):
                for pi, (x, n) in enumerate(parts):
                    p = self.ps()
                    for f in range(NFF):
                        self.mm(p[:, 0:n], wd[:, f, dc * 128:(dc + 1) * 128], acts[pi][:, f, 0:n],
                                start=(f == 0), stop=(f == NFF - 1))
                    self.stt(x[:, dc, 0:n], p[:, 0:n], 0.5, x[:, dc, 0:n], ALU.mult, ALU.add)

        def hn_exchange(gidx, n, x, gin, gout):
            self.phase_begin(('ffn',))
            sq = A("sq", [128, 8, TB], BF16)
            rmsnorm(n, lambda kc: gcol[:, gidx, kc:kc + 1], hn, sq, x)
            dma(gin.v(gin.ap.rearrange("(k p) t -> p k t", p=128)), hn[:, :, 0:n])
            allgather(gin, gout)

        def out_proj_sel(w_dt, nfc, gouts, n, x, tag):
            self.phase_begin(('ops', nfc))
            wb = [A("wb0", [128, 8, 512], BF16), A("wb1", [128, 8, 512], BF16)]
            cand = [A("cand0", [128, nfc, TB], BF16), A("cand1", [128, nfc, TB], BF16)]
            hT = A("hTf", [128, nfc, TB], BF16)
            half = nfc // 2 * 128
            for ci, go in enumerate(gouts):
                for r in range(2):
                    src = go.ap[r * half:(r + 1) * half, :].rearrange("(f p) t -> p f t", p=128)
                    if nfc == 16:
                        dma(cand[ci][:, r * 4:r * 4 + 4, 0:n], go.v(src[:, 0:4, :]))
                        dma(cand[ci][:, 8 + r * 4:8 + r * 4 + 4, 0:n], go.v(src[:, 4:8, :]))
                    else:
                        dma(cand[ci][:, r * 4:r * 4 + 4, 0:n], go.v(src[:, 0:4, :]))
            self.ts(hT[:, :, 0:n], cand[0][:, :, 0:n], rmask[:, 0:1], ALU.mult)
            self.stt(hT[:, :, 0:n], cand[1][:, :, 0:n], rmask[:, 1:2], hT[:, :, 0:n], ALU.mult, ALU.add)
            wo_src = w_dt.ap.rearrange("(f p) d -> p f d", p=128)
            nfh = nfc // 8
            li = 0
            for dh in range(2):
                pacc = [self.ps() for _ in range(4)]
                for fh in range(nfh):
                    w = wb[li % 2]
                    li += 1
                    self.wload(w[:], w_dt, wo_src[:, fh * 8:(fh + 1) * 8, dh * 512:(dh + 1) * 512], ('wo', nfc, dh, fh))
                    for j in range(4):
                        for f8 in range(8):
                            self.mm(pacc[j][:, 0:n], w[:, f8, j * 128:(j + 1) * 128], hT[:, fh * 8 + f8, 0:n],
                                    start=(fh == 0 and f8 == 0), stop=(fh == nfh - 1 and f8 == 7))
                for j in range(4):
                    dc = dh * 4 + j
                    self.tt(x[:, dc, 0:n], pacc[j][:, 0:n], x[:, dc, 0:n], ALU.add)

        def conv_chunk(p, n, fc, hist, cwt, cbt, stage, cacc, outT):
            stg = stage
            acc = cacc
            self.cp(stg[:, 0:3], hist[:, fc, :])
            self.cp(stg[:, 3:3 + n], p[:, 0:n], eng='act')
            self.ts(acc[:, 0:n], stg[:, 0:n], cwt[:, fc, 0:1], ALU.mult)
            for j in range(1, 4):
                self.stt(acc[:, 0:n], stg[:, j:j + n], cwt[:, fc, j:j + 1], acc[:, 0:n], ALU.mult, ALU.add)
            if cbt is not None:
                self.act(outT[:, fc, 0:n], acc[:, 0:n], AF.Silu, bias=cbt[:, fc:fc + 1])
            else:
                self.act(outT[:, fc, 0:n], acc[:, 0:n], AF.Silu)
            self.cp(hist[:, fc, :], stg[:, n:n + 3])

        def mixer_ab(n, L, g1, r, g2in, g2out):
            nch = n // L
            NM = 2
            self.phase_begin(('ab', n))
            hT = A("hT", [128, 8, n], BF16)
            wb = [A("wb0", [128, 8, 512], BF16), A("wb1", [128, 8, 512], BF16)]
            wgt = A("wgt", [128, 8, 4], BF16); wdt = A("wdt", [128, 8, 8], BF16)
            qT = A("qT", [128, NM, n], BF16); kT = A("kT", [128, NM, n], BF16)
            k_tok = A("k_tok", [128, nch, 256], BF16)
            v_ext = A("v_ext", [128, nch, NM, 257], BF16)
            so = A("so", [128, nch, 512], BF16); zs = A("zs", [128, nch, 512], BF16)
            xbcT = A("xbcT", [128, 6, n], BF16)
            x_tok = A("x_tok", [128, nch, 512], BF16); bm_tok = A("bm_tok", [128, nch, 128], BF16)
            h_tok = A("h_tok", [128, 1024], BF16)
            stage = A("stage0", [128, 3 + n]); cacc = A("cacc0", [128, n])
            R = [A("row%d" % i, [16, n]) for i in range(10)]
            gT = A("gT", [128, nch, 6]); gS = A("gS", [128, nch, 32])
            decB = A("decB", [128, nch, NM]); decS = A("decS", [128, nch, 8])
            D4 = A("D4", [NM, nch, NM]); D16 = A("D16", [8, nch, 8]); Gpv = A("Gpv", [NM, nch]); dec4 = A("dec4", [NM, nch])
            PTm = A("PTm", [128, 128], BF16)
            vu = [A("vu0", [128, 257], BF16), A("vu1", [128, 257], BF16)]
            Cb = A("Cb", [128, NM, 257], BF16)
            cbm = A("cbm", [128, 128])
            seg = [A("seg0", [128, 128]), A("seg1", [128, 128])]
            MT = A("MT", [128, 8, 128], BF16)
            xd = A("xd", [128, 512], BF16); xw = A("xw", [128, 512], BF16)
            ya = A("ya", [128, 512]); yb = A("yb", [128, 512]); hraw = A("hraw", [128, 256]); junk = yb
            c1 = A("c1", [128, 1]); c2 = A("c2", [128, 1]); c3 = A("c3", [128, 1]); c4 = A("c4", [128, 1])

            dma(hn[:, :, 0:n], g1.v(g1.ap[r * 1024:(r + 1) * 1024, :].rearrange("(k p) t -> p k t", p=128)))
            win = abw.ap.rearrange("(k p) c -> p k c", p=128)
            lw = [0]

            def loadw(c0, ncol):
                w = wb[lw[0] % 2]
                lw[0] += 1
                self.wload(w[:, :, 0:ncol], abw, win[:, :, c0:c0 + ncol], ('abin', c0))
                return w

            def fm_proj(w, j, M=128):
                p = self.ps()
                for kc in range(8):
                    self.mm(p[0:M, 0:n], w[:, kc, j * 128:j * 128 + M], hn[:, kc, 0:n], start=(kc == 0), stop=(kc == 7))
                return p

            def tm_proj(w, c, c0=0, ncol=512):
                p = self.ps()
                for kc in range(8):
                    self.mm(p[0:L, 0:ncol], hn[:, kc, c * L:(c + 1) * L], w[:, kc, c0:c0 + ncol], start=(kc == 0), stop=(kc == 7))
                return p

            self.wload(wgt[:], abw, win[:, :, 1536:1540], ('abg',))
            self.wload(wdt[:], abw, win[:, :, 2820:2828], ('abdt',))
            w = loadw(0, 512)
            for h in range(NM):
                p = fm_proj(w, h)
                self.cp(qT[:, h, 0:n], p[:, 0:n], eng='act')
            for h in range(NM):
                p = fm_proj(w, NM + h)
                self.ts(kT[:, h, 0:n], p[:, 0:n], DKs, ALU.mult)
            for c in range(nch):
                p = tm_proj(w, c, 256, 256)
                self.ts(k_tok[0:L, c, :], p[0:L, 0:256], DKs, ALU.mult)
            self.memset(v_ext[:, :, :, 256:257], 1.0)
            w = loadw(512, 512)
            for c in range(nch):
                p = tm_proj(w, c)
                self.cp(v_ext[0:L, c, 0:NM, 0:256], V(p.h[0:L, 0:512].rearrange("p (a b) -> p a b", b=256), p.buf), eng='act')
            w = loadw(1024, 512)
            for c in range(nch):
                p = tm_proj(w, c)
                self.act(so[0:L, c, :], p[0:L, 0:512], AF.Sigmoid)
            w = loadw(1540, 512)
            for c in range(nch):
                p = tm_proj(w, c)
                self.act(zs[0:L, c, :], p[0:L, 0:512], AF.Silu)
            w = loadw(2052, 512)
            for j in range(4):
                p = fm_proj(w, j)
                conv_chunk(p, n, j, hist_b, cw, cb, stage, cacc, xbcT)
            w = loadw(2564, 256)
            for j in range(2):
                p = fm_proj(w, j)
                conv_chunk(p, n, 4 + j, hist_b, cw, cb, stage, cacc, xbcT)
            t1, Fn, a_, G_, em, u_, w_, tmp = R[0], R[1], R[2], R[3], R[4], R[5], R[6], R[7]
            pig = self.ps()
            for kc in range(8):
                self.mm(pig[0:NM, 0:n], wgt[:, kc, 0:NM], hn[:, kc, 0:n], start=(kc == 0), stop=(kc == 7))
            pfg = self.ps()
            for kc in range(8):
                self.mm(pfg[0:NM, 0:n], wgt[:, kc, NM:2 * NM], hn[:, kc, 0:n], start=(kc == 0), stop=(kc == 7))
            self.act(t1[0:NM, 0:n], pfg[0:NM, 0:n], AF.Exp, bias=nbf[:], scale=-1.0)
            self.act(t1[0:NM, 0:n], t1[0:NM, 0:n], AF.Ln, bias=1.0)
            self.scan(Fn[0:NM, 0:n], onesr[0:NM, 0:n], t1[0:NM, 0:n], Fc[:], ALU.mult, ALU.add)
            self.stt(a_[0:NM, 0:n], pig[0:NM, 0:n], bi[:], Fn[0:NM, 0:n], ALU.add, ALU.add)
            self.scan(G_[0:NM, 0:n], onesr[0:NM, 0:n], a_[0:NM, 0:n], Gc[:], ALU.mult, ALU.max)
            self.tt(tmp[0:NM, 0:n], Fn[0:NM, 0:n], G_[0:NM, 0:n], ALU.subtract)
            self.act(em[0:NM, 0:n], tmp[0:NM, 0:n], AF.Exp)

            def r3(t, np_):
                return t.h[0:np_, 0:n].rearrange("p (c l) -> p c l", l=L)
            gend = V(r3(G_, NM)[:, :, L - 1:L].to_broadcast([NM, nch, L]), G_.buf)
            self.tt(V(r3(tmp, NM), tmp.buf), V(r3(a_, NM), a_.buf), gend, ALU.subtract)
            self.act(u_[0:NM, 0:n], tmp[0:NM, 0:n], AF.Exp)
            self.tt(V(r3(tmp, NM), tmp.buf), V(r3(G_, NM), G_.buf), gend, ALU.subtract)
            self.act(w_[0:NM, 0:n], tmp[0:NM, 0:n], AF.Exp, scale=-1.0)
            self.cp(Gpv[0:NM, 0:1], Gc[:])
            if nch > 1:
                self.cp(V(Gpv.h[0:NM, 1:nch].unsqueeze(2), Gpv.buf), V(r3(G_, NM)[:, 0:nch - 1, L - 1:L], G_.buf))
            self.tt(V(dec4.h[0:NM, 0:nch].unsqueeze(2), dec4.buf), V(Gpv.h[0:NM, 0:nch].unsqueeze(2), Gpv.buf),
                    V(r3(G_, NM)[:, :, L - 1:L], G_.buf), ALU.subtract)
            self.act(dec4[0:NM, 0:nch], dec4[0:NM, 0:nch], AF.Exp)
            self.tt(D4[0:NM, 0:nch, :], V(dec4.h[0:NM, 0:nch].unsqueeze(2).to_broadcast([NM, nch, NM]), dec4.buf),
                    V(ident_f.h[0:NM, 0:NM].unsqueeze(1).to_broadcast([NM, nch, NM]), ident_f.buf), ALU.mult)
            p = self.ps()
            self.mm(p[:, 0:nch * NM], ones16[0:NM, :], V(D4.h[0:NM, 0:nch, :].rearrange("p c h -> p (c h)"), D4.buf))
            self.cp(V(decB.h[:, 0:nch, :].rearrange("p c h -> p (c h)"), decB.buf), p[:, 0:nch * NM])
            self.cp(Fc[:], Fn[0:NM, n - 1:n])
            self.cp(Gc[:], G_[0:NM, n - 1:n])
            p = self.ps()
            for c in range(nch):
                for qi, rt in enumerate((u_, w_, em)):
                    self.tr(p[0:L, c * 6 + qi * NM:c * 6 + qi * NM + NM], rt[0:NM, c * L:(c + 1) * L], ident_f[0:NM, 0:NM])
            self.cp(V(gT.h[0:L, 0:nch, :].rearrange("p c h -> p (c h)"), gT.buf), p[0:L, 0:nch * 6])
            dt_, ar, b_, eb, nb, e2 = R[0], R[1], R[2], R[8], R[9], R[7]
            pdt = self.ps()
            for kc in range(8):
                self.mm(pdt[0:8, 0:n], wdt[:, kc, :], hn[:, kc, 0:n], start=(kc == 0), stop=(kc == 7))
            self.act(dt_[0:8, 0:n], pdt[0:8, 0:n], AF.Exp, bias=dtb[:])
            self.act(dt_[0:8, 0:n], dt_[0:8, 0:n], AF.Ln, bias=1.0)
            self.ts(ar[0:8, 0:n], dt_[0:8, 0:n], negA[:], ALU.mult)
            for c in range(nch):
                self.scan(b_[0:8, c * L:(c + 1) * L], onesr[0:8, 0:L], ar[0:8, c * L:(c + 1) * L], 0.0, ALU.mult, ALU.add)
            self.act(eb[0:8, 0:n], b_[0:8, 0:n], AF.Exp)
            self.ts(nb[0:8, 0:n], b_[0:8, 0:n], -1.0, ALU.mult)
            bLb = V(r3(b_, 8)[:, :, L - 1:L].to_broadcast([8, nch, L]), b_.buf)
            self.tt(V(r3(e2, 8), e2.buf), bLb, V(r3(b_, 8), b_.buf), ALU.subtract)
            self.act(e2[0:8, 0:n], e2[0:8, 0:n], AF.Exp)
            self.tt(e2[0:8, 0:n], e2[0:8, 0:n], dt_[0:8, 0:n], ALU.mult)
            self.tt(D16[:, 0:nch, :], V(r3(eb, 8)[:, :, L - 1:L].to_broadcast([8, nch, 8]), eb.buf),
                    V(ident_f.h[0:8, 0:8].unsqueeze(1).to_broadcast([8, nch, 8]), ident_f.buf), ALU.mult)
            p = self.ps()
            self.mm(p[:, 0:nch * 8], ones16[0:8, :], V(D16.h[:, 0:nch, :].rearrange("p c h -> p (c h)"), D16.buf))
            self.cp(V(decS.h[:, 0:nch, :].rearrange("p c h -> p (c h)"), decS.buf), p[:, 0:nch * 8])
            p = self.ps()
            for c in range(nch):
                for qi, rt in enumerate((dt_, nb, eb, e2)):
                    self.tr(p[0:L, c * 32 + qi * 8:c * 32 + qi * 8 + 8], rt[0:8, c * L:(c + 1) * L], ident_f[0:8, 0:8])
            self.cp(V(gS.h[0:L, 0:nch, :].rearrange("p c h -> p (c h)"), gS.buf), p[0:L, 0:nch * 32])
            for c in range(nch):
                p = self.ps()
                for fc in range(4):
                    self.tr(pbv(p, slice(0, L), slice(fc * 128, (fc + 1) * 128)), xbcT[:, fc, c * L:(c + 1) * L], ident_b[:])
                self.tr(pbv(p, slice(0, L), slice(512, 640)), xbcT[:, 4, c * L:(c + 1) * L], ident_b[:])
                self.cp(x_tok[0:L, c, :], pbv(p, slice(0, L), slice(0, 512)), eng='act')
                self.cp(bm_tok[0:L, c, :], pbv(p, slice(0, L), slice(512, 640)))
            junkm = [A("junkm%d" % i, [128, 256]) for i in range(NM)]
            PTms = [A("PTm%d" % i, [128, 128], BF16) for i in range(NM)]
            hraws = [A("hraw%d" % i, [128, 256]) for i in range(NM)]
            ccols = [[A("cm%d_%d" % (i, j), [128, 1]) for j in range(3)] for i in range(NM)]

            def m_chain(c, h):
                cs, ce = c * L, (c + 1) * L
                ht = h_tok
                PTm = PTms[h]; junk = junkm[h]; hraw = hraws[h]; c1, c2, c3 = ccols[h]
                v_u = vu[h % 2]
                p1 = self.psb[2 * h]
                self.mm(p1[0:L, 0:L], kT[:, h, cs:ce], qT[:, h, cs:ce])
                yield
                self.tt(PTm[0:L, 0:L], p1[0:L, 0:L], maskT[0:L, 0:L], ALU.mult)
                yield
                self.act(v_u[0:L, :], v_ext[0:L, c, h, :], AF.Copy, scale=gT[0:L, c, h:h + 1])
                yield
                self.ts(Cx[:, h, :], Cx[:, h, :], decB[:, c, h:h + 1], ALU.mult)
                yield
                self.cp(Cb[:, h, :], Cx[:, h, :], eng='act')
                yield
                p2 = self.psb[2 * h + 1]
                self.mm(p2[0:L, 0:257], PTm[0:L, 0:L], v_u[0:L, :], start=True, stop=False)
                yield
                self.mm(p2[0:L, 0:257], qT[:, h, cs:ce], Cb[:, h, :], start=False, stop=True)
                yield
                p3 = self.psb[2 * h]
                self.mm(p3[:, 0:257], k_tok[0:L, c, h * 128:(h + 1) * 128], v_u[0:L, :])
                yield
                self.tt(Cx[:, h, :], p3[:, 0:257], Cx[:, h, :], ALU.add)
                yield
                wcol = gT[0:L, c, NM + h:NM + h + 1]
                emcol = gT[0:L, c, 2 * NM + h:2 * NM + h + 1]
                self.act(c1[0:L, :], p2[0:L, 256:257], AF.Abs, scale=wcol)
                yield
                self.tt(c1[0:L, :], c1[0:L, :], emcol, ALU.max)
                yield
                self.recip(c1[0:L, :], c1[0:L, :])
                yield
                self.tt(c2[0:L, :], c1[0:L, :], wcol, ALU.mult)
                yield
                self.act(junk[0:L, 0:256], p2[0:L, 0:256], AF.Square, scale=c2[0:L, :])
                yield
                self.rsum(c3[0:L, :], junk[0:L, 0:256])
                yield
                self.act(hraw[0:L, :], p2[0:L, 0:256], AF.Copy, scale=c2[0:L, :])
                yield
                self.act(c3[0:L, :], c3[0:L, :], AF.Sqrt, bias=EPS, scale=1.0 / 256)
                yield
                self.recip(c3[0:L, :], c3[0:L, :])
                yield
                self.stt(hraw[0:L, :], hraw[0:L, :], c3[0:L, :], gnB[0:L, h * 256:(h + 1) * 256], ALU.mult, ALU.mult)
                yield
                self.tt(ht[0:L, h * 256:(h + 1) * 256], hraw[0:L, :], so[0:L, c, h * 256:(h + 1) * 256], ALU.mult)
                yield

            def s_chain(c):
                cs, ce = c * L, (c + 1) * L
                ht = h_tok
                junk = yb
                p1 = self.psb[4]
                self.mm(p1[0:L, 0:L], xbcT[:, 4, cs:ce], xbcT[:, 5, cs:ce])
                yield
                self.tt(cbm[0:L, 0:L], p1[0:L, 0:L], maskT[0:L, 0:L], ALU.mult)
                yield
                pbb = None
                for hh in range(8):
                    j = hh % 4
                    if j == 0:
                        pbb = self.psb[5 + hh // 4]
                    self.mm(pbb[0:L, j * 128:j * 128 + L], sel16[0:8, hh, 0:L], b_[0:8, cs:ce])
                    sg = seg[hh % 2]
                    self.ts(sg[0:L, 0:L], pbb[0:L, j * 128:j * 128 + L], gS[0:L, c, 8 + hh:9 + hh], ALU.add, 0.0, ALU.min)
                    self.act(sg[0:L, 0:L], sg[0:L, 0:L], AF.Exp)
                    self.tt(MT[0:L, hh, 0:L], sg[0:L, 0:L], cbm[0:L, 0:L], ALU.mult)

                def v3(t, ap):
                    return V(ap.rearrange("p (h e) -> p h e", e=64), t.buf)
                xg = x_tok.h[0:L, c, :]
                self.tt(v3(xd, xd.h[0:L, :]), v3(x_tok, xg),
                        V(gS.h[0:L, c, 0:8].unsqueeze(2).to_broadcast([L, 8, 64]), gS.buf), ALU.mult)
                self.tt(v3(xw, xw.h[0:L, :]), v3(x_tok, xg),
                        V(gS.h[0:L, c, 24:32].unsqueeze(2).to_broadcast([L, 8, 64]), gS.buf), ALU.mult)
                pY1 = self.psb[7]
                for hh in range(8):
                    self.mm(pY1[0:L, hh * 64:(hh + 1) * 64], MT[0:L, hh, 0:L], xd[0:L, hh * 64:(hh + 1) * 64])
                pY2 = self.psb[4]
                self.mm(pY2[0:L, 0:512], xbcT[:, 5, cs:ce], V(Sb.h[:, :, :].rearrange("p h e -> p (h e)"), Sb.buf))
                yield
                self.tt(v3(ya, ya.h[0:L, :]), v3(pY2, pY2.h[0:L, 0:512]),
                        V(gS.h[0:L, c, 16:24].unsqueeze(2).to_broadcast([L, 8, 64]), gS.buf), ALU.mult)
                self.tt(ya[0:L, :], pY1[0:L, 0:512], ya[0:L, :], ALU.add)
                yield
                self.tt(v3(yb, yb.h[0:L, :]), v3(x_tok, xg),
                        V(dskB.h[0:L, 0:8].unsqueeze(2).to_broadcast([L, 8, 64]), dskB.buf), ALU.mult)
                self.tt(ya[0:L, :], ya[0:L, :], yb[0:L, :], ALU.add)
                yield
                self.tt(ya[0:L, :], ya[0:L, :], zs[0:L, c, :], ALU.mult)
                yield
                self.act(junk[0:L, :], ya[0:L, :], AF.Square)
                yield
                self.rsum(c4[0:L, :], junk[0:L, :])
                yield
                self.act(c4[0:L, :], c4[0:L, :], AF.Sqrt, bias=EPS, scale=1.0 / 512)
                yield
                self.recip(c4[0:L, :], c4[0:L, :])
                yield
                self.stt(ht[0:L, 512:1024], ya[0:L, :], c4[0:L, :], snB[0:L, :], ALU.mult, ALU.mult)
                yield
                pS = self.psb[5]
                self.mm(pS[:, 0:512], bm_tok[0:L, c, :], xw[0:L, :])
                yield
                self.tt(ST[:, :, :], ST[:, :, :],
                        V(decS.h[:, c, 0:8].unsqueeze(2).to_broadcast([128, 8, 64]), decS.buf), ALU.mult)
                stf = V(ST.h[:, :, :].rearrange("p h e -> p (h e)"), ST.buf)
                self.tt(stf, pS[:, 0:512], stf, ALU.add)
                yield
                self.cp(Sb[:, :, :], ST[:, :, :], eng='act')
                yield

            def interleave(gens):
                gens = list(gens)
                while gens:
                    for g in list(gens):
                        try:
                            next(g)
                        except StopIteration:
                            gens.remove(g)

            for c in range(nch):
                cs, ce = c * L, (c + 1) * L
                ht = h_tok
                interleave([m_chain(c, h) for h in range(NM)] + [s_chain(c)])
                p = self.ps()
                for f8 in range(8):
                    self.tr(pbv(p, slice(0, 128), slice(f8 * 128, f8 * 128 + L)), ht[0:L, f8 * 128:(f8 + 1) * 128],
                            ident_b[0:L, 0:L])
                self.cp(hT[:, 0:8, cs:ce], V(p.hb[:, 0:1024].rearrange("p (f t) -> p f t", t=128)[:, :, 0:L], p.buf), eng='act')
            dma(g2in.v(g2in.ap.rearrange("(f p) t -> p f t", p=128)), hT[:, :, 0:n])
            allgather(g2in, g2out)

        def mixer_c(n, L, g3, r, g4in, g4out):
            nch = n // L
            nsq = {128: 6, 16: 3}[L]
            NH = 4
            self.phase_begin(('c', n))
            hT = A("hT", [128, NH, n], BF16)
            wb = [A("wb0", [128, 8, 512], BF16), A("wb1", [128, 8, 512], BF16)]
            wba = A("wba", [128, 8, 8], BF16)
            qkvT = A("qkvT", [128, 12, n], BF16)
            zs = A("zs", [128, nch, 512], BF16)
            k_tok = A("k_tok", [128, nch, 512], BF16); v_tok = A("v_tok", [128, nch, 512], BF16)
            h_tok = A("h_tok", [128, 512], BF16)
            stage = A("stage0", [128, 3 + n]); cacc = A("cacc0", [128, n])
            R = [A("row%d" % i, [16, n]) for i in range(7)]
            gC = A("gC", [128, nch, 24])
            decC = A("decC", [128, nch, NH]); D8 = A("D8", [NH, nch, NH])
            sqh = A("sqh0", [128, n], BF16); rst = A("rst0", [128, n])
            Xs = [A("X%d" % i, [128, 128]) for i in range(NH)]
            XTs = [A("XT%d" % i, [128, 128]) for i in range(NH)]
            TTs = [A("TT%d" % i, [128, 128]) for i in range(NH)]
            KKs = [A("KK%d" % i, [128, 128]) for i in range(NH)]
            dTs_ = [A("dT%d" % i, [128, 128]) for i in range(NH)]
            tmpm = [A("tmpm%d" % i, [128, 128]) for i in range(NH)]
            R1 = tmpm
            R2 = KKs
            U0s = [[A("U0s_%d_%d" % (c, h), [128, 128]) for h in range(NH)] for c in range(min(2, nch))]
            WTb = [[A("WTb_%d_%d" % (c, h), [128, 128], BF16) for h in range(NH)] for c in range(min(2, nch))]
            QKd = [[A("QKd_%d_%d" % (c, h), [128, 128], BF16) for h in range(NH)] for c in range(min(2, nch))]
            ub = [A("ub%d" % i, [128, 128], BF16) for i in range(NH)]
            kw = [A("kw%d" % i, [128, 128], BF16) for i in range(NH)]
            o1s = [A("o1s%d" % i, [128, 128]) for i in range(NH)]
            oo = [A("oo%d" % i, [128, 128]) for i in range(NH)]
            jk = o1s
            cc = [A("cc%d" % i, [128, 1]) for i in range(NH)]

            dma(hn[:, :, 0:n], g3.v(g3.ap[r * 1024:(r + 1) * 1024, :].rearrange("(k p) t -> p k t", p=128)))
            win = gw.ap.rearrange("(k p) c -> p k c", p=128)
            lw = [0]

            def loadw(c0, ncol):
                w = wb[lw[0] % 2]
                lw[0] += 1
                self.wload(w[:, :, 0:ncol], gw, win[:, :, c0:c0 + ncol], ('gin', c0))
                return w

            self.wload(wba[:], gw, win[:, :, 2048:2056], ('gba',))
            for g3_ in range(3):
                w = loadw(g3_ * 512, 512)
                for j in range(4):
                    fc = g3_ * 4 + j
                    p = self.ps()
                    for kc in range(8):
                        self.mm(p[:, 0:n], w[:, kc, j * 128:(j + 1) * 128], hn[:, kc, 0:n], start=(kc == 0), stop=(kc == 7))
                    conv_chunk(p, n, fc, hist_c, gcw, None, stage, cacc, qkvT)
            w = loadw(1536, 512)
            for c in range(nch):
                p = self.ps()
                for kc in range(8):
                    self.mm(p[0:L, 0:512], hn[:, kc, c * L:(c + 1) * L], w[:, kc, 0:512], start=(kc == 0), stop=(kc == 7))
                self.act(zs[0:L, c, :], p[0:L, 0:512], AF.Silu)
            for fc in range(2 * NH):
                self.act(sqh[:, 0:n], qkvT[:, fc, 0:n], AF.Square)
                p = self.ps()
                self.mm(p[:, 0:n], ones_b[:], sqh[:, 0:n])
                self.act(rst[:, 0:n], p[:, 0:n], AF.Sqrt, bias=EPS, scale=float(D))
                self.recip(rst[:, 0:n], rst[:, 0:n])
                if fc < NH:
                    self.stt(qkvT[:, fc, 0:n], qkvT[:, fc, 0:n], DKs, rst[:, 0:n], ALU.mult, ALU.mult)
                else:
                    self.tt(qkvT[:, fc, 0:n], qkvT[:, fc, 0:n], rst[:, 0:n], ALU.mult)
            beta, nbeta, sp_, gam, egam, ngam, e2 = R
            g_ = sp_
            pb_ = self.ps()
            for kc in range(8):
                self.mm(pb_[0:NH, 0:n], wba[:, kc, 0:NH], hn[:, kc, 0:n], start=(kc == 0), stop=(kc == 7))
            pa_ = self.ps()
            for kc in range(8):
                self.mm(pa_[0:NH, 0:n], wba[:, kc, NH:2 * NH], hn[:, kc, 0:n], start=(kc == 0), stop=(kc == 7))
            self.act(beta[0:NH, 0:n], pb_[0:NH, 0:n], AF.Sigmoid)
            self.ts(nbeta[0:NH, 0:n], beta[0:NH, 0:n], -1.0, ALU.mult)
            self.act(sp_[0:NH, 0:n], pa_[0:NH, 0:n], AF.Exp, bias=gdtb[:])
            self.act(sp_[0:NH, 0:n], sp_[0:NH, 0:n], AF.Ln, bias=1.0)
            self.ts(g_[0:NH, 0:n], sp_[0:NH, 0:n], gnegA[:], ALU.mult)
            for c in range(nch):
                self.scan(gam[0:NH, c * L:(c + 1) * L], onesr[0:NH, 0:L], g_[0:NH, c * L:(c + 1) * L], 0.0, ALU.mult, ALU.add)
            self.act(egam[0:NH, 0:n], gam[0:NH, 0:n], AF.Exp)
            self.ts(ngam[0:NH, 0:n], gam[0:NH, 0:n], -1.0, ALU.mult)

            def r3(t, np_):
                return t.h[0:np_, 0:n].rearrange("p (c l) -> p c l", l=L)
            gLb = V(r3(gam, NH)[:, :, L - 1:L].to_broadcast([NH, nch, L]), gam.buf)
            self.tt(V(r3(e2, NH), e2.buf), gLb, V(r3(gam, NH), gam.buf), ALU.subtract)
            self.act(e2[0:NH, 0:n], e2[0:NH, 0:n], AF.Exp)
            self.tt(sp_[0:NH, 0:n], beta[0:NH, 0:n], egam[0:NH, 0:n], ALU.mult)
            self.tt(D8[:, 0:nch, :], V(r3(egam, NH)[:, :, L - 1:L].to_broadcast([NH, nch, NH]), egam.buf),
                    V(ident_f.h[0:NH, 0:NH].unsqueeze(1).to_broadcast([NH, nch, NH]), ident_f.buf), ALU.mult)
            p = self.ps()
            self.mm(p[:, 0:nch * NH], ones16[0:NH, :], V(D8.h[:, 0:nch, :].rearrange("p c h -> p (c h)"), D8.buf))
            self.cp(V(decC.h[:, 0:nch, :].rearrange("p c h -> p (c h)"), decC.buf), p[:, 0:nch * NH])
            p = self.ps()
            for c in range(nch):
                for qi, rt in enumerate((beta, ngam, egam, e2, sp_, nbeta)):
                    self.tr(p[0:L, c * 24 + qi * NH:c * 24 + qi * NH + NH], rt[0:NH, c * L:(c + 1) * L], ident_f[0:NH, 0:NH])
            self.cp(V(gC.h[0:L, 0:nch, :].rearrange("p c h -> p (c h)"), gC.buf), p[0:L, 0:nch * 24])
            for c in range(nch):
                p = self.ps()
                for fc in range(NH):
                    self.tr(pbv(p, slice(0, L), slice(fc * 128, (fc + 1) * 128)), qkvT[:, NH + fc, c * L:(c + 1) * L], ident_b[:])
                    self.tr(pbv(p, slice(0, L), slice(512 + fc * 128, 512 + (fc + 1) * 128)), qkvT[:, 2 * NH + fc, c * L:(c + 1) * L], ident_b[:])
                self.cp(k_tok[0:L, c, :], pbv(p, slice(0, L), slice(0, 512)), eng='act')
                self.cp(v_tok[0:L, c, :], pbv(p, slice(0, L), slice(512, 1024)))

            def phaseA(c):
                cs, ce = c * L, (c + 1) * L
                hs = list(range(NH))
                for i, h in enumerate(hs):
                    pk = self.ps()
                    self.mm(pk[0:L, 0:L], qkvT[:, NH + h, cs:ce], qkvT[:, NH + h, cs:ce])
                    self.cp(KKs[i][0:L, 0:L], pk[0:L, 0:L], eng='act')
                    pbb = self.ps()
                    self.mm(pbb[0:L, 0:L], sel16[0:NH, h, 0:L], gam[0:NH, cs:ce])
                    self.ts(tmpm[i][0:L, 0:L], pbb[0:L, 0:L], gC[0:L, c, NH + h:NH + h + 1], ALU.add, 0.0, ALU.min)
                    self.act(tmpm[i][0:L, 0:L], tmpm[i][0:L, 0:L], AF.Exp)
                    self.tt(dTs_[i][0:L, 0:L], tmpm[i][0:L, 0:L], maskS[0:L, 0:L], ALU.mult)
                    self.tt(tmpm[i][0:L, 0:L], tmpm[i][0:L, 0:L], maskT[0:L, 0:L], ALU.mult)
                    pq = self.ps()
                    self.mm(pq[0:L, 0:L], qkvT[:, NH + h, cs:ce], qkvT[:, h, cs:ce])
                    self.tt(QKd[c % 2][h][0:L, 0:L], pq[0:L, 0:L], tmpm[i][0:L, 0:L], ALU.mult)
                for i, h in enumerate(hs):
                    pd = self.ps()
                    self.tr(pd[0:L, 0:L], dTs_[i][0:L, 0:L], ident_f[0:L, 0:L])
                    self.stt(Xs[i][0:L, 0:L], pd[0:L, 0:L], gC[0:L, c, 5 * NH + h:5 * NH + h + 1], KKs[i][0:L, 0:L], ALU.mult, ALU.mult)
                for i, h in enumerate(hs):
                    px = self.ps()
                    self.tr(px[0:L, 0:L], Xs[i][0:L, 0:L], ident_f[0:L, 0:L])
                    self.cp(XTs[i][0:L, 0:L], px[0:L, 0:L], eng='act')
                    self.tt(TTs[i][0:L, 0:L], px[0:L, 0:L], ident_f[0:L, 0:L], ALU.add)
                for j in range(nsq):
                    last = (j == nsq - 1)
                    for i, h in enumerate(hs):
                        pa2 = self.ps()
                        self.mm(pa2[0:L, 0:L], XTs[i][0:L, 0:L], Xs[i][0:L, 0:L])
                        if not last:
                            pb2 = self.ps()
                            self.mm(pb2[0:L, 0:L], Xs[i][0:L, 0:L], XTs[i][0:L, 0:L])
                        self.cp(Xs[i][0:L, 0:L], pa2[0:L, 0:L], eng='act')
                        if not last:
                            self.cp(XTs[i][0:L, 0:L], pb2[0:L, 0:L], eng='act')
                    for i, h in enumerate(hs):
                        pc = self.ps()
                        self.mm(pc[0:L, 0:L], Xs[i][0:L, 0:L], TTs[i][0:L, 0:L])
                        self.tt(TTs[i][0:L, 0:L], pc[0:L, 0:L], TTs[i][0:L, 0:L], ALU.add)
                for i, h in enumerate(hs):
                    self.act(R1[i][0:L, :], v_tok[0:L, c, h * 128:(h + 1) * 128], AF.Copy, scale=gC[0:L, c, h:h + 1])
                    self.act(R2[i][0:L, :], k_tok[0:L, c, h * 128:(h + 1) * 128], AF.Copy, scale=gC[0:L, c, 4 * NH + h:4 * NH + h + 1])
                    pu = self.ps()
                    self.mm(pu[0:L, 0:128], TTs[i][0:L, 0:L], R1[i][0:L, :])
                    self.cp(U0s[c % 2][h][0:L, :], pu[0:L, 0:128], eng='act')
                    pw = self.ps()
                    self.mm(pw[:, 0:L], R2[i][0:L, :], TTs[i][0:L, 0:L])
                    self.cp(WTb[c % 2][h][:, 0:L], pw[:, 0:L])

            def phaseB(c):
                cs, ce = c * L, (c + 1) * L
                hs = list(range(NH))
                for i, h in enumerate(hs):
                    pws = self.ps()
                    self.mm(pws[0:L, 0:128], WTb[c % 2][h][:, 0:L], SGb[:, h, :])
                    self.tt(ub[i][0:L, :], U0s[c % 2][h][0:L, :], pws[0:L, 0:128], ALU.subtract)
                    self.act(kw[i][0:L, :], k_tok[0:L, c, h * 128:(h + 1) * 128], AF.Copy, scale=gC[0:L, c, 3 * NH + h:3 * NH + h + 1])
                for i, h in enumerate(hs):
                    po1 = self.ps()
                    self.mm(po1[0:L, 0:128], QKd[c % 2][h][0:L, 0:L], ub[i][0:L, :])
                    po2 = self.ps()
                    self.mm(po2[0:L, 0:128], qkvT[:, h, cs:ce], SGb[:, h, :])
                    self.cp(o1s[i][0:L, :], po1[0:L, 0:128], eng='act')
                    self.stt(oo[i][0:L, :], po2[0:L, 0:128], gC[0:L, c, 2 * NH + h:2 * NH + h + 1], o1s[i][0:L, :], ALU.mult, ALU.add)
                for i, h in enumerate(hs):
                    pS = self.ps()
                    self.mm(pS[:, 0:128], kw[i][0:L, :], ub[i][0:L, :])
                    self.stt(SG[:, h, :], SG[:, h, :], decC[:, c, h:h + 1], pS[:, 0:128], ALU.mult, ALU.add)
                    self.cp(SGb[:, h, :], SG[:, h, :], eng='act')
                for i, h in enumerate(hs):
                    self.act(jk[i][0:L, :], oo[i][0:L, :], AF.Square)
                    self.rsum(cc[i][0:L, :], jk[i][0:L, :])
                    self.act(cc[i][0:L, :], cc[i][0:L, :], AF.Sqrt, bias=EPS, scale=1.0 / 128)
                    self.recip(cc[i][0:L, :], cc[i][0:L, :])
                    self.stt(oo[i][0:L, :], oo[i][0:L, :], cc[i][0:L, :], gdnB[0:L, :], ALU.mult, ALU.mult)
                    self.tt(h_tok[0:L, h * 128:(h + 1) * 128], oo[i][0:L, :], zs[0:L, c, h * 128:(h + 1) * 128], ALU.mult)
                p = self.ps()
                for f8 in range(NH):
                    self.tr(pbv(p, slice(0, 128), slice(f8 * 128, f8 * 128 + L)), h_tok[0:L, f8 * 128:(f8 + 1) * 128],
                            ident_b[0:L, 0:L])
                self.cp(hT[:, 0:NH, cs:ce], V(p.hb[:, 0:512].rearrange("p (f t) -> p f t", t=128)[:, :, 0:L], p.buf), eng='act')

            phaseA(0)
            for c in range(nch):
                if c + 1 < nch:
                    phaseA(c + 1)
                phaseB(c)
            dma(g4in.v(g4in.ap.rearrange("(f p) t -> p f t", p=128)), hT[:, :, 0:n])
            allgather(g4in, g4out)

        xsrc = xT.ap.rearrange("(k p) t -> p k t", p=128)
        ydst = yT.ap.rearrange("(k p) t -> p k t", p=128)
        for b, (t0, n) in enumerate(own):
            dma(xb[b][:, :, 0:n], xT.v(xsrc[:, :, t0:t0 + n]))
        ffn_parts = [[(xb[b], TB)] for b in range(3)] + [[(xb[3], TB), (xb[4], DEC_SEQ)]]
        for pi_, parts in enumerate(ffn_parts):
            ffn(0, 0, parts)
            for (xx, n) in parts:
                b = 4 if n == DEC_SEQ else pi_
                hn_exchange(1, n, xx, G1in[b], G1out[b])
        seqs = [(r * 4 + i, TB, 128, G1out[i], r) for r in range(2) for i in range(4)] + \
               [(8 + r, DEC_SEQ, DEC_SEQ, G1out[4], r) for r in range(2)]

        def ab_zero():
            self.memset(Cx[:], 0.0); self.memset(ST[:], 0.0); self.memset(hist_b[:], 0.0)
            self.memset(Fc[:], 0.0); self.memset(Gc[:], 0.0)
            self.cp(Sb[:], ST[:])

        def ab_store(k):
            self.tt(mo[:], Gc[:], Fc[:], ALU.subtract)
            dma(o_Cx[k].v(), Cx[:], is_output=True); dma(o_m[k].v(), mo[:], is_output=True)
            dma(o_ST[k].v(), ST[:], is_output=True); dma(o_hb[k].v(), hist_b[:], is_output=True)

        def ffn_items(l, i):
            groups = [(g * 256, 256) for g in range(11)]
            wg_src = w_gate.ap[l, i].rearrange("(k p) f -> p k f", p=128)
            wu_src = w_up.ap[l, i].rearrange("(k p) f -> p k f", p=128)
            wd_src = w_down.ap[l, i].rearrange("(f p) d -> p f d", p=128)
            it = []
            for gi, (c0, ncol) in enumerate(groups):
                it.append(([128, 8, ncol], w_gate, wg_src[:, :, c0:c0 + ncol], ('wg', l, i, gi)))
                it.append(([128, 8, ncol], w_up, wu_src[:, :, c0:c0 + ncol], ('wu', l, i, gi)))
                if gi == 2:
                    for q in range(2):
                        it.append(([128, 11, D], w_down, wd_src[:, q * 11:(q + 1) * 11, :], ('wd', l, i, q)))
            return it
        wo_ab = ab_w_out.ap.rearrange("(f p) d -> p f d", p=128)
        pf_B = [([128, 8, 512], ab_w_out, wo_ab[:, fh * 8:(fh + 1) * 8, dh * 512:(dh + 1) * 512], ('wo', 16, dh, fh))
                for dh in range(2) for fh in range(2)] + ffn_items(0, 1) + ffn_items(1, 0)
        gwin = gw.ap.rearrange("(k p) c -> p k c", p=128)
        wo_g = gdn_w_out.ap.rearrange("(f p) d -> p f d", p=128)
        pf_C = [([128, 8, 8], gw, gwin[:, :, 2048:2056], ('gba',))] + \
               [([128, 8, 512], gw, gwin[:, :, c0:c0 + 512], ('gin', c0)) for c0 in (0, 512, 1024, 1536)] + \
               [([128, 8, 512], gdn_w_out, wo_g[:, 0:8, dh * 512:(dh + 1) * 512], ('wo', 8, dh, 0)) for dh in range(2)]
        pf_D = ffn_items(1, 1)

        def pf_emit(lst, cnt):
            for _ in range(cnt):
                if lst:
                    self.wprefetch(*lst.pop(0))

        ab_zero()
        for (k, n, L, g1, r) in seqs:
            if k >= 8:
                j = k - 8
                if j == 0:
                    ab_store(2)
                dma(Cx[:], s_Cx[j].v()); dma(ST[:], s_ST[j].v()); dma(hist_b[:], s_hb[j].v()); dma(Gc[:], s_m[j].v())
                self.memset(Fc[:], 0.0)
                self.cp(Sb[:], ST[:])
            mixer_ab(n, L, g1, r, G2in[k], G2out[k])
            pf_emit(pf_B, 7)
            if k >= 8:
                ab_store(k - 8)
        for b, (t0, n) in enumerate(own):
            gouts = (G2out[b], G2out[4 + b]) if b < 4 else (G2out[8], G2out[9])
            out_proj_sel(ab_w_out, 16, gouts, n, xb[b], 'ab')
            pf_emit(pf_C, 2)
        for parts in ffn_parts:
            ffn(0, 1, parts)
        for pi_, parts in enumerate(ffn_parts):
            ffn(1, 0, parts)
            for (xx, n) in parts:
                b = 4 if n == DEC_SEQ else pi_
                hn_exchange(4, n, xx, G3in[b], G3out[b])
        seqs_c = [(r * 4 + i, TB, 128, G3out[i], r) for r in range(2) for i in range(4)] + \
                 [(8 + r, DEC_SEQ, DEC_SEQ, G3out[4], r) for r in range(2)]

        def c_store(k):
            dma(o_SG[k].v(), SG[:], is_output=True); dma(o_hc[k].v(), hist_c[:], is_output=True)

        self.memset(SG[:], 0.0); self.memset(hist_c[:], 0.0)
        self.cp(SGb[:], SG[:])
        for (k, n, L, g3, r) in seqs_c:
            if k >= 8:
                j = k - 8
                if j == 0:
                    c_store(2)
                dma(SG[:], s_SG[j].v()); dma(hist_c[:], s_hc[j].v())
                self.cp(SGb[:], SG[:])
            mixer_c(n, L, g3, r, G4in[k], G4out[k])
            pf_emit(pf_D, 4)
            if k >= 8:
                c_store(k - 8)
        for b, (t0, n) in enumerate(own):
            gouts = (G4out[b], G4out[4 + b]) if b < 4 else (G4out[8], G4out[9])
            out_proj_sel(gdn_w_out, 8, gouts, n, xb[b], 'gdn')
        for parts in ffn_parts:
            ffn(1, 1, parts)
        for b, (t0, n) in enumerate(own):
            self.phase_begin(('fin',))
            yo = A("yo", [128, 8, TB]); sq = A("sq", [128, 8, TB], BF16)
            rmsnorm(n, lambda kc: gfcol[:, kc:kc + 1], yo, sq, xb[b])
            dma(yT.v(ydst[:, :, t0:t0 + n]), yo[:, :, 0:n], is_output=True)
        S.finish()
        S.emit()
        return nc


_NC_CACHE = {}


def _own_ch_ssd(r):
    return np.concatenate([np.arange(r * 512, (r + 1) * 512), 1024 + np.arange(r * 128, (r + 1) * 128),
                           1280 + np.arange(r * 128, (r + 1) * 128)])


def _own_ch_gdn(r):
    return np.concatenate([np.arange(r * 512, (r + 1) * 512), 1024 + np.arange(r * 512, (r + 1) * 512),
                           2048 + np.arange(r * 512, (r + 1) * 512)])


def _host_inputs(d, p, r):
    f = np.ascontiguousarray
    x = np.concatenate([d['x_prompt'][p, r * 2048:(r + 1) * 2048], d['x_sample'][2 * p + r]], 0)
    wi = d['ab_w_in']
    abw = np.concatenate([wi[:, r * 256:(r + 1) * 256], wi[:, 512 + r * 256:512 + (r + 1) * 256],
                          wi[:, 1024 + r * 512:1024 + (r + 1) * 512], wi[:, 2048 + r * 512:2048 + (r + 1) * 512],
                          wi[:, 3072 + 2 * r:3074 + 2 * r], wi[:, 3076 + 2 * r:3078 + 2 * r],
                          wi[:, 3080 + r * 512:3080 + (r + 1) * 512], wi[:, 4104 + r * 512:4104 + (r + 1) * 512],
                          wi[:, 5128 + r * 128:5128 + (r + 1) * 128], wi[:, 5384 + r * 128:5384 + (r + 1) * 128],
                          wi[:, 5640 + r * 8:5640 + (r + 1) * 8]], axis=1)
    gi = d['gdn_w_in']
    gw = np.concatenate([gi[:, r * 512:(r + 1) * 512], gi[:, 1024 + r * 512:1024 + (r + 1) * 512],
                         gi[:, 2048 + r * 512:2048 + (r + 1) * 512], gi[:, 3072 + r * 512:3072 + (r + 1) * 512],
                         gi[:, 4096 + 4 * r:4100 + 4 * r], gi[:, 4104 + 4 * r:4108 + 4 * r]], axis=1)
    cs, cg = _own_ch_ssd(r), _own_ch_gdn(r)
    rm = np.zeros((128, 2), np.float32)
    rm[:, r] = 1.0
    ins = {
        "xT": f(x.T), "ffn_w_gate": d['ffn_w_gate'], "ffn_w_up": d['ffn_w_up'], "ffn_w_down": d['ffn_w_down'],
        "g_col": f(d['norm_g'].reshape(6, 8, 128).transpose(2, 0, 1)),
        "gf_col": f(d['norm_f'].reshape(8, 128).T), "rmask": rm,
        "abw": f(abw), "ab_w_out": d['ab_w_out'],
        "b_i": f(d['mlstm_b_i'][2 * r:2 * r + 2].reshape(2, 1)), "b_f": f(d['mlstm_b_f'][2 * r:2 * r + 2].reshape(2, 1)),
        "mlstm_norm_b": f(np.broadcast_to(d['mlstm_norm'][2 * r:2 * r + 2].reshape(1, 512), (128, 512))),
        "ssd_cw": f(d['ssd_conv_w'][:, cs].reshape(4, 6, 128).transpose(2, 1, 0)),
        "ssd_cb": f(d['ssd_conv_b'][cs].reshape(6, 128).T),
        "ssd_dtb": f(d['ssd_dt_bias'][r * 8:(r + 1) * 8].reshape(8, 1)),
        "ssd_alog": f(d['ssd_a_log'][r * 8:(r + 1) * 8].reshape(8, 1)),
        "ssd_d_b": f(np.broadcast_to(d['ssd_d'][r * 8:(r + 1) * 8].reshape(1, 8), (128, 8))),
        "ssd_norm_b": f(np.broadcast_to(d['ssd_norm'][r * 512:(r + 1) * 512].reshape(1, 512), (128, 512))),
        "gw": f(gw), "gdn_w_out": d['gdn_w_out'],
        "gdn_cw": f(d['gdn_conv_w'][:, cg].reshape(4, 12, 128).transpose(2, 1, 0)),
        "gdn_dtb": f(d['gdn_dt_bias'][4 * r:4 * r + 4].reshape(4, 1)),
        "gdn_alog": f(d['gdn_a_log'][4 * r:4 * r + 4].reshape(4, 1)),
        "gdn_norm_b": f(np.broadcast_to(d['gdn_norm'].reshape(1, 128), (128, 128))),
    }
    for j in range(2):
        s = 2 * p + j
        ins["s_Cx%d" % j] = f(np.concatenate([d['state_mlstm_C'][s, 2 * r:2 * r + 2].transpose(1, 0, 2),
                                              d['state_mlstm_n'][s, 2 * r:2 * r + 2].T[:, :, None]], 2))
        ins["s_m%d" % j] = f(d['state_mlstm_m'][s, 2 * r:2 * r + 2].reshape(2, 1))
        ins["s_ST%d" % j] = f(d['state_ssd'][s, r * 8:(r + 1) * 8].transpose(2, 0, 1))
        ins["s_hb%d" % j] = f(d['cache_ssd_conv'][s][:, cs].reshape(3, 6, 128).transpose(2, 1, 0))
        ins["s_SG%d" % j] = f(d['state_gdn'][s, 4 * r:4 * r + 4].transpose(1, 0, 2))
        ins["s_hc%d" % j] = f(d['cache_gdn_conv'][s][:, cg].reshape(3, 12, 128).transpose(2, 1, 0))
    return ins


def _assemble(R, npair):
    f32 = lambda a: np.ascontiguousarray(np.asarray(a, dtype=np.float32))
    y_prompt = np.zeros((npair, 4096, 1024), np.float32)
    y_sample = np.zeros((2 * npair, 16, 1024), np.float32)

    def alloc(n):
        return [np.zeros((n, 4, 128, 256), np.float32), np.zeros((n, 4, 128), np.float32), np.zeros((n, 4), np.float32),
                np.zeros((n, 16, 64, 128), np.float32), np.zeros((n, 3, 1536), np.float32),
                np.zeros((n, 8, 128, 128), np.float32), np.zeros((n, 3, 3072), np.float32)]
    PS, SS = alloc(npair), alloc(2 * npair)

    def put(dst, idx, r, res, k):
        cs, cg = _own_ch_ssd(r), _own_ch_gdn(r)
        Cx = np.asarray(res["o_Cx%d" % k], np.float32)
        dst[0][idx, 2 * r:2 * r + 2] = Cx[:, :, :256].transpose(1, 0, 2)
        dst[1][idx, 2 * r:2 * r + 2] = Cx[:, :, 256].T
        dst[2][idx, 2 * r:2 * r + 2] = np.asarray(res["o_m%d" % k], np.float32)[:, 0]
        dst[3][idx, r * 8:(r + 1) * 8] = np.asarray(res["o_ST%d" % k], np.float32).transpose(1, 2, 0)
        dst[4][idx][:, cs] = np.asarray(res["o_hb%d" % k], np.float32).transpose(2, 1, 0).reshape(3, 768)
        dst[5][idx, 4 * r:4 * r + 4] = np.asarray(res["o_SG%d" % k], np.float32).transpose(1, 0, 2)
        dst[6][idx][:, cg] = np.asarray(res["o_hc%d" % k], np.float32).transpose(2, 1, 0).reshape(3, 1536)

    for p in range(npair):
        for r in range(2):
            res = R[2 * p + r]
            yT = np.asarray(res['yT'], np.float32)
            y_prompt[p, r * 2048:(r + 1) * 2048] = yT[:, :2048].T
            y_sample[2 * p + r] = yT[:, 2048:].T
            put(PS, p, r, res, 2)
            for j in range(2):
                put(SS, 2 * p + j, r, res, j)
    return tuple([y_prompt, y_sample] + PS + SS)


def kernel(**inputs):
    d = {k: np.asarray(v, dtype=np.float32) for k, v in inputs.items()}
    ncores = 8
    if 'nc' not in _NC_CACHE:
        B = Builder(8)
        _NC_CACHE['nc'] = (B.build(), set(B.din))
    nc, din = _NC_CACHE['nc']
    in_maps = []
    for c in range(ncores):
        hi = _host_inputs(d, c // 2, c % 2)
        in_maps.append({k: v for k, v in hi.items() if k in din})
    res = run_bass_kernel_spmd(nc, in_maps, core_ids=list(range(ncores)))
    return _assemble(list(res.results), 4)
```
